# Optimizing a Trainium2 kernel written in Bass

```python
import jax, jax.numpy as jnp
from jax import lax
import numpy as np

D_MODEL = 1024
BATCH = 2
SEQ = 8192
DEPTH = 1
DEC_BATCH = 128
DEC_SEQ = 1
PAST_LEN = 2048
PAGE_SIZE = 128

N_HEADS = 16
HEAD_DIM = 64
KV_HEADS = 4
Q_PER_KV = N_HEADS // KV_HEADS
ATTN_SCALE = HEAD_DIM ** -0.5
IDX_HEADS = 8
IDX_DIM = 64
IDX_SCALE = IDX_DIM ** -0.5
IDX_W_SCALE = IDX_HEADS ** -0.5
TOPK_MAX = 256
QBLK = 128
D_SGU = D_MODEL
SGU_GROUPS = 8
SGU_CH = D_SGU // SGU_GROUPS
CHUNK = 128
D_FF = 4 * D_MODEL
PLE_DIM = 256
EPS = 1e-6
IN_SPLITS = (N_HEADS * HEAD_DIM, KV_HEADS * HEAD_DIM, KV_HEADS * HEAD_DIM,
             IDX_HEADS * IDX_DIM, IDX_DIM, IDX_HEADS, D_SGU, D_SGU, D_MODEL, D_MODEL)
IN_WIDTH = sum(IN_SPLITS)

kernel_name = "dsa_gmlp_gated_hybrid_step"


def rmsnorm(x, g):
    xf = x.astype(jnp.float32)
    y = xf * lax.rsqrt(jnp.mean(xf * xf, axis=-1, keepdims=True) + EPS)
    return (y * g.astype(jnp.float32)).astype(x.dtype)


def project_in(x, g_mix, w_in, g_q, g_k):
    B, T, _ = x.shape
    z = rmsnorm(x, g_mix) @ w_in
    pieces = []
    start = 0
    for n in IN_SPLITS:
        pieces.append(z[..., start:start + n])
        start += n
    q, k, v, qi, ki, wi, u, vb, ga, gb = pieces
    q = rmsnorm(q.reshape(B, T, N_HEADS, HEAD_DIM), g_q) * ATTN_SCALE
    k = rmsnorm(k.reshape(B, T, KV_HEADS, HEAD_DIM), g_k)
    v = v.reshape(B, T, KV_HEADS, HEAD_DIM)
    qi = qi.reshape(B, T, IDX_HEADS, IDX_DIM) * IDX_SCALE
    wi = wi * IDX_W_SCALE
    u = jax.nn.gelu(u)
    vb = jax.nn.gelu(vb)
    return q, k, v, qi, ki, wi, u, vb, ga, gb


def dsa_attend(q, qi, wi, qpos, k_all, v_all, ki_all):
    B, Q = q.shape[:2]
    L = k_all.shape[1]
    topk = min(TOPK_MAX, L // 4)
    logits = jnp.einsum("bqhd,bld->bqhl", qi, ki_all)
    score = jnp.einsum("bqhl,bqh->bql", jax.nn.relu(logits), wi).astype(jnp.float32)
    allowed = jnp.arange(L, dtype=jnp.int32)[None, :] <= qpos[:, None]
    score = jnp.where(allowed[None], score, -jnp.inf)
    _, idx = lax.top_k(score, topk)
    valid = idx <= qpos[None, :, None]
    gather = jax.vmap(lambda a, i: a[i])
    kg = gather(k_all, idx)
    vg = gather(v_all, idx)
    qg = q.reshape(B, Q, KV_HEADS, Q_PER_KV, HEAD_DIM)
    s = jnp.einsum("bqcgd,bqkcd->bqcgk", qg, kg).astype(jnp.float32)
    s = jnp.where(valid[:, :, None, None, :], s, -jnp.inf)
    p = jax.nn.softmax(s, axis=-1).astype(vg.dtype)
    o = jnp.einsum("bqcgk,bqkcd->bqcgd", p, vg)
    return o.reshape(B, Q, N_HEADS * HEAD_DIM)


def prompt_attention(q, qi, wi, k, v, ki):
    B, S = q.shape[:2]
    nblk = S // QBLK

    def blocks(a):
        return jnp.moveaxis(a.reshape(B, nblk, QBLK, *a.shape[2:]), 1, 0)

    qpos = jnp.arange(S, dtype=jnp.int32).reshape(nblk, QBLK)
    out = lax.map(lambda args: dsa_attend(args[0], args[1], args[2], args[3], k, v, ki),
                  (blocks(q), blocks(qi), blocks(wi), qpos))
    return jnp.moveaxis(out, 0, 1).reshape(B, S, N_HEADS * HEAD_DIM)


def sgu(u, v, g_v, w_s, b_s):
    B, T, _ = v.shape
    vn = rmsnorm(v, g_v)
    vc = vn.reshape(B, T // CHUNK, CHUNK, SGU_GROUPS, SGU_CH)
    tril = jnp.tril(jnp.ones((CHUNK, CHUNK), dtype=bool))
    ws = jnp.where(tril[None], w_s, jnp.zeros_like(w_s))
    mixed = jnp.einsum("bnsgc,gts->bntgc", vc, ws) + jnp.transpose(b_s)[None, None, :, :, None]
    return vn, u * mixed.reshape(B, T, D_SGU)


def finish(x, o_att, o_sgu, ga, gb, w_o, g_ffn, w_up, w_down, p, g_ple, w_pg, w_p):
    m = jax.nn.sigmoid(ga) * o_att + jax.nn.sigmoid(gb) * o_sgu
    x = x + m @ w_o
    hf = rmsnorm(x, g_ffn)
    x = x + jnp.square(jax.nn.relu(hf @ w_up)) @ w_down
    hp = rmsnorm(x, g_ple)
    x = x + jax.nn.sigmoid(hp @ w_pg) * (p @ w_p)
    return x


def gather_pages(cache, page_table):
    pages = cache[page_table]
    nb, npg, ps = pages.shape[:3]
    return pages.reshape(nb, npg * ps, *cache.shape[2:])


def setup_inputs(seed: int = 0) -> dict:
    key = jax.random.key(seed)
    ks = jax.random.split(key, 24)
    n_pages = PAST_LEN // PAGE_SIZE
    n_phys = (5 * DEC_BATCH * n_pages + 3) // 4
    nrm = jax.random.normal
    f32 = jnp.float32
    page_table = jax.random.permutation(ks[5], n_phys)[:DEC_BATCH * n_pages]
    page_table = page_table.reshape(DEC_BATCH, n_pages).astype(jnp.int32)
    return {
        "x_prompt": nrm(ks[0], (BATCH, SEQ, D_MODEL), f32),
        "x_sample": nrm(ks[1], (DEC_BATCH, DEC_SEQ, D_MODEL), f32),
        "cache_k": nrm(ks[2], (DEPTH, n_phys, PAGE_SIZE, KV_HEADS, HEAD_DIM), f32),
        "cache_v": nrm(ks[3], (DEPTH, n_phys, PAGE_SIZE, KV_HEADS, HEAD_DIM), f32),
        "cache_kidx": nrm(ks[4], (DEPTH, n_phys, PAGE_SIZE, IDX_DIM), f32),
        "page_table": page_table,
        "p_prompt": nrm(ks[6], (DEPTH, BATCH, SEQ, PLE_DIM), f32),
        "p_sample": nrm(ks[7], (DEPTH, DEC_BATCH, DEC_SEQ, PLE_DIM), f32),
        "g_mix": 1.0 + 0.02 * nrm(ks[8], (DEPTH, D_MODEL), f32),
        "w_in": nrm(ks[9], (DEPTH, D_MODEL, IN_WIDTH), f32) * D_MODEL ** -0.5,
        "g_q": 1.0 + 0.02 * nrm(ks[10], (DEPTH, HEAD_DIM), f32),
        "g_k": 1.0 + 0.02 * nrm(ks[11], (DEPTH, HEAD_DIM), f32),
        "g_sgu": 1.0 + 0.02 * nrm(ks[12], (DEPTH, D_SGU), f32),
        "w_s": nrm(ks[13], (DEPTH, SGU_GROUPS, CHUNK, CHUNK), f32) * CHUNK ** -0.5,
        "b_s": 1.0 + 0.01 * nrm(ks[14], (DEPTH, SGU_GROUPS, CHUNK), f32),
        "w_o": nrm(ks[15], (DEPTH, D_MODEL, D_MODEL), f32) * D_MODEL ** -0.5,
        "g_ffn": 1.0 + 0.02 * nrm(ks[16], (DEPTH, D_MODEL), f32),
        "w_up": nrm(ks[17], (DEPTH, D_MODEL, D_FF), f32) * D_MODEL ** -0.5,
        "w_down": nrm(ks[18], (DEPTH, D_FF, D_MODEL), f32) * D_FF ** -0.5,
        "g_ple": 1.0 + 0.02 * nrm(ks[19], (DEPTH, D_MODEL), f32),
        "w_pg": nrm(ks[20], (DEPTH, D_MODEL, D_MODEL), f32) * D_MODEL ** -0.5,
        "w_p": nrm(ks[21], (DEPTH, PLE_DIM, D_MODEL), f32) * PLE_DIM ** -0.5,
    }


def reference(x_prompt, x_sample, cache_k, cache_v, cache_kidx, page_table, p_prompt, p_sample,
              g_mix, w_in, g_q, g_k, g_sgu, w_s, b_s, w_o, g_ffn, w_up, w_down, g_ple, w_pg, w_p):
    xp = x_prompt
    xs = x_sample
    past_len = page_table.shape[1] * cache_k.shape[2]
    dec_seq = x_sample.shape[1]
    pad = (-dec_seq) % CHUNK
    kp_l, vp_l, kip_l, sp_l = [], [], [], []
    ks_l, vs_l, kis_l, ss_l = [], [], [], []
    for i in range(DEPTH):
        q, k, v, qi, ki, wi, u, vb, ga, gb = project_in(xp, g_mix[i], w_in[i], g_q[i], g_k[i])
        o_att = prompt_attention(q, qi, wi, k, v, ki)
        vn, o_sgu = sgu(u, vb, g_sgu[i], w_s[i], b_s[i])
        xp = finish(xp, o_att, o_sgu, ga, gb, w_o[i], g_ffn[i], w_up[i], w_down[i],
                    p_prompt[i], g_ple[i], w_pg[i], w_p[i])
        kp_l.append(k); vp_l.append(v); kip_l.append(ki); sp_l.append(vn)
        q, k, v, qi, ki, wi, u, vb, ga, gb = project_in(xs, g_mix[i], w_in[i], g_q[i], g_k[i])
        k_all = jnp.concatenate([gather_pages(cache_k[i], page_table), k], axis=1)
        v_all = jnp.concatenate([gather_pages(cache_v[i], page_table), v], axis=1)
        ki_all = jnp.concatenate([gather_pages(cache_kidx[i], page_table), ki], axis=1)
        qpos = past_len + jnp.arange(dec_seq, dtype=jnp.int32)
        o_att = dsa_attend(q, qi, wi, qpos, k_all, v_all, ki_all)
        u_p = jnp.pad(u, ((0, 0), (0, pad), (0, 0)))
        vb_p = jnp.pad(vb, ((0, 0), (0, pad), (0, 0)))
        vn, o_sgu = sgu(u_p, vb_p, g_sgu[i], w_s[i], b_s[i])
        vn = vn[:, :dec_seq]
        o_sgu = o_sgu[:, :dec_seq]
        xs = finish(xs, o_att, o_sgu, ga, gb, w_o[i], g_ffn[i], w_up[i], w_down[i],
                    p_sample[i], g_ple[i], w_pg[i], w_p[i])
        ks_l.append(k); vs_l.append(v); kis_l.append(ki); ss_l.append(vn)
    new_k_prompt = jnp.stack(kp_l)
    new_v_prompt = jnp.stack(vp_l)
    new_kidx_prompt = jnp.stack(kip_l)
    new_sgu_v_prompt = jnp.stack(sp_l)
    new_k_sample = jnp.stack(ks_l)
    new_v_sample = jnp.stack(vs_l)
    new_kidx_sample = jnp.stack(kis_l)
    new_sgu_v_sample = jnp.stack(ss_l)
    return (xp, xs, new_k_prompt, new_v_prompt, new_kidx_prompt, new_sgu_v_prompt,
            new_k_sample, new_v_sample, new_kidx_sample, new_sgu_v_sample)
```

```python
import contextlib
import numpy as np
import concourse.bass as bass
import concourse.mybir as mybir
from concourse.bass_utils import run_bass_kernel_spmd

F32 = mybir.dt.float32
BF16 = mybir.dt.bfloat16
I32 = mybir.dt.int32
ALU = mybir.AluOpType
AF = mybir.ActivationFunctionType
AX = mybir.AxisListType

D = 1024
SEQ = 8192
NBLK = 64
NSLOT = 16
NS = 16
NPG = 16
S_S = 2176
N_PHYS = 2560
INW = 6216
DFF = 4096
PLE = 256
EPS = 1e-6
ATTN_SCALE = 64 ** -0.5
IDXS = (64 ** -0.5) * (8 ** -0.5)
NIT = 20
BRK = 16.0
NEGM = -30000.0
GRPS = [(0, 512), (512, 512), (1024, 512), (1536, 512), (2048, 72), (2120, 512), (2632, 512),
        (3144, 512), (3656, 512), (4168, 512), (4680, 512), (5192, 512), (5704, 512)]


class St:
    __slots__ = ("w", "r")

    def __init__(self):
        self.w = None
        self.r = {}


class Eng:
    def __init__(self, nc, es, name, h):
        self.h = h
        self.name = name
        self.sem = es.enter_context(nc.semaphore("e_" + name))
        self.cnt = 0
        self.seen = {}

    def wait(self, tok):
        sem, val = tok
        if self.seen.get(sem.num, 0) < val:
            self.h.wait_ge(sem, val)
            self.seen[sem.num] = val


def build():
    nc = bass.Bass("TRN2", target_bir_lowering=False)
    dt_in = lambda n, s, d=F32: nc.dram_tensor(n, s, d, kind="ExternalInput").ap()
    dt_out = lambda n, s, d=F32: nc.dram_tensor(n, s, d, kind="ExternalOutput").ap()
    xb_d = dt_in("xb", [SEQ, D])
    xo_d = dt_in("xo", [NSLOT, 128, D])
    po_d = dt_in("po", [NSLOT, 128, PLE])
    xs_d = dt_in("xs", [NS, D])
    ps_d = dt_in("ps", [NS, PLE])
    ck_d = dt_in("ck", [N_PHYS * 128, 256])
    cv_d = dt_in("cv", [N_PHYS * 128, 256])
    cki_d = dt_in("cki", [N_PHYS * 128, 64])
    pt_d = dt_in("pt", [NS * NPG], I32)
    pen_d = dt_in("pen", [128, 2, 512])
    win_d = dt_in("w_in", [D, INW])
    wo_d = dt_in("w_o", [D, D])
    wup_d = dt_in("w_up", [D, DFF])
    wdn_d = dt_in("w_down", [DFF, D])
    wpg_d = dt_in("w_pg", [D, D])
    wp_d = dt_in("w_p", [PLE, D])
    gmix_d = dt_in("g_mix", [D])
    gq_d = dt_in("g_q", [64])
    gk_d = dt_in("g_k", [64])
    gsgu_d = dt_in("g_sgu", [D])
    ws_d = dt_in("w_s", [8, 128, 128])
    bs_d = dt_in("b_s", [8, 128])
    gffn_d = dt_in("g_ffn", [D])
    gple_d = dt_in("g_ple", [D])

    yo_d = dt_out("yo", [NSLOT, 128, D])
    ko_d = dt_out("ko", [NSLOT, 128, 256])
    vo_d = dt_out("vo", [NSLOT, 128, 256])
    kio_d = dt_out("kio", [NSLOT, 128, 64])
    sgo_d = dt_out("sgo", [NSLOT, 128, D])
    ys_d = dt_out("ys", [NS, D])
    kss_d = dt_out("kss", [NS, 256])
    vss_d = dt_out("vss", [NS, 256])
    kis_d = dt_out("kis", [NS, 64])
    sgs_d = dt_out("sgs", [NS, D])

    def scr(n, shape):
        return nc.dram_tensor(n, shape, BF16, kind="Internal").ap()
    win_s = [scr("win_s%d" % i, [128, 8, w]) for i, (o, w) in enumerate(GRPS)]
    wo_s = [scr("wo_s%d" % i, [128, 8, 512]) for i in range(2)]
    wup_s = [scr("wup_s%d" % i, [128, 8, 512]) for i in range(8)]
    wdn_s = [scr("wdn_s%d" % i, [128, 8, 512]) for i in range(8)]
    wpg_s = [scr("wpg_s%d" % i, [128, 8, 512]) for i in range(2)]
    wp_s = [scr("wp_s%d" % i, [128, 2, 512]) for i in range(2)]

    es = contextlib.ExitStack()
    with es:
        PE = Eng(nc, es, "pe", nc.tensor)
        ACT = Eng(nc, es, "act", nc.scalar)
        DVE = Eng(nc, es, "dve", nc.vector)
        POOL = Eng(nc, es, "pool", nc.gpsimd)
        SP = Eng(nc, es, "sp", nc.sync)
        NDS = 24
        dsems = [es.enter_context(nc.semaphore("d%d" % i)) for i in range(NDS)]
        dcnt = [0] * NDS
        dstate = {"i": 0}

        def deps_of(rd, wr):
            deps = []
            for s in rd:
                if s.w is not None:
                    deps.append(s.w)
            for s in wr:
                if s.w is not None:
                    deps.append(s.w)
                deps.extend(s.r.values())
            return deps

        def op(E, fn, rd=(), wr=()):
            for tok in deps_of(rd, wr):
                if E is PE and tok[0] is PE.sem:
                    continue
                E.wait(tok)
            ins = fn()
            E.cnt += 1
            ins.then_inc(E.sem, 1)
            tok = (E.sem, E.cnt)
            for s in rd:
                s.r[E.name] = tok
            for s in wr:
                s.w = tok
                s.r = {}
            return tok

        def dma(Q, out, in_, rd=(), wr=(), fn=None):
            for tok in deps_of(rd, wr):
                Q.wait(tok)
            i = dstate["i"]
            dstate["i"] = (i + 1) % NDS
            if dcnt[i] > 0:
                Q.wait((dsems[i], dcnt[i]))
            if fn is None:
                ins = Q.h.dma_start(out=out, in_=in_)
            else:
                ins = fn()
            dcnt[i] += 16
            ins.then_inc(dsems[i], 16)
            tok = (dsems[i], dcnt[i])
            for s in rd:
                s.r["dma%d" % i] = tok
            for s in wr:
                s.w = tok
                s.r = {}
            return tok

        def fence(frm, to):
            for t in to:
                for f in frm:
                    if f is t:
                        continue
                    if f.w is not None:
                        t.r["f%d" % id(f)] = f.w
                    for k, v in list(f.r.items()):
                        t.r["f%d%s" % (id(f), k)] = v

        def sb(name, shape, dt):
            return nc.alloc_sbuf_tensor("sb_" + name, shape, dt)

        KT = sb("KT", [128, 2, SEQ], BF16); KT_s = St()
        Vt = sb("Vt", [128, NBLK, 4, 65], BF16); V_s = St()
        kiT = sb("kiT", [128, SEQ], BF16); kiT_s = St()
        arena_base = nc.sbuf_base
        scores = sb("scores", [128, SEQ], F32); sc_s = St()
        al = lambda n, shape, dt, off: nc.alloc_sbuf_tensor_at("al_" + n, shape, dt, offset=arena_base + off)
        h_bf = al("h_bf", [128, DFF], BF16, 0); hbf_s = St()
        hT = al("hT", [128, 32, 128], BF16, 8192); hT_s = St()
        xn_bf = al("xn_bf", [128, D], BF16, 16384); xnbf_s = St()
        xnT = al("xnT", [128, 8, 128], BF16, 18432); xnT_s = St()
        vn_bf = al("vn_bf", [128, D], BF16, 20480); vnbf_s = St()
        wkvk = al("wkvk", [128, 8, 576], BF16, 22528); wkvk_s = St()
        arena_states = [hbf_s, hT_s, xnbf_s, xnT_s, vnbf_s, wkvk_s]
        sc16 = scores[0:16, 0:S_S]; sc16_s = St()
        sct = scores[0:16, S_S:2 * S_S]; sct_s = St()
        sctR = [sct, scores[0:16, 2 * S_S:3 * S_S]]; sctR_s = [sct_s, St()]
        WS = [sb("ws%d" % i, [128, 8, 512], BF16) for i in range(3)]
        WS_s = [St() for _ in range(3)]
        FB0 = sb("fb0", [128, D], F32)
        FB12 = sb("fb12", [128, 2 * D], F32)
        FB3 = sb("fb3", [128, D], F32)
        FB4 = sb("fb4", [128, D], F32)
        FB = [FB0[:, :], FB12[:, 0:D], FB12[:, D:2 * D], FB3[:, :], FB4[:, :]]
        FB_s = [St() for _ in range(5)]
        junk8 = FB12[:, :].bitcast(mybir.dt.uint8)
        junki8 = FB12[:, :].bitcast(mybir.dt.int8)
        gmixT = sb("gmixT", [128, 8], F32); gffnT = sb("gffnT", [128, 8], F32); gpleT = sb("gpleT", [128, 8], F32)
        gsgub = sb("gsgub", [128, D], F32)
        coefs = sb("coefs", [16, 16], F32)
        gqb = sb("gqb", [128, 64], F32); gkb = sb("gkb", [128, 64], F32)
        cst_s = St()
        qn_bf = sb("qn_bf", [128, 2, 4, 128], BF16); qn_s = St()
        qT = sb("qT", [128, 2, 4, 128], BF16); qT_s = St()
        qi_bf = sb("qi_bf", [128, 512], BF16); qibf_s = St()
        qiT = sb("qiT", [128, 4, 128], BF16); qiT_s = St()
        absw = sb("absw", [128, 8], F32); sgnw = sb("sgnw", [128, 8], F32); w_s_ = St()
        kn_bf = sb("kn_bf", [128, 256], BF16); knbf_s = St()
        v_bf = sb("v_bf", [128, 256], BF16); vbf_s = St()
        kid_bf = sb("kid_bf", [128, 2, 64], BF16); kid_s = St()
        k32 = sb("k32", [128, 512], F32); k32_s = St()
        o32 = sb("o32", [128, 512], F32); o32_s = St()
        ki32 = sb("ki32", [128, 72], F32); ki32_s = St()
        rt = [sb("rt%d" % i, [128, 512], F32) for i in range(2)]; rt_s = [St(), St()]
        PT = [sb("PT%d" % i, [128, 512], BF16) for i in range(3)]; PT_s = [St(), St(), St()]
        mT = [sb("mT%d" % i, [128, 4, 128], BF16) for i in range(2)]; mT_s = [St(), St()]
        MBg = [sb("MBg%d" % i, [128, 512], BF16) for i in range(2)]; MB_s = [St(), St()]
        identb = sb("identb", [128, 128], BF16); identf = sb("identf", [16, 16], F32)
        I4 = sb("I4", [128, 4, 128], BF16); I416 = sb("I416", [16, 4, 16], BF16)
        wsT = sb("wsT", [128, 8, 128], BF16); bsT = sb("bsT", [128, 8], F32)
        pen = sb("pen", [128, 2, 512], F32)
        p32 = sb("p32", [128, PLE], F32); p32_s = St()
        p_bf = sb("p_bf", [128, PLE], BF16); pbf_s = St()
        pT = sb("pT", [128, 2, 128], BF16); pT_s = St()
        sm = sb("sm", [128, 64], F32); sm_s = St()
        lo = sb("lo", [128, 1], F32); mid = sb("mid", [128, 1], F32); cntt = sb("cntt", [128, 1], F32)
        geq = sb("geq", [128, 1], F32); bis_s = St(); junk_s = St()
        cnta = sb("cnta", [128, 1], F32); cnta_s = St(); mid_s = St(); junka_s = St()
        mhalf = sb("mhalf", [128, 16], F32)
        ss16 = sb("ss16", [128, 16], F32); ss16_s = St()
        rs16 = sb("rs16", [128, 16], F32); rs16_s = St()
        ptb = sb("ptb", [128, NS * NPG], I32); idxa = sb("idxa", [128, NS * NPG], I32); iop = sb("iop", [128, 1], I32)
        NPB = 4
        pgK = [sb("pgK%d" % i, [128, 256], BF16) for i in range(NPB)]; pgK_s = [St() for _ in range(NPB)]
        pgV = [sb("pgV%d" % i, [128, 256], BF16) for i in range(NPB)]; pgV_s = [St() for _ in range(NPB)]
        pgI = [sb("pgI%d" % i, [128, 2, 64], BF16) for i in range(NPB)]; pgI_s = [St() for _ in range(NPB)]
        ksT = sb("ksT", [128, 2, 16], BF16); kisT = sb("kisT", [128, 16], BF16); vs_bf = sb("vs_bf", [16, 256], BF16)
        smp_s = St()
        ws32 = sb("ws32", [128, 128], F32); ws32_s = St()
        wsb = sb("wsb", [128, 128], BF16); wsb_s = St()

        PB = [nc.alloc_psum_tensor("pb%d" % i, [128, 512], F32) for i in range(8)]
        PB_s = [St() for _ in range(8)]
        PBb = [PB[i][:].bitcast(BF16).rearrange("p (a b) -> p a b", a=8) for i in range(8)]

        def transposes(src_ap_fn, nblk, n, bank0, dst_fn, dst_states, src_states, evac=None, gainT=None):
            for b0 in range(0, nblk, 8):
                nb = min(8, nblk - b0)
                bank = 2 + (bank0 + b0 // 8) % 2
                if gainT is not None:
                    for j in range(nb):
                        op(PE, lambda j=j: nc.tensor.transpose(PBb[bank][:, j, 0:n], src_ap_fn(b0 + j), identb[0:n, 0:n]),
                           rd=list(src_states) + [cst_s], wr=[PB_s[bank]])
                    op(DVE, lambda: nc.vector.tensor_tensor(out=dst_fn(b0, nb), in0=PBb[bank][:, 0:nb, 0:n],
                                                            in1=gainT[:, b0:b0 + nb].unsqueeze(2).to_broadcast([128, nb, n]), op=ALU.mult),
                       rd=[PB_s[bank], cst_s], wr=dst_states)
                    continue
                for j in range(nb):
                    tok = op(PE, lambda j=j: nc.tensor.transpose(PBb[bank][:, j, 0:n], src_ap_fn(b0 + j), identb[0:n, 0:n]),
                             rd=list(src_states) + [cst_s], wr=[PB_s[bank]])
                E = evac or ACT
                if E is ACT:
                    op(ACT, lambda: nc.scalar.copy(out=dst_fn(b0, nb), in_=PBb[bank][:, 0:nb, 0:n]),
                       rd=[PB_s[bank]], wr=dst_states)
                else:
                    op(DVE, lambda: nc.vector.tensor_copy(out=dst_fn(b0, nb), in_=PBb[bank][:, 0:nb, 0:n]),
                       rd=[PB_s[bank]], wr=dst_states)

        RS = {"act": False}

        def rstd_from_ss(ss_ap, out_ap, n, ncol, inv_d, rd_s, wr_s):
            op(DVE, lambda: nc.vector.tensor_scalar(out=ss_ap, in0=ss_ap, scalar1=inv_d, scalar2=EPS, op0=ALU.mult, op1=ALU.add),
               rd=[], wr=[rd_s])
            if RS["act"]:
                op(ACT, lambda: nc.scalar.activation(out=out_ap, in_=ss_ap, func=AF.Sqrt), rd=[rd_s], wr=[wr_s])
                op(DVE, lambda: nc.vector.reciprocal(out=out_ap, in_=out_ap), rd=[], wr=[wr_s])
            else:
                op(POOL, lambda: nc.gpsimd.tensor_tensor(out=out_ap, in0=ss_ap, in1=mhalf[0:n, 0:ncol], op=ALU.pow),
                   rd=[rd_s, cst_s], wr=[wr_s])

        def rmsnorm_to_T(x_ap, x_s, gain, n):
            op(DVE, lambda: nc.vector.scalar_tensor_tensor(out=xn_bf[0:n, :], in0=x_ap, scalar=1.0, in1=x_ap, op0=ALU.mult, op1=ALU.mult,
                                                           accum_out=ss16[0:n, 0:1]),
               rd=[x_s], wr=[xnbf_s, ss16_s])
            rstd_from_ss(ss16[0:n, 0:1], rs16[0:n, 0:1], n, 1, 1.0 / D, ss16_s, rs16_s)
            op(DVE, lambda: nc.vector.tensor_scalar(out=xn_bf[0:n, :], in0=x_ap, scalar1=rs16[0:n, 0:1], scalar2=None, op0=ALU.mult),
               rd=[x_s, rs16_s], wr=[xnbf_s])
            transposes(lambda j: xn_bf[0:n, j * 128:(j + 1) * 128], 8, n, 0,
                       lambda b0, nb: xnT[:, b0:b0 + nb, 0:n], [xnT_s], [xnbf_s], gainT=gain)

        wseq = []
        one_slot = ([(win_s[i], 8, GRPS[i][1]) for i in range(13)] + [(wo_s[i], 8, 512) for i in range(2)]
                    + [(wup_s[i], 8, 512) for i in range(8)] + [(wdn_s[i], 8, 512) for i in range(8)]
                    + [(wpg_s[i], 8, 512) for i in range(2)] + [(wp_s[i], 2, 512) for i in range(2)])
        for _ in range(NSLOT + 1):
            wseq.extend(one_slot)
        wst = {"issued": 0, "next": 0}
        conv_s = St()

        def w_issue(upto):
            while wst["issued"] < min(upto, len(wseq)):
                k = wst["issued"]
                src, kc, ncol = wseq[k]
                sl = k % 3
                dma(SP, WS[sl][:, 0:kc, 0:ncol], src[:, :, :], rd=[conv_s], wr=[WS_s[sl]])
                wst["issued"] += 1

        def w_get():
            k = wst["next"]
            wst["next"] += 1
            w_issue(k + 1)
            return k % 3, k

        def w_done(k):
            w_issue(k + 3)

        def dense(lhsT_fn, kcs, n, ncols_list, bank_fn, lhs_states, after_fn):
            for gi, ncol in enumerate(ncols_list):
                sl, k = w_get()
                bank = bank_fn(gi)
                for kc in range(kcs):
                    op(PE, lambda kc=kc: nc.tensor.matmul(PB[bank][0:n, 0:ncol], lhsT=lhsT_fn(kc), rhs=WS[sl][:, kc, 0:ncol],
                                                          start=(kc == 0), stop=(kc == kcs - 1)),
                       rd=list(lhs_states) + [WS_s[sl]], wr=[PB_s[bank]])
                w_done(k)
                after_fn(gi, bank)

        with nc.allow_non_contiguous_dma(reason="tiny constant loads"):
            for (dst, src) in ((gsgub, gsgu_d), (gqb, gq_d), (gkb, gk_d)):
                dma(SP, dst[:], src.partition_broadcast(128), wr=[cst_s])
            for (dst, src) in ((gmixT, gmix_d), (gffnT, gffn_d), (gpleT, gple_d)):
                dma(SP, None, None, wr=[cst_s], fn=lambda dst=dst, src=src: nc.sync.dma_start(out=dst[:], in_=src.rearrange("(k p) -> p k", p=128)))
            dma(SP, None, None, wr=[cst_s], fn=lambda: nc.sync.dma_start(
                out=coefs[:, 0:8], in_=ws_d[:, 0, 0:1].rearrange("g a -> (g a)").partition_broadcast(16)))
            dma(SP, None, None, wr=[cst_s], fn=lambda: nc.sync.dma_start(
                out=coefs[:, 8:16], in_=bs_d[:, 0:1].rearrange("g a -> (g a)").partition_broadcast(16)))
            dma(SP, bsT[:], bs_d.rearrange("g t -> t g"), wr=[cst_s], fn=lambda: nc.sync.dma_start(
                out=bsT[:], in_=bs_d.rearrange("g t -> t g")))
            dma(SP, pen[:], pen_d[:, :, :], wr=[cst_s])
            dma(SP, ptb[:], pt_d.partition_broadcast(128), wr=[cst_s])
        op(POOL, lambda: nc.gpsimd.memset(identb[:], 1.0), wr=[cst_s])
        op(POOL, lambda: nc.gpsimd.affine_select(out=identb[:], in_=identb[:], pattern=[[-1, 128]], compare_op=ALU.is_equal,
                                                 fill=0.0, base=0, channel_multiplier=1), wr=[cst_s])
        op(POOL, lambda: nc.gpsimd.memset(identf[:], 1.0), wr=[cst_s])
        op(POOL, lambda: nc.gpsimd.affine_select(out=identf[:], in_=identf[:], pattern=[[-1, 16]], compare_op=ALU.is_equal,
                                                 fill=0.0, base=0, channel_multiplier=1), wr=[cst_s])
        op(POOL, lambda: nc.gpsimd.memset(mhalf[:], -0.5), wr=[cst_s])
        op(POOL, lambda: nc.gpsimd.iota(iop[:], pattern=[[0, 1]], base=0, channel_multiplier=1), wr=[cst_s])
        for g in range(4):
            op(DVE, lambda g=g: nc.vector.tensor_copy(out=I4[:, g, :], in_=identb[:]), rd=[], wr=[cst_s])
            op(DVE, lambda g=g: nc.vector.tensor_copy(out=I416[:, g, :], in_=identb[0:16, 0:16]), rd=[], wr=[cst_s])
        op(DVE, lambda: nc.vector.tensor_scalar(out=gqb[:], in0=gqb[:], scalar1=ATTN_SCALE, scalar2=None, op0=ALU.mult), wr=[cst_s])
        op(DVE, lambda: nc.vector.tensor_scalar(out=idxa[:], in0=ptb[:], scalar1=128, scalar2=iop[:, 0:1], op0=ALU.mult, op1=ALU.add),
           wr=[cst_s])
        op(POOL, lambda: nc.gpsimd.memset(Vt[:], 0.0), wr=[V_s])
        op(POOL, lambda: nc.gpsimd.memset(Vt[:, :, :, 64:65], 1.0), wr=[V_s])
        for g in range(8):
            dma(SP, ws32[:], ws_d[g, :, :], wr=[ws32_s])
            op(POOL, lambda: nc.gpsimd.affine_select(out=ws32[:], in_=ws32[:], pattern=[[-1, 128]], compare_op=ALU.is_ge,
                                                     fill=0.0, base=0, channel_multiplier=1), wr=[ws32_s])
            op(DVE, lambda: nc.vector.tensor_copy(out=wsb[:], in_=ws32[:]), rd=[ws32_s], wr=[wsb_s])
            op(PE, lambda: nc.tensor.transpose(PBb[3][:, 0, :], wsb[:], identb[:]), rd=[wsb_s, cst_s], wr=[PB_s[3]])
            op(ACT, lambda g=g: nc.scalar.copy(out=wsT[:, g, :], in_=PBb[3][:, 0, :]), rd=[PB_s[3]], wr=[cst_s])

        conv_jobs = []
        conv_toks = []

        def conv(dst, src2d, kc, col0, ncol, first=False):
            for k in range(kc):
                job = (lambda k=k: conv_toks.append(dma(
                    POOL, None, None, wr=[],
                    fn=lambda: nc.gpsimd.dma_start(out=dst[:, k, :], in_=src2d[k * 128:(k + 1) * 128, col0:col0 + ncol]))))
                if first:
                    job()
                else:
                    conv_jobs.append(job)
        for i, (o, w) in enumerate(GRPS):
            conv(win_s[i], win_d, 8, o, w, first=(i in (2, 4)))
        for i in range(2):
            conv(wo_s[i], wo_d, 8, i * 512, 512)
        for i in range(8):
            conv(wup_s[i], wup_d, 8, i * 512, 512)
        for i in range(8):
            nn, kg = i // 4, i % 4
            conv(wdn_s[i], wdn_d[kg * 1024:(kg + 1) * 1024, :], 8, nn * 512, 512)
        for i in range(2):
            conv(wpg_s[i], wpg_d, 8, i * 512, 512)
        for i in range(2):
            conv(wp_s[i], wp_d, 2, i * 512, 512)
        for tok in conv_toks:
            SP.wait(tok)

        def head_norm(src32, n, nh, gain, out_fn, out_states, e=None):
            v3 = src32.rearrange("p (h d) -> p h d", d=64)
            if e is not None:
                tmp4 = o32[0:n, 0:nh * 64].rearrange("p (e g d) -> p e g d", e=e, d=64)
                op(DVE, lambda: nc.vector.tensor_tensor(out=o32[0:n, 0:nh * 64], in0=src32, in1=src32, op=ALU.mult), rd=[k32_s], wr=[o32_s])
                op(DVE, lambda: nc.vector.tensor_reduce(out=ss16[0:n, 0:nh], in_=o32[0:n, 0:nh * 64].rearrange("p (h d) -> p h d", d=64),
                                                        axis=AX.X, op=ALU.add), rd=[o32_s], wr=[ss16_s])
                rstd_from_ss(ss16[0:n, 0:nh], rs16[0:n, 0:nh], n, nh, 1.0 / 64, ss16_s, rs16_s)
                op(DVE, lambda: nc.vector.tensor_tensor(out=o32[0:n, 0:nh * 64].rearrange("p (h d) -> p h d", d=64), in0=v3,
                                                        in1=rs16[0:n, 0:nh].unsqueeze(2).to_broadcast([n, nh, 64]), op=ALU.mult),
                   rd=[k32_s, rs16_s], wr=[o32_s])
                for ee in range(e):
                    op(DVE, lambda ee=ee: nc.vector.tensor_tensor(out=out_fn(ee), in0=tmp4[:, ee, :, :],
                                                                  in1=gain[0:n, :].unsqueeze(1).to_broadcast([n, nh // e, 64]), op=ALU.mult),
                       rd=[o32_s, cst_s], wr=out_states)
                return
            op(DVE, lambda: nc.vector.tensor_tensor(out=o32[0:n, 0:nh * 64], in0=src32, in1=src32, op=ALU.mult), rd=[k32_s], wr=[o32_s])
            op(DVE, lambda: nc.vector.tensor_reduce(out=ss16[0:n, 0:nh], in_=o32[0:n, 0:nh * 64].rearrange("p (h d) -> p h d", d=64),
                                                    axis=AX.X, op=ALU.add), rd=[o32_s], wr=[ss16_s])
            rstd_from_ss(ss16[0:n, 0:nh], rs16[0:n, 0:nh], n, nh, 1.0 / 64, ss16_s, rs16_s)
            op(DVE, lambda: nc.vector.tensor_tensor(out=o32[0:n, 0:nh * 64].rearrange("p (h d) -> p h d", d=64), in0=v3,
                                                    in1=rs16[0:n, 0:nh].unsqueeze(2).to_broadcast([n, nh, 64]), op=ALU.mult),
               rd=[k32_s, rs16_s], wr=[o32_s])
            op(DVE, lambda: nc.vector.tensor_tensor(out=out_fn(), in0=o32[0:n, 0:nh * 64].rearrange("p (h d) -> p h d", d=64),
                                                    in1=gain[0:n, :].unsqueeze(1).to_broadcast([n, nh, 64]), op=ALU.mult),
               rd=[o32_s, cst_s], wr=out_states)

        def kv_append(blk, n, k_ap, k_s, v_ap, v_s, kid_ap, kid_s_):
            c0 = blk * 128
            for j in range(2):
                op(PE, lambda j=j: nc.tensor.transpose(PBb[3][:, j, 0:n], k_ap[0:n, j * 128:(j + 1) * 128], identb[0:n, 0:n]),
                   rd=[k_s, cst_s], wr=[PB_s[3]])
            op(PE, lambda: nc.tensor.transpose(PBb[3][:, 2, 0:n], kid_ap[0:n, :, :].rearrange("p a d -> p (a d)"), identb[0:n, 0:n]),
               rd=[kid_s_, cst_s], wr=[PB_s[3]])
            op(ACT, lambda: nc.scalar.copy(out=KT[:, :, c0:c0 + n], in_=PBb[3][:, 0:2, 0:n]), rd=[PB_s[3]], wr=[KT_s])
            op(ACT, lambda: nc.scalar.copy(out=kiT[:, c0:c0 + n], in_=PBb[3][:, 2, 0:n]), rd=[PB_s[3]], wr=[kiT_s])
            op(ACT, lambda: nc.scalar.copy(out=Vt[0:n, blk, :, 0:64], in_=v_ap[0:n, :].rearrange("p (c d) -> p c d", d=64)),
               rd=[v_s], wr=[V_s])

        def indexer(n, nch, sc_ap, sc_state, k0=0, ki_state=None, hook=None):
            ki_state = ki_state or kiT_s
            S = nch * 128
            for g0 in range(0, S, 512):
                w = min(512, S - g0)
                for h in range(8):
                    if hook is not None:
                        hook()
                    bank = h % 2
                    pb = (h % 2) * 64
                    op(PE, lambda: nc.tensor.matmul(PB[bank][0:n, 0:w], lhsT=qiT[pb:pb + 64, h // 2, 0:n], rhs=kiT[pb:pb + 64, k0 + g0:k0 + g0 + w],
                                                    start=True, stop=True), rd=[qiT_s, ki_state], wr=[PB_s[bank]])
                    r = rt[h % 2]
                    op(ACT, lambda: nc.scalar.activation(out=r[0:n, 0:w], in_=PB[bank][0:n, 0:w], func=AF.Relu, scale=absw[0:n, h:h + 1]),
                       rd=[PB_s[bank], w_s_], wr=[rt_s[h % 2]])
                    if h == 0:
                        op(DVE, lambda: nc.vector.tensor_scalar(out=sc_ap[0:n, g0:g0 + w], in0=r[0:n, 0:w], scalar1=sgnw[0:n, 0:1], scalar2=None,
                                                                op0=ALU.mult), rd=[rt_s[0], w_s_], wr=[sc_state])
                    else:
                        op(DVE, lambda: nc.vector.scalar_tensor_tensor(out=sc_ap[0:n, g0:g0 + w], in0=r[0:n, 0:w], scalar=sgnw[0:n, h:h + 1],
                                                                       in1=sc_ap[0:n, g0:g0 + w], op0=ALU.mult, op1=ALU.add),
                           rd=[rt_s[h % 2], w_s_], wr=[sc_state])

        def bisect(n, S, sc_ap, sc_state, junk_ap, junk_state, junk_act=None):
            op(DVE, lambda: nc.vector.memset(lo[0:n, :], -BRK), wr=[bis_s])
            split = junk_act is not None and S >= 2048
            S1 = (int(S * 0.46) // 128) * 128 if split else S
            S2 = S - S1
            wd = BRK
            for it in range(NIT):
                op(DVE, lambda: nc.vector.tensor_scalar(out=mid[0:n, :], in0=lo[0:n, :], scalar1=wd, scalar2=None, op0=ALU.add),
                   rd=[bis_s], wr=[bis_s, mid_s])
                if split:
                    op(ACT, lambda: nc.scalar.activation(out=junk_act[0:n, S1:S], in_=sc_ap[0:n, S1:S], func=AF.Sign, bias=mid[0:n, 0:1], scale=-1.0,
                                                         accum_out=cnta[0:n, 0:1]), rd=[sc_state, mid_s], wr=[junka_s, cnta_s])
                op(DVE, lambda: nc.vector.tensor_scalar(out=junk_ap[0:n, 0:S1], in0=sc_ap[0:n, 0:S1], scalar1=mid[0:n, 0:1], scalar2=None, op0=ALU.is_ge,
                                                        op1=ALU.add, accum_out=cntt[0:n, 0:1]),
                   rd=[sc_state, bis_s], wr=[junk_state, bis_s])
                thr = 255.5
                if split:
                    op(DVE, lambda: nc.vector.scalar_tensor_tensor(out=cntt[0:n, :], in0=cnta[0:n, :], scalar=-0.5, in1=cntt[0:n, :], op0=ALU.mult,
                                                                   op1=ALU.add), rd=[cnta_s, bis_s], wr=[bis_s])
                    thr = 255.5 - S2 / 2.0
                op(DVE, lambda: nc.vector.tensor_scalar(out=geq[0:n, :], in0=cntt[0:n, :], scalar1=thr, scalar2=wd, op0=ALU.is_ge,
                                                        op1=ALU.mult), rd=[bis_s], wr=[bis_s])
                op(DVE, lambda: nc.vector.tensor_tensor(out=lo[0:n, :], in0=lo[0:n, :], in1=geq[0:n, :], op=ALU.add), rd=[bis_s, mid_s], wr=[bis_s])
                wd = wd / 2.0

        def attend(n, nch, sc_ap, sc_state, sel_ap, out_ap, out_state, ch0=0, kt_state=None, v_state=None, hook=None):
            kt_state = kt_state or KT_s
            v_state = v_state or V_s
            steps = [(ch, c) for ch in range(nch) for c in range(4)]

            def prep_group(gi):
                w = min(512, nch * 128 - gi * 512)
                ng = w // 128
                mb = MBg[gi % 2]
                bT = 0
                op(DVE, lambda: nc.vector.tensor_scalar(out=mb[0:n, 0:w], in0=sc_ap[0:n, gi * 512:gi * 512 + w], scalar1=lo[0:n, 0:1],
                                                        scalar2=None, op0=ALU.is_ge), rd=[sc_state, bis_s], wr=[MB_s[gi % 2]])
                for cj in range(ng):
                    op(PE, lambda cj=cj: nc.tensor.transpose(PBb[bT][:, cj, 0:n], mb[0:n, cj * 128:(cj + 1) * 128], identb[0:n, 0:n]),
                       rd=[MB_s[gi % 2], cst_s], wr=[PB_s[bT]])
                op(DVE, lambda: nc.vector.tensor_copy(out=mT[gi % 2][:, 0:ng, 0:n], in_=PBb[bT][:, 0:ng, 0:n]), rd=[PB_s[bT]], wr=[mT_s[gi % 2]])

            def emit_qk(k):
                ch, c = steps[k]
                gi, cj = ch // 4, ch % 4
                if cj == 0 and c == 0:
                    prep_group(gi)
                bank = 1 + (k % 3)
                pb = (c % 2) * 64
                outv = PB[bank][:, 0:4 * n].rearrange("p (g t) -> p g t", g=4)
                op(PE, lambda: nc.tensor.matmul(outv, lhsT=KT[pb:pb + 64, c // 2, (ch0 + ch) * 128:(ch0 + ch + 1) * 128], rhs=qT[pb:pb + 64, c // 2, :, 0:n],
                                                start=True, stop=True), rd=[kt_state, qT_s], wr=[PB_s[bank]])

            emit_qk(0)
            if len(steps) > 1:
                emit_qk(1)
            for k in range(len(steps)):
                ch, c = steps[k]
                gi, cj = ch // 4, ch % 4
                if hook is not None:
                    hook()
                if k + 2 < len(steps):
                    emit_qk(k + 2)
                bank = 1 + (k % 3)
                pt_ = PT[k % 3]
                pts = PT_s[k % 3]
                op(ACT, lambda: nc.scalar.activation(out=pt_[:, 0:4 * n], in_=PB[bank][:, 0:4 * n], func=AF.Exp), rd=[PB_s[bank]], wr=[pts])
                pt3 = pt_[:, 0:4 * n].rearrange("p (g t) -> p g t", g=4)
                op(DVE, lambda: nc.vector.tensor_tensor(out=pt3, in0=pt3, in1=mT[gi % 2][:, cj, 0:n].unsqueeze(1).to_broadcast([128, 4, n]),
                                                        op=ALU.mult), rd=[mT_s[gi % 2]], wr=[pts])
                for g in range(4):
                    op(PE, lambda g=g: nc.tensor.matmul(PB[4 + c][0:n, g * 65:(g + 1) * 65], lhsT=pt_[:, g * n:(g + 1) * n], rhs=Vt[:, ch0 + ch, c, :],
                                                        start=(ch == 0 and g == 0), stop=(ch == nch - 1), skip_group_check=True),
                       rd=[pts, v_state], wr=[PB_s[4 + c]])
            for c in range(4):
                acc = PB[4 + c][0:n, 0:260].rearrange("p (g e) -> p g e", e=65)
                op(DVE, lambda: nc.vector.reciprocal(out=sm[0:n, 4 * c:4 * c + 4], in_=acc[:, :, 64]), rd=[PB_s[4 + c]], wr=[sm_s])
                op(DVE, lambda: nc.vector.tensor_tensor(out=out_ap[0:n, c * 256:(c + 1) * 256].rearrange("p (g d) -> p g d", d=64),
                                                        in0=acc[:, :, 0:64], in1=sm[0:n, 4 * c:4 * c + 4].unsqueeze(2).to_broadcast([n, 4, 64]),
                                                        op=ALU.mult), rd=[PB_s[4 + c], sm_s], wr=[out_state])

        STh_s = [[St(), St()], [St(), St()]]

        def attend_sample(nch, sc_ap, sc_state, out_ap, out_state, ch0, kt_state, v_state, hook=None):
            n = 16

            def prep_group(gi):
                w = min(512, nch * 128 - gi * 512)
                ng = w // 128
                mb = MBg[gi % 2]
                op(DVE, lambda: nc.vector.tensor_scalar(out=mb[0:n, 0:w], in0=sc_ap[0:n, gi * 512:gi * 512 + w], scalar1=lo[0:n, 0:1],
                                                        scalar2=None, op0=ALU.is_ge), rd=[sc_state, bis_s], wr=[MB_s[gi % 2]])
                for cj in range(ng):
                    op(PE, lambda cj=cj: nc.tensor.transpose(PBb[0][:, cj, 0:n], mb[0:n, cj * 128:(cj + 1) * 128], identb[0:n, 0:n]),
                       rd=[MB_s[gi % 2], cst_s], wr=[PB_s[0]])
                op(DVE, lambda: nc.vector.tensor_copy(out=mT[gi % 2][:, 0:ng, 0:n], in_=PBb[0][:, 0:ng, 0:n]), rd=[PB_s[0]], wr=[mT_s[gi % 2]])

            def emit_qk(ch):
                gi, cj = ch // 4, ch % 4
                if cj == 0:
                    prep_group(gi)
                hf = 0
                for c in range(4):
                    pb = (c % 2) * 64
                    col = hf * 128 + (c // 2) * 64
                    outv = PB[1 + (c % 2)][:, col:col + 64].rearrange("p (g t) -> p g t", g=4)
                    op(PE, lambda: nc.tensor.matmul(outv, lhsT=KT[pb:pb + 64, c // 2, (ch0 + ch) * 128:(ch0 + ch + 1) * 128],
                                                    rhs=qT[pb:pb + 64, c // 2, :, 0:n], start=True, stop=True, skip_group_check=True),
                       rd=[kt_state, qT_s], wr=[PB_s[1 + (c % 2)]])

            emit_qk(0)
            for ch in range(nch):
                gi, cj = ch // 4, ch % 4
                hf = 0
                if hook is not None:
                    hook()
                    hook()
                pt_ = PT[ch % 3]
                pts = PT_s[ch % 3]
                for e in range(2):
                    op(ACT, lambda e=e: nc.scalar.activation(out=pt_[:, e * 128:(e + 1) * 128], in_=PB[1 + e][:, hf * 128:(hf + 1) * 128], func=AF.Exp),
                       rd=[PB_s[1 + e]], wr=[pts])
                pt3 = pt_[:, 0:256].rearrange("p (a t) -> p a t", t=n)
                op(DVE, lambda: nc.vector.tensor_tensor(out=pt3, in0=pt3, in1=mT[gi % 2][:, cj, 0:n].unsqueeze(1).to_broadcast([128, 16, n]),
                                                        op=ALU.mult), rd=[mT_s[gi % 2]], wr=[pts])
                if ch + 1 < nch:
                    emit_qk(ch + 1)
                for c in range(4):
                    for g in range(4):
                        a0 = (c % 2) * 128 + (c // 2) * 64 + g * n
                        op(PE, lambda: nc.tensor.matmul(PB[4 + c][0:n, g * 65:(g + 1) * 65], lhsT=pt_[:, a0:a0 + n], rhs=Vt[:, ch0 + ch, c, :],
                                                        start=(ch == 0 and g == 0), stop=(ch == nch - 1), skip_group_check=True),
                           rd=[pts, v_state], wr=[PB_s[4 + c]])
            for c in range(4):
                acc = PB[4 + c][0:n, 0:260].rearrange("p (g e) -> p g e", e=65)
                op(DVE, lambda: nc.vector.reciprocal(out=sm[0:n, 4 * c:4 * c + 4], in_=acc[:, :, 64]), rd=[PB_s[4 + c]], wr=[sm_s])
                op(DVE, lambda: nc.vector.tensor_tensor(out=out_ap[0:n, c * 256:(c + 1) * 256].rearrange("p (g d) -> p g d", d=64),
                                                        in0=acc[:, :, 0:64], in1=sm[0:n, 4 * c:4 * c + 4].unsqueeze(2).to_broadcast([n, 4, 64]),
                                                        op=ALU.mult), rd=[PB_s[4 + c], sm_s], wr=[out_state])

        def project(n, x_ap, x_s, outs):
            fence([sc_s], arena_states)
            rmsnorm_to_T(x_ap, x_s, gmixT, n)
            ug, ug_s = FB[1], FB_s[1]
            vn, vn_s = FB[2], FB_s[2]
            sga, sga_s = FB[3], FB_s[3]
            sgb, sgb_s = FB[4], FB_s[4]

            def after(gi, bank):
                pb_s = PB_s[bank]
                pbk = PB[bank]
                if gi in (0, 1):
                    op(ACT, lambda: nc.scalar.copy(out=k32[0:n, :], in_=pbk[0:n, :]), rd=[pb_s], wr=[k32_s])
                    head_norm(k32[0:n, :], n, 8, gqb, lambda ee: qn_bf[0:n, gi, :, ee * 64:(ee + 1) * 64], [qn_s], e=2)
                elif gi == 2:
                    op(ACT, lambda: nc.scalar.copy(out=k32[0:n, :], in_=pbk[0:n, :]), rd=[pb_s], wr=[k32_s])
                    head_norm(k32[0:n, 0:256], n, 4, gkb, lambda: o32[0:n, 256:512].rearrange("p (h d) -> p h d", d=64), [o32_s])
                    dma(POOL, outs["k"], o32[0:n, 256:512], rd=[o32_s])
                    dma(POOL, outs["v"], k32[0:n, 256:512], rd=[k32_s])
                    if "kv" in outs:
                        op(DVE, lambda: nc.vector.tensor_copy(out=kn_bf[0:n, :], in_=o32[0:n, 256:512]), rd=[o32_s], wr=[knbf_s])
                        op(DVE, lambda: nc.vector.tensor_copy(out=vs_bf[0:n, :], in_=k32[0:n, 256:512]), rd=[k32_s], wr=[smp_s])
                elif gi == 3:
                    op(ACT, lambda: nc.scalar.copy(out=qi_bf[0:n, :], in_=pbk[0:n, :]), rd=[pb_s], wr=[qibf_s])
                elif gi == 4:
                    op(ACT, lambda: nc.scalar.copy(out=ki32[0:n, :], in_=pbk[0:n, 0:72]), rd=[pb_s], wr=[ki32_s])
                    dma(POOL, outs["ki"], ki32[0:n, 0:64], rd=[ki32_s])
                    op(ACT, lambda: nc.scalar.activation(out=absw[0:n, :], in_=ki32[0:n, 64:72], func=AF.Abs, scale=IDXS), rd=[ki32_s], wr=[w_s_])
                    op(ACT, lambda: nc.scalar.activation(out=sgnw[0:n, :], in_=ki32[0:n, 64:72], func=AF.Sign), rd=[ki32_s], wr=[w_s_])
                    if "kv" in outs:
                        for a in range(2):
                            op(DVE, lambda a=a: nc.vector.tensor_copy(out=kid_bf[0:n, a, :], in_=ki32[0:n, 0:64]), rd=[ki32_s], wr=[kid_s])
                elif gi in (5, 6):
                    o = (gi - 5) * 512
                    op(ACT, lambda: nc.scalar.activation(out=ug[0:n, o:o + 512], in_=pbk[0:n, :], func=AF.Gelu_apprx_tanh), rd=[pb_s], wr=[ug_s])
                elif gi in (7, 8):
                    o = (gi - 7) * 512
                    op(ACT, lambda: nc.scalar.activation(out=vn[0:n, o:o + 512], in_=pbk[0:n, :], func=AF.Gelu_apprx_tanh), rd=[pb_s], wr=[vn_s])
                elif gi in (9, 10):
                    o = (gi - 9) * 512
                    op(ACT, lambda: nc.scalar.activation(out=sga[0:n, o:o + 512], in_=pbk[0:n, :], func=AF.Sigmoid), rd=[pb_s], wr=[sga_s])
                else:
                    o = (gi - 11) * 512
                    op(ACT, lambda: nc.scalar.activation(out=sgb[0:n, o:o + 512], in_=pbk[0:n, :], func=AF.Sigmoid), rd=[pb_s], wr=[sgb_s])

            dense(lambda kc: xnT[:, kc, 0:n], 8, n, [g[1] for g in GRPS], lambda gi: gi % 2, [xnT_s], after)
            transposes(lambda j: qn_bf[0:n, j // 4, j % 4, :], 8, n, 2,
                       lambda b0, nb: qT[:, :, :, 0:n].rearrange("p a g t -> p (a g) t")[:, b0:b0 + nb, :], [qT_s], [qn_s])
            transposes(lambda j: qi_bf[0:n, j * 128:(j + 1) * 128], 4, n, 3,
                       lambda b0, nb: qiT[:, b0:b0 + nb, 0:n], [qiT_s], [qibf_s])
            op(DVE, lambda: nc.vector.scalar_tensor_tensor(out=vn_bf[0:n, :], in0=vn[0:n, :], scalar=1.0, in1=vn[0:n, :], op0=ALU.mult,
                                                           op1=ALU.mult, accum_out=ss16[0:n, 0:1]), rd=[vn_s], wr=[vnbf_s, ss16_s])
            rstd_from_ss(ss16[0:n, 0:1], rs16[0:n, 0:1], n, 1, 1.0 / D, ss16_s, rs16_s)
            op(DVE, lambda: nc.vector.scalar_tensor_tensor(out=vn[0:n, :], in0=vn[0:n, :], scalar=rs16[0:n, 0:1], in1=gsgub[0:n, :],
                                                           op0=ALU.mult, op1=ALU.mult), rd=[rs16_s, cst_s], wr=[vn_s])
            dma(POOL, outs["sg"], vn[0:n, :], rd=[vn_s])
            if n == 128:
                op(DVE, lambda: nc.vector.tensor_copy(out=vn_bf[:, :], in_=vn[:, :]), rd=[vn_s], wr=[vnbf_s])
                for g in range(8):
                    bank = 4 + g // 4
                    op(PE, lambda g=g: nc.tensor.matmul(PB[bank][:, (g % 4) * 128:(g % 4 + 1) * 128], lhsT=wsT[:, g, :],
                                                        rhs=vn_bf[:, g * 128:(g + 1) * 128], start=True, stop=True, skip_group_check=True),
                       rd=[vnbf_s, cst_s], wr=[PB_s[bank]])
                for hb in range(2):
                    op(DVE, lambda hb=hb: nc.vector.tensor_tensor(
                        out=vn[:, hb * 512:(hb + 1) * 512].rearrange("p (g c) -> p g c", c=128),
                        in0=PB[4 + hb][:, :].rearrange("p (g c) -> p g c", c=128),
                        in1=bsT[:, hb * 4:hb * 4 + 4].unsqueeze(2).to_broadcast([128, 4, 128]), op=ALU.add),
                       rd=[PB_s[4 + hb], cst_s], wr=[vn_s])
            else:
                v3s = vn[0:n, :].rearrange("p (g c) -> p g c", c=128)
                op(DVE, lambda: nc.vector.tensor_tensor(out=v3s, in0=v3s, in1=coefs[0:n, 0:8].unsqueeze(2).to_broadcast([n, 8, 128]), op=ALU.mult),
                   rd=[cst_s], wr=[vn_s])
                op(DVE, lambda: nc.vector.tensor_tensor(out=v3s, in0=v3s, in1=coefs[0:n, 8:16].unsqueeze(2).to_broadcast([n, 8, 128]), op=ALU.add),
                   rd=[cst_s], wr=[vn_s])
            op(DVE, lambda: nc.vector.tensor_tensor(out=vn[0:n, :], in0=vn[0:n, :], in1=ug[0:n, :], op=ALU.mult), rd=[ug_s], wr=[vn_s])
            op(DVE, lambda: nc.vector.tensor_tensor(out=sgb[0:n, :], in0=sgb[0:n, :], in1=vn[0:n, :], op=ALU.mult), rd=[vn_s], wr=[sgb_s])

        def finish(n, x_ap, x_s, oatt, oatt_s, p_src, y_dst):
            sga, sga_s = FB[3], FB_s[3]
            msgu, msgu_s = FB[4], FB_s[4]
            fence([sc_s], arena_states)
            dma(SP, p32[0:n, :], p_src, wr=[p32_s])
            op(DVE, lambda: nc.vector.tensor_tensor(out=oatt[0:n, :], in0=oatt[0:n, :], in1=sga[0:n, :], op=ALU.mult), rd=[sga_s], wr=[oatt_s])
            op(DVE, lambda: nc.vector.tensor_tensor(out=xn_bf[0:n, :], in0=oatt[0:n, :], in1=msgu[0:n, :], op=ALU.add),
               rd=[oatt_s, msgu_s], wr=[xnbf_s])
            transposes(lambda j: xn_bf[0:n, j * 128:(j + 1) * 128], 8, n, 2, lambda b0, nb: xnT[:, b0:b0 + nb, 0:n], [xnT_s], [xnbf_s])

            def after_o(gi, bank):
                op(DVE, lambda: nc.vector.tensor_tensor(out=x_ap[:, gi * 512:(gi + 1) * 512], in0=x_ap[:, gi * 512:(gi + 1) * 512],
                                                        in1=PB[bank][0:n, :], op=ALU.add), rd=[PB_s[bank]], wr=[x_s])
            dense(lambda kc: xnT[:, kc, 0:n], 8, n, [512, 512], lambda gi: gi % 2, [xnT_s], after_o)
            rmsnorm_to_T(x_ap, x_s, gffnT, n)

            def after_up(gi, bank):
                op(ACT, lambda: nc.scalar.activation(out=rt[gi % 2][0:n, :], in_=PB[bank][0:n, :], func=AF.Relu), rd=[PB_s[bank]], wr=[rt_s[gi % 2]])
                op(DVE, lambda: nc.vector.tensor_tensor(out=h_bf[0:n, gi * 512:(gi + 1) * 512], in0=rt[gi % 2][0:n, :], in1=rt[gi % 2][0:n, :],
                                                        op=ALU.mult), rd=[rt_s[gi % 2]], wr=[hbf_s])
            dense(lambda kc: xnT[:, kc, 0:n], 8, n, [512] * 8, lambda gi: gi % 2, [xnT_s], after_up)
            transposes(lambda j: h_bf[0:n, j * 128:(j + 1) * 128], 32, n, 2, lambda b0, nb: hT[:, b0:b0 + nb, 0:n], [hT_s], [hbf_s])
            for nn in range(2):
                bank = nn % 2
                for kg in range(4):
                    sl, k = w_get()
                    for kc in range(8):
                        op(PE, lambda kc=kc: nc.tensor.matmul(PB[bank][0:n, :], lhsT=hT[:, kg * 8 + kc, 0:n], rhs=WS[sl][:, kc, :],
                                                              start=(kg == 0 and kc == 0), stop=(kg == 3 and kc == 7)),
                           rd=[hT_s, WS_s[sl]], wr=[PB_s[bank]])
                    w_done(k)
                op(DVE, lambda: nc.vector.tensor_tensor(out=x_ap[:, nn * 512:(nn + 1) * 512], in0=x_ap[:, nn * 512:(nn + 1) * 512],
                                                        in1=PB[bank][0:n, :], op=ALU.add), rd=[PB_s[bank]], wr=[x_s])
            rmsnorm_to_T(x_ap, x_s, gpleT, n)
            gate, gate_s = FB[1], FB_s[1]

            def after_pg(gi, bank):
                op(ACT, lambda: nc.scalar.activation(out=gate[0:n, gi * 512:(gi + 1) * 512], in_=PB[bank][0:n, :], func=AF.Sigmoid),
                   rd=[PB_s[bank]], wr=[gate_s])
            dense(lambda kc: xnT[:, kc, 0:n], 8, n, [512, 512], lambda gi: gi % 2, [xnT_s], after_pg)
            op(DVE, lambda: nc.vector.tensor_copy(out=p_bf[0:n, :], in_=p32[0:n, :]), rd=[p32_s], wr=[pbf_s])
            transposes(lambda j: p_bf[0:n, j * 128:(j + 1) * 128], 2, n, 3, lambda b0, nb: pT[:, b0:b0 + nb, 0:n], [pT_s], [pbf_s])

            def after_p(gi, bank):
                op(DVE, lambda: nc.vector.tensor_tensor(out=gate[0:n, gi * 512:(gi + 1) * 512], in0=gate[0:n, gi * 512:(gi + 1) * 512],
                                                        in1=PB[bank][0:n, :], op=ALU.mult), rd=[PB_s[bank]], wr=[gate_s])
                op(DVE, lambda: nc.vector.tensor_tensor(out=x_ap[:, gi * 512:(gi + 1) * 512], in0=x_ap[:, gi * 512:(gi + 1) * 512],
                                                        in1=gate[0:n, gi * 512:(gi + 1) * 512], op=ALU.add), rd=[gate_s], wr=[x_s])
            dense(lambda kc: pT[:, kc, 0:n], 2, n, [512, 512], lambda gi: gi % 2, [pT_s], after_p)
            dma(POOL, y_dst, x_ap, rd=[x_s])

        n = 128
        fence(arena_states + [sc_s, sc16_s, sct_s], [wkvk_s])
        dma(SP, wkvk[:, :, 0:512], win_s[2][:, :, :], wr=[wkvk_s])
        dma(SP, wkvk[:, :, 512:576], win_s[4][:, :, 0:64], wr=[wkvk_s])
        xnT2 = al("xnT2", [128, 8, 128], BF16, 0); xnT2_s = St()
        fence(arena_states + [sc_s, sc16_s, sct_s], [xnT2_s])
        xnTs = [(xnT, xnT_s), (xnT2, xnT2_s)]
        ssA = sb("ssA", [128, 2], F32); ssA_s = St(); rsA_s = St()

        def stageA(kb):
            xt, xt_s = FB[kb % 2], FB_s[kb % 2]
            xT, xT_s = xnTs[kb % 2]
            dma(SP, xt[:, :], xb_d[kb * 128:(kb + 1) * 128, :], wr=[xt_s])
            op(DVE, lambda: nc.vector.scalar_tensor_tensor(out=xn_bf[:, :], in0=xt[:, :], scalar=1.0, in1=xt[:, :], op0=ALU.mult, op1=ALU.mult,
                                                           accum_out=ssA[:, 0:1]), rd=[xt_s], wr=[xnbf_s, ssA_s])
            rstd_from_ss(ssA[:, 0:1], ssA[:, 1:2], 128, 1, 1.0 / D, ssA_s, rsA_s)
            op(DVE, lambda: nc.vector.tensor_scalar(out=xn_bf[:, :], in0=xt[:, :], scalar1=ssA[:, 1:2], scalar2=None, op0=ALU.mult),
               rd=[xt_s, rsA_s], wr=[xnbf_s])
            transposes(lambda j: xn_bf[:, j * 128:(j + 1) * 128], 8, 128, 0,
                       lambda b0, nb: xT[:, b0:b0 + nb, :], [xT_s], [xnbf_s], gainT=gmixT)

        def stageB(kb):
            xT, xT_s = xnTs[kb % 2]
            for kc in range(8):
                op(PE, lambda kc=kc: nc.tensor.matmul(PB[0][:, :], lhsT=xT[:, kc, :], rhs=wkvk[:, kc, 0:512], start=(kc == 0), stop=(kc == 7)),
                   rd=[xT_s, wkvk_s], wr=[PB_s[0]])
            for kc in range(8):
                op(PE, lambda kc=kc: nc.tensor.matmul(PB[1][:, 0:64], lhsT=xT[:, kc, :], rhs=wkvk[:, kc, 512:576], start=(kc == 0), stop=(kc == 7)),
                   rd=[xT_s, wkvk_s], wr=[PB_s[1]])
            op(ACT, lambda: nc.scalar.copy(out=k32[:, :], in_=PB[0][:, :]), rd=[PB_s[0]], wr=[k32_s])
            head_norm(k32[:, 0:256], n, 4, gkb, lambda: kn_bf[:, :].rearrange("p (h d) -> p h d", d=64), [knbf_s])
            op(ACT, lambda: nc.scalar.copy(out=v_bf[:, :], in_=k32[:, 256:512]), rd=[k32_s], wr=[vbf_s])
            for a in range(2):
                op(ACT, lambda a=a: nc.scalar.copy(out=kid_bf[:, a, :], in_=PB[1][:, 0:64]), rd=[PB_s[1]], wr=[kid_s])
            kv_append(kb, n, kn_bf, knbf_s, v_bf, vbf_s, kid_bf, kid_s)

        RS["act"] = True
        stageA(0)
        for kb in range(NBLK):
            if kb + 1 < NBLK:
                stageA(kb + 1)
            stageB(kb)
            for _ in range(4):
                if conv_jobs:
                    conv_jobs.pop(0)()
        fence([xnT2_s], arena_states)
        RS["act"] = False
        while conv_jobs:
            conv_jobs.pop(0)()
        for i in range(NDS):
            if dcnt[i] > 0:
                SP.wait((dsems[i], dcnt[i]))

        for i in range(NSLOT):
            m, second = i // 2, i % 2
            nch = 8 * m + (8 if second else 4)
            xt, xt_s = FB[0], FB_s[0]
            dma(SP, xt[:, :], xo_d[i, :, :], wr=[xt_s])
            project(n, xt[:, :], xt_s, {"k": ko_d[i, :, :], "v": vo_d[i, :, :], "ki": kio_d[i, :, :], "sg": sgo_d[i, :, :]})
            fence(arena_states, [sc_s])
            indexer(n, nch, scores, sc_s)
            S = nch * 128
            op(DVE, lambda: nc.vector.tensor_tensor(out=scores[:, S - 512:S], in0=scores[:, S - 512:S], in1=pen[:, second, :], op=ALU.add),
               rd=[cst_s], wr=[sc_s])
            fence([FB_s[1], FB_s[2]], [junk_s, junka_s])
            bisect(n, S, scores[:, 0:S], sc_s, junk8[:, 0:S], junk_s, junk_act=junki8)
            fence([junk_s, junka_s], [FB_s[1], FB_s[2]])
            oat, oat_s = FB[1], FB_s[1]
            attend(n, nch, scores, sc_s, I4[:, :, :], oat, oat_s)
            finish(n, xt[:, :], xt_s, oat, oat_s, po_d[i, :, :], yo_d[i, :, :])

        n = NS
        xs_t, xs_s = FB[0], FB_s[0]
        dma(SP, xs_t[0:n, :], xs_d[:, :], wr=[xs_s])
        project(n, xs_t[0:n, :], xs_s, {"k": kss_d[:, :], "v": vss_d[:, :], "ki": kis_d[:, :], "sg": sgs_d[:, :], "kv": True})
        for j in range(2):
            op(PE, lambda j=j: nc.tensor.transpose(PBb[3][:, j, 0:n], kn_bf[0:n, j * 128:(j + 1) * 128], identb[0:n, 0:n]),
               rd=[knbf_s, cst_s], wr=[PB_s[3]])
        op(PE, lambda: nc.tensor.transpose(PBb[3][:, 2, 0:n], kid_bf[0:n, :, :].rearrange("p a d -> p (a d)"), identb[0:n, 0:n]),
           rd=[kid_s, cst_s], wr=[PB_s[3]])
        op(ACT, lambda: nc.scalar.copy(out=ksT[:, :, :], in_=PBb[3][:, 0:2, 0:n]), rd=[PB_s[3]], wr=[smp_s])
        op(ACT, lambda: nc.scalar.copy(out=kisT[:, :], in_=PBb[3][:, 2, 0:n]), rd=[PB_s[3]], wr=[smp_s])

        def gather(dst_ap, src2d, col):
            return lambda: nc.gpsimd.indirect_dma_start(out=dst_ap, out_offset=None, in_=src2d,
                                                        in_offset=bass.IndirectOffsetOnAxis(ap=idxa[:, col:col + 1], axis=0))
        fence(arena_states, [sc16_s] + sctR_s)
        kiR_s = [St(), St()]; KR_s = [St(), St()]; VR_s = [St(), St()]
        fence([kiT_s], kiR_s); fence([KT_s], KR_s); fence([V_s], VR_s)
        op(POOL, lambda: nc.gpsimd.memset(sc16[:, :], -1e30), wr=[sc16_s])
        for r in range(2):
            op(POOL, lambda r=r: nc.gpsimd.memset(kiT[:, r * S_S + 2048:(r + 1) * S_S], 0.0), wr=[kiR_s[r]])
            op(POOL, lambda r=r: nc.gpsimd.memset(KT[:, :, r * S_S + 2048:(r + 1) * S_S], 0.0), wr=[KR_s[r]])
            op(POOL, lambda r=r: nc.gpsimd.memset(Vt[:, r * 17 + 16, :, 0:64], 0.0), wr=[VR_s[r]])
        def s1_loads(b):
            r = b % 2
            k0 = r * S_S
            jobsA, jobsB = [], []
            for pg in range(NPG):
                def jobA(pg=pg):
                    sl = (b * NPG + pg) % NPB
                    col = b * NPG + pg
                    dma(POOL, None, None, rd=[cst_s], wr=[pgI_s[sl]], fn=gather(pgI[sl][:, 0, :], cki_d, col))

                def jobB(pg=pg):
                    sl = (b * NPG + pg) % NPB
                    bk = 2 + (pg % 2)
                    for a in range(2):
                        op(PE, lambda a=a: nc.tensor.transpose(PBb[bk][a * 64:(a + 1) * 64, 0, :], pgI[sl][:, 0, :], identb[:]),
                           rd=[pgI_s[sl], cst_s], wr=[PB_s[bk]])
                    op(ACT, lambda: nc.scalar.copy(out=kiT[:, k0 + pg * 128:k0 + (pg + 1) * 128], in_=PBb[bk][:, 0, :]),
                       rd=[PB_s[bk]], wr=[kiR_s[r]])
                jobsA.append(jobA)
                jobsB.append(jobB)
            jobs = skew(jobsA, jobsB)
            jobs.append(lambda: op(ACT, lambda: nc.scalar.copy(out=kiT[:, k0 + 2048:k0 + 2049], in_=kisT[:, b:b + 1]), rd=[smp_s], wr=[kiR_s[r]]))
            return jobs

        def skew(jobsA, jobsB, ahead=3):
            out = list(jobsA[:ahead])
            for p in range(len(jobsB)):
                out.append(jobsB[p])
                if p + ahead < len(jobsA):
                    out.append(jobsA[p + ahead])
            return out

        def make_hook(jobs, every):
            st = {"i": 0}

            def hook():
                st["i"] += 1
                if st["i"] % every == 0 and jobs:
                    jobs.pop(0)()
            return hook

        pending = s1_loads(0)
        for b in range(NS):
            r = b % 2
            k0 = r * S_S
            while pending:
                pending.pop(0)()
            pending = s1_loads(b + 1) if b + 1 < NS else []
            indexer(n, 17, sctR[r], sctR_s[r], k0=k0, ki_state=kiR_s[r], hook=make_hook(pending, 1))
            op(DVE, lambda b=b: nc.vector.scalar_tensor_tensor(out=sc16[:, 0:2049], in0=sctR[r][:, 0:2049], scalar=identf[0:16, b:b + 1],
                                                               in1=sc16[:, 0:2049], op0=ALU.mult, op1=ALU.add) if b > 0 else
               nc.vector.tensor_scalar(out=sc16[:, 0:2049], in0=sctR[r][:, 0:2049], scalar1=identf[0:16, 0:1], scalar2=None, op0=ALU.mult),
               rd=[sctR_s[r], cst_s], wr=[sc16_s])
        fence([sctR_s[1]], [sct_s])
        bisect(n, 2049, sc16[:, 0:2049], sc16_s, sct[:, 0:2049], sct_s)
        oat, oat_s = FB[1], FB_s[1]
        oatt_s16, oas_s = FB[2], FB_s[2]
        def s2_loads(b):
            r = b % 2
            k0 = r * S_S
            c0 = r * 17
            jobsA, jobsB = [], []
            for pg in range(NPG):
                def jobA(pg=pg):
                    sl = (b * NPG + pg) % NPB
                    col = b * NPG + pg
                    dma(POOL, None, None, rd=[cst_s], wr=[pgK_s[sl]], fn=gather(pgK[sl][:, :], ck_d, col))
                    dma(POOL, None, None, rd=[cst_s], wr=[pgV_s[sl]], fn=gather(pgV[sl][:, :], cv_d, col))
                jobsA.append(jobA)

                def job(pg=pg):
                    sl = (b * NPG + pg) % NPB
                    bk = 0
                    for j in range(2):
                        op(PE, lambda j=j: nc.tensor.transpose(PBb[bk][:, j, :], pgK[sl][:, j * 128:(j + 1) * 128], identb[:]),
                           rd=[pgK_s[sl], cst_s], wr=[PB_s[bk]])
                    op(ACT, lambda: nc.scalar.copy(out=KT[:, :, k0 + pg * 128:k0 + (pg + 1) * 128], in_=PBb[bk][:, 0:2, :]),
                       rd=[PB_s[bk]], wr=[KR_s[r]])
                    op(DVE, lambda: nc.vector.tensor_copy(out=Vt[:, c0 + pg, :, 0:64], in_=pgV[sl][:, :].rearrange("p (c d) -> p c d", d=64)),
                       rd=[pgV_s[sl]], wr=[VR_s[r]])
                jobsB.append(job)
            jobs = skew(jobsA, jobsB)

            def last():
                op(ACT, lambda: nc.scalar.copy(out=KT[:, :, k0 + 2048:k0 + 2049], in_=ksT[:, :, b:b + 1]), rd=[smp_s], wr=[KR_s[r]])
                op(PE, lambda: nc.tensor.matmul(PB[0][0:1, 0:256], lhsT=identb[0:16, b:b + 1], rhs=vs_bf[0:16, :], start=True, stop=True),
                   rd=[smp_s, cst_s], wr=[PB_s[0]])
                op(ACT, lambda: nc.scalar.copy(out=Vt[0:1, c0 + 16, :, 0:64], in_=PB[0][0:1, 0:256].rearrange("p (c d) -> p c d", d=64)),
                   rd=[PB_s[0]], wr=[VR_s[r]])
            jobs.append(last)
            return jobs

        pending = s2_loads(0)
        for b in range(NS):
            r = b % 2
            c0 = r * 17
            while pending:
                pending.pop(0)()
            pending = s2_loads(b + 1) if b + 1 < NS else []
            attend_sample(17, sc16, sc16_s, oat, oat_s, c0, KR_s[r], VR_s[r], hook=make_hook(pending, 1))
            if b == 0:
                op(DVE, lambda: nc.vector.tensor_scalar(out=oatt_s16[0:16, :], in0=oat[0:16, :], scalar1=identf[0:16, 0:1], scalar2=None,
                                                        op0=ALU.mult), rd=[oat_s, cst_s], wr=[oas_s])
            else:
                op(DVE, lambda b=b: nc.vector.scalar_tensor_tensor(out=oatt_s16[0:16, :], in0=oat[0:16, :], scalar=identf[0:16, b:b + 1],
                                                                   in1=oatt_s16[0:16, :], op0=ALU.mult, op1=ALU.add),
                   rd=[oat_s, cst_s], wr=[oas_s])
        fence(sctR_s, [sct_s])
        fence([sc16_s, sct_s], arena_states)
        finish(n, xs_t[0:n, :], xs_s, oatt_s16, oas_s, ps_d[:, :], ys_d[:, :])

        for i in range(NDS):
            if dcnt[i] > 0:
                POOL.wait((dsems[i], dcnt[i]))
    return nc


_NC_CACHE = {}


def _slot_block(r, i):
    m, second = i // 2, i % 2
    return 8 * m + (7 - r if second else r)


def kernel(x_prompt, x_sample, cache_k, cache_v, cache_kidx, page_table, p_prompt, p_sample,
           g_mix, w_in, g_q, g_k, g_sgu, w_s, b_s, w_o, g_ffn, w_up, w_down, g_ple, w_pg, w_p):
    f32 = np.float32
    A = lambda a: np.ascontiguousarray(np.asarray(a))
    x_prompt = A(x_prompt); x_sample = A(x_sample); p_prompt = A(p_prompt); p_sample = A(p_sample)
    ck = A(cache_k).reshape(N_PHYS * 128, 256)
    cv = A(cache_v).reshape(N_PHYS * 128, 256)
    cki = A(cache_kidx).reshape(N_PHYS * 128, 64)
    pt_all = A(page_table).astype(np.int32)
    shared = {
        "ck": ck, "cv": cv, "cki": cki,
        "w_in": A(w_in)[0], "w_o": A(w_o)[0], "w_up": A(w_up)[0], "w_down": A(w_down)[0], "w_pg": A(w_pg)[0], "w_p": A(w_p)[0],
        "g_mix": A(g_mix)[0], "g_q": A(g_q)[0], "g_k": A(g_k)[0], "g_sgu": A(g_sgu)[0], "w_s": A(w_s)[0], "b_s": A(b_s)[0],
        "g_ffn": A(g_ffn)[0], "g_ple": A(g_ple)[0],
    }
    tt = np.arange(128)[:, None]
    ss = np.arange(512)[None, :]
    in_maps = []
    for c in range(8):
        bi, r = c // 4, c % 4
        blocks = [_slot_block(r, i) for i in range(NSLOT)]
        pen = np.zeros((128, 2, 512), f32)
        pen[:, 0, :] = np.where(ss <= r * 128 + tt, 0.0, -1e30)
        pen[:, 1, :] = np.where(ss <= (3 - r) * 128 + tt, 0.0, -1e30)
        m = dict(shared)
        m["xb"] = x_prompt[bi]
        m["xo"] = np.stack([x_prompt[bi, j * 128:(j + 1) * 128] for j in blocks])
        m["po"] = np.stack([p_prompt[0, bi, j * 128:(j + 1) * 128] for j in blocks])
        m["xs"] = x_sample[c * NS:(c + 1) * NS, 0]
        m["ps"] = p_sample[0, c * NS:(c + 1) * NS, 0]
        m["pt"] = np.ascontiguousarray(pt_all[c * NS:(c + 1) * NS].reshape(-1))
        m["pen"] = pen
        in_maps.append(m)
    if "nc" not in _NC_CACHE:
        _NC_CACHE["nc"] = build()
    res = run_bass_kernel_spmd(_NC_CACHE["nc"], in_maps, core_ids=list(range(8)))
    R = res.results
    y_p = np.zeros((2, SEQ, D), f32); y_s = np.zeros((128, 1, D), f32)
    nk = np.zeros((1, 2, SEQ, 4, 64), f32); nv = np.zeros((1, 2, SEQ, 4, 64), f32)
    nki = np.zeros((1, 2, SEQ, 64), f32); nsg = np.zeros((1, 2, SEQ, D), f32)
    sk = np.zeros((1, 128, 1, 4, 64), f32); sv = np.zeros((1, 128, 1, 4, 64), f32)
    ski = np.zeros((1, 128, 1, 64), f32); ssg = np.zeros((1, 128, 1, D), f32)
    for c in range(8):
        bi, r = c // 4, c % 4
        o = R[c]
        for i in range(NSLOT):
            j = _slot_block(r, i)
            sl = slice(j * 128, (j + 1) * 128)
            y_p[bi, sl] = o["yo"][i]
            nk[0, bi, sl] = o["ko"][i].reshape(128, 4, 64)
            nv[0, bi, sl] = o["vo"][i].reshape(128, 4, 64)
            nki[0, bi, sl] = o["kio"][i]
            nsg[0, bi, sl] = o["sgo"][i]
        s2 = slice(c * NS, (c + 1) * NS)
        y_s[s2, 0] = o["ys"]
        sk[0, s2, 0] = o["kss"].reshape(NS, 4, 64)
        sv[0, s2, 0] = o["vss"].reshape(NS, 4, 64)
        ski[0, s2, 0] = o["kis"]
        ssg[0, s2, 0] = o["sgs"]
    return (y_p, y_s, nk, nv, nki, nsg, sk, sv, ski, ssg)
```

```python
import contextlib
import numpy as np
import concourse.bass as bass
import concourse.mybir as mybir
from concourse.bass_utils import run_bass_kernel_spmd

F32 = mybir.dt.float32
BF16 = mybir.dt.bfloat16
I32 = mybir.dt.int32
ALU = mybir.AluOpType
AF = mybir.ActivationFunctionType
AX = mybir.AxisListType

D = 1024
SEQ = 8192
NBLK = 64
NSLOT = 16
NS = 16
NPG = 16
S_S = 2176
N_PHYS = 2560
INW = 6216
DFF = 4096
PLE = 256
EPS = 1e-6
ATTN_SCALE = 64 ** -0.5
IDXS = (64 ** -0.5) * (8 ** -0.5)
NIT = 20
BRK = 16.0
NEGM = -30000.0
GRPS = [(0, 512), (512, 512), (1024, 512), (1536, 512), (2048, 72), (2120, 512), (2632, 512),
        (3144, 512), (3656, 512), (4168, 512), (4680, 512), (5192, 512), (5704, 512)]


class St:
    __slots__ = ("w", "r")

    def __init__(self):
        self.w = None
        self.r = {}


class Eng:
    def __init__(self, nc, es, name, h):
        self.h = h
        self.name = name
        self.sem = es.enter_context(nc.semaphore("e_" + name))
        self.cnt = 0
        self.seen = {}

    def wait(self, tok):
        sem, val = tok
        if self.seen.get(sem.num, 0) < val:
            self.h.wait_ge(sem, val)
            self.seen[sem.num] = val


def build():
    nc = bass.Bass("TRN2", target_bir_lowering=False)
    dt_in = lambda n, s, d=F32: nc.dram_tensor(n, s, d, kind="ExternalInput").ap()
    dt_out = lambda n, s, d=F32: nc.dram_tensor(n, s, d, kind="ExternalOutput").ap()
    xb_d = dt_in("xb", [SEQ, D])
    xo_d = dt_in("xo", [NSLOT, 128, D])
    po_d = dt_in("po", [NSLOT, 128, PLE])
    xs_d = dt_in("xs", [NS, D])
    ps_d = dt_in("ps", [NS, PLE])
    ck_d = dt_in("ck", [N_PHYS * 128, 256])
    cv_d = dt_in("cv", [N_PHYS * 128, 256])
    cki_d = dt_in("cki", [N_PHYS * 128, 64])
    pt_d = dt_in("pt", [NS * NPG], I32)
    pen_d = dt_in("pen", [128, 2, 512])
    win_d = dt_in("w_in", [D, INW])
    wo_d = dt_in("w_o", [D, D])
    wup_d = dt_in("w_up", [D, DFF])
    wdn_d = dt_in("w_down", [DFF, D])
    wpg_d = dt_in("w_pg", [D, D])
    wp_d = dt_in("w_p", [PLE, D])
    gmix_d = dt_in("g_mix", [D])
    gq_d = dt_in("g_q", [64])
    gk_d = dt_in("g_k", [64])
    gsgu_d = dt_in("g_sgu", [D])
    ws_d = dt_in("w_s", [8, 128, 128])
    bs_d = dt_in("b_s", [8, 128])
    gffn_d = dt_in("g_ffn", [D])
    gple_d = dt_in("g_ple", [D])

    yo_d = dt_out("yo", [NSLOT, 128, D])
    ko_d = dt_out("ko", [NSLOT, 128, 256])
    vo_d = dt_out("vo", [NSLOT, 128, 256])
    kio_d = dt_out("kio", [NSLOT, 128, 64])
    sgo_d = dt_out("sgo", [NSLOT, 128, D])
    ys_d = dt_out("ys", [NS, D])
    kss_d = dt_out("kss", [NS, 256])
    vss_d = dt_out("vss", [NS, 256])
    kis_d = dt_out("kis", [NS, 64])
    sgs_d = dt_out("sgs", [NS, D])

    def scr(n, shape):
        return nc.dram_tensor(n, shape, BF16, kind="Internal").ap()
    win_s = [scr("win_s%d" % i, [128, 8, w]) for i, (o, w) in enumerate(GRPS)]
    wo_s = [scr("wo_s%d" % i, [128, 8, 512]) for i in range(2)]
    wup_s = [scr("wup_s%d" % i, [128, 8, 512]) for i in range(8)]
    wdn_s = [scr("wdn_s%d" % i, [128, 8, 512]) for i in range(8)]
    wpg_s = [scr("wpg_s%d" % i, [128, 8, 512]) for i in range(2)]
    wp_s = [scr("wp_s%d" % i, [128, 2, 512]) for i in range(2)]

    es = contextlib.ExitStack()
    with es:
        PE = Eng(nc, es, "pe", nc.tensor)
        ACT = Eng(nc, es, "act", nc.scalar)
        DVE = Eng(nc, es, "dve", nc.vector)
        POOL = Eng(nc, es, "pool", nc.gpsimd)
        SP = Eng(nc, es, "sp", nc.sync)
        NDS = 24
        dsems = [es.enter_context(nc.semaphore("d%d" % i)) for i in range(NDS)]
        dcnt = [0] * NDS
        dstate = {"i": 0}

        def deps_of(rd, wr):
            deps = []
            for s in rd:
                if s.w is not None:
                    deps.append(s.w)
            for s in wr:
                if s.w is not None:
                    deps.append(s.w)
                deps.extend(s.r.values())
            return deps

        def op(E, fn, rd=(), wr=()):
            for tok in deps_of(rd, wr):
                if E is PE and tok[0] is PE.sem:
                    continue
                E.wait(tok)
            ins = fn()
            E.cnt += 1
            ins.then_inc(E.sem, 1)
            tok = (E.sem, E.cnt)
            for s in rd:
                s.r[E.name] = tok
            for s in wr:
                s.w = tok
                s.r = {}
            return tok

        def dma(Q, out, in_, rd=(), wr=(), fn=None):
            for tok in deps_of(rd, wr):
                Q.wait(tok)
            i = dstate["i"]
            dstate["i"] = (i + 1) % NDS
            if dcnt[i] > 0:
                Q.wait((dsems[i], dcnt[i]))
            if fn is None:
                ins = Q.h.dma_start(out=out, in_=in_)
            else:
                ins = fn()
            dcnt[i] += 16
            ins.then_inc(dsems[i], 16)
            tok = (dsems[i], dcnt[i])
            for s in rd:
                s.r["dma%d" % i] = tok
            for s in wr:
                s.w = tok
                s.r = {}
            return tok

        def fence(frm, to):
            for t in to:
                for f in frm:
                    if f is t:
                        continue
                    if f.w is not None:
                        t.r["f%d" % id(f)] = f.w
                    for k, v in list(f.r.items()):
                        t.r["f%d%s" % (id(f), k)] = v

        def sb(name, shape, dt):
            return nc.alloc_sbuf_tensor("sb_" + name, shape, dt)

        KT = sb("KT", [128, 2, SEQ], BF16); KT_s = St()
        Vt = sb("Vt", [128, NBLK, 4, 65], BF16); V_s = St()
        kiT = sb("kiT", [128, SEQ], BF16); kiT_s = St()
        arena_base = nc.sbuf_base
        scores = sb("scores", [128, SEQ], F32); sc_s = St()
        al = lambda n, shape, dt, off: nc.alloc_sbuf_tensor_at("al_" + n, shape, dt, offset=arena_base + off)
        h_bf = al("h_bf", [128, DFF], BF16, 0); hbf_s = St()
        hT = al("hT", [128, 32, 128], BF16, 8192); hT_s = St()
        xn_bf = al("xn_bf", [128, D], BF16, 16384); xnbf_s = St()
        xnT = al("xnT", [128, 8, 128], BF16, 18432); xnT_s = St()
        vn_bf = al("vn_bf", [128, D], BF16, 20480); vnbf_s = St()
        wkvk = al("wkvk", [128, 8, 576], BF16, 22528); wkvk_s = St()
        arena_states = [hbf_s, hT_s, xnbf_s, xnT_s, vnbf_s, wkvk_s]
        sc16 = scores[0:16, 0:S_S]; sc16_s = St()
        sct = scores[0:16, S_S:2 * S_S]; sct_s = St()
        sctR = [sct, scores[0:16, 2 * S_S:3 * S_S]]; sctR_s = [sct_s, St()]
        WS = [sb("ws%d" % i, [128, 8, 512], BF16) for i in range(3)]
        WS_s = [St() for _ in range(3)]
        FB0 = sb("fb0", [128, D], F32)
        FB12 = sb("fb12", [128, 2 * D], F32)
        FB3 = sb("fb3", [128, D], F32)
        FB4 = sb("fb4", [128, D], F32)
        FB = [FB0[:, :], FB12[:, 0:D], FB12[:, D:2 * D], FB3[:, :], FB4[:, :]]
        FB_s = [St() for _ in range(5)]
        junk8 = FB12[:, :].bitcast(mybir.dt.uint8)
        junki8 = FB12[:, :].bitcast(mybir.dt.int8)
        gmixT = sb("gmixT", [128, 8], F32); gffnT = sb("gffnT", [128, 8], F32); gpleT = sb("gpleT", [128, 8], F32)
        gsgub = sb("gsgub", [128, D], F32)
        coefs = sb("coefs", [16, 16], F32)
        gqb = sb("gqb", [128, 64], F32); gkb = sb("gkb", [128, 64], F32)
        cst_s = St()
        qn_bf = sb("qn_bf", [128, 2, 4, 128], BF16); qn_s = St()
        qT = sb("qT", [128, 2, 4, 128], BF16); qT_s = St()
        qi_bf = sb("qi_bf", [128, 512], BF16); qibf_s = St()
        qiT = sb("qiT", [128, 4, 128], BF16); qiT_s = St()
        absw = sb("absw", [128, 8], F32); sgnw = sb("sgnw", [128, 8], F32); w_s_ = St()
        kn_bf = sb("kn_bf", [128, 256], BF16); knbf_s = St()
        v_bf = sb("v_bf", [128, 256], BF16); vbf_s = St()
        kid_bf = sb("kid_bf", [128, 2, 64], BF16); kid_s = St()
        k32 = sb("k32", [128, 512], F32); k32_s = St()
        o32 = sb("o32", [128, 512], F32); o32_s = St()
        ki32 = sb("ki32", [128, 72], F32); ki32_s = St()
        rt = [sb("rt%d" % i, [128, 512], F32) for i in range(3)]; rt_s = [St(), St(), St()]
        PT = [sb("PT%d" % i, [128, 512], BF16) for i in range(3)]; PT_s = [St(), St(), St()]
        mT = [sb("mT%d" % i, [128, 4, 128], BF16) for i in range(2)]; mT_s = [St(), St()]
        MBg = [sb("MBg%d" % i, [128, 512], BF16) for i in range(2)]; MB_s = [St(), St()]
        identb = sb("identb", [128, 128], BF16); identf = sb("identf", [16, 16], F32)
        I4 = sb("I4", [128, 4, 128], BF16); I416 = sb("I416", [16, 4, 16], BF16)
        wsT = sb("wsT", [128, 8, 128], BF16); bsT = sb("bsT", [128, 8], F32)
        pen = sb("pen", [128, 2, 512], F32)
        p32 = sb("p32", [128, PLE], F32); p32_s = St()
        p_bf = sb("p_bf", [128, PLE], BF16); pbf_s = St()
        pT = sb("pT", [128, 2, 128], BF16); pT_s = St()
        sm = sb("sm", [128, 64], F32); sm_s = St()
        lo = sb("lo", [128, 1], F32); mid = sb("mid", [128, 1], F32); cntt = sb("cntt", [128, 1], F32)
        geq = sb("geq", [128, 1], F32); bis_s = St(); junk_s = St()
        cnta = sb("cnta", [128, 1], F32); cnta_s = St(); mid_s = St(); junka_s = St()
        mhalf = sb("mhalf", [128, 16], F32)
        ss16 = sb("ss16", [128, 16], F32); ss16_s = St()
        rs16 = sb("rs16", [128, 16], F32); rs16_s = St()
        ptb = sb("ptb", [128, NS * NPG], I32); idxa = sb("idxa", [128, NS * NPG], I32); iop = sb("iop", [128, 1], I32)
        NPB = 4
        pgK = [sb("pgK%d" % i, [128, 256], BF16) for i in range(NPB)]; pgK_s = [St() for _ in range(NPB)]
        pgV = [sb("pgV%d" % i, [128, 256], BF16) for i in range(NPB)]; pgV_s = [St() for _ in range(NPB)]
        pgI = [sb("pgI%d" % i, [128, 2, 64], BF16) for i in range(NPB)]; pgI_s = [St() for _ in range(NPB)]
        ksT = sb("ksT", [128, 2, 16], BF16); kisT = sb("kisT", [128, 16], BF16); vs_bf = sb("vs_bf", [16, 256], BF16)
        smp_s = St()
        ws32 = sb("ws32", [128, 128], F32); ws32_s = St()
        wsb = sb("wsb", [128, 128], BF16); wsb_s = St()

        PB = [nc.alloc_psum_tensor("pb%d" % i, [128, 512], F32) for i in range(8)]
        PB_s = [St() for _ in range(8)]
        PBb = [PB[i][:].bitcast(BF16).rearrange("p (a b) -> p a b", a=8) for i in range(8)]

        def transposes(src_ap_fn, nblk, n, bank0, dst_fn, dst_states, src_states, evac=None, gainT=None):
            for b0 in range(0, nblk, 8):
                nb = min(8, nblk - b0)
                bank = 2 + (bank0 + b0 // 8) % 2
                if gainT is not None:
                    for j in range(nb):
                        op(PE, lambda j=j: nc.tensor.transpose(PBb[bank][:, j, 0:n], src_ap_fn(b0 + j), identb[0:n, 0:n]),
                           rd=list(src_states) + [cst_s], wr=[PB_s[bank]])
                    op(DVE, lambda: nc.vector.tensor_tensor(out=dst_fn(b0, nb), in0=PBb[bank][:, 0:nb, 0:n],
                                                            in1=gainT[:, b0:b0 + nb].unsqueeze(2).to_broadcast([128, nb, n]), op=ALU.mult),
                       rd=[PB_s[bank], cst_s], wr=dst_states)
                    continue
                for j in range(nb):
                    tok = op(PE, lambda j=j: nc.tensor.transpose(PBb[bank][:, j, 0:n], src_ap_fn(b0 + j), identb[0:n, 0:n]),
                             rd=list(src_states) + [cst_s], wr=[PB_s[bank]])
                E = evac or ACT
                if E is ACT:
                    op(ACT, lambda: nc.scalar.copy(out=dst_fn(b0, nb), in_=PBb[bank][:, 0:nb, 0:n]),
                       rd=[PB_s[bank]], wr=dst_states)
                else:
                    op(DVE, lambda: nc.vector.tensor_copy(out=dst_fn(b0, nb), in_=PBb[bank][:, 0:nb, 0:n]),
                       rd=[PB_s[bank]], wr=dst_states)

        RS = {"act": False}

        def rstd_from_ss(ss_ap, out_ap, n, ncol, inv_d, rd_s, wr_s):
            op(DVE, lambda: nc.vector.tensor_scalar(out=ss_ap, in0=ss_ap, scalar1=inv_d, scalar2=EPS, op0=ALU.mult, op1=ALU.add),
               rd=[], wr=[rd_s])
            if RS["act"]:
                op(ACT, lambda: nc.scalar.activation(out=out_ap, in_=ss_ap, func=AF.Sqrt), rd=[rd_s], wr=[wr_s])
                op(DVE, lambda: nc.vector.reciprocal(out=out_ap, in_=out_ap), rd=[], wr=[wr_s])
            else:
                op(POOL, lambda: nc.gpsimd.tensor_tensor(out=out_ap, in0=ss_ap, in1=mhalf[0:n, 0:ncol], op=ALU.pow),
                   rd=[rd_s, cst_s], wr=[wr_s])

        def rmsnorm_to_T(x_ap, x_s, gain, n):
            op(DVE, lambda: nc.vector.scalar_tensor_tensor(out=xn_bf[0:n, :], in0=x_ap, scalar=1.0, in1=x_ap, op0=ALU.mult, op1=ALU.mult,
                                                           accum_out=ss16[0:n, 0:1]),
               rd=[x_s], wr=[xnbf_s, ss16_s])
            rstd_from_ss(ss16[0:n, 0:1], rs16[0:n, 0:1], n, 1, 1.0 / D, ss16_s, rs16_s)
            op(DVE, lambda: nc.vector.tensor_scalar(out=xn_bf[0:n, :], in0=x_ap, scalar1=rs16[0:n, 0:1], scalar2=None, op0=ALU.mult),
               rd=[x_s, rs16_s], wr=[xnbf_s])
            transposes(lambda j: xn_bf[0:n, j * 128:(j + 1) * 128], 8, n, 0,
                       lambda b0, nb: xnT[:, b0:b0 + nb, 0:n], [xnT_s], [xnbf_s], gainT=gain)

        wseq = []
        one_slot = ([(win_s[i], 8, GRPS[i][1]) for i in range(13)] + [(wo_s[i], 8, 512) for i in range(2)]
                    + [(wup_s[i], 8, 512) for i in range(8)] + [(wdn_s[i], 8, 512) for i in range(8)]
                    + [(wpg_s[i], 8, 512) for i in range(2)] + [(wp_s[i], 2, 512) for i in range(2)])
        for _ in range(NSLOT + 1):
            wseq.extend(one_slot)
        wst = {"issued": 0, "next": 0}
        conv_s = St()

        def w_issue(upto):
            while wst["issued"] < min(upto, len(wseq)):
                k = wst["issued"]
                src, kc, ncol = wseq[k]
                sl = k % 3
                dma(SP, WS[sl][:, 0:kc, 0:ncol], src[:, :, :], rd=[conv_s], wr=[WS_s[sl]])
                wst["issued"] += 1

        def w_get():
            k = wst["next"]
            wst["next"] += 1
            w_issue(k + 1)
            return k % 3, k

        def w_done(k):
            w_issue(k + 3)

        def dense(lhsT_fn, kcs, n, ncols_list, bank_fn, lhs_states, after_fn):
            for gi, ncol in enumerate(ncols_list):
                sl, k = w_get()
                bank = bank_fn(gi)
                for kc in range(kcs):
                    op(PE, lambda kc=kc: nc.tensor.matmul(PB[bank][0:n, 0:ncol], lhsT=lhsT_fn(kc), rhs=WS[sl][:, kc, 0:ncol],
                                                          start=(kc == 0), stop=(kc == kcs - 1)),
                       rd=list(lhs_states) + [WS_s[sl]], wr=[PB_s[bank]])
                w_done(k)
                after_fn(gi, bank)

        with nc.allow_non_contiguous_dma(reason="tiny constant loads"):
            for (dst, src) in ((gsgub, gsgu_d), (gqb, gq_d), (gkb, gk_d)):
                dma(SP, dst[:], src.partition_broadcast(128), wr=[cst_s])
            for (dst, src) in ((gmixT, gmix_d), (gffnT, gffn_d), (gpleT, gple_d)):
                dma(SP, None, None, wr=[cst_s], fn=lambda dst=dst, src=src: nc.sync.dma_start(out=dst[:], in_=src.rearrange("(k p) -> p k", p=128)))
            dma(SP, None, None, wr=[cst_s], fn=lambda: nc.sync.dma_start(
                out=coefs[:, 0:8], in_=ws_d[:, 0, 0:1].rearrange("g a -> (g a)").partition_broadcast(16)))
            dma(SP, None, None, wr=[cst_s], fn=lambda: nc.sync.dma_start(
                out=coefs[:, 8:16], in_=bs_d[:, 0:1].rearrange("g a -> (g a)").partition_broadcast(16)))
            dma(SP, bsT[:], bs_d.rearrange("g t -> t g"), wr=[cst_s], fn=lambda: nc.sync.dma_start(
                out=bsT[:], in_=bs_d.rearrange("g t -> t g")))
            dma(SP, pen[:], pen_d[:, :, :], wr=[cst_s])
            dma(SP, ptb[:], pt_d.partition_broadcast(128), wr=[cst_s])
        op(POOL, lambda: nc.gpsimd.memset(identb[:], 1.0), wr=[cst_s])
        op(POOL, lambda: nc.gpsimd.affine_select(out=identb[:], in_=identb[:], pattern=[[-1, 128]], compare_op=ALU.is_equal,
                                                 fill=0.0, base=0, channel_multiplier=1), wr=[cst_s])
        op(POOL, lambda: nc.gpsimd.memset(identf[:], 1.0), wr=[cst_s])
        op(POOL, lambda: nc.gpsimd.affine_select(out=identf[:], in_=identf[:], pattern=[[-1, 16]], compare_op=ALU.is_equal,
                                                 fill=0.0, base=0, channel_multiplier=1), wr=[cst_s])
        op(POOL, lambda: nc.gpsimd.memset(mhalf[:], -0.5), wr=[cst_s])
        op(POOL, lambda: nc.gpsimd.iota(iop[:], pattern=[[0, 1]], base=0, channel_multiplier=1), wr=[cst_s])
        for g in range(4):
            op(DVE, lambda g=g: nc.vector.tensor_copy(out=I4[:, g, :], in_=identb[:]), rd=[], wr=[cst_s])
            op(DVE, lambda g=g: nc.vector.tensor_copy(out=I416[:, g, :], in_=identb[0:16, 0:16]), rd=[], wr=[cst_s])
        op(DVE, lambda: nc.vector.tensor_scalar(out=gqb[:], in0=gqb[:], scalar1=ATTN_SCALE, scalar2=None, op0=ALU.mult), wr=[cst_s])
        op(DVE, lambda: nc.vector.tensor_scalar(out=idxa[:], in0=ptb[:], scalar1=128, scalar2=iop[:, 0:1], op0=ALU.mult, op1=ALU.add),
           wr=[cst_s])
        op(POOL, lambda: nc.gpsimd.memset(Vt[:], 0.0), wr=[V_s])
        op(POOL, lambda: nc.gpsimd.memset(Vt[:, :, :, 64:65], 1.0), wr=[V_s])
        for g in range(8):
            dma(SP, ws32[:], ws_d[g, :, :], wr=[ws32_s])
            op(POOL, lambda: nc.gpsimd.affine_select(out=ws32[:], in_=ws32[:], pattern=[[-1, 128]], compare_op=ALU.is_ge,
                                                     fill=0.0, base=0, channel_multiplier=1), wr=[ws32_s])
            op(DVE, lambda: nc.vector.tensor_copy(out=wsb[:], in_=ws32[:]), rd=[ws32_s], wr=[wsb_s])
            op(PE, lambda: nc.tensor.transpose(PBb[3][:, 0, :], wsb[:], identb[:]), rd=[wsb_s, cst_s], wr=[PB_s[3]])
            op(ACT, lambda g=g: nc.scalar.copy(out=wsT[:, g, :], in_=PBb[3][:, 0, :]), rd=[PB_s[3]], wr=[cst_s])

        conv_jobs = []
        conv_toks = []

        def conv(dst, src2d, kc, col0, ncol, first=False):
            for k in range(kc):
                job = (lambda k=k: conv_toks.append(dma(
                    POOL, None, None, wr=[],
                    fn=lambda: nc.gpsimd.dma_start(out=dst[:, k, :], in_=src2d[k * 128:(k + 1) * 128, col0:col0 + ncol]))))
                if first:
                    job()
                else:
                    conv_jobs.append(job)
        for i, (o, w) in enumerate(GRPS):
            conv(win_s[i], win_d, 8, o, w, first=(i in (2, 4)))
        for i in range(2):
            conv(wo_s[i], wo_d, 8, i * 512, 512)
        for i in range(8):
            conv(wup_s[i], wup_d, 8, i * 512, 512)
        for i in range(8):
            nn, kg = i // 4, i % 4
            conv(wdn_s[i], wdn_d[kg * 1024:(kg + 1) * 1024, :], 8, nn * 512, 512)
        for i in range(2):
            conv(wpg_s[i], wpg_d, 8, i * 512, 512)
        for i in range(2):
            conv(wp_s[i], wp_d, 2, i * 512, 512)
        for tok in conv_toks:
            SP.wait(tok)

        def head_norm(src32, n, nh, gain, out_fn, out_states, e=None):
            v3 = src32.rearrange("p (h d) -> p h d", d=64)
            if e is not None:
                tmp4 = o32[0:n, 0:nh * 64].rearrange("p (e g d) -> p e g d", e=e, d=64)
                op(DVE, lambda: nc.vector.tensor_tensor(out=o32[0:n, 0:nh * 64], in0=src32, in1=src32, op=ALU.mult), rd=[k32_s], wr=[o32_s])
                op(DVE, lambda: nc.vector.tensor_reduce(out=ss16[0:n, 0:nh], in_=o32[0:n, 0:nh * 64].rearrange("p (h d) -> p h d", d=64),
                                                        axis=AX.X, op=ALU.add), rd=[o32_s], wr=[ss16_s])
                rstd_from_ss(ss16[0:n, 0:nh], rs16[0:n, 0:nh], n, nh, 1.0 / 64, ss16_s, rs16_s)
                op(DVE, lambda: nc.vector.tensor_tensor(out=o32[0:n, 0:nh * 64].rearrange("p (h d) -> p h d", d=64), in0=v3,
                                                        in1=rs16[0:n, 0:nh].unsqueeze(2).to_broadcast([n, nh, 64]), op=ALU.mult),
                   rd=[k32_s, rs16_s], wr=[o32_s])
                for ee in range(e):
                    op(DVE, lambda ee=ee: nc.vector.tensor_tensor(out=out_fn(ee), in0=tmp4[:, ee, :, :],
                                                                  in1=gain[0:n, :].unsqueeze(1).to_broadcast([n, nh // e, 64]), op=ALU.mult),
                       rd=[o32_s, cst_s], wr=out_states)
                return
            op(DVE, lambda: nc.vector.tensor_tensor(out=o32[0:n, 0:nh * 64], in0=src32, in1=src32, op=ALU.mult), rd=[k32_s], wr=[o32_s])
            op(DVE, lambda: nc.vector.tensor_reduce(out=ss16[0:n, 0:nh], in_=o32[0:n, 0:nh * 64].rearrange("p (h d) -> p h d", d=64),
                                                    axis=AX.X, op=ALU.add), rd=[o32_s], wr=[ss16_s])
            rstd_from_ss(ss16[0:n, 0:nh], rs16[0:n, 0:nh], n, nh, 1.0 / 64, ss16_s, rs16_s)
            op(DVE, lambda: nc.vector.tensor_tensor(out=o32[0:n, 0:nh * 64].rearrange("p (h d) -> p h d", d=64), in0=v3,
                                                    in1=rs16[0:n, 0:nh].unsqueeze(2).to_broadcast([n, nh, 64]), op=ALU.mult),
               rd=[k32_s, rs16_s], wr=[o32_s])
            op(DVE, lambda: nc.vector.tensor_tensor(out=out_fn(), in0=o32[0:n, 0:nh * 64].rearrange("p (h d) -> p h d", d=64),
                                                    in1=gain[0:n, :].unsqueeze(1).to_broadcast([n, nh, 64]), op=ALU.mult),
               rd=[o32_s, cst_s], wr=out_states)

        def kv_append(blk, n, k_ap, k_s, v_ap, v_s, kid_ap, kid_s_):
            c0 = blk * 128
            for j in range(2):
                op(PE, lambda j=j: nc.tensor.transpose(PBb[3][:, j, 0:n], k_ap[0:n, j * 128:(j + 1) * 128], identb[0:n, 0:n]),
                   rd=[k_s, cst_s], wr=[PB_s[3]])
            op(PE, lambda: nc.tensor.transpose(PBb[3][:, 2, 0:n], kid_ap[0:n, :, :].rearrange("p a d -> p (a d)"), identb[0:n, 0:n]),
               rd=[kid_s_, cst_s], wr=[PB_s[3]])
            op(ACT, lambda: nc.scalar.copy(out=KT[:, :, c0:c0 + n], in_=PBb[3][:, 0:2, 0:n]), rd=[PB_s[3]], wr=[KT_s])
            op(ACT, lambda: nc.scalar.copy(out=kiT[:, c0:c0 + n], in_=PBb[3][:, 2, 0:n]), rd=[PB_s[3]], wr=[kiT_s])
            op(ACT, lambda: nc.scalar.copy(out=Vt[0:n, blk, :, 0:64], in_=v_ap[0:n, :].rearrange("p (c d) -> p c d", d=64)),
               rd=[v_s], wr=[V_s])

        def indexer(n, nch, sc_ap, sc_state, k0=0, ki_state=None, hook=None):
            ki_state = ki_state or kiT_s
            S = nch * 128
            j = 0
            for g0 in range(0, S, 512):
                w = min(512, S - g0)
                for h in range(8):
                    if hook is not None:
                        hook()
                    bank = j % 3
                    r = rt[j % 3]
                    rs = rt_s[j % 3]
                    j += 1
                    pb = (h % 2) * 64
                    op(PE, lambda: nc.tensor.matmul(PB[bank][0:n, 0:w], lhsT=qiT[pb:pb + 64, h // 2, 0:n], rhs=kiT[pb:pb + 64, k0 + g0:k0 + g0 + w],
                                                    start=True, stop=True), rd=[qiT_s, ki_state], wr=[PB_s[bank]])
                    op(ACT, lambda: nc.scalar.activation(out=r[0:n, 0:w], in_=PB[bank][0:n, 0:w], func=AF.Relu, scale=absw[0:n, h:h + 1]),
                       rd=[PB_s[bank], w_s_], wr=[rs])
                    if h == 0:
                        op(DVE, lambda: nc.vector.tensor_scalar(out=sc_ap[0:n, g0:g0 + w], in0=r[0:n, 0:w], scalar1=sgnw[0:n, 0:1], scalar2=None,
                                                                op0=ALU.mult), rd=[rs, w_s_], wr=[sc_state])
                    else:
                        op(DVE, lambda: nc.vector.scalar_tensor_tensor(out=sc_ap[0:n, g0:g0 + w], in0=r[0:n, 0:w], scalar=sgnw[0:n, h:h + 1],
                                                                       in1=sc_ap[0:n, g0:g0 + w], op0=ALU.mult, op1=ALU.add),
                           rd=[rs, w_s_], wr=[sc_state])

        def bisect(n, S, sc_ap, sc_state, junk_ap, junk_state, junk_act=None):
            op(DVE, lambda: nc.vector.memset(lo[0:n, :], -BRK), wr=[bis_s])
            split = junk_act is not None and S >= 2048
            S1 = (int(S * 0.46) // 128) * 128 if split else S
            S2 = S - S1
            wd = BRK
            for it in range(NIT):
                op(DVE, lambda: nc.vector.tensor_scalar(out=mid[0:n, :], in0=lo[0:n, :], scalar1=wd, scalar2=None, op0=ALU.add),
                   rd=[bis_s], wr=[bis_s, mid_s])
                if split:
                    op(ACT, lambda: nc.scalar.activation(out=junk_act[0:n, S1:S], in_=sc_ap[0:n, S1:S], func=AF.Sign, bias=mid[0:n, 0:1], scale=-1.0,
                                                         accum_out=cnta[0:n, 0:1]), rd=[sc_state, mid_s], wr=[junka_s, cnta_s])
                op(DVE, lambda: nc.vector.tensor_scalar(out=junk_ap[0:n, 0:S1], in0=sc_ap[0:n, 0:S1], scalar1=mid[0:n, 0:1], scalar2=None, op0=ALU.is_ge,
                                                        op1=ALU.add, accum_out=cntt[0:n, 0:1]),
                   rd=[sc_state, bis_s], wr=[junk_state, bis_s])
                thr = 255.5
                if split:
                    op(DVE, lambda: nc.vector.scalar_tensor_tensor(out=cntt[0:n, :], in0=cnta[0:n, :], scalar=-0.5, in1=cntt[0:n, :], op0=ALU.mult,
                                                                   op1=ALU.add), rd=[cnta_s, bis_s], wr=[bis_s])
                    thr = 255.5 - S2 / 2.0
                op(DVE, lambda: nc.vector.tensor_scalar(out=geq[0:n, :], in0=cntt[0:n, :], scalar1=thr, scalar2=wd, op0=ALU.is_ge,
                                                        op1=ALU.mult), rd=[bis_s], wr=[bis_s])
                op(DVE, lambda: nc.vector.tensor_tensor(out=lo[0:n, :], in0=lo[0:n, :], in1=geq[0:n, :], op=ALU.add), rd=[bis_s, mid_s], wr=[bis_s])
                wd = wd / 2.0

        def attend(n, nch, sc_ap, sc_state, sel_ap, out_ap, out_state, ch0=0, kt_state=None, v_state=None, hook=None):
            kt_state = kt_state or KT_s
            v_state = v_state or V_s
            steps = [(ch, c) for ch in range(nch) for c in range(4)]

            def prep_group(gi):
                w = min(512, nch * 128 - gi * 512)
                ng = w // 128
                mb = MBg[gi % 2]
                bT = 0
                op(DVE, lambda: nc.vector.tensor_scalar(out=mb[0:n, 0:w], in0=sc_ap[0:n, gi * 512:gi * 512 + w], scalar1=lo[0:n, 0:1],
                                                        scalar2=None, op0=ALU.is_ge), rd=[sc_state, bis_s], wr=[MB_s[gi % 2]])
                for cj in range(ng):
                    op(PE, lambda cj=cj: nc.tensor.transpose(PBb[bT][:, cj, 0:n], mb[0:n, cj * 128:(cj + 1) * 128], identb[0:n, 0:n]),
                       rd=[MB_s[gi % 2], cst_s], wr=[PB_s[bT]])
                op(DVE, lambda: nc.vector.tensor_copy(out=mT[gi % 2][:, 0:ng, 0:n], in_=PBb[bT][:, 0:ng, 0:n]), rd=[PB_s[bT]], wr=[mT_s[gi % 2]])

            def emit_qk(k):
                ch, c = steps[k]
                gi, cj = ch // 4, ch % 4
                if cj == 0 and c == 0:
                    prep_group(gi)
                bank = 1 + (k % 3)
                pb = (c % 2) * 64
                outv = PB[bank][:, 0:4 * n].rearrange("p (g t) -> p g t", g=4)
                op(PE, lambda: nc.tensor.matmul(outv, lhsT=KT[pb:pb + 64, c // 2, (ch0 + ch) * 128:(ch0 + ch + 1) * 128], rhs=qT[pb:pb + 64, c // 2, :, 0:n],
                                                start=True, stop=True), rd=[kt_state, qT_s], wr=[PB_s[bank]])

            emit_qk(0)
            if len(steps) > 1:
                emit_qk(1)
            for k in range(len(steps)):
                ch, c = steps[k]
                gi, cj = ch // 4, ch % 4
                if hook is not None:
                    hook()
                if k + 2 < len(steps):
                    emit_qk(k + 2)
                bank = 1 + (k % 3)
                pt_ = PT[k % 3]
                pts = PT_s[k % 3]
                op(ACT, lambda: nc.scalar.activation(out=pt_[:, 0:4 * n], in_=PB[bank][:, 0:4 * n], func=AF.Exp), rd=[PB_s[bank]], wr=[pts])
                pt3 = pt_[:, 0:4 * n].rearrange("p (g t) -> p g t", g=4)
                op(DVE, lambda: nc.vector.tensor_tensor(out=pt3, in0=pt3, in1=mT[gi % 2][:, cj, 0:n].unsqueeze(1).to_broadcast([128, 4, n]),
                                                        op=ALU.mult), rd=[mT_s[gi % 2]], wr=[pts])
                for g in range(4):
                    op(PE, lambda g=g: nc.tensor.matmul(PB[4 + c][0:n, g * 65:(g + 1) * 65], lhsT=pt_[:, g * n:(g + 1) * n], rhs=Vt[:, ch0 + ch, c, :],
                                                        start=(ch == 0 and g == 0), stop=(ch == nch - 1), skip_group_check=True),
                       rd=[pts, v_state], wr=[PB_s[4 + c]])
            for c in range(4):
                acc = PB[4 + c][0:n, 0:260].rearrange("p (g e) -> p g e", e=65)
                op(DVE, lambda: nc.vector.reciprocal(out=sm[0:n, 4 * c:4 * c + 4], in_=acc[:, :, 64]), rd=[PB_s[4 + c]], wr=[sm_s])
                op(DVE, lambda: nc.vector.tensor_tensor(out=out_ap[0:n, c * 256:(c + 1) * 256].rearrange("p (g d) -> p g d", d=64),
                                                        in0=acc[:, :, 0:64], in1=sm[0:n, 4 * c:4 * c + 4].unsqueeze(2).to_broadcast([n, 4, 64]),
                                                        op=ALU.mult), rd=[PB_s[4 + c], sm_s], wr=[out_state])

        STh_s = [[St(), St()], [St(), St()]]

        def attend_sample(nch, sc_ap, sc_state, out_ap, out_state, ch0, kt_state, v_state, hook=None):
            n = 16

            def prep_group(gi):
                w = min(512, nch * 128 - gi * 512)
                ng = w // 128
                mb = MBg[gi % 2]
                op(DVE, lambda: nc.vector.tensor_scalar(out=mb[0:n, 0:w], in0=sc_ap[0:n, gi * 512:gi * 512 + w], scalar1=lo[0:n, 0:1],
                                                        scalar2=None, op0=ALU.is_ge), rd=[sc_state, bis_s], wr=[MB_s[gi % 2]])
                for cj in range(ng):
                    op(PE, lambda cj=cj: nc.tensor.transpose(PBb[0][:, cj, 0:n], mb[0:n, cj * 128:(cj + 1) * 128], identb[0:n, 0:n]),
                       rd=[MB_s[gi % 2], cst_s], wr=[PB_s[0]])
                op(DVE, lambda: nc.vector.tensor_copy(out=mT[gi % 2][:, 0:ng, 0:n], in_=PBb[0][:, 0:ng, 0:n]), rd=[PB_s[0]], wr=[mT_s[gi % 2]])

            def emit_qk(ch):
                gi, cj = ch // 4, ch % 4
                if cj == 0:
                    prep_group(gi)
                hf = 0
                for c in range(4):
                    pb = (c % 2) * 64
                    col = hf * 128 + (c // 2) * 64
                    outv = PB[1 + (c % 2)][:, col:col + 64].rearrange("p (g t) -> p g t", g=4)
                    op(PE, lambda: nc.tensor.matmul(outv, lhsT=KT[pb:pb + 64, c // 2, (ch0 + ch) * 128:(ch0 + ch + 1) * 128],
                                                    rhs=qT[pb:pb + 64, c // 2, :, 0:n], start=True, stop=True, skip_group_check=True),
                       rd=[kt_state, qT_s], wr=[PB_s[1 + (c % 2)]])

            emit_qk(0)
            for ch in range(nch):
                gi, cj = ch // 4, ch % 4
                hf = 0
                if hook is not None:
                    hook()
                    hook()
                pt_ = PT[ch % 3]
                pts = PT_s[ch % 3]
                for e in range(2):
                    op(ACT, lambda e=e: nc.scalar.activation(out=pt_[:, e * 128:(e + 1) * 128], in_=PB[1 + e][:, hf * 128:(hf + 1) * 128], func=AF.Exp),
                       rd=[PB_s[1 + e]], wr=[pts])
                pt3 = pt_[:, 0:256].rearrange("p (a t) -> p a t", t=n)
                op(DVE, lambda: nc.vector.tensor_tensor(out=pt3, in0=pt3, in1=mT[gi % 2][:, cj, 0:n].unsqueeze(1).to_broadcast([128, 16, n]),
                                                        op=ALU.mult), rd=[mT_s[gi % 2]], wr=[pts])
                if ch + 1 < nch:
                    emit_qk(ch + 1)
                for c in range(4):
                    for g in range(4):
                        a0 = (c % 2) * 128 + (c // 2) * 64 + g * n
                        op(PE, lambda: nc.tensor.matmul(PB[4 + c][0:n, g * 65:(g + 1) * 65], lhsT=pt_[:, a0:a0 + n], rhs=Vt[:, ch0 + ch, c, :],
                                                        start=(ch == 0 and g == 0), stop=(ch == nch - 1), skip_group_check=True),
                           rd=[pts, v_state], wr=[PB_s[4 + c]])
            for c in range(4):
                acc = PB[4 + c][0:n, 0:260].rearrange("p (g e) -> p g e", e=65)
                op(DVE, lambda: nc.vector.reciprocal(out=sm[0:n, 4 * c:4 * c + 4], in_=acc[:, :, 64]), rd=[PB_s[4 + c]], wr=[sm_s])
                op(DVE, lambda: nc.vector.tensor_tensor(out=out_ap[0:n, c * 256:(c + 1) * 256].rearrange("p (g d) -> p g d", d=64),
                                                        in0=acc[:, :, 0:64], in1=sm[0:n, 4 * c:4 * c + 4].unsqueeze(2).to_broadcast([n, 4, 64]),
                                                        op=ALU.mult), rd=[PB_s[4 + c], sm_s], wr=[out_state])

        def project(n, x_ap, x_s, outs):
            fence([sc_s], arena_states)
            rmsnorm_to_T(x_ap, x_s, gmixT, n)
            ug, ug_s = FB[1], FB_s[1]
            vn, vn_s = FB[2], FB_s[2]
            sga, sga_s = FB[3], FB_s[3]
            sgb, sgb_s = FB[4], FB_s[4]

            def after(gi, bank):
                pb_s = PB_s[bank]
                pbk = PB[bank]
                if gi in (0, 1):
                    op(ACT, lambda: nc.scalar.copy(out=k32[0:n, :], in_=pbk[0:n, :]), rd=[pb_s], wr=[k32_s])
                    head_norm(k32[0:n, :], n, 8, gqb, lambda ee: qn_bf[0:n, gi, :, ee * 64:(ee + 1) * 64], [qn_s], e=2)
                elif gi == 2:
                    op(ACT, lambda: nc.scalar.copy(out=k32[0:n, :], in_=pbk[0:n, :]), rd=[pb_s], wr=[k32_s])
                    head_norm(k32[0:n, 0:256], n, 4, gkb, lambda: o32[0:n, 256:512].rearrange("p (h d) -> p h d", d=64), [o32_s])
                    dma(POOL, outs["k"], o32[0:n, 256:512], rd=[o32_s])
                    dma(POOL, outs["v"], k32[0:n, 256:512], rd=[k32_s])
                    if "kv" in outs:
                        op(DVE, lambda: nc.vector.tensor_copy(out=kn_bf[0:n, :], in_=o32[0:n, 256:512]), rd=[o32_s], wr=[knbf_s])
                        op(DVE, lambda: nc.vector.tensor_copy(out=vs_bf[0:n, :], in_=k32[0:n, 256:512]), rd=[k32_s], wr=[smp_s])
                elif gi == 3:
                    op(ACT, lambda: nc.scalar.copy(out=qi_bf[0:n, :], in_=pbk[0:n, :]), rd=[pb_s], wr=[qibf_s])
                elif gi == 4:
                    op(ACT, lambda: nc.scalar.copy(out=ki32[0:n, :], in_=pbk[0:n, 0:72]), rd=[pb_s], wr=[ki32_s])
                    dma(POOL, outs["ki"], ki32[0:n, 0:64], rd=[ki32_s])
                    op(ACT, lambda: nc.scalar.activation(out=absw[0:n, :], in_=ki32[0:n, 64:72], func=AF.Abs, scale=IDXS), rd=[ki32_s], wr=[w_s_])
                    op(ACT, lambda: nc.scalar.activation(out=sgnw[0:n, :], in_=ki32[0:n, 64:72], func=AF.Sign), rd=[ki32_s], wr=[w_s_])
                    if "kv" in outs:
                        for a in range(2):
                            op(DVE, lambda a=a: nc.vector.tensor_copy(out=kid_bf[0:n, a, :], in_=ki32[0:n, 0:64]), rd=[ki32_s], wr=[kid_s])
                elif gi in (5, 6):
                    o = (gi - 5) * 512
                    op(ACT, lambda: nc.scalar.activation(out=ug[0:n, o:o + 512], in_=pbk[0:n, :], func=AF.Gelu_apprx_tanh), rd=[pb_s], wr=[ug_s])
                elif gi in (7, 8):
                    o = (gi - 7) * 512
                    op(ACT, lambda: nc.scalar.activation(out=vn[0:n, o:o + 512], in_=pbk[0:n, :], func=AF.Gelu_apprx_tanh), rd=[pb_s], wr=[vn_s])
                elif gi in (9, 10):
                    o = (gi - 9) * 512
                    op(ACT, lambda: nc.scalar.activation(out=sga[0:n, o:o + 512], in_=pbk[0:n, :], func=AF.Sigmoid), rd=[pb_s], wr=[sga_s])
                else:
                    o = (gi - 11) * 512
                    op(ACT, lambda: nc.scalar.activation(out=sgb[0:n, o:o + 512], in_=pbk[0:n, :], func=AF.Sigmoid), rd=[pb_s], wr=[sgb_s])

            dense(lambda kc: xnT[:, kc, 0:n], 8, n, [g[1] for g in GRPS], lambda gi: gi % 2, [xnT_s], after)
            transposes(lambda j: qn_bf[0:n, j // 4, j % 4, :], 8, n, 2,
                       lambda b0, nb: qT[:, :, :, 0:n].rearrange("p a g t -> p (a g) t")[:, b0:b0 + nb, :], [qT_s], [qn_s])
            transposes(lambda j: qi_bf[0:n, j * 128:(j + 1) * 128], 4, n, 3,
                       lambda b0, nb: qiT[:, b0:b0 + nb, 0:n], [qiT_s], [qibf_s])
            op(DVE, lambda: nc.vector.scalar_tensor_tensor(out=vn_bf[0:n, :], in0=vn[0:n, :], scalar=1.0, in1=vn[0:n, :], op0=ALU.mult,
                                                           op1=ALU.mult, accum_out=ss16[0:n, 0:1]), rd=[vn_s], wr=[vnbf_s, ss16_s])
            rstd_from_ss(ss16[0:n, 0:1], rs16[0:n, 0:1], n, 1, 1.0 / D, ss16_s, rs16_s)
            op(DVE, lambda: nc.vector.scalar_tensor_tensor(out=vn[0:n, :], in0=vn[0:n, :], scalar=rs16[0:n, 0:1], in1=gsgub[0:n, :],
                                                           op0=ALU.mult, op1=ALU.mult), rd=[rs16_s, cst_s], wr=[vn_s])
            dma(POOL, outs["sg"], vn[0:n, :], rd=[vn_s])
            if n == 128:
                op(DVE, lambda: nc.vector.tensor_copy(out=vn_bf[:, :], in_=vn[:, :]), rd=[vn_s], wr=[vnbf_s])
                for g in range(8):
                    bank = 4 + g // 4
                    op(PE, lambda g=g: nc.tensor.matmul(PB[bank][:, (g % 4) * 128:(g % 4 + 1) * 128], lhsT=wsT[:, g, :],
                                                        rhs=vn_bf[:, g * 128:(g + 1) * 128], start=True, stop=True, skip_group_check=True),
                       rd=[vnbf_s, cst_s], wr=[PB_s[bank]])
                for hb in range(2):
                    op(DVE, lambda hb=hb: nc.vector.tensor_tensor(
                        out=vn[:, hb * 512:(hb + 1) * 512].rearrange("p (g c) -> p g c", c=128),
                        in0=PB[4 + hb][:, :].rearrange("p (g c) -> p g c", c=128),
                        in1=bsT[:, hb * 4:hb * 4 + 4].unsqueeze(2).to_broadcast([128, 4, 128]), op=ALU.add),
                       rd=[PB_s[4 + hb], cst_s], wr=[vn_s])
            else:
                v3s = vn[0:n, :].rearrange("p (g c) -> p g c", c=128)
                op(DVE, lambda: nc.vector.tensor_tensor(out=v3s, in0=v3s, in1=coefs[0:n, 0:8].unsqueeze(2).to_broadcast([n, 8, 128]), op=ALU.mult),
                   rd=[cst_s], wr=[vn_s])
                op(DVE, lambda: nc.vector.tensor_tensor(out=v3s, in0=v3s, in1=coefs[0:n, 8:16].unsqueeze(2).to_broadcast([n, 8, 128]), op=ALU.add),
                   rd=[cst_s], wr=[vn_s])
            op(DVE, lambda: nc.vector.tensor_tensor(out=vn[0:n, :], in0=vn[0:n, :], in1=ug[0:n, :], op=ALU.mult), rd=[ug_s], wr=[vn_s])
            op(DVE, lambda: nc.vector.tensor_tensor(out=sgb[0:n, :], in0=sgb[0:n, :], in1=vn[0:n, :], op=ALU.mult), rd=[vn_s], wr=[sgb_s])

        def finish(n, x_ap, x_s, oatt, oatt_s, p_src, y_dst):
            sga, sga_s = FB[3], FB_s[3]
            msgu, msgu_s = FB[4], FB_s[4]
            fence([sc_s], arena_states)
            dma(SP, p32[0:n, :], p_src, wr=[p32_s])
            op(DVE, lambda: nc.vector.tensor_tensor(out=oatt[0:n, :], in0=oatt[0:n, :], in1=sga[0:n, :], op=ALU.mult), rd=[sga_s], wr=[oatt_s])
            op(DVE, lambda: nc.vector.tensor_tensor(out=xn_bf[0:n, :], in0=oatt[0:n, :], in1=msgu[0:n, :], op=ALU.add),
               rd=[oatt_s, msgu_s], wr=[xnbf_s])
            transposes(lambda j: xn_bf[0:n, j * 128:(j + 1) * 128], 8, n, 2, lambda b0, nb: xnT[:, b0:b0 + nb, 0:n], [xnT_s], [xnbf_s])

            def after_o(gi, bank):
                op(DVE, lambda: nc.vector.tensor_tensor(out=x_ap[:, gi * 512:(gi + 1) * 512], in0=x_ap[:, gi * 512:(gi + 1) * 512],
                                                        in1=PB[bank][0:n, :], op=ALU.add), rd=[PB_s[bank]], wr=[x_s])
            dense(lambda kc: xnT[:, kc, 0:n], 8, n, [512, 512], lambda gi: gi % 2, [xnT_s], after_o)
            rmsnorm_to_T(x_ap, x_s, gffnT, n)

            def after_up(gi, bank):
                op(ACT, lambda: nc.scalar.activation(out=rt[gi % 2][0:n, :], in_=PB[bank][0:n, :], func=AF.Relu), rd=[PB_s[bank]], wr=[rt_s[gi % 2]])
                op(DVE, lambda: nc.vector.tensor_tensor(out=h_bf[0:n, gi * 512:(gi + 1) * 512], in0=rt[gi % 2][0:n, :], in1=rt[gi % 2][0:n, :],
                                                        op=ALU.mult), rd=[rt_s[gi % 2]], wr=[hbf_s])
            dense(lambda kc: xnT[:, kc, 0:n], 8, n, [512] * 8, lambda gi: gi % 2, [xnT_s], after_up)
            transposes(lambda j: h_bf[0:n, j * 128:(j + 1) * 128], 32, n, 2, lambda b0, nb: hT[:, b0:b0 + nb, 0:n], [hT_s], [hbf_s])
            for nn in range(2):
                bank = nn % 2
                for kg in range(4):
                    sl, k = w_get()
                    for kc in range(8):
                        op(PE, lambda kc=kc: nc.tensor.matmul(PB[bank][0:n, :], lhsT=hT[:, kg * 8 + kc, 0:n], rhs=WS[sl][:, kc, :],
                                                              start=(kg == 0 and kc == 0), stop=(kg == 3 and kc == 7)),
                           rd=[hT_s, WS_s[sl]], wr=[PB_s[bank]])
                    w_done(k)
                op(DVE, lambda: nc.vector.tensor_tensor(out=x_ap[:, nn * 512:(nn + 1) * 512], in0=x_ap[:, nn * 512:(nn + 1) * 512],
                                                        in1=PB[bank][0:n, :], op=ALU.add), rd=[PB_s[bank]], wr=[x_s])
            rmsnorm_to_T(x_ap, x_s, gpleT, n)
            gate, gate_s = FB[1], FB_s[1]

            def after_pg(gi, bank):
                op(ACT, lambda: nc.scalar.activation(out=gate[0:n, gi * 512:(gi + 1) * 512], in_=PB[bank][0:n, :], func=AF.Sigmoid),
                   rd=[PB_s[bank]], wr=[gate_s])
            dense(lambda kc: xnT[:, kc, 0:n], 8, n, [512, 512], lambda gi: gi % 2, [xnT_s], after_pg)
            op(DVE, lambda: nc.vector.tensor_copy(out=p_bf[0:n, :], in_=p32[0:n, :]), rd=[p32_s], wr=[pbf_s])
            transposes(lambda j: p_bf[0:n, j * 128:(j + 1) * 128], 2, n, 3, lambda b0, nb: pT[:, b0:b0 + nb, 0:n], [pT_s], [pbf_s])

            def after_p(gi, bank):
                op(DVE, lambda: nc.vector.tensor_tensor(out=gate[0:n, gi * 512:(gi + 1) * 512], in0=gate[0:n, gi * 512:(gi + 1) * 512],
                                                        in1=PB[bank][0:n, :], op=ALU.mult), rd=[PB_s[bank]], wr=[gate_s])
                op(DVE, lambda: nc.vector.tensor_tensor(out=x_ap[:, gi * 512:(gi + 1) * 512], in0=x_ap[:, gi * 512:(gi + 1) * 512],
                                                        in1=gate[0:n, gi * 512:(gi + 1) * 512], op=ALU.add), rd=[gate_s], wr=[x_s])
            dense(lambda kc: pT[:, kc, 0:n], 2, n, [512, 512], lambda gi: gi % 2, [pT_s], after_p)
            dma(POOL, y_dst, x_ap, rd=[x_s])

        n = 128
        fence(arena_states + [sc_s, sc16_s, sct_s], [wkvk_s])
        dma(SP, wkvk[:, :, 0:512], win_s[2][:, :, :], wr=[wkvk_s])
        dma(SP, wkvk[:, :, 512:576], win_s[4][:, :, 0:64], wr=[wkvk_s])
        xnT2 = al("xnT2", [128, 8, 128], BF16, 0); xnT2_s = St()
        fence(arena_states + [sc_s, sc16_s, sct_s], [xnT2_s])
        xnTs = [(xnT, xnT_s), (xnT2, xnT2_s)]
        ssA = sb("ssA", [128, 2], F32); ssA_s = St(); rsA_s = St()

        def stageA(kb):
            xt, xt_s = FB[kb % 2], FB_s[kb % 2]
            xT, xT_s = xnTs[kb % 2]
            dma(SP, xt[:, :], xb_d[kb * 128:(kb + 1) * 128, :], wr=[xt_s])
            op(DVE, lambda: nc.vector.scalar_tensor_tensor(out=xn_bf[:, :], in0=xt[:, :], scalar=1.0, in1=xt[:, :], op0=ALU.mult, op1=ALU.mult,
                                                           accum_out=ssA[:, 0:1]), rd=[xt_s], wr=[xnbf_s, ssA_s])
            rstd_from_ss(ssA[:, 0:1], ssA[:, 1:2], 128, 1, 1.0 / D, ssA_s, rsA_s)
            op(DVE, lambda: nc.vector.tensor_scalar(out=xn_bf[:, :], in0=xt[:, :], scalar1=ssA[:, 1:2], scalar2=None, op0=ALU.mult),
               rd=[xt_s, rsA_s], wr=[xnbf_s])
            transposes(lambda j: xn_bf[:, j * 128:(j + 1) * 128], 8, 128, 0,
                       lambda b0, nb: xT[:, b0:b0 + nb, :], [xT_s], [xnbf_s], gainT=gmixT)

        def stageB(kb):
            xT, xT_s = xnTs[kb % 2]
            for kc in range(8):
                op(PE, lambda kc=kc: nc.tensor.matmul(PB[0][:, :], lhsT=xT[:, kc, :], rhs=wkvk[:, kc, 0:512], start=(kc == 0), stop=(kc == 7)),
                   rd=[xT_s, wkvk_s], wr=[PB_s[0]])
            for kc in range(8):
                op(PE, lambda kc=kc: nc.tensor.matmul(PB[1][:, 0:64], lhsT=xT[:, kc, :], rhs=wkvk[:, kc, 512:576], start=(kc == 0), stop=(kc == 7)),
                   rd=[xT_s, wkvk_s], wr=[PB_s[1]])
            op(ACT, lambda: nc.scalar.copy(out=k32[:, :], in_=PB[0][:, :]), rd=[PB_s[0]], wr=[k32_s])
            head_norm(k32[:, 0:256], n, 4, gkb, lambda: kn_bf[:, :].rearrange("p (h d) -> p h d", d=64), [knbf_s])
            op(ACT, lambda: nc.scalar.copy(out=v_bf[:, :], in_=k32[:, 256:512]), rd=[k32_s], wr=[vbf_s])
            for a in range(2):
                op(ACT, lambda a=a: nc.scalar.copy(out=kid_bf[:, a, :], in_=PB[1][:, 0:64]), rd=[PB_s[1]], wr=[kid_s])
            kv_append(kb, n, kn_bf, knbf_s, v_bf, vbf_s, kid_bf, kid_s)

        RS["act"] = True
        stageA(0)
        for kb in range(NBLK):
            if kb + 1 < NBLK:
                stageA(kb + 1)
            stageB(kb)
            for _ in range(4):
                if conv_jobs:
                    conv_jobs.pop(0)()
        fence([xnT2_s], arena_states)
        RS["act"] = False
        while conv_jobs:
            conv_jobs.pop(0)()
        for i in range(NDS):
            if dcnt[i] > 0:
                SP.wait((dsems[i], dcnt[i]))

        for i in range(NSLOT):
            m, second = i // 2, i % 2
            nch = 8 * m + (8 if second else 4)
            xt, xt_s = FB[0], FB_s[0]
            dma(SP, xt[:, :], xo_d[i, :, :], wr=[xt_s])
            project(n, xt[:, :], xt_s, {"k": ko_d[i, :, :], "v": vo_d[i, :, :], "ki": kio_d[i, :, :], "sg": sgo_d[i, :, :]})
            fence(arena_states, [sc_s])
            indexer(n, nch, scores, sc_s)
            S = nch * 128
            op(DVE, lambda: nc.vector.tensor_tensor(out=scores[:, S - 512:S], in0=scores[:, S - 512:S], in1=pen[:, second, :], op=ALU.add),
               rd=[cst_s], wr=[sc_s])
            fence([FB_s[1], FB_s[2]], [junk_s, junka_s])
            bisect(n, S, scores[:, 0:S], sc_s, junk8[:, 0:S], junk_s, junk_act=junki8)
            fence([junk_s, junka_s], [FB_s[1], FB_s[2]])
            oat, oat_s = FB[1], FB_s[1]
            attend(n, nch, scores, sc_s, I4[:, :, :], oat, oat_s)
            finish(n, xt[:, :], xt_s, oat, oat_s, po_d[i, :, :], yo_d[i, :, :])

        n = NS
        xs_t, xs_s = FB[0], FB_s[0]
        dma(SP, xs_t[0:n, :], xs_d[:, :], wr=[xs_s])
        project(n, xs_t[0:n, :], xs_s, {"k": kss_d[:, :], "v": vss_d[:, :], "ki": kis_d[:, :], "sg": sgs_d[:, :], "kv": True})
        for j in range(2):
            op(PE, lambda j=j: nc.tensor.transpose(PBb[3][:, j, 0:n], kn_bf[0:n, j * 128:(j + 1) * 128], identb[0:n, 0:n]),
               rd=[knbf_s, cst_s], wr=[PB_s[3]])
        op(PE, lambda: nc.tensor.transpose(PBb[3][:, 2, 0:n], kid_bf[0:n, :, :].rearrange("p a d -> p (a d)"), identb[0:n, 0:n]),
           rd=[kid_s, cst_s], wr=[PB_s[3]])
        op(ACT, lambda: nc.scalar.copy(out=ksT[:, :, :], in_=PBb[3][:, 0:2, 0:n]), rd=[PB_s[3]], wr=[smp_s])
        op(ACT, lambda: nc.scalar.copy(out=kisT[:, :], in_=PBb[3][:, 2, 0:n]), rd=[PB_s[3]], wr=[smp_s])

        def gather(dst_ap, src2d, col):
            return lambda: nc.gpsimd.indirect_dma_start(out=dst_ap, out_offset=None, in_=src2d,
                                                        in_offset=bass.IndirectOffsetOnAxis(ap=idxa[:, col:col + 1], axis=0))
        fence(arena_states, [sc16_s] + sctR_s)
        kiR_s = [St(), St()]; KR_s = [St(), St()]; VR_s = [St(), St()]
        fence([kiT_s], kiR_s); fence([KT_s], KR_s); fence([V_s], VR_s)
        op(POOL, lambda: nc.gpsimd.memset(sc16[:, :], -1e30), wr=[sc16_s])
        for r in range(2):
            op(POOL, lambda r=r: nc.gpsimd.memset(kiT[:, r * S_S + 2048:(r + 1) * S_S], 0.0), wr=[kiR_s[r]])
            op(POOL, lambda r=r: nc.gpsimd.memset(KT[:, :, r * S_S + 2048:(r + 1) * S_S], 0.0), wr=[KR_s[r]])
            op(POOL, lambda r=r: nc.gpsimd.memset(Vt[:, r * 17 + 16, :, 0:64], 0.0), wr=[VR_s[r]])
        def s1_loads(b):
            r = b % 2
            k0 = r * S_S
            jobsA, jobsB = [], []
            for pg in range(NPG):
                def jobA(pg=pg):
                    sl = (b * NPG + pg) % NPB
                    col = b * NPG + pg
                    dma(POOL, None, None, rd=[cst_s], wr=[pgI_s[sl]], fn=gather(pgI[sl][:, 0, :], cki_d, col))

                def jobB(pg=pg):
                    sl = (b * NPG + pg) % NPB
                    bk = 3
                    for a in range(2):
                        op(PE, lambda a=a: nc.tensor.transpose(PBb[bk][a * 64:(a + 1) * 64, 0, :], pgI[sl][:, 0, :], identb[:]),
                           rd=[pgI_s[sl], cst_s], wr=[PB_s[bk]])
                    op(ACT, lambda: nc.scalar.copy(out=kiT[:, k0 + pg * 128:k0 + (pg + 1) * 128], in_=PBb[bk][:, 0, :]),
                       rd=[PB_s[bk]], wr=[kiR_s[r]])
                jobsA.append(jobA)
                jobsB.append(jobB)
            jobs = skew(jobsA, jobsB)
            jobs.append(lambda: op(ACT, lambda: nc.scalar.copy(out=kiT[:, k0 + 2048:k0 + 2049], in_=kisT[:, b:b + 1]), rd=[smp_s], wr=[kiR_s[r]]))
            return jobs

        def skew(jobsA, jobsB, ahead=3):
            out = list(jobsA[:ahead])
            for p in range(len(jobsB)):
                out.append(jobsB[p])
                if p + ahead < len(jobsA):
                    out.append(jobsA[p + ahead])
            return out

        def make_hook(jobs, every):
            st = {"i": 0}

            def hook():
                st["i"] += 1
                if st["i"] % every == 0 and jobs:
                    jobs.pop(0)()
            return hook

        pending = s1_loads(0)
        for b in range(NS):
            r = b % 2
            k0 = r * S_S
            while pending:
                pending.pop(0)()
            pending = s1_loads(b + 1) if b + 1 < NS else []
            indexer(n, 17, sctR[r], sctR_s[r], k0=k0, ki_state=kiR_s[r], hook=make_hook(pending, 1))
            op(DVE, lambda b=b: nc.vector.scalar_tensor_tensor(out=sc16[:, 0:2049], in0=sctR[r][:, 0:2049], scalar=identf[0:16, b:b + 1],
                                                               in1=sc16[:, 0:2049], op0=ALU.mult, op1=ALU.add) if b > 0 else
               nc.vector.tensor_scalar(out=sc16[:, 0:2049], in0=sctR[r][:, 0:2049], scalar1=identf[0:16, 0:1], scalar2=None, op0=ALU.mult),
               rd=[sctR_s[r], cst_s], wr=[sc16_s])
        fence([sctR_s[1]], [sct_s])
        bisect(n, 2049, sc16[:, 0:2049], sc16_s, sct[:, 0:2049], sct_s)
        oat, oat_s = FB[1], FB_s[1]
        oatt_s16, oas_s = FB[2], FB_s[2]
        def s2_loads(b):
            r = b % 2
            k0 = r * S_S
            c0 = r * 17
            jobsA, jobsB = [], []
            for pg in range(NPG):
                def jobA(pg=pg):
                    sl = (b * NPG + pg) % NPB
                    col = b * NPG + pg
                    dma(POOL, None, None, rd=[cst_s], wr=[pgK_s[sl]], fn=gather(pgK[sl][:, :], ck_d, col))
                    dma(POOL, None, None, rd=[cst_s], wr=[pgV_s[sl]], fn=gather(pgV[sl][:, :], cv_d, col))
                jobsA.append(jobA)

                def job(pg=pg):
                    sl = (b * NPG + pg) % NPB
                    bk = 0
                    for j in range(2):
                        op(PE, lambda j=j: nc.tensor.transpose(PBb[bk][:, j, :], pgK[sl][:, j * 128:(j + 1) * 128], identb[:]),
                           rd=[pgK_s[sl], cst_s], wr=[PB_s[bk]])
                    op(ACT, lambda: nc.scalar.copy(out=KT[:, :, k0 + pg * 128:k0 + (pg + 1) * 128], in_=PBb[bk][:, 0:2, :]),
                       rd=[PB_s[bk]], wr=[KR_s[r]])
                    op(DVE, lambda: nc.vector.tensor_copy(out=Vt[:, c0 + pg, :, 0:64], in_=pgV[sl][:, :].rearrange("p (c d) -> p c d", d=64)),
                       rd=[pgV_s[sl]], wr=[VR_s[r]])
                jobsB.append(job)
            jobs = skew(jobsA, jobsB)

            def last():
                op(ACT, lambda: nc.scalar.copy(out=KT[:, :, k0 + 2048:k0 + 2049], in_=ksT[:, :, b:b + 1]), rd=[smp_s], wr=[KR_s[r]])
                op(PE, lambda: nc.tensor.matmul(PB[0][0:1, 0:256], lhsT=identb[0:16, b:b + 1], rhs=vs_bf[0:16, :], start=True, stop=True),
                   rd=[smp_s, cst_s], wr=[PB_s[0]])
                op(ACT, lambda: nc.scalar.copy(out=Vt[0:1, c0 + 16, :, 0:64], in_=PB[0][0:1, 0:256].rearrange("p (c d) -> p c d", d=64)),
                   rd=[PB_s[0]], wr=[VR_s[r]])
            jobs.append(last)
            return jobs

        pending = s2_loads(0)
        for b in range(NS):
            r = b % 2
            c0 = r * 17
            while pending:
                pending.pop(0)()
            pending = s2_loads(b + 1) if b + 1 < NS else []
            attend_sample(17, sc16, sc16_s, oat, oat_s, c0, KR_s[r], VR_s[r], hook=make_hook(pending, 1))
            if b == 0:
                op(DVE, lambda: nc.vector.tensor_scalar(out=oatt_s16[0:16, :], in0=oat[0:16, :], scalar1=identf[0:16, 0:1], scalar2=None,
                                                        op0=ALU.mult), rd=[oat_s, cst_s], wr=[oas_s])
            else:
                op(DVE, lambda b=b: nc.vector.scalar_tensor_tensor(out=oatt_s16[0:16, :], in0=oat[0:16, :], scalar=identf[0:16, b:b + 1],
                                                                   in1=oatt_s16[0:16, :], op0=ALU.mult, op1=ALU.add),
                   rd=[oat_s, cst_s], wr=[oas_s])
        fence(sctR_s, [sct_s])
        fence([sc16_s, sct_s], arena_states)
        finish(n, xs_t[0:n, :], xs_s, oatt_s16, oas_s, ps_d[:, :], ys_d[:, :])

        for i in range(NDS):
            if dcnt[i] > 0:
                POOL.wait((dsems[i], dcnt[i]))
    return nc


_NC_CACHE = {}


def _slot_block(r, i):
    m, second = i // 2, i % 2
    return 8 * m + (7 - r if second else r)


def kernel(x_prompt, x_sample, cache_k, cache_v, cache_kidx, page_table, p_prompt, p_sample,
           g_mix, w_in, g_q, g_k, g_sgu, w_s, b_s, w_o, g_ffn, w_up, w_down, g_ple, w_pg, w_p):
    f32 = np.float32
    A = lambda a: np.ascontiguousarray(np.asarray(a))
    x_prompt = A(x_prompt); x_sample = A(x_sample); p_prompt = A(p_prompt); p_sample = A(p_sample)
    ck = A(cache_k).reshape(N_PHYS * 128, 256)
    cv = A(cache_v).reshape(N_PHYS * 128, 256)
    cki = A(cache_kidx).reshape(N_PHYS * 128, 64)
    pt_all = A(page_table).astype(np.int32)
    shared = {
        "ck": ck, "cv": cv, "cki": cki,
        "w_in": A(w_in)[0], "w_o": A(w_o)[0], "w_up": A(w_up)[0], "w_down": A(w_down)[0], "w_pg": A(w_pg)[0], "w_p": A(w_p)[0],
        "g_mix": A(g_mix)[0], "g_q": A(g_q)[0], "g_k": A(g_k)[0], "g_sgu": A(g_sgu)[0], "w_s": A(w_s)[0], "b_s": A(b_s)[0],
        "g_ffn": A(g_ffn)[0], "g_ple": A(g_ple)[0],
    }
    tt = np.arange(128)[:, None]
    ss = np.arange(512)[None, :]
    in_maps = []
    for c in range(8):
        bi, r = c // 4, c % 4
        blocks = [_slot_block(r, i) for i in range(NSLOT)]
        pen = np.zeros((128, 2, 512), f32)
        pen[:, 0, :] = np.where(ss <= r * 128 + tt, 0.0, -1e30)
        pen[:, 1, :] = np.where(ss <= (3 - r) * 128 + tt, 0.0, -1e30)
        m = dict(shared)
        m["xb"] = x_prompt[bi]
        m["xo"] = np.stack([x_prompt[bi, j * 128:(j + 1) * 128] for j in blocks])
        m["po"] = np.stack([p_prompt[0, bi, j * 128:(j + 1) * 128] for j in blocks])
        m["xs"] = x_sample[c * NS:(c + 1) * NS, 0]
        m["ps"] = p_sample[0, c * NS:(c + 1) * NS, 0]
        m["pt"] = np.ascontiguousarray(pt_all[c * NS:(c + 1) * NS].reshape(-1))
        m["pen"] = pen
        in_maps.append(m)
    if "nc" not in _NC_CACHE:
        _NC_CACHE["nc"] = build()
    res = run_bass_kernel_spmd(_NC_CACHE["nc"], in_maps, core_ids=list(range(8)))
    R = res.results
    y_p = np.zeros((2, SEQ, D), f32); y_s = np.zeros((128, 1, D), f32)
    nk = np.zeros((1, 2, SEQ, 4, 64), f32); nv = np.zeros((1, 2, SEQ, 4, 64), f32)
    nki = np.zeros((1, 2, SEQ, 64), f32); nsg = np.zeros((1, 2, SEQ, D), f32)
    sk = np.zeros((1, 128, 1, 4, 64), f32); sv = np.zeros((1, 128, 1, 4, 64), f32)
    ski = np.zeros((1, 128, 1, 64), f32); ssg = np.zeros((1, 128, 1, D), f32)
    for c in range(8):
        bi, r = c // 4, c % 4
        o = R[c]
        for i in range(NSLOT):
            j = _slot_block(r, i)
            sl = slice(j * 128, (j + 1) * 128)
            y_p[bi, sl] = o["yo"][i]
            nk[0, bi, sl] = o["ko"][i].reshape(128, 4, 64)
            nv[0, bi, sl] = o["vo"][i].reshape(128, 4, 64)
            nki[0, bi, sl] = o["kio"][i]
            nsg[0, bi, sl] = o["sgo"][i]
        s2 = slice(c * NS, (c + 1) * NS)
        y_s[s2, 0] = o["ys"]
        sk[0, s2, 0] = o["kss"].reshape(NS, 4, 64)
        sv[0, s2, 0] = o["vss"].reshape(NS, 4, 64)
        ski[0, s2, 0] = o["kis"]
        ssg[0, s2, 0] = o["sgs"]
    return (y_p, y_s, nk, nv, nki, nsg, sk, sv, ski, ssg)
```

```python
import contextlib
import numpy as np
import concourse.bass as bass
import concourse.mybir as mybir
from concourse.bass_utils import run_bass_kernel_spmd

F32 = mybir.dt.float32
BF16 = mybir.dt.bfloat16
I32 = mybir.dt.int32
ALU = mybir.AluOpType
AF = mybir.ActivationFunctionType
AX = mybir.AxisListType

D = 1024
SEQ = 8192
NBLK = 64
NSLOT = 16
NS = 16
NPG = 16
S_S = 2176
N_PHYS = 2560
INW = 6216
DFF = 4096
PLE = 256
EPS = 1e-6
ATTN_SCALE = 64 ** -0.5
IDXS = (64 ** -0.5) * (8 ** -0.5)
NIT = 20
BRK = 16.0
NEGM = -30000.0
GRPS = [(0, 512), (512, 512), (1024, 512), (1536, 512), (2048, 72), (2120, 512), (2632, 512),
        (3144, 512), (3656, 512), (4168, 512), (4680, 512), (5192, 512), (5704, 512)]


class St:
    __slots__ = ("w", "r")

    def __init__(self):
        self.w = None
        self.r = {}


class Eng:
    def __init__(self, nc, es, name, h):
        self.h = h
        self.name = name
        self.sem = es.enter_context(nc.semaphore("e_" + name))
        self.cnt = 0
        self.seen = {}

    def wait(self, tok):
        sem, val = tok
        if self.seen.get(sem.num, 0) < val:
            self.h.wait_ge(sem, val)
            self.seen[sem.num] = val


def build():
    nc = bass.Bass("TRN2", target_bir_lowering=False)
    dt_in = lambda n, s, d=F32: nc.dram_tensor(n, s, d, kind="ExternalInput").ap()
    dt_out = lambda n, s, d=F32: nc.dram_tensor(n, s, d, kind="ExternalOutput").ap()
    xb_d = dt_in("xb", [SEQ, D])
    xo_d = dt_in("xo", [NSLOT, 128, D])
    po_d = dt_in("po", [NSLOT, 128, PLE])
    xs_d = dt_in("xs", [NS, D])
    ps_d = dt_in("ps", [NS, PLE])
    ck_d = dt_in("ck", [N_PHYS * 128, 256])
    cv_d = dt_in("cv", [N_PHYS * 128, 256])
    cki_d = dt_in("cki", [N_PHYS * 128, 64])
    pt_d = dt_in("pt", [NS * NPG], I32)
    pen_d = dt_in("pen", [128, 2, 512])
    win_d = dt_in("w_in", [D, INW])
    wo_d = dt_in("w_o", [D, D])
    wup_d = dt_in("w_up", [D, DFF])
    wdn_d = dt_in("w_down", [DFF, D])
    wpg_d = dt_in("w_pg", [D, D])
    wp_d = dt_in("w_p", [PLE, D])
    gmix_d = dt_in("g_mix", [D])
    gq_d = dt_in("g_q", [64])
    gk_d = dt_in("g_k", [64])
    gsgu_d = dt_in("g_sgu", [D])
    ws_d = dt_in("w_s", [8, 128, 128])
    bs_d = dt_in("b_s", [8, 128])
    gffn_d = dt_in("g_ffn", [D])
    gple_d = dt_in("g_ple", [D])

    yo_d = dt_out("yo", [NSLOT, 128, D])
    ko_d = dt_out("ko", [NSLOT, 128, 256])
    vo_d = dt_out("vo", [NSLOT, 128, 256])
    kio_d = dt_out("kio", [NSLOT, 128, 64])
    sgo_d = dt_out("sgo", [NSLOT, 128, D])
    ys_d = dt_out("ys", [NS, D])
    kss_d = dt_out("kss", [NS, 256])
    vss_d = dt_out("vss", [NS, 256])
    kis_d = dt_out("kis", [NS, 64])
    sgs_d = dt_out("sgs", [NS, D])

    def scr(n, shape):
        return nc.dram_tensor(n, shape, BF16, kind="Internal").ap()
    win_s = [scr("win_s%d" % i, [128, 8, w]) for i, (o, w) in enumerate(GRPS)]
    wo_s = [scr("wo_s%d" % i, [128, 8, 512]) for i in range(2)]
    wup_s = [scr("wup_s%d" % i, [128, 8, 512]) for i in range(8)]
    wdn_s = [scr("wdn_s%d" % i, [128, 8, 512]) for i in range(8)]
    wpg_s = [scr("wpg_s%d" % i, [128, 8, 512]) for i in range(2)]
    wp_s = [scr("wp_s%d" % i, [128, 2, 512]) for i in range(2)]

    es = contextlib.ExitStack()
    with es:
        PE = Eng(nc, es, "pe", nc.tensor)
        ACT = Eng(nc, es, "act", nc.scalar)
        DVE = Eng(nc, es, "dve", nc.vector)
        POOL = Eng(nc, es, "pool", nc.gpsimd)
        SP = Eng(nc, es, "sp", nc.sync)
        NDS = 24
        dsems = [es.enter_context(nc.semaphore("d%d" % i)) for i in range(NDS)]
        dcnt = [0] * NDS
        dstate = {"i": 0}

        def deps_of(rd, wr):
            deps = []
            for s in rd:
                if s.w is not None:
                    deps.append(s.w)
            for s in wr:
                if s.w is not None:
                    deps.append(s.w)
                deps.extend(s.r.values())
            return deps

        def op(E, fn, rd=(), wr=()):
            for tok in deps_of(rd, wr):
                if E is PE and tok[0] is PE.sem:
                    continue
                E.wait(tok)
            ins = fn()
            E.cnt += 1
            ins.then_inc(E.sem, 1)
            tok = (E.sem, E.cnt)
            for s in rd:
                s.r[E.name] = tok
            for s in wr:
                s.w = tok
                s.r = {}
            return tok

        def dma(Q, out, in_, rd=(), wr=(), fn=None):
            for tok in deps_of(rd, wr):
                Q.wait(tok)
            i = dstate["i"]
            dstate["i"] = (i + 1) % NDS
            if dcnt[i] > 0:
                Q.wait((dsems[i], dcnt[i]))
            if fn is None:
                ins = Q.h.dma_start(out=out, in_=in_)
            else:
                ins = fn()
            dcnt[i] += 16
            ins.then_inc(dsems[i], 16)
            tok = (dsems[i], dcnt[i])
            for s in rd:
                s.r["dma%d" % i] = tok
            for s in wr:
                s.w = tok
                s.r = {}
            return tok

        def fence(frm, to):
            for t in to:
                for f in frm:
                    if f is t:
                        continue
                    if f.w is not None:
                        t.r["f%d" % id(f)] = f.w
                    for k, v in list(f.r.items()):
                        t.r["f%d%s" % (id(f), k)] = v

        def sb(name, shape, dt):
            return nc.alloc_sbuf_tensor("sb_" + name, shape, dt)

        KT = sb("KT", [128, 2, SEQ], BF16); KT_s = St()
        Vt = sb("Vt", [128, NBLK, 4, 65], BF16); V_s = St()
        kiT = sb("kiT", [128, SEQ], BF16); kiT_s = St()
        arena_base = nc.sbuf_base
        scores = sb("scores", [128, SEQ], F32); sc_s = St()
        al = lambda n, shape, dt, off: nc.alloc_sbuf_tensor_at("al_" + n, shape, dt, offset=arena_base + off)
        h_bf = al("h_bf", [128, DFF], BF16, 0); hbf_s = St()
        hT = al("hT", [128, 32, 128], BF16, 8192); hT_s = St()
        xn_bf = al("xn_bf", [128, D], BF16, 16384); xnbf_s = St()
        xnT = al("xnT", [128, 8, 128], BF16, 18432); xnT_s = St()
        vn_bf = al("vn_bf", [128, D], BF16, 20480); vnbf_s = St()
        wkvk = al("wkvk", [128, 8, 576], BF16, 22528); wkvk_s = St()
        arena_states = [hbf_s, hT_s, xnbf_s, xnT_s, vnbf_s, wkvk_s]
        sc16 = scores[0:16, 0:S_S]; sc16_s = St()
        sct = scores[0:16, S_S:2 * S_S]; sct_s = St()
        sctR = [sct, scores[0:16, 2 * S_S:3 * S_S]]; sctR_s = [sct_s, St()]
        WS = [sb("ws%d" % i, [128, 8, 512], BF16) for i in range(3)]
        WS_s = [St() for _ in range(3)]
        FB0 = sb("fb0", [128, D], F32)
        FB12 = sb("fb12", [128, 2 * D], F32)
        FB3 = sb("fb3", [128, D], F32)
        FB4 = sb("fb4", [128, D], F32)
        FB = [FB0[:, :], FB12[:, 0:D], FB12[:, D:2 * D], FB3[:, :], FB4[:, :]]
        FB_s = [St() for _ in range(5)]
        junk8 = FB12[:, :].bitcast(mybir.dt.uint8)
        junki8 = FB12[:, :].bitcast(mybir.dt.int8)
        gmixT = sb("gmixT", [128, 8], F32); gffnT = sb("gffnT", [128, 8], F32); gpleT = sb("gpleT", [128, 8], F32)
        gsgub = sb("gsgub", [128, D], F32)
        coefs = sb("coefs", [16, 16], F32)
        gqb = sb("gqb", [128, 64], F32); gkb = sb("gkb", [128, 64], F32)
        cst_s = St()
        qn_bf = sb("qn_bf", [128, 2, 4, 128], BF16); qn_s = St()
        qT = sb("qT", [128, 4, 4, 128], BF16); qT_s = St()
        qi_bf = sb("qi_bf", [128, 512], BF16); qibf_s = St()
        qiT = sb("qiT", [128, 4, 128], BF16); qiT_s = St()
        absw = sb("absw", [128, 8], F32); sgnw = sb("sgnw", [128, 8], F32); w_s_ = St()
        kn_bf = sb("kn_bf", [128, 256], BF16); knbf_s = St()
        v_bf = sb("v_bf", [128, 256], BF16); vbf_s = St()
        kid_bf = sb("kid_bf", [128, 2, 64], BF16); kid_s = St()
        k32 = sb("k32", [128, 512], F32); k32_s = St()
        o32 = sb("o32", [128, 512], F32); o32_s = St()
        ki32 = sb("ki32", [128, 72], F32); ki32_s = St()
        rt = [sb("rt%d" % i, [128, 512], F32) for i in range(3)]; rt_s = [St(), St(), St()]
        PT = [sb("PT%d" % i, [128, 512], BF16) for i in range(3)]; PT_s = [St(), St(), St()]
        mT = [sb("mT%d" % i, [128, 4, 128], BF16) for i in range(2)]; mT_s = [St(), St()]
        MBg = [sb("MBg%d" % i, [128, 512], BF16) for i in range(2)]; MB_s = [St(), St()]
        identb = sb("identb", [128, 128], BF16); identf = sb("identf", [16, 16], F32)
        I4 = sb("I4", [128, 4, 128], BF16); I416 = sb("I416", [16, 4, 16], BF16)
        wsT = sb("wsT", [128, 8, 128], BF16); bsT = sb("bsT", [128, 8], F32)
        pen = sb("pen", [128, 2, 512], F32)
        p32 = sb("p32", [128, PLE], F32); p32_s = St()
        p_bf = sb("p_bf", [128, PLE], BF16); pbf_s = St()
        pT = sb("pT", [128, 2, 128], BF16); pT_s = St()
        sm = sb("sm", [128, 64], F32); sm_s = St()
        lo = sb("lo", [128, 1], F32); mid = sb("mid", [128, 1], F32); cntt = sb("cntt", [128, 1], F32)
        geq = sb("geq", [128, 1], F32); bis_s = St(); junk_s = St()
        cnta = sb("cnta", [128, 1], F32); cnta_s = St(); mid_s = St(); junka_s = St()
        mhalf = sb("mhalf", [128, 16], F32)
        ss16 = sb("ss16", [128, 16], F32); ss16_s = St()
        rs16 = sb("rs16", [128, 16], F32); rs16_s = St()
        ptb = sb("ptb", [128, NS * NPG], I32); idxa = sb("idxa", [128, NS * NPG], I32); iop = sb("iop", [128, 1], I32)
        NPB = 4
        pgK = [sb("pgK%d" % i, [128, 256], BF16) for i in range(NPB)]; pgK_s = [St() for _ in range(NPB)]
        pgV = [sb("pgV%d" % i, [128, 256], BF16) for i in range(NPB)]; pgV_s = [St() for _ in range(NPB)]
        pgI = [sb("pgI%d" % i, [128, 2, 64], BF16) for i in range(NPB)]; pgI_s = [St() for _ in range(NPB)]
        ksT = sb("ksT", [128, 2, 16], BF16); kisT = sb("kisT", [128, 16], BF16); vs_bf = sb("vs_bf", [16, 256], BF16)
        smp_s = St()
        ws32 = sb("ws32", [128, 128], F32); ws32_s = St()
        wsb = sb("wsb", [128, 128], BF16); wsb_s = St()

        PB = [nc.alloc_psum_tensor("pb%d" % i, [128, 512], F32) for i in range(8)]
        PB_s = [St() for _ in range(8)]
        PBb = [PB[i][:].bitcast(BF16).rearrange("p (a b) -> p a b", a=8) for i in range(8)]

        def transposes(src_ap_fn, nblk, n, bank0, dst_fn, dst_states, src_states, evac=None, gainT=None):
            for b0 in range(0, nblk, 8):
                nb = min(8, nblk - b0)
                bank = 2 + (bank0 + b0 // 8) % 2
                if gainT is not None:
                    for j in range(nb):
                        op(PE, lambda j=j: nc.tensor.transpose(PBb[bank][:, j, 0:n], src_ap_fn(b0 + j), identb[0:n, 0:n]),
                           rd=list(src_states) + [cst_s], wr=[PB_s[bank]])
                    op(DVE, lambda: nc.vector.tensor_tensor(out=dst_fn(b0, nb), in0=PBb[bank][:, 0:nb, 0:n],
                                                            in1=gainT[:, b0:b0 + nb].unsqueeze(2).to_broadcast([128, nb, n]), op=ALU.mult),
                       rd=[PB_s[bank], cst_s], wr=dst_states)
                    continue
                for j in range(nb):
                    tok = op(PE, lambda j=j: nc.tensor.transpose(PBb[bank][:, j, 0:n], src_ap_fn(b0 + j), identb[0:n, 0:n]),
                             rd=list(src_states) + [cst_s], wr=[PB_s[bank]])
                E = evac or ACT
                if E is ACT:
                    op(ACT, lambda: nc.scalar.copy(out=dst_fn(b0, nb), in_=PBb[bank][:, 0:nb, 0:n]),
                       rd=[PB_s[bank]], wr=dst_states)
                else:
                    op(DVE, lambda: nc.vector.tensor_copy(out=dst_fn(b0, nb), in_=PBb[bank][:, 0:nb, 0:n]),
                       rd=[PB_s[bank]], wr=dst_states)

        RS = {"act": False}

        def rstd_from_ss(ss_ap, out_ap, n, ncol, inv_d, rd_s, wr_s):
            op(DVE, lambda: nc.vector.tensor_scalar(out=ss_ap, in0=ss_ap, scalar1=inv_d, scalar2=EPS, op0=ALU.mult, op1=ALU.add),
               rd=[], wr=[rd_s])
            if RS["act"]:
                op(ACT, lambda: nc.scalar.activation(out=out_ap, in_=ss_ap, func=AF.Sqrt), rd=[rd_s], wr=[wr_s])
                op(DVE, lambda: nc.vector.reciprocal(out=out_ap, in_=out_ap), rd=[], wr=[wr_s])
            else:
                op(POOL, lambda: nc.gpsimd.tensor_tensor(out=out_ap, in0=ss_ap, in1=mhalf[0:n, 0:ncol], op=ALU.pow),
                   rd=[rd_s, cst_s], wr=[wr_s])

        def rmsnorm_to_T(x_ap, x_s, gain, n):
            op(DVE, lambda: nc.vector.scalar_tensor_tensor(out=xn_bf[0:n, :], in0=x_ap, scalar=1.0, in1=x_ap, op0=ALU.mult, op1=ALU.mult,
                                                           accum_out=ss16[0:n, 0:1]),
               rd=[x_s], wr=[xnbf_s, ss16_s])
            rstd_from_ss(ss16[0:n, 0:1], rs16[0:n, 0:1], n, 1, 1.0 / D, ss16_s, rs16_s)
            op(DVE, lambda: nc.vector.tensor_scalar(out=xn_bf[0:n, :], in0=x_ap, scalar1=rs16[0:n, 0:1], scalar2=None, op0=ALU.mult),
               rd=[x_s, rs16_s], wr=[xnbf_s])
            transposes(lambda j: xn_bf[0:n, j * 128:(j + 1) * 128], 8, n, 0,
                       lambda b0, nb: xnT[:, b0:b0 + nb, 0:n], [xnT_s], [xnbf_s], gainT=gain)

        wseq = []
        one_slot = ([(win_s[i], 8, GRPS[i][1]) for i in range(13)] + [(wo_s[i], 8, 512) for i in range(2)]
                    + [(wup_s[i], 8, 512) for i in range(8)] + [(wdn_s[i], 8, 512) for i in range(8)]
                    + [(wpg_s[i], 8, 512) for i in range(2)] + [(wp_s[i], 2, 512) for i in range(2)])
        for _ in range(NSLOT + 1):
            wseq.extend(one_slot)
        wst = {"issued": 0, "next": 0}
        conv_s = St()

        def w_issue(upto):
            while wst["issued"] < min(upto, len(wseq)):
                k = wst["issued"]
                src, kc, ncol = wseq[k]
                sl = k % 3
                dma(SP, WS[sl][:, 0:kc, 0:ncol], src[:, :, :], rd=[conv_s], wr=[WS_s[sl]])
                wst["issued"] += 1

        def w_get():
            k = wst["next"]
            wst["next"] += 1
            w_issue(k + 1)
            return k % 3, k

        def w_done(k):
            w_issue(k + 3)

        def dense(lhsT_fn, kcs, n, ncols_list, bank_fn, lhs_states, after_fn):
            for gi, ncol in enumerate(ncols_list):
                sl, k = w_get()
                bank = bank_fn(gi)
                for kc in range(kcs):
                    op(PE, lambda kc=kc: nc.tensor.matmul(PB[bank][0:n, 0:ncol], lhsT=lhsT_fn(kc), rhs=WS[sl][:, kc, 0:ncol],
                                                          start=(kc == 0), stop=(kc == kcs - 1)),
                       rd=list(lhs_states) + [WS_s[sl]], wr=[PB_s[bank]])
                w_done(k)
                after_fn(gi, bank)

        with nc.allow_non_contiguous_dma(reason="tiny constant loads"):
            for (dst, src) in ((gsgub, gsgu_d), (gqb, gq_d), (gkb, gk_d)):
                dma(SP, dst[:], src.partition_broadcast(128), wr=[cst_s])
            for (dst, src) in ((gmixT, gmix_d), (gffnT, gffn_d), (gpleT, gple_d)):
                dma(SP, None, None, wr=[cst_s], fn=lambda dst=dst, src=src: nc.sync.dma_start(out=dst[:], in_=src.rearrange("(k p) -> p k", p=128)))
            dma(SP, None, None, wr=[cst_s], fn=lambda: nc.sync.dma_start(
                out=coefs[:, 0:8], in_=ws_d[:, 0, 0:1].rearrange("g a -> (g a)").partition_broadcast(16)))
            dma(SP, None, None, wr=[cst_s], fn=lambda: nc.sync.dma_start(
                out=coefs[:, 8:16], in_=bs_d[:, 0:1].rearrange("g a -> (g a)").partition_broadcast(16)))
            dma(SP, bsT[:], bs_d.rearrange("g t -> t g"), wr=[cst_s], fn=lambda: nc.sync.dma_start(
                out=bsT[:], in_=bs_d.rearrange("g t -> t g")))
            dma(SP, pen[:], pen_d[:, :, :], wr=[cst_s])
            dma(SP, ptb[:], pt_d.partition_broadcast(128), wr=[cst_s])
        op(POOL, lambda: nc.gpsimd.memset(identb[:], 1.0), wr=[cst_s])
        op(POOL, lambda: nc.gpsimd.affine_select(out=identb[:], in_=identb[:], pattern=[[-1, 128]], compare_op=ALU.is_equal,
                                                 fill=0.0, base=0, channel_multiplier=1), wr=[cst_s])
        op(POOL, lambda: nc.gpsimd.memset(identf[:], 1.0), wr=[cst_s])
        op(POOL, lambda: nc.gpsimd.affine_select(out=identf[:], in_=identf[:], pattern=[[-1, 16]], compare_op=ALU.is_equal,
                                                 fill=0.0, base=0, channel_multiplier=1), wr=[cst_s])
        op(POOL, lambda: nc.gpsimd.memset(mhalf[:], -0.5), wr=[cst_s])
        op(POOL, lambda: nc.gpsimd.iota(iop[:], pattern=[[0, 1]], base=0, channel_multiplier=1), wr=[cst_s])
        for g in range(4):
            op(DVE, lambda g=g: nc.vector.tensor_copy(out=I4[:, g, :], in_=identb[:]), rd=[], wr=[cst_s])
            op(DVE, lambda g=g: nc.vector.tensor_copy(out=I416[:, g, :], in_=identb[0:16, 0:16]), rd=[], wr=[cst_s])
        op(DVE, lambda: nc.vector.tensor_scalar(out=gqb[:], in0=gqb[:], scalar1=ATTN_SCALE, scalar2=None, op0=ALU.mult), wr=[cst_s])
        op(DVE, lambda: nc.vector.tensor_scalar(out=idxa[:], in0=ptb[:], scalar1=128, scalar2=iop[:, 0:1], op0=ALU.mult, op1=ALU.add),
           wr=[cst_s])
        op(POOL, lambda: nc.gpsimd.memset(qT[:], 0.0), wr=[qT_s])
        op(POOL, lambda: nc.gpsimd.memset(Vt[:], 0.0), wr=[V_s])
        op(POOL, lambda: nc.gpsimd.memset(Vt[:, :, :, 64:65], 1.0), wr=[V_s])
        for g in range(8):
            dma(SP, ws32[:], ws_d[g, :, :], wr=[ws32_s])
            op(POOL, lambda: nc.gpsimd.affine_select(out=ws32[:], in_=ws32[:], pattern=[[-1, 128]], compare_op=ALU.is_ge,
                                                     fill=0.0, base=0, channel_multiplier=1), wr=[ws32_s])
            op(DVE, lambda: nc.vector.tensor_copy(out=wsb[:], in_=ws32[:]), rd=[ws32_s], wr=[wsb_s])
            op(PE, lambda: nc.tensor.transpose(PBb[3][:, 0, :], wsb[:], identb[:]), rd=[wsb_s, cst_s], wr=[PB_s[3]])
            op(ACT, lambda g=g: nc.scalar.copy(out=wsT[:, g, :], in_=PBb[3][:, 0, :]), rd=[PB_s[3]], wr=[cst_s])

        conv_jobs = []
        conv_toks = []

        def conv(dst, src2d, kc, col0, ncol, first=False):
            for k in range(kc):
                job = (lambda k=k: conv_toks.append(dma(
                    POOL, None, None, wr=[],
                    fn=lambda: nc.gpsimd.dma_start(out=dst[:, k, :], in_=src2d[k * 128:(k + 1) * 128, col0:col0 + ncol]))))
                if first:
                    job()
                else:
                    conv_jobs.append(job)
        for i, (o, w) in enumerate(GRPS):
            conv(win_s[i], win_d, 8, o, w, first=(i in (2, 4)))
        for i in range(2):
            conv(wo_s[i], wo_d, 8, i * 512, 512)
        for i in range(8):
            conv(wup_s[i], wup_d, 8, i * 512, 512)
        for i in range(8):
            nn, kg = i // 4, i % 4
            conv(wdn_s[i], wdn_d[kg * 1024:(kg + 1) * 1024, :], 8, nn * 512, 512)
        for i in range(2):
            conv(wpg_s[i], wpg_d, 8, i * 512, 512)
        for i in range(2):
            conv(wp_s[i], wp_d, 2, i * 512, 512)
        for tok in conv_toks:
            SP.wait(tok)

        def head_norm(src32, n, nh, gain, out_fn, out_states, e=None):
            v3 = src32.rearrange("p (h d) -> p h d", d=64)
            if e is not None:
                tmp4 = o32[0:n, 0:nh * 64].rearrange("p (e g d) -> p e g d", e=e, d=64)
                op(DVE, lambda: nc.vector.tensor_tensor(out=o32[0:n, 0:nh * 64], in0=src32, in1=src32, op=ALU.mult), rd=[k32_s], wr=[o32_s])
                op(DVE, lambda: nc.vector.tensor_reduce(out=ss16[0:n, 0:nh], in_=o32[0:n, 0:nh * 64].rearrange("p (h d) -> p h d", d=64),
                                                        axis=AX.X, op=ALU.add), rd=[o32_s], wr=[ss16_s])
                rstd_from_ss(ss16[0:n, 0:nh], rs16[0:n, 0:nh], n, nh, 1.0 / 64, ss16_s, rs16_s)
                op(DVE, lambda: nc.vector.tensor_tensor(out=o32[0:n, 0:nh * 64].rearrange("p (h d) -> p h d", d=64), in0=v3,
                                                        in1=rs16[0:n, 0:nh].unsqueeze(2).to_broadcast([n, nh, 64]), op=ALU.mult),
                   rd=[k32_s, rs16_s], wr=[o32_s])
                for ee in range(e):
                    op(DVE, lambda ee=ee: nc.vector.tensor_tensor(out=out_fn(ee), in0=tmp4[:, ee, :, :],
                                                                  in1=gain[0:n, :].unsqueeze(1).to_broadcast([n, nh // e, 64]), op=ALU.mult),
                       rd=[o32_s, cst_s], wr=out_states)
                return
            op(DVE, lambda: nc.vector.tensor_tensor(out=o32[0:n, 0:nh * 64], in0=src32, in1=src32, op=ALU.mult), rd=[k32_s], wr=[o32_s])
            op(DVE, lambda: nc.vector.tensor_reduce(out=ss16[0:n, 0:nh], in_=o32[0:n, 0:nh * 64].rearrange("p (h d) -> p h d", d=64),
                                                    axis=AX.X, op=ALU.add), rd=[o32_s], wr=[ss16_s])
            rstd_from_ss(ss16[0:n, 0:nh], rs16[0:n, 0:nh], n, nh, 1.0 / 64, ss16_s, rs16_s)
            op(DVE, lambda: nc.vector.tensor_tensor(out=o32[0:n, 0:nh * 64].rearrange("p (h d) -> p h d", d=64), in0=v3,
                                                    in1=rs16[0:n, 0:nh].unsqueeze(2).to_broadcast([n, nh, 64]), op=ALU.mult),
               rd=[k32_s, rs16_s], wr=[o32_s])
            op(DVE, lambda: nc.vector.tensor_tensor(out=out_fn(), in0=o32[0:n, 0:nh * 64].rearrange("p (h d) -> p h d", d=64),
                                                    in1=gain[0:n, :].unsqueeze(1).to_broadcast([n, nh, 64]), op=ALU.mult),
               rd=[o32_s, cst_s], wr=out_states)

        def kv_append(blk, n, k_ap, k_s, v_ap, v_s, kid_ap, kid_s_):
            c0 = blk * 128
            for j in range(2):
                op(PE, lambda j=j: nc.tensor.transpose(PBb[3][:, j, 0:n], k_ap[0:n, j * 128:(j + 1) * 128], identb[0:n, 0:n]),
                   rd=[k_s, cst_s], wr=[PB_s[3]])
            op(PE, lambda: nc.tensor.transpose(PBb[3][:, 2, 0:n], kid_ap[0:n, :, :].rearrange("p a d -> p (a d)"), identb[0:n, 0:n]),
               rd=[kid_s_, cst_s], wr=[PB_s[3]])
            op(ACT, lambda: nc.scalar.copy(out=KT[:, :, c0:c0 + n], in_=PBb[3][:, 0:2, 0:n]), rd=[PB_s[3]], wr=[KT_s])
            op(ACT, lambda: nc.scalar.copy(out=kiT[:, c0:c0 + n], in_=PBb[3][:, 2, 0:n]), rd=[PB_s[3]], wr=[kiT_s])
            op(ACT, lambda: nc.scalar.copy(out=Vt[0:n, blk, :, 0:64], in_=v_ap[0:n, :].rearrange("p (c d) -> p c d", d=64)),
               rd=[v_s], wr=[V_s])

        def indexer(n, nch, sc_ap, sc_state, k0=0, ki_state=None, hook=None):
            ki_state = ki_state or kiT_s
            S = nch * 128
            j = 0
            for g0 in range(0, S, 512):
                w = min(512, S - g0)
                for h in range(8):
                    if hook is not None:
                        hook()
                    bank = j % 3
                    r = rt[j % 3]
                    rs = rt_s[j % 3]
                    j += 1
                    pb = (h % 2) * 64
                    op(PE, lambda: nc.tensor.matmul(PB[bank][0:n, 0:w], lhsT=qiT[pb:pb + 64, h // 2, 0:n], rhs=kiT[pb:pb + 64, k0 + g0:k0 + g0 + w],
                                                    start=True, stop=True), rd=[qiT_s, ki_state], wr=[PB_s[bank]])
                    op(ACT, lambda: nc.scalar.activation(out=r[0:n, 0:w], in_=PB[bank][0:n, 0:w], func=AF.Relu, scale=absw[0:n, h:h + 1]),
                       rd=[PB_s[bank], w_s_], wr=[rs])
                    if h == 0:
                        op(DVE, lambda: nc.vector.tensor_scalar(out=sc_ap[0:n, g0:g0 + w], in0=r[0:n, 0:w], scalar1=sgnw[0:n, 0:1], scalar2=None,
                                                                op0=ALU.mult), rd=[rs, w_s_], wr=[sc_state])
                    else:
                        op(DVE, lambda: nc.vector.scalar_tensor_tensor(out=sc_ap[0:n, g0:g0 + w], in0=r[0:n, 0:w], scalar=sgnw[0:n, h:h + 1],
                                                                       in1=sc_ap[0:n, g0:g0 + w], op0=ALU.mult, op1=ALU.add),
                           rd=[rs, w_s_], wr=[sc_state])

        def bisect(n, S, sc_ap, sc_state, junk_ap, junk_state, junk_act=None):
            op(DVE, lambda: nc.vector.memset(lo[0:n, :], -BRK), wr=[bis_s])
            split = junk_act is not None and S >= 2048
            S1 = (int(S * 0.46) // 128) * 128 if split else S
            S2 = S - S1
            wd = BRK
            for it in range(NIT):
                op(DVE, lambda: nc.vector.tensor_scalar(out=mid[0:n, :], in0=lo[0:n, :], scalar1=wd, scalar2=None, op0=ALU.add),
                   rd=[bis_s], wr=[bis_s, mid_s])
                if split:
                    op(ACT, lambda: nc.scalar.activation(out=junk_act[0:n, S1:S], in_=sc_ap[0:n, S1:S], func=AF.Sign, bias=mid[0:n, 0:1], scale=-1.0,
                                                         accum_out=cnta[0:n, 0:1]), rd=[sc_state, mid_s], wr=[junka_s, cnta_s])
                op(DVE, lambda: nc.vector.tensor_scalar(out=junk_ap[0:n, 0:S1], in0=sc_ap[0:n, 0:S1], scalar1=mid[0:n, 0:1], scalar2=None, op0=ALU.is_ge,
                                                        op1=ALU.add, accum_out=cntt[0:n, 0:1]),
                   rd=[sc_state, bis_s], wr=[junk_state, bis_s])
                thr = 255.5
                if split:
                    op(DVE, lambda: nc.vector.scalar_tensor_tensor(out=cntt[0:n, :], in0=cnta[0:n, :], scalar=-0.5, in1=cntt[0:n, :], op0=ALU.mult,
                                                                   op1=ALU.add), rd=[cnta_s, bis_s], wr=[bis_s])
                    thr = 255.5 - S2 / 2.0
                op(DVE, lambda: nc.vector.tensor_scalar(out=geq[0:n, :], in0=cntt[0:n, :], scalar1=thr, scalar2=wd, op0=ALU.is_ge,
                                                        op1=ALU.mult), rd=[bis_s], wr=[bis_s])
                op(DVE, lambda: nc.vector.tensor_tensor(out=lo[0:n, :], in0=lo[0:n, :], in1=geq[0:n, :], op=ALU.add), rd=[bis_s, mid_s], wr=[bis_s])
                wd = wd / 2.0

        def attend(n, nch, sc_ap, sc_state, sel_ap, out_ap, out_state, ch0=0, kt_state=None, v_state=None, hook=None):
            kt_state = kt_state or KT_s
            v_state = v_state or V_s
            steps = [(ch, c) for ch in range(nch) for c in range(4)]

            def prep_group(gi):
                w = min(512, nch * 128 - gi * 512)
                ng = w // 128
                mb = MBg[gi % 2]
                bT = 0
                op(DVE, lambda: nc.vector.tensor_scalar(out=mb[0:n, 0:w], in0=sc_ap[0:n, gi * 512:gi * 512 + w], scalar1=lo[0:n, 0:1],
                                                        scalar2=None, op0=ALU.is_ge), rd=[sc_state, bis_s], wr=[MB_s[gi % 2]])
                for cj in range(ng):
                    op(PE, lambda cj=cj: nc.tensor.transpose(PBb[bT][:, cj, 0:n], mb[0:n, cj * 128:(cj + 1) * 128], identb[0:n, 0:n]),
                       rd=[MB_s[gi % 2], cst_s], wr=[PB_s[bT]])
                op(DVE, lambda: nc.vector.tensor_copy(out=mT[gi % 2][:, 0:ng, 0:n], in_=PBb[bT][:, 0:ng, 0:n]), rd=[PB_s[bT]], wr=[mT_s[gi % 2]])

            def emit_qk(k):
                ch, c = steps[k]
                gi, cj = ch // 4, ch % 4
                if cj == 0 and c == 0:
                    prep_group(gi)
                bank = 1 + (k % 3)
                pb = (c % 2) * 64
                outv = PB[bank][:, 0:4 * n].rearrange("p (g t) -> p g t", g=4)
                op(PE, lambda: nc.tensor.matmul(outv, lhsT=KT[:, c // 2, (ch0 + ch) * 128:(ch0 + ch + 1) * 128], rhs=qT[:, c, :, 0:n],
                                                start=True, stop=True), rd=[kt_state, qT_s], wr=[PB_s[bank]])

            emit_qk(0)
            if len(steps) > 1:
                emit_qk(1)
            for k in range(len(steps)):
                ch, c = steps[k]
                gi, cj = ch // 4, ch % 4
                if hook is not None:
                    hook()
                if k + 2 < len(steps):
                    emit_qk(k + 2)
                bank = 1 + (k % 3)
                pt_ = PT[k % 3]
                pts = PT_s[k % 3]
                op(ACT, lambda: nc.scalar.activation(out=pt_[:, 0:4 * n], in_=PB[bank][:, 0:4 * n], func=AF.Exp), rd=[PB_s[bank]], wr=[pts])
                pt3 = pt_[:, 0:4 * n].rearrange("p (g t) -> p g t", g=4)
                op(DVE, lambda: nc.vector.tensor_tensor(out=pt3, in0=pt3, in1=mT[gi % 2][:, cj, 0:n].unsqueeze(1).to_broadcast([128, 4, n]),
                                                        op=ALU.mult), rd=[mT_s[gi % 2]], wr=[pts])
                for g in range(4):
                    op(PE, lambda g=g: nc.tensor.matmul(PB[4 + c][0:n, g * 65:(g + 1) * 65], lhsT=pt_[:, g * n:(g + 1) * n], rhs=Vt[:, ch0 + ch, c, :],
                                                        start=(ch == 0 and g == 0), stop=(ch == nch - 1), skip_group_check=True),
                       rd=[pts, v_state], wr=[PB_s[4 + c]])
            for c in range(4):
                acc = PB[4 + c][0:n, 0:260].rearrange("p (g e) -> p g e", e=65)
                op(DVE, lambda: nc.vector.reciprocal(out=sm[0:n, 4 * c:4 * c + 4], in_=acc[:, :, 64]), rd=[PB_s[4 + c]], wr=[sm_s])
                op(DVE, lambda: nc.vector.tensor_tensor(out=out_ap[0:n, c * 256:(c + 1) * 256].rearrange("p (g d) -> p g d", d=64),
                                                        in0=acc[:, :, 0:64], in1=sm[0:n, 4 * c:4 * c + 4].unsqueeze(2).to_broadcast([n, 4, 64]),
                                                        op=ALU.mult), rd=[PB_s[4 + c], sm_s], wr=[out_state])

        STh_s = [[St(), St()], [St(), St()]]

        def attend_sample(nch, sc_ap, sc_state, out_ap, out_state, ch0, kt_state, v_state, hook=None):
            n = 16

            def prep_group(gi):
                w = min(512, nch * 128 - gi * 512)
                ng = w // 128
                mb = MBg[gi % 2]
                op(DVE, lambda: nc.vector.tensor_scalar(out=mb[0:n, 0:w], in0=sc_ap[0:n, gi * 512:gi * 512 + w], scalar1=lo[0:n, 0:1],
                                                        scalar2=None, op0=ALU.is_ge), rd=[sc_state, bis_s], wr=[MB_s[gi % 2]])
                for cj in range(ng):
                    op(PE, lambda cj=cj: nc.tensor.transpose(PBb[0][:, cj, 0:n], mb[0:n, cj * 128:(cj + 1) * 128], identb[0:n, 0:n]),
                       rd=[MB_s[gi % 2], cst_s], wr=[PB_s[0]])
                op(DVE, lambda: nc.vector.tensor_copy(out=mT[gi % 2][:, 0:ng, 0:n], in_=PBb[0][:, 0:ng, 0:n]), rd=[PB_s[0]], wr=[mT_s[gi % 2]])

            def emit_qk(ch):
                gi, cj = ch // 4, ch % 4
                if cj == 0:
                    prep_group(gi)
                hf = 0
                for c in range(4):
                    pb = (c % 2) * 64
                    col = hf * 128 + (c // 2) * 64
                    outv = PB[1 + (c % 2)][:, col:col + 64].rearrange("p (g t) -> p g t", g=4)
                    op(PE, lambda: nc.tensor.matmul(outv, lhsT=KT[:, c // 2, (ch0 + ch) * 128:(ch0 + ch + 1) * 128],
                                                    rhs=qT[:, c, :, 0:n], start=True, stop=True, skip_group_check=True),
                       rd=[kt_state, qT_s], wr=[PB_s[1 + (c % 2)]])

            emit_qk(0)
            for ch in range(nch):
                gi, cj = ch // 4, ch % 4
                hf = 0
                if hook is not None:
                    hook()
                    hook()
                pt_ = PT[ch % 3]
                pts = PT_s[ch % 3]
                for e in range(2):
                    op(ACT, lambda e=e: nc.scalar.activation(out=pt_[:, e * 128:(e + 1) * 128], in_=PB[1 + e][:, hf * 128:(hf + 1) * 128], func=AF.Exp),
                       rd=[PB_s[1 + e]], wr=[pts])
                pt3 = pt_[:, 0:256].rearrange("p (a t) -> p a t", t=n)
                op(DVE, lambda: nc.vector.tensor_tensor(out=pt3, in0=pt3, in1=mT[gi % 2][:, cj, 0:n].unsqueeze(1).to_broadcast([128, 16, n]),
                                                        op=ALU.mult), rd=[mT_s[gi % 2]], wr=[pts])
                if ch + 1 < nch:
                    emit_qk(ch + 1)
                for c in range(4):
                    for g in range(4):
                        a0 = (c % 2) * 128 + (c // 2) * 64 + g * n
                        op(PE, lambda: nc.tensor.matmul(PB[4 + c][0:n, g * 65:(g + 1) * 65], lhsT=pt_[:, a0:a0 + n], rhs=Vt[:, ch0 + ch, c, :],
                                                        start=(ch == 0 and g == 0), stop=(ch == nch - 1), skip_group_check=True),
                           rd=[pts, v_state], wr=[PB_s[4 + c]])
            for c in range(4):
                acc = PB[4 + c][0:n, 0:260].rearrange("p (g e) -> p g e", e=65)
                op(DVE, lambda: nc.vector.reciprocal(out=sm[0:n, 4 * c:4 * c + 4], in_=acc[:, :, 64]), rd=[PB_s[4 + c]], wr=[sm_s])
                op(DVE, lambda: nc.vector.tensor_tensor(out=out_ap[0:n, c * 256:(c + 1) * 256].rearrange("p (g d) -> p g d", d=64),
                                                        in0=acc[:, :, 0:64], in1=sm[0:n, 4 * c:4 * c + 4].unsqueeze(2).to_broadcast([n, 4, 64]),
                                                        op=ALU.mult), rd=[PB_s[4 + c], sm_s], wr=[out_state])

        def project(n, x_ap, x_s, outs):
            fence([sc_s], arena_states)
            rmsnorm_to_T(x_ap, x_s, gmixT, n)
            ug, ug_s = FB[1], FB_s[1]
            vn, vn_s = FB[2], FB_s[2]
            sga, sga_s = FB[3], FB_s[3]
            sgb, sgb_s = FB[4], FB_s[4]

            def after(gi, bank):
                pb_s = PB_s[bank]
                pbk = PB[bank]
                if gi in (0, 1):
                    op(ACT, lambda: nc.scalar.copy(out=k32[0:n, :], in_=pbk[0:n, :]), rd=[pb_s], wr=[k32_s])
                    head_norm(k32[0:n, :], n, 8, gqb, lambda ee: qn_bf[0:n, gi, :, ee * 64:(ee + 1) * 64], [qn_s], e=2)
                elif gi == 2:
                    op(ACT, lambda: nc.scalar.copy(out=k32[0:n, :], in_=pbk[0:n, :]), rd=[pb_s], wr=[k32_s])
                    head_norm(k32[0:n, 0:256], n, 4, gkb, lambda: o32[0:n, 256:512].rearrange("p (h d) -> p h d", d=64), [o32_s])
                    dma(POOL, outs["k"], o32[0:n, 256:512], rd=[o32_s])
                    dma(POOL, outs["v"], k32[0:n, 256:512], rd=[k32_s])
                    if "kv" in outs:
                        op(DVE, lambda: nc.vector.tensor_copy(out=kn_bf[0:n, :], in_=o32[0:n, 256:512]), rd=[o32_s], wr=[knbf_s])
                        op(DVE, lambda: nc.vector.tensor_copy(out=vs_bf[0:n, :], in_=k32[0:n, 256:512]), rd=[k32_s], wr=[smp_s])
                elif gi == 3:
                    op(ACT, lambda: nc.scalar.copy(out=qi_bf[0:n, :], in_=pbk[0:n, :]), rd=[pb_s], wr=[qibf_s])
                elif gi == 4:
                    op(ACT, lambda: nc.scalar.copy(out=ki32[0:n, :], in_=pbk[0:n, 0:72]), rd=[pb_s], wr=[ki32_s])
                    dma(POOL, outs["ki"], ki32[0:n, 0:64], rd=[ki32_s])
                    op(ACT, lambda: nc.scalar.activation(out=absw[0:n, :], in_=ki32[0:n, 64:72], func=AF.Abs, scale=IDXS), rd=[ki32_s], wr=[w_s_])
                    op(ACT, lambda: nc.scalar.activation(out=sgnw[0:n, :], in_=ki32[0:n, 64:72], func=AF.Sign), rd=[ki32_s], wr=[w_s_])
                    if "kv" in outs:
                        for a in range(2):
                            op(DVE, lambda a=a: nc.vector.tensor_copy(out=kid_bf[0:n, a, :], in_=ki32[0:n, 0:64]), rd=[ki32_s], wr=[kid_s])
                elif gi in (5, 6):
                    o = (gi - 5) * 512
                    op(ACT, lambda: nc.scalar.activation(out=ug[0:n, o:o + 512], in_=pbk[0:n, :], func=AF.Gelu_apprx_tanh), rd=[pb_s], wr=[ug_s])
                elif gi in (7, 8):
                    o = (gi - 7) * 512
                    op(ACT, lambda: nc.scalar.activation(out=vn[0:n, o:o + 512], in_=pbk[0:n, :], func=AF.Gelu_apprx_tanh), rd=[pb_s], wr=[vn_s])
                elif gi in (9, 10):
                    o = (gi - 9) * 512
                    op(ACT, lambda: nc.scalar.activation(out=sga[0:n, o:o + 512], in_=pbk[0:n, :], func=AF.Sigmoid), rd=[pb_s], wr=[sga_s])
                else:
                    o = (gi - 11) * 512
                    op(ACT, lambda: nc.scalar.activation(out=sgb[0:n, o:o + 512], in_=pbk[0:n, :], func=AF.Sigmoid), rd=[pb_s], wr=[sgb_s])

            dense(lambda kc: xnT[:, kc, 0:n], 8, n, [g[1] for g in GRPS], lambda gi: gi % 2, [xnT_s], after)
            for j in range(8):
                op(PE, lambda j=j: nc.tensor.transpose(PBb[2][:, j, 0:n], qn_bf[0:n, j // 4, j % 4, :], identb[0:n, 0:n]),
                   rd=[qn_s, cst_s], wr=[PB_s[2]])
            q5 = qT[:, :, :, :].rearrange("p (a e) g t -> p a e g t", e=2)
            for e in range(2):
                op(ACT, lambda e=e: nc.scalar.copy(out=q5[e * 64:(e + 1) * 64, :, e, :, 0:n],
                                                    in_=PBb[2][e * 64:(e + 1) * 64, 0:8, 0:n].rearrange("p (a g) t -> p a g t", g=4)),
                   rd=[PB_s[2]], wr=[qT_s])
            transposes(lambda j: qi_bf[0:n, j * 128:(j + 1) * 128], 4, n, 3,
                       lambda b0, nb: qiT[:, b0:b0 + nb, 0:n], [qiT_s], [qibf_s])
            op(DVE, lambda: nc.vector.scalar_tensor_tensor(out=vn_bf[0:n, :], in0=vn[0:n, :], scalar=1.0, in1=vn[0:n, :], op0=ALU.mult,
                                                           op1=ALU.mult, accum_out=ss16[0:n, 0:1]), rd=[vn_s], wr=[vnbf_s, ss16_s])
            rstd_from_ss(ss16[0:n, 0:1], rs16[0:n, 0:1], n, 1, 1.0 / D, ss16_s, rs16_s)
            op(DVE, lambda: nc.vector.scalar_tensor_tensor(out=vn[0:n, :], in0=vn[0:n, :], scalar=rs16[0:n, 0:1], in1=gsgub[0:n, :],
                                                           op0=ALU.mult, op1=ALU.mult), rd=[rs16_s, cst_s], wr=[vn_s])
            dma(POOL, outs["sg"], vn[0:n, :], rd=[vn_s])
            if n == 128:
                op(DVE, lambda: nc.vector.tensor_copy(out=vn_bf[:, :], in_=vn[:, :]), rd=[vn_s], wr=[vnbf_s])
                for g in range(8):
                    bank = 4 + g // 4
                    op(PE, lambda g=g: nc.tensor.matmul(PB[bank][:, (g % 4) * 128:(g % 4 + 1) * 128], lhsT=wsT[:, g, :],
                                                        rhs=vn_bf[:, g * 128:(g + 1) * 128], start=True, stop=True, skip_group_check=True),
                       rd=[vnbf_s, cst_s], wr=[PB_s[bank]])
                for hb in range(2):
                    op(DVE, lambda hb=hb: nc.vector.tensor_tensor(
                        out=vn[:, hb * 512:(hb + 1) * 512].rearrange("p (g c) -> p g c", c=128),
                        in0=PB[4 + hb][:, :].rearrange("p (g c) -> p g c", c=128),
                        in1=bsT[:, hb * 4:hb * 4 + 4].unsqueeze(2).to_broadcast([128, 4, 128]), op=ALU.add),
                       rd=[PB_s[4 + hb], cst_s], wr=[vn_s])
            else:
                v3s = vn[0:n, :].rearrange("p (g c) -> p g c", c=128)
                op(DVE, lambda: nc.vector.tensor_tensor(out=v3s, in0=v3s, in1=coefs[0:n, 0:8].unsqueeze(2).to_broadcast([n, 8, 128]), op=ALU.mult),
                   rd=[cst_s], wr=[vn_s])
                op(DVE, lambda: nc.vector.tensor_tensor(out=v3s, in0=v3s, in1=coefs[0:n, 8:16].unsqueeze(2).to_broadcast([n, 8, 128]), op=ALU.add),
                   rd=[cst_s], wr=[vn_s])
            op(DVE, lambda: nc.vector.tensor_tensor(out=vn[0:n, :], in0=vn[0:n, :], in1=ug[0:n, :], op=ALU.mult), rd=[ug_s], wr=[vn_s])
            op(DVE, lambda: nc.vector.tensor_tensor(out=sgb[0:n, :], in0=sgb[0:n, :], in1=vn[0:n, :], op=ALU.mult), rd=[vn_s], wr=[sgb_s])

        def finish(n, x_ap, x_s, oatt, oatt_s, p_src, y_dst):
            sga, sga_s = FB[3], FB_s[3]
            msgu, msgu_s = FB[4], FB_s[4]
            fence([sc_s], arena_states)
            dma(SP, p32[0:n, :], p_src, wr=[p32_s])
            op(DVE, lambda: nc.vector.tensor_tensor(out=oatt[0:n, :], in0=oatt[0:n, :], in1=sga[0:n, :], op=ALU.mult), rd=[sga_s], wr=[oatt_s])
            op(DVE, lambda: nc.vector.tensor_tensor(out=xn_bf[0:n, :], in0=oatt[0:n, :], in1=msgu[0:n, :], op=ALU.add),
               rd=[oatt_s, msgu_s], wr=[xnbf_s])
            transposes(lambda j: xn_bf[0:n, j * 128:(j + 1) * 128], 8, n, 2, lambda b0, nb: xnT[:, b0:b0 + nb, 0:n], [xnT_s], [xnbf_s])

            def after_o(gi, bank):
                op(DVE, lambda: nc.vector.tensor_tensor(out=x_ap[:, gi * 512:(gi + 1) * 512], in0=x_ap[:, gi * 512:(gi + 1) * 512],
                                                        in1=PB[bank][0:n, :], op=ALU.add), rd=[PB_s[bank]], wr=[x_s])
            dense(lambda kc: xnT[:, kc, 0:n], 8, n, [512, 512], lambda gi: gi % 2, [xnT_s], after_o)
            rmsnorm_to_T(x_ap, x_s, gffnT, n)

            def after_up(gi, bank):
                op(ACT, lambda: nc.scalar.activation(out=rt[gi % 2][0:n, :], in_=PB[bank][0:n, :], func=AF.Relu), rd=[PB_s[bank]], wr=[rt_s[gi % 2]])
                op(DVE, lambda: nc.vector.tensor_tensor(out=h_bf[0:n, gi * 512:(gi + 1) * 512], in0=rt[gi % 2][0:n, :], in1=rt[gi % 2][0:n, :],
                                                        op=ALU.mult), rd=[rt_s[gi % 2]], wr=[hbf_s])
            dense(lambda kc: xnT[:, kc, 0:n], 8, n, [512] * 8, lambda gi: gi % 2, [xnT_s], after_up)
            transposes(lambda j: h_bf[0:n, j * 128:(j + 1) * 128], 32, n, 2, lambda b0, nb: hT[:, b0:b0 + nb, 0:n], [hT_s], [hbf_s])
            for nn in range(2):
                bank = nn % 2
                for kg in range(4):
                    sl, k = w_get()
                    for kc in range(8):
                        op(PE, lambda kc=kc: nc.tensor.matmul(PB[bank][0:n, :], lhsT=hT[:, kg * 8 + kc, 0:n], rhs=WS[sl][:, kc, :],
                                                              start=(kg == 0 and kc == 0), stop=(kg == 3 and kc == 7)),
                           rd=[hT_s, WS_s[sl]], wr=[PB_s[bank]])
                    w_done(k)
                op(DVE, lambda: nc.vector.tensor_tensor(out=x_ap[:, nn * 512:(nn + 1) * 512], in0=x_ap[:, nn * 512:(nn + 1) * 512],
                                                        in1=PB[bank][0:n, :], op=ALU.add), rd=[PB_s[bank]], wr=[x_s])
            rmsnorm_to_T(x_ap, x_s, gpleT, n)
            gate, gate_s = FB[1], FB_s[1]

            def after_pg(gi, bank):
                op(ACT, lambda: nc.scalar.activation(out=gate[0:n, gi * 512:(gi + 1) * 512], in_=PB[bank][0:n, :], func=AF.Sigmoid),
                   rd=[PB_s[bank]], wr=[gate_s])
            dense(lambda kc: xnT[:, kc, 0:n], 8, n, [512, 512], lambda gi: gi % 2, [xnT_s], after_pg)
            op(DVE, lambda: nc.vector.tensor_copy(out=p_bf[0:n, :], in_=p32[0:n, :]), rd=[p32_s], wr=[pbf_s])
            transposes(lambda j: p_bf[0:n, j * 128:(j + 1) * 128], 2, n, 3, lambda b0, nb: pT[:, b0:b0 + nb, 0:n], [pT_s], [pbf_s])

            def after_p(gi, bank):
                op(DVE, lambda: nc.vector.tensor_tensor(out=gate[0:n, gi * 512:(gi + 1) * 512], in0=gate[0:n, gi * 512:(gi + 1) * 512],
                                                        in1=PB[bank][0:n, :], op=ALU.mult), rd=[PB_s[bank]], wr=[gate_s])
                op(DVE, lambda: nc.vector.tensor_tensor(out=x_ap[:, gi * 512:(gi + 1) * 512], in0=x_ap[:, gi * 512:(gi + 1) * 512],
                                                        in1=gate[0:n, gi * 512:(gi + 1) * 512], op=ALU.add), rd=[gate_s], wr=[x_s])
            dense(lambda kc: pT[:, kc, 0:n], 2, n, [512, 512], lambda gi: gi % 2, [pT_s], after_p)
            dma(POOL, y_dst, x_ap, rd=[x_s])

        n = 128
        fence(arena_states + [sc_s, sc16_s, sct_s], [wkvk_s])
        dma(SP, wkvk[:, :, 0:512], win_s[2][:, :, :], wr=[wkvk_s])
        dma(SP, wkvk[:, :, 512:576], win_s[4][:, :, 0:64], wr=[wkvk_s])
        xnT2 = al("xnT2", [128, 8, 128], BF16, 0); xnT2_s = St()
        fence(arena_states + [sc_s, sc16_s, sct_s], [xnT2_s])
        xnTs = [(xnT, xnT_s), (xnT2, xnT2_s)]
        ssA = sb("ssA", [128, 2], F32); ssA_s = St(); rsA_s = St()

        def stageA(kb):
            xt, xt_s = FB[kb % 2], FB_s[kb % 2]
            xT, xT_s = xnTs[kb % 2]
            dma(SP, xt[:, :], xb_d[kb * 128:(kb + 1) * 128, :], wr=[xt_s])
            op(DVE, lambda: nc.vector.scalar_tensor_tensor(out=xn_bf[:, :], in0=xt[:, :], scalar=1.0, in1=xt[:, :], op0=ALU.mult, op1=ALU.mult,
                                                           accum_out=ssA[:, 0:1]), rd=[xt_s], wr=[xnbf_s, ssA_s])
            rstd_from_ss(ssA[:, 0:1], ssA[:, 1:2], 128, 1, 1.0 / D, ssA_s, rsA_s)
            op(DVE, lambda: nc.vector.tensor_scalar(out=xn_bf[:, :], in0=xt[:, :], scalar1=ssA[:, 1:2], scalar2=None, op0=ALU.mult),
               rd=[xt_s, rsA_s], wr=[xnbf_s])
            transposes(lambda j: xn_bf[:, j * 128:(j + 1) * 128], 8, 128, 0,
                       lambda b0, nb: xT[:, b0:b0 + nb, :], [xT_s], [xnbf_s], gainT=gmixT)

        def stageB(kb):
            xT, xT_s = xnTs[kb % 2]
            for kc in range(8):
                op(PE, lambda kc=kc: nc.tensor.matmul(PB[0][:, :], lhsT=xT[:, kc, :], rhs=wkvk[:, kc, 0:512], start=(kc == 0), stop=(kc == 7)),
                   rd=[xT_s, wkvk_s], wr=[PB_s[0]])
            for kc in range(8):
                op(PE, lambda kc=kc: nc.tensor.matmul(PB[1][:, 0:64], lhsT=xT[:, kc, :], rhs=wkvk[:, kc, 512:576], start=(kc == 0), stop=(kc == 7)),
                   rd=[xT_s, wkvk_s], wr=[PB_s[1]])
            op(ACT, lambda: nc.scalar.copy(out=k32[:, :], in_=PB[0][:, :]), rd=[PB_s[0]], wr=[k32_s])
            head_norm(k32[:, 0:256], n, 4, gkb, lambda: kn_bf[:, :].rearrange("p (h d) -> p h d", d=64), [knbf_s])
            op(ACT, lambda: nc.scalar.copy(out=v_bf[:, :], in_=k32[:, 256:512]), rd=[k32_s], wr=[vbf_s])
            for a in range(2):
                op(ACT, lambda a=a: nc.scalar.copy(out=kid_bf[:, a, :], in_=PB[1][:, 0:64]), rd=[PB_s[1]], wr=[kid_s])
            kv_append(kb, n, kn_bf, knbf_s, v_bf, vbf_s, kid_bf, kid_s)

        RS["act"] = True
        stageA(0)
        for kb in range(NBLK):
            if kb + 1 < NBLK:
                stageA(kb + 1)
            stageB(kb)
            for _ in range(4):
                if conv_jobs:
                    conv_jobs.pop(0)()
        fence([xnT2_s], arena_states)
        RS["act"] = False
        while conv_jobs:
            conv_jobs.pop(0)()
        for i in range(NDS):
            if dcnt[i] > 0:
                SP.wait((dsems[i], dcnt[i]))

        for i in range(NSLOT):
            m, second = i // 2, i % 2
            nch = 8 * m + (8 if second else 4)
            xt, xt_s = FB[0], FB_s[0]
            dma(SP, xt[:, :], xo_d[i, :, :], wr=[xt_s])
            project(n, xt[:, :], xt_s, {"k": ko_d[i, :, :], "v": vo_d[i, :, :], "ki": kio_d[i, :, :], "sg": sgo_d[i, :, :]})
            fence(arena_states, [sc_s])
            indexer(n, nch, scores, sc_s)
            S = nch * 128
            op(DVE, lambda: nc.vector.tensor_tensor(out=scores[:, S - 512:S], in0=scores[:, S - 512:S], in1=pen[:, second, :], op=ALU.add),
               rd=[cst_s], wr=[sc_s])
            fence([FB_s[1], FB_s[2]], [junk_s, junka_s])
            bisect(n, S, scores[:, 0:S], sc_s, junk8[:, 0:S], junk_s, junk_act=junki8)
            fence([junk_s, junka_s], [FB_s[1], FB_s[2]])
            oat, oat_s = FB[1], FB_s[1]
            attend(n, nch, scores, sc_s, I4[:, :, :], oat, oat_s)
            finish(n, xt[:, :], xt_s, oat, oat_s, po_d[i, :, :], yo_d[i, :, :])

        n = NS
        xs_t, xs_s = FB[0], FB_s[0]
        dma(SP, xs_t[0:n, :], xs_d[:, :], wr=[xs_s])
        project(n, xs_t[0:n, :], xs_s, {"k": kss_d[:, :], "v": vss_d[:, :], "ki": kis_d[:, :], "sg": sgs_d[:, :], "kv": True})
        for j in range(2):
            op(PE, lambda j=j: nc.tensor.transpose(PBb[3][:, j, 0:n], kn_bf[0:n, j * 128:(j + 1) * 128], identb[0:n, 0:n]),
               rd=[knbf_s, cst_s], wr=[PB_s[3]])
        op(PE, lambda: nc.tensor.transpose(PBb[3][:, 2, 0:n], kid_bf[0:n, :, :].rearrange("p a d -> p (a d)"), identb[0:n, 0:n]),
           rd=[kid_s, cst_s], wr=[PB_s[3]])
        op(ACT, lambda: nc.scalar.copy(out=ksT[:, :, :], in_=PBb[3][:, 0:2, 0:n]), rd=[PB_s[3]], wr=[smp_s])
        op(ACT, lambda: nc.scalar.copy(out=kisT[:, :], in_=PBb[3][:, 2, 0:n]), rd=[PB_s[3]], wr=[smp_s])

        def gather(dst_ap, src2d, col):
            return lambda: nc.gpsimd.indirect_dma_start(out=dst_ap, out_offset=None, in_=src2d,
                                                        in_offset=bass.IndirectOffsetOnAxis(ap=idxa[:, col:col + 1], axis=0))
        fence(arena_states, [sc16_s] + sctR_s)
        kiR_s = [St(), St()]; KR_s = [St(), St()]; VR_s = [St(), St()]
        fence([kiT_s], kiR_s); fence([KT_s], KR_s); fence([V_s], VR_s)
        op(POOL, lambda: nc.gpsimd.memset(sc16[:, :], -1e30), wr=[sc16_s])
        for r in range(2):
            op(POOL, lambda r=r: nc.gpsimd.memset(kiT[:, r * S_S + 2048:(r + 1) * S_S], 0.0), wr=[kiR_s[r]])
            op(POOL, lambda r=r: nc.gpsimd.memset(KT[:, :, r * S_S + 2048:(r + 1) * S_S], 0.0), wr=[KR_s[r]])
            op(POOL, lambda r=r: nc.gpsimd.memset(Vt[:, r * 17 + 16, :, 0:64], 0.0), wr=[VR_s[r]])
        def s1_loads(b):
            r = b % 2
            k0 = r * S_S
            jobsA, jobsB = [], []
            for pg in range(NPG):
                def jobA(pg=pg):
                    sl = (b * NPG + pg) % NPB
                    col = b * NPG + pg
                    dma(POOL, None, None, rd=[cst_s], wr=[pgI_s[sl]], fn=gather(pgI[sl][:, 0, :], cki_d, col))

                def jobB(pg=pg):
                    sl = (b * NPG + pg) % NPB
                    bk = 3
                    for a in range(2):
                        op(PE, lambda a=a: nc.tensor.transpose(PBb[bk][a * 64:(a + 1) * 64, 0, :], pgI[sl][:, 0, :], identb[:]),
                           rd=[pgI_s[sl], cst_s], wr=[PB_s[bk]])
                    op(ACT, lambda: nc.scalar.copy(out=kiT[:, k0 + pg * 128:k0 + (pg + 1) * 128], in_=PBb[bk][:, 0, :]),
                       rd=[PB_s[bk]], wr=[kiR_s[r]])
                jobsA.append(jobA)
                jobsB.append(jobB)
            jobs = skew(jobsA, jobsB)
            jobs.append(lambda: op(ACT, lambda: nc.scalar.copy(out=kiT[:, k0 + 2048:k0 + 2049], in_=kisT[:, b:b + 1]), rd=[smp_s], wr=[kiR_s[r]]))
            return jobs

        def skew(jobsA, jobsB, ahead=3):
            out = list(jobsA[:ahead])
            for p in range(len(jobsB)):
                out.append(jobsB[p])
                if p + ahead < len(jobsA):
                    out.append(jobsA[p + ahead])
            return out

        def make_hook(jobs, every):
            st = {"i": 0}

            def hook():
                st["i"] += 1
                if st["i"] % every == 0 and jobs:
                    jobs.pop(0)()
            return hook

        pending = s1_loads(0)
        for b in range(NS):
            r = b % 2
            k0 = r * S_S
            while pending:
                pending.pop(0)()
            pending = s1_loads(b + 1) if b + 1 < NS else []
            indexer(n, 17, sctR[r], sctR_s[r], k0=k0, ki_state=kiR_s[r], hook=make_hook(pending, 1))
            op(DVE, lambda b=b: nc.vector.scalar_tensor_tensor(out=sc16[:, 0:2049], in0=sctR[r][:, 0:2049], scalar=identf[0:16, b:b + 1],
                                                               in1=sc16[:, 0:2049], op0=ALU.mult, op1=ALU.add) if b > 0 else
               nc.vector.tensor_scalar(out=sc16[:, 0:2049], in0=sctR[r][:, 0:2049], scalar1=identf[0:16, 0:1], scalar2=None, op0=ALU.mult),
               rd=[sctR_s[r], cst_s], wr=[sc16_s])
        fence([sctR_s[1]], [sct_s])
        bisect(n, 2049, sc16[:, 0:2049], sc16_s, sct[:, 0:2049], sct_s)
        oat, oat_s = FB[1], FB_s[1]
        oatt_s16, oas_s = FB[2], FB_s[2]
        def s2_loads(b):
            r = b % 2
            k0 = r * S_S
            c0 = r * 17
            jobsA, jobsB = [], []
            for pg in range(NPG):
                def jobA(pg=pg):
                    sl = (b * NPG + pg) % NPB
                    col = b * NPG + pg
                    dma(POOL, None, None, rd=[cst_s], wr=[pgK_s[sl]], fn=gather(pgK[sl][:, :], ck_d, col))
                    dma(POOL, None, None, rd=[cst_s], wr=[pgV_s[sl]], fn=gather(pgV[sl][:, :], cv_d, col))
                jobsA.append(jobA)

                def job(pg=pg):
                    sl = (b * NPG + pg) % NPB
                    bk = 0
                    for j in range(2):
                        op(PE, lambda j=j: nc.tensor.transpose(PBb[bk][:, j, :], pgK[sl][:, j * 128:(j + 1) * 128], identb[:]),
                           rd=[pgK_s[sl], cst_s], wr=[PB_s[bk]])
                    op(ACT, lambda: nc.scalar.copy(out=KT[:, :, k0 + pg * 128:k0 + (pg + 1) * 128], in_=PBb[bk][:, 0:2, :]),
                       rd=[PB_s[bk]], wr=[KR_s[r]])
                    op(DVE, lambda: nc.vector.tensor_copy(out=Vt[:, c0 + pg, :, 0:64], in_=pgV[sl][:, :].rearrange("p (c d) -> p c d", d=64)),
                       rd=[pgV_s[sl]], wr=[VR_s[r]])
                jobsB.append(job)
            jobs = skew(jobsA, jobsB)

            def last():
                op(ACT, lambda: nc.scalar.copy(out=KT[:, :, k0 + 2048:k0 + 2049], in_=ksT[:, :, b:b + 1]), rd=[smp_s], wr=[KR_s[r]])
                op(PE, lambda: nc.tensor.matmul(PB[0][0:1, 0:256], lhsT=identb[0:16, b:b + 1], rhs=vs_bf[0:16, :], start=True, stop=True),
                   rd=[smp_s, cst_s], wr=[PB_s[0]])
                op(ACT, lambda: nc.scalar.copy(out=Vt[0:1, c0 + 16, :, 0:64], in_=PB[0][0:1, 0:256].rearrange("p (c d) -> p c d", d=64)),
                   rd=[PB_s[0]], wr=[VR_s[r]])
            jobs.append(last)
            return jobs

        pending = s2_loads(0)
        for b in range(NS):
            r = b % 2
            c0 = r * 17
            while pending:
                pending.pop(0)()
            pending = s2_loads(b + 1) if b + 1 < NS else []
            attend_sample(17, sc16, sc16_s, oat, oat_s, c0, KR_s[r], VR_s[r], hook=make_hook(pending, 1))
            if b == 0:
                op(DVE, lambda: nc.vector.tensor_scalar(out=oatt_s16[0:16, :], in0=oat[0:16, :], scalar1=identf[0:16, 0:1], scalar2=None,
                                                        op0=ALU.mult), rd=[oat_s, cst_s], wr=[oas_s])
            else:
                op(DVE, lambda b=b: nc.vector.scalar_tensor_tensor(out=oatt_s16[0:16, :], in0=oat[0:16, :], scalar=identf[0:16, b:b + 1],
                                                                   in1=oatt_s16[0:16, :], op0=ALU.mult, op1=ALU.add),
                   rd=[oat_s, cst_s], wr=[oas_s])
        fence(sctR_s, [sct_s])
        fence([sc16_s, sct_s], arena_states)
        finish(n, xs_t[0:n, :], xs_s, oatt_s16, oas_s, ps_d[:, :], ys_d[:, :])

        for i in range(NDS):
            if dcnt[i] > 0:
                POOL.wait((dsems[i], dcnt[i]))
    return nc


_NC_CACHE = {}


def _slot_block(r, i):
    m, second = i // 2, i % 2
    return 8 * m + (7 - r if second else r)


def kernel(x_prompt, x_sample, cache_k, cache_v, cache_kidx, page_table, p_prompt, p_sample,
           g_mix, w_in, g_q, g_k, g_sgu, w_s, b_s, w_o, g_ffn, w_up, w_down, g_ple, w_pg, w_p):
    f32 = np.float32
    A = lambda a: np.ascontiguousarray(np.asarray(a))
    x_prompt = A(x_prompt); x_sample = A(x_sample); p_prompt = A(p_prompt); p_sample = A(p_sample)
    ck = A(cache_k).reshape(N_PHYS * 128, 256)
    cv = A(cache_v).reshape(N_PHYS * 128, 256)
    cki = A(cache_kidx).reshape(N_PHYS * 128, 64)
    pt_all = A(page_table).astype(np.int32)
    shared = {
        "ck": ck, "cv": cv, "cki": cki,
        "w_in": A(w_in)[0], "w_o": A(w_o)[0], "w_up": A(w_up)[0], "w_down": A(w_down)[0], "w_pg": A(w_pg)[0], "w_p": A(w_p)[0],
        "g_mix": A(g_mix)[0], "g_q": A(g_q)[0], "g_k": A(g_k)[0], "g_sgu": A(g_sgu)[0], "w_s": A(w_s)[0], "b_s": A(b_s)[0],
        "g_ffn": A(g_ffn)[0], "g_ple": A(g_ple)[0],
    }
    tt = np.arange(128)[:, None]
    ss = np.arange(512)[None, :]
    in_maps = []
    for c in range(8):
        bi, r = c // 4, c % 4
        blocks = [_slot_block(r, i) for i in range(NSLOT)]
        pen = np.zeros((128, 2, 512), f32)
        pen[:, 0, :] = np.where(ss <= r * 128 + tt, 0.0, -1e30)
        pen[:, 1, :] = np.where(ss <= (3 - r) * 128 + tt, 0.0, -1e30)
        m = dict(shared)
        m["xb"] = x_prompt[bi]
        m["xo"] = np.stack([x_prompt[bi, j * 128:(j + 1) * 128] for j in blocks])
        m["po"] = np.stack([p_prompt[0, bi, j * 128:(j + 1) * 128] for j in blocks])
        m["xs"] = x_sample[c * NS:(c + 1) * NS, 0]
        m["ps"] = p_sample[0, c * NS:(c + 1) * NS, 0]
        m["pt"] = np.ascontiguousarray(pt_all[c * NS:(c + 1) * NS].reshape(-1))
        m["pen"] = pen
        in_maps.append(m)
    if "nc" not in _NC_CACHE:
        _NC_CACHE["nc"] = build()
    res = run_bass_kernel_spmd(_NC_CACHE["nc"], in_maps, core_ids=list(range(8)))
    R = res.results
    y_p = np.zeros((2, SEQ, D), f32); y_s = np.zeros((128, 1, D), f32)
    nk = np.zeros((1, 2, SEQ, 4, 64), f32); nv = np.zeros((1, 2, SEQ, 4, 64), f32)
    nki = np.zeros((1, 2, SEQ, 64), f32); nsg = np.zeros((1, 2, SEQ, D), f32)
    sk = np.zeros((1, 128, 1, 4, 64), f32); sv = np.zeros((1, 128, 1, 4, 64), f32)
    ski = np.zeros((1, 128, 1, 64), f32); ssg = np.zeros((1, 128, 1, D), f32)
    for c in range(8):
        bi, r = c // 4, c % 4
        o = R[c]
        for i in range(NSLOT):
            j = _slot_block(r, i)
            sl = slice(j * 128, (j + 1) * 128)
            y_p[bi, sl] = o["yo"][i]
            nk[0, bi, sl] = o["ko"][i].reshape(128, 4, 64)
            nv[0, bi, sl] = o["vo"][i].reshape(128, 4, 64)
            nki[0, bi, sl] = o["kio"][i]
            nsg[0, bi, sl] = o["sgo"][i]
        s2 = slice(c * NS, (c + 1) * NS)
        y_s[s2, 0] = o["ys"]
        sk[0, s2, 0] = o["kss"].reshape(NS, 4, 64)
        sv[0, s2, 0] = o["vss"].reshape(NS, 4, 64)
        ski[0, s2, 0] = o["kis"]
        ssg[0, s2, 0] = o["sgs"]
    return (y_p, y_s, nk, nv, nki, nsg, sk, sv, ski, ssg)
```

```python
import contextlib
import numpy as np
import concourse.bass as bass
import concourse.mybir as mybir
from concourse.bass_utils import run_bass_kernel_spmd

F32 = mybir.dt.float32
BF16 = mybir.dt.bfloat16
I32 = mybir.dt.int32
ALU = mybir.AluOpType
AF = mybir.ActivationFunctionType
AX = mybir.AxisListType

D = 1024
SEQ = 8192
NBLK = 64
NSLOT = 16
NS = 16
NPG = 16
S_S = 2176
N_PHYS = 2560
INW = 6216
DFF = 4096
PLE = 256
EPS = 1e-6
ATTN_SCALE = 64 ** -0.5
IDXS = (64 ** -0.5) * (8 ** -0.5)
NIT = 20
BRK = 16.0
NEGM = -30000.0
GRPS = [(0, 512), (512, 512), (1024, 512), (1536, 512), (2048, 72), (2120, 512), (2632, 512),
        (3144, 512), (3656, 512), (4168, 512), (4680, 512), (5192, 512), (5704, 512)]


class St:
    __slots__ = ("w", "r")

    def __init__(self):
        self.w = None
        self.r = {}


class Eng:
    def __init__(self, nc, es, name, h):
        self.h = h
        self.name = name
        self.sem = es.enter_context(nc.semaphore("e_" + name))
        self.cnt = 0
        self.seen = {}

    def wait(self, tok):
        sem, val = tok
        if self.seen.get(sem.num, 0) < val:
            self.h.wait_ge(sem, val)
            self.seen[sem.num] = val


def build():
    nc = bass.Bass("TRN2", target_bir_lowering=False)
    dt_in = lambda n, s, d=F32: nc.dram_tensor(n, s, d, kind="ExternalInput").ap()
    dt_out = lambda n, s, d=F32: nc.dram_tensor(n, s, d, kind="ExternalOutput").ap()
    xb_d = dt_in("xb", [SEQ, D])
    xo_d = dt_in("xo", [NSLOT, 128, D])
    po_d = dt_in("po", [NSLOT, 128, PLE])
    xs_d = dt_in("xs", [NS, D])
    ps_d = dt_in("ps", [NS, PLE])
    ck_d = dt_in("ck", [N_PHYS * 128, 256])
    cv_d = dt_in("cv", [N_PHYS * 128, 256])
    cki_d = dt_in("cki", [N_PHYS * 128, 64])
    pt_d = dt_in("pt", [NS * NPG], I32)
    pen_d = dt_in("pen", [128, 2, 512])
    win_d = dt_in("w_in", [D, INW])
    wo_d = dt_in("w_o", [D, D])
    wup_d = dt_in("w_up", [D, DFF])
    wdn_d = dt_in("w_down", [DFF, D])
    wpg_d = dt_in("w_pg", [D, D])
    wp_d = dt_in("w_p", [PLE, D])
    gmix_d = dt_in("g_mix", [D])
    gq_d = dt_in("g_q", [64])
    gk_d = dt_in("g_k", [64])
    gsgu_d = dt_in("g_sgu", [D])
    ws_d = dt_in("w_s", [8, 128, 128])
    bs_d = dt_in("b_s", [8, 128])
    gffn_d = dt_in("g_ffn", [D])
    gple_d = dt_in("g_ple", [D])

    yo_d = dt_out("yo", [NSLOT, 128, D])
    ko_d = dt_out("ko", [NSLOT, 128, 256])
    vo_d = dt_out("vo", [NSLOT, 128, 256])
    kio_d = dt_out("kio", [NSLOT, 128, 64])
    sgo_d = dt_out("sgo", [NSLOT, 128, D])
    ys_d = dt_out("ys", [NS, D])
    kss_d = dt_out("kss", [NS, 256])
    vss_d = dt_out("vss", [NS, 256])
    kis_d = dt_out("kis", [NS, 64])
    sgs_d = dt_out("sgs", [NS, D])

    def scr(n, shape):
        return nc.dram_tensor(n, shape, BF16, kind="Internal").ap()
    win_s = [scr("win_s%d" % i, [128, 8, w]) for i, (o, w) in enumerate(GRPS)]
    wo_s = [scr("wo_s%d" % i, [128, 8, 512]) for i in range(2)]
    wup_s = [scr("wup_s%d" % i, [128, 8, 512]) for i in range(8)]
    wdn_s = [scr("wdn_s%d" % i, [128, 8, 512]) for i in range(8)]
    wpg_s = [scr("wpg_s%d" % i, [128, 8, 512]) for i in range(2)]
    wp_s = [scr("wp_s%d" % i, [128, 2, 512]) for i in range(2)]

    es = contextlib.ExitStack()
    with es:
        PE = Eng(nc, es, "pe", nc.tensor)
        ACT = Eng(nc, es, "act", nc.scalar)
        DVE = Eng(nc, es, "dve", nc.vector)
        POOL = Eng(nc, es, "pool", nc.gpsimd)
        SP = Eng(nc, es, "sp", nc.sync)
        NDS = 24
        dsems = [es.enter_context(nc.semaphore("d%d" % i)) for i in range(NDS)]
        dcnt = [0] * NDS
        dstate = {"i": 0}

        def deps_of(rd, wr):
            deps = []
            for s in rd:
                if s.w is not None:
                    deps.append(s.w)
            for s in wr:
                if s.w is not None:
                    deps.append(s.w)
                deps.extend(s.r.values())
            return deps

        def op(E, fn, rd=(), wr=()):
            for tok in deps_of(rd, wr):
                if E is PE and tok[0] is PE.sem:
                    continue
                E.wait(tok)
            ins = fn()
            E.cnt += 1
            ins.then_inc(E.sem, 1)
            tok = (E.sem, E.cnt)
            for s in rd:
                s.r[E.name] = tok
            for s in wr:
                s.w = tok
                s.r = {}
            return tok

        def dma(Q, out, in_, rd=(), wr=(), fn=None):
            for tok in deps_of(rd, wr):
                Q.wait(tok)
            i = dstate["i"]
            dstate["i"] = (i + 1) % NDS
            if dcnt[i] > 0:
                Q.wait((dsems[i], dcnt[i]))
            if fn is None:
                ins = Q.h.dma_start(out=out, in_=in_)
            else:
                ins = fn()
            dcnt[i] += 16
            ins.then_inc(dsems[i], 16)
            tok = (dsems[i], dcnt[i])
            for s in rd:
                s.r["dma%d" % i] = tok
            for s in wr:
                s.w = tok
                s.r = {}
            return tok

        def fence(frm, to):
            for t in to:
                for f in frm:
                    if f is t:
                        continue
                    if f.w is not None:
                        t.r["f%d" % id(f)] = f.w
                    for k, v in list(f.r.items()):
                        t.r["f%d%s" % (id(f), k)] = v

        def sb(name, shape, dt):
            return nc.alloc_sbuf_tensor("sb_" + name, shape, dt)

        KT = sb("KT", [128, 2, SEQ], BF16); KT_s = St()
        Vt = sb("Vt", [128, NBLK, 4, 65], BF16); V_s = St()
        kiT = sb("kiT", [128, SEQ], BF16); kiT_s = St()
        arena_base = nc.sbuf_base
        scores = sb("scores", [128, SEQ], F32); sc_s = St()
        al = lambda n, shape, dt, off: nc.alloc_sbuf_tensor_at("al_" + n, shape, dt, offset=arena_base + off)
        h_bf = al("h_bf", [128, DFF], BF16, 0); hbf_s = St()
        hT = al("hT", [128, 32, 128], BF16, 8192); hT_s = St()
        xn_bf = al("xn_bf", [128, D], BF16, 16384); xnbf_s = St()
        xnT = al("xnT", [128, 8, 128], BF16, 18432); xnT_s = St()
        vn_bf = al("vn_bf", [128, D], BF16, 20480); vnbf_s = St()
        wkvk = al("wkvk", [128, 8, 576], BF16, 22528); wkvk_s = St()
        arena_states = [hbf_s, hT_s, xnbf_s, xnT_s, vnbf_s, wkvk_s]
        sc16 = scores[0:16, 0:S_S]; sc16_s = St()
        sct = scores[0:16, S_S:2 * S_S]; sct_s = St()
        sctR = [sct, scores[0:16, 2 * S_S:3 * S_S]]; sctR_s = [sct_s, St()]
        WS = [sb("ws%d" % i, [128, 8, 512], BF16) for i in range(3)]
        WS_s = [St() for _ in range(3)]
        FB0 = sb("fb0", [128, D], F32)
        FB12 = sb("fb12", [128, 2 * D], F32)
        FB3 = sb("fb3", [128, D], F32)
        FB4 = sb("fb4", [128, D], F32)
        FB = [FB0[:, :], FB12[:, 0:D], FB12[:, D:2 * D], FB3[:, :], FB4[:, :]]
        FB_s = [St() for _ in range(5)]
        junk8 = FB12[:, :].bitcast(mybir.dt.uint8)
        junki8 = FB12[:, :].bitcast(mybir.dt.int8)
        gmixT = sb("gmixT", [128, 8], F32); gffnT = sb("gffnT", [128, 8], F32); gpleT = sb("gpleT", [128, 8], F32)
        gsgub = sb("gsgub", [128, D], F32)
        coefs = sb("coefs", [16, 16], F32)
        gqb = sb("gqb", [128, 64], F32); gkb = sb("gkb", [128, 64], F32)
        cst_s = St()
        qn_bf = sb("qn_bf", [128, 2, 4, 128], BF16); qn_s = St()
        qT = sb("qT", [128, 4, 4, 128], BF16); qT_s = St()
        qi_bf = sb("qi_bf", [128, 512], BF16); qibf_s = St()
        qiT = sb("qiT", [128, 4, 128], BF16); qiT_s = St()
        absw = sb("absw", [128, 8], F32); sgnw = sb("sgnw", [128, 8], F32); w_s_ = St()
        kn_bf = sb("kn_bf", [128, 256], BF16); knbf_s = St()
        v_bf = sb("v_bf", [128, 256], BF16); vbf_s = St()
        kid_bf = sb("kid_bf", [128, 2, 64], BF16); kid_s = St()
        k32 = sb("k32", [128, 512], F32); k32_s = St()
        o32 = sb("o32", [128, 512], F32); o32_s = St()
        ki32 = sb("ki32", [128, 72], F32); ki32_s = St()
        rt = [sb("rt%d" % i, [128, 512], F32) for i in range(3)]; rt_s = [St(), St(), St()]
        PT = [sb("PT%d" % i, [128, 512], BF16) for i in range(3)]; PT_s = [St(), St(), St()]
        mT = [sb("mT%d" % i, [128, 4, 128], BF16) for i in range(2)]; mT_s = [St(), St()]
        MBg = [sb("MBg%d" % i, [128, 512], BF16) for i in range(2)]; MB_s = [St(), St()]
        identb = sb("identb", [128, 128], BF16); identf = sb("identf", [16, 16], F32)
        I4 = sb("I4", [128, 4, 128], BF16); I416 = sb("I416", [16, 4, 16], BF16)
        wsT = sb("wsT", [128, 8, 128], BF16); bsT = sb("bsT", [128, 8], F32)
        pen = sb("pen", [128, 2, 512], F32)
        p32 = sb("p32", [128, PLE], F32); p32_s = St()
        p_bf = sb("p_bf", [128, PLE], BF16); pbf_s = St()
        pT = sb("pT", [128, 2, 128], BF16); pT_s = St()
        sm = sb("sm", [128, 64], F32); sm_s = St()
        lo = sb("lo", [128, 1], F32); mid = sb("mid", [128, 1], F32); cntt = sb("cntt", [128, 1], F32)
        geq = sb("geq", [128, 1], F32); bis_s = St(); junk_s = St()
        cnta = sb("cnta", [128, 1], F32); cnta_s = St(); mid_s = St(); junka_s = St()
        mhalf = sb("mhalf", [128, 16], F32)
        ss16 = sb("ss16", [128, 16], F32); ss16_s = St()
        rs16 = sb("rs16", [128, 16], F32); rs16_s = St()
        ptb = sb("ptb", [128, NS * NPG], I32); idxa = sb("idxa", [128, NS * NPG], I32); iop = sb("iop", [128, 1], I32)
        NPB = 4
        pgK = [sb("pgK%d" % i, [128, 256], BF16) for i in range(NPB)]; pgK_s = [St() for _ in range(NPB)]
        pgV = [sb("pgV%d" % i, [128, 256], BF16) for i in range(NPB)]; pgV_s = [St() for _ in range(NPB)]
        pgI = [sb("pgI%d" % i, [128, 2, 64], BF16) for i in range(NPB)]; pgI_s = [St() for _ in range(NPB)]
        ksT = sb("ksT", [128, 2, 16], BF16); kisT = sb("kisT", [128, 16], BF16); vs_bf = sb("vs_bf", [16, 256], BF16)
        smp_s = St()
        ws32 = sb("ws32", [128, 128], F32); ws32_s = St()
        wsb = sb("wsb", [128, 128], BF16); wsb_s = St()

        PB = [nc.alloc_psum_tensor("pb%d" % i, [128, 512], F32) for i in range(8)]
        PB_s = [St() for _ in range(8)]
        PBb = [PB[i][:].bitcast(BF16).rearrange("p (a b) -> p a b", a=8) for i in range(8)]

        def transposes(src_ap_fn, nblk, n, bank0, dst_fn, dst_states, src_states, evac=None, gainT=None):
            for b0 in range(0, nblk, 8):
                nb = min(8, nblk - b0)
                bank = 2 + (bank0 + b0 // 8) % 2
                if gainT is not None:
                    for j in range(nb):
                        op(PE, lambda j=j: nc.tensor.transpose(PBb[bank][:, j, 0:n], src_ap_fn(b0 + j), identb[0:n, 0:n]),
                           rd=list(src_states) + [cst_s], wr=[PB_s[bank]])
                    op(DVE, lambda: nc.vector.tensor_tensor(out=dst_fn(b0, nb), in0=PBb[bank][:, 0:nb, 0:n],
                                                            in1=gainT[:, b0:b0 + nb].unsqueeze(2).to_broadcast([128, nb, n]), op=ALU.mult),
                       rd=[PB_s[bank], cst_s], wr=dst_states)
                    continue
                for j in range(nb):
                    tok = op(PE, lambda j=j: nc.tensor.transpose(PBb[bank][:, j, 0:n], src_ap_fn(b0 + j), identb[0:n, 0:n]),
                             rd=list(src_states) + [cst_s], wr=[PB_s[bank]])
                E = evac or ACT
                if E is ACT:
                    op(ACT, lambda: nc.scalar.copy(out=dst_fn(b0, nb), in_=PBb[bank][:, 0:nb, 0:n]),
                       rd=[PB_s[bank]], wr=dst_states)
                else:
                    op(DVE, lambda: nc.vector.tensor_copy(out=dst_fn(b0, nb), in_=PBb[bank][:, 0:nb, 0:n]),
                       rd=[PB_s[bank]], wr=dst_states)

        RS = {"act": False}

        def rstd_from_ss(ss_ap, out_ap, n, ncol, inv_d, rd_s, wr_s):
            op(DVE, lambda: nc.vector.tensor_scalar(out=ss_ap, in0=ss_ap, scalar1=inv_d, scalar2=EPS, op0=ALU.mult, op1=ALU.add),
               rd=[], wr=[rd_s])
            if RS["act"]:
                op(ACT, lambda: nc.scalar.activation(out=out_ap, in_=ss_ap, func=AF.Sqrt), rd=[rd_s], wr=[wr_s])
                op(DVE, lambda: nc.vector.reciprocal(out=out_ap, in_=out_ap), rd=[], wr=[wr_s])
            else:
                op(POOL, lambda: nc.gpsimd.tensor_tensor(out=out_ap, in0=ss_ap, in1=mhalf[0:n, 0:ncol], op=ALU.pow),
                   rd=[rd_s, cst_s], wr=[wr_s])

        def rmsnorm_to_T(x_ap, x_s, gain, n):
            op(DVE, lambda: nc.vector.scalar_tensor_tensor(out=xn_bf[0:n, :], in0=x_ap, scalar=1.0, in1=x_ap, op0=ALU.mult, op1=ALU.mult,
                                                           accum_out=ss16[0:n, 0:1]),
               rd=[x_s], wr=[xnbf_s, ss16_s])
            rstd_from_ss(ss16[0:n, 0:1], rs16[0:n, 0:1], n, 1, 1.0 / D, ss16_s, rs16_s)
            op(DVE, lambda: nc.vector.tensor_scalar(out=xn_bf[0:n, :], in0=x_ap, scalar1=rs16[0:n, 0:1], scalar2=None, op0=ALU.mult),
               rd=[x_s, rs16_s], wr=[xnbf_s])
            transposes(lambda j: xn_bf[0:n, j * 128:(j + 1) * 128], 8, n, 0,
                       lambda b0, nb: xnT[:, b0:b0 + nb, 0:n], [xnT_s], [xnbf_s], gainT=gain)

        wseq = []
        one_slot = ([(win_s[i], 8, GRPS[i][1]) for i in range(13)] + [(wo_s[i], 8, 512) for i in range(2)]
                    + [(wup_s[i], 8, 512) for i in range(8)] + [(wdn_s[i], 8, 512) for i in range(8)]
                    + [(wpg_s[i], 8, 512) for i in range(2)] + [(wp_s[i], 2, 512) for i in range(2)])
        for _ in range(NSLOT + 1):
            wseq.extend(one_slot)
        wst = {"issued": 0, "next": 0}
        conv_s = St()

        def w_issue(upto):
            while wst["issued"] < min(upto, len(wseq)):
                k = wst["issued"]
                src, kc, ncol = wseq[k]
                sl = k % 3
                dma(SP, WS[sl][:, 0:kc, 0:ncol], src[:, :, :], rd=[conv_s], wr=[WS_s[sl]])
                wst["issued"] += 1

        def w_get():
            k = wst["next"]
            wst["next"] += 1
            w_issue(k + 1)
            return k % 3, k

        def w_done(k):
            w_issue(k + 3)

        def dense(lhsT_fn, kcs, n, ncols_list, bank_fn, lhs_states, after_fn):
            for gi, ncol in enumerate(ncols_list):
                sl, k = w_get()
                bank = bank_fn(gi)
                for kc in range(kcs):
                    op(PE, lambda kc=kc: nc.tensor.matmul(PB[bank][0:n, 0:ncol], lhsT=lhsT_fn(kc), rhs=WS[sl][:, kc, 0:ncol],
                                                          start=(kc == 0), stop=(kc == kcs - 1)),
                       rd=list(lhs_states) + [WS_s[sl]], wr=[PB_s[bank]])
                w_done(k)
                after_fn(gi, bank)

        with nc.allow_non_contiguous_dma(reason="tiny constant loads"):
            for (dst, src) in ((gsgub, gsgu_d), (gqb, gq_d), (gkb, gk_d)):
                dma(SP, dst[:], src.partition_broadcast(128), wr=[cst_s])
            for (dst, src) in ((gmixT, gmix_d), (gffnT, gffn_d), (gpleT, gple_d)):
                dma(SP, None, None, wr=[cst_s], fn=lambda dst=dst, src=src: nc.sync.dma_start(out=dst[:], in_=src.rearrange("(k p) -> p k", p=128)))
            dma(SP, None, None, wr=[cst_s], fn=lambda: nc.sync.dma_start(
                out=coefs[:, 0:8], in_=ws_d[:, 0, 0:1].rearrange("g a -> (g a)").partition_broadcast(16)))
            dma(SP, None, None, wr=[cst_s], fn=lambda: nc.sync.dma_start(
                out=coefs[:, 8:16], in_=bs_d[:, 0:1].rearrange("g a -> (g a)").partition_broadcast(16)))
            dma(SP, bsT[:], bs_d.rearrange("g t -> t g"), wr=[cst_s], fn=lambda: nc.sync.dma_start(
                out=bsT[:], in_=bs_d.rearrange("g t -> t g")))
            dma(SP, pen[:], pen_d[:, :, :], wr=[cst_s])
            dma(SP, ptb[:], pt_d.partition_broadcast(128), wr=[cst_s])
        op(POOL, lambda: nc.gpsimd.memset(identb[:], 1.0), wr=[cst_s])
        op(POOL, lambda: nc.gpsimd.affine_select(out=identb[:], in_=identb[:], pattern=[[-1, 128]], compare_op=ALU.is_equal,
                                                 fill=0.0, base=0, channel_multiplier=1), wr=[cst_s])
        op(POOL, lambda: nc.gpsimd.memset(identf[:], 1.0), wr=[cst_s])
        op(POOL, lambda: nc.gpsimd.affine_select(out=identf[:], in_=identf[:], pattern=[[-1, 16]], compare_op=ALU.is_equal,
                                                 fill=0.0, base=0, channel_multiplier=1), wr=[cst_s])
        op(POOL, lambda: nc.gpsimd.memset(mhalf[:], -0.5), wr=[cst_s])
        op(POOL, lambda: nc.gpsimd.iota(iop[:], pattern=[[0, 1]], base=0, channel_multiplier=1), wr=[cst_s])
        for g in range(4):
            op(DVE, lambda g=g: nc.vector.tensor_copy(out=I4[:, g, :], in_=identb[:]), rd=[], wr=[cst_s])
            op(DVE, lambda g=g: nc.vector.tensor_copy(out=I416[:, g, :], in_=identb[0:16, 0:16]), rd=[], wr=[cst_s])
        op(DVE, lambda: nc.vector.tensor_scalar(out=gqb[:], in0=gqb[:], scalar1=ATTN_SCALE, scalar2=None, op0=ALU.mult), wr=[cst_s])
        op(DVE, lambda: nc.vector.tensor_scalar(out=idxa[:], in0=ptb[:], scalar1=128, scalar2=iop[:, 0:1], op0=ALU.mult, op1=ALU.add),
           wr=[cst_s])
        op(POOL, lambda: nc.gpsimd.memset(qT[:], 0.0), wr=[qT_s])
        op(POOL, lambda: nc.gpsimd.memset(Vt[:], 0.0), wr=[V_s])
        op(POOL, lambda: nc.gpsimd.memset(Vt[:, :, :, 64:65], 1.0), wr=[V_s])
        for g in range(8):
            dma(SP, ws32[:], ws_d[g, :, :], wr=[ws32_s])
            op(POOL, lambda: nc.gpsimd.affine_select(out=ws32[:], in_=ws32[:], pattern=[[-1, 128]], compare_op=ALU.is_ge,
                                                     fill=0.0, base=0, channel_multiplier=1), wr=[ws32_s])
            op(DVE, lambda: nc.vector.tensor_copy(out=wsb[:], in_=ws32[:]), rd=[ws32_s], wr=[wsb_s])
            op(PE, lambda: nc.tensor.transpose(PBb[3][:, 0, :], wsb[:], identb[:]), rd=[wsb_s, cst_s], wr=[PB_s[3]])
            op(ACT, lambda g=g: nc.scalar.copy(out=wsT[:, g, :], in_=PBb[3][:, 0, :]), rd=[PB_s[3]], wr=[cst_s])

        conv_jobs = []
        conv_toks = []

        def conv(dst, src2d, kc, col0, ncol, first=False):
            for k in range(kc):
                job = (lambda k=k: conv_toks.append(dma(
                    POOL, None, None, wr=[],
                    fn=lambda: nc.gpsimd.dma_start(out=dst[:, k, :], in_=src2d[k * 128:(k + 1) * 128, col0:col0 + ncol]))))
                if first:
                    job()
                else:
                    conv_jobs.append(job)
        for i, (o, w) in enumerate(GRPS):
            conv(win_s[i], win_d, 8, o, w, first=(i in (2, 4)))
        for i in range(2):
            conv(wo_s[i], wo_d, 8, i * 512, 512)
        for i in range(8):
            conv(wup_s[i], wup_d, 8, i * 512, 512)
        for i in range(8):
            nn, kg = i // 4, i % 4
            conv(wdn_s[i], wdn_d[kg * 1024:(kg + 1) * 1024, :], 8, nn * 512, 512)
        for i in range(2):
            conv(wpg_s[i], wpg_d, 8, i * 512, 512)
        for i in range(2):
            conv(wp_s[i], wp_d, 2, i * 512, 512)
        for tok in conv_toks:
            SP.wait(tok)

        def head_norm(src32, n, nh, gain, out_fn, out_states, e=None):
            v3 = src32.rearrange("p (h d) -> p h d", d=64)
            if e is not None:
                tmp4 = o32[0:n, 0:nh * 64].rearrange("p (e g d) -> p e g d", e=e, d=64)
                op(DVE, lambda: nc.vector.tensor_tensor(out=o32[0:n, 0:nh * 64], in0=src32, in1=src32, op=ALU.mult), rd=[k32_s], wr=[o32_s])
                op(DVE, lambda: nc.vector.tensor_reduce(out=ss16[0:n, 0:nh], in_=o32[0:n, 0:nh * 64].rearrange("p (h d) -> p h d", d=64),
                                                        axis=AX.X, op=ALU.add), rd=[o32_s], wr=[ss16_s])
                rstd_from_ss(ss16[0:n, 0:nh], rs16[0:n, 0:nh], n, nh, 1.0 / 64, ss16_s, rs16_s)
                op(DVE, lambda: nc.vector.tensor_tensor(out=o32[0:n, 0:nh * 64].rearrange("p (h d) -> p h d", d=64), in0=v3,
                                                        in1=rs16[0:n, 0:nh].unsqueeze(2).to_broadcast([n, nh, 64]), op=ALU.mult),
                   rd=[k32_s, rs16_s], wr=[o32_s])
                for ee in range(e):
                    op(DVE, lambda ee=ee: nc.vector.tensor_tensor(out=out_fn(ee), in0=tmp4[:, ee, :, :],
                                                                  in1=gain[0:n, :].unsqueeze(1).to_broadcast([n, nh // e, 64]), op=ALU.mult),
                       rd=[o32_s, cst_s], wr=out_states)
                return
            op(DVE, lambda: nc.vector.tensor_tensor(out=o32[0:n, 0:nh * 64], in0=src32, in1=src32, op=ALU.mult), rd=[k32_s], wr=[o32_s])
            op(DVE, lambda: nc.vector.tensor_reduce(out=ss16[0:n, 0:nh], in_=o32[0:n, 0:nh * 64].rearrange("p (h d) -> p h d", d=64),
                                                    axis=AX.X, op=ALU.add), rd=[o32_s], wr=[ss16_s])
            rstd_from_ss(ss16[0:n, 0:nh], rs16[0:n, 0:nh], n, nh, 1.0 / 64, ss16_s, rs16_s)
            op(DVE, lambda: nc.vector.tensor_tensor(out=o32[0:n, 0:nh * 64].rearrange("p (h d) -> p h d", d=64), in0=v3,
                                                    in1=rs16[0:n, 0:nh].unsqueeze(2).to_broadcast([n, nh, 64]), op=ALU.mult),
               rd=[k32_s, rs16_s], wr=[o32_s])
            op(DVE, lambda: nc.vector.tensor_tensor(out=out_fn(), in0=o32[0:n, 0:nh * 64].rearrange("p (h d) -> p h d", d=64),
                                                    in1=gain[0:n, :].unsqueeze(1).to_broadcast([n, nh, 64]), op=ALU.mult),
               rd=[o32_s, cst_s], wr=out_states)

        def kv_append(blk, n, k_ap, k_s, v_ap, v_s, kid_ap, kid_s_):
            c0 = blk * 128
            for j in range(2):
                op(PE, lambda j=j: nc.tensor.transpose(PBb[3][:, j, 0:n], k_ap[0:n, j * 128:(j + 1) * 128], identb[0:n, 0:n]),
                   rd=[k_s, cst_s], wr=[PB_s[3]])
            op(PE, lambda: nc.tensor.transpose(PBb[3][:, 2, 0:n], kid_ap[0:n, :, :].rearrange("p a d -> p (a d)"), identb[0:n, 0:n]),
               rd=[kid_s_, cst_s], wr=[PB_s[3]])
            op(ACT, lambda: nc.scalar.copy(out=KT[:, :, c0:c0 + n], in_=PBb[3][:, 0:2, 0:n]), rd=[PB_s[3]], wr=[KT_s])
            op(ACT, lambda: nc.scalar.copy(out=kiT[:, c0:c0 + n], in_=PBb[3][:, 2, 0:n]), rd=[PB_s[3]], wr=[kiT_s])
            op(ACT, lambda: nc.scalar.copy(out=Vt[0:n, blk, :, 0:64], in_=v_ap[0:n, :].rearrange("p (c d) -> p c d", d=64)),
               rd=[v_s], wr=[V_s])

        def indexer(n, nch, sc_ap, sc_state, k0=0, ki_state=None, hook=None):
            ki_state = ki_state or kiT_s
            S = nch * 128
            j = 0
            for g0 in range(0, S, 512):
                w = min(512, S - g0)
                for h in range(8):
                    if hook is not None:
                        hook()
                    bank = j % 3
                    r = rt[j % 3]
                    rs = rt_s[j % 3]
                    j += 1
                    pb = (h % 2) * 64
                    op(PE, lambda: nc.tensor.matmul(PB[bank][0:n, 0:w], lhsT=qiT[pb:pb + 64, h // 2, 0:n], rhs=kiT[pb:pb + 64, k0 + g0:k0 + g0 + w],
                                                    start=True, stop=True), rd=[qiT_s, ki_state], wr=[PB_s[bank]])
                    op(ACT, lambda: nc.scalar.activation(out=r[0:n, 0:w], in_=PB[bank][0:n, 0:w], func=AF.Relu, scale=absw[0:n, h:h + 1]),
                       rd=[PB_s[bank], w_s_], wr=[rs])
                    if h == 0:
                        op(DVE, lambda: nc.vector.tensor_scalar(out=sc_ap[0:n, g0:g0 + w], in0=r[0:n, 0:w], scalar1=sgnw[0:n, 0:1], scalar2=None,
                                                                op0=ALU.mult), rd=[rs, w_s_], wr=[sc_state])
                    else:
                        op(DVE, lambda: nc.vector.scalar_tensor_tensor(out=sc_ap[0:n, g0:g0 + w], in0=r[0:n, 0:w], scalar=sgnw[0:n, h:h + 1],
                                                                       in1=sc_ap[0:n, g0:g0 + w], op0=ALU.mult, op1=ALU.add),
                           rd=[rs, w_s_], wr=[sc_state])

        def bisect(n, S, sc_ap, sc_state, junk_ap, junk_state, junk_act=None):
            op(DVE, lambda: nc.vector.memset(lo[0:n, :], -BRK), wr=[bis_s])
            split = junk_act is not None and S >= 2048
            S1 = (int(S * 0.46) // 128) * 128 if split else S
            S2 = S - S1
            wd = BRK
            for it in range(NIT):
                op(DVE, lambda: nc.vector.tensor_scalar(out=mid[0:n, :], in0=lo[0:n, :], scalar1=wd, scalar2=None, op0=ALU.add),
                   rd=[bis_s], wr=[bis_s, mid_s])
                if split:
                    op(ACT, lambda: nc.scalar.activation(out=junk_act[0:n, S1:S], in_=sc_ap[0:n, S1:S], func=AF.Sign, bias=mid[0:n, 0:1], scale=-1.0,
                                                         accum_out=cnta[0:n, 0:1]), rd=[sc_state, mid_s], wr=[junka_s, cnta_s])
                op(DVE, lambda: nc.vector.tensor_scalar(out=junk_ap[0:n, 0:S1], in0=sc_ap[0:n, 0:S1], scalar1=mid[0:n, 0:1], scalar2=None, op0=ALU.is_ge,
                                                        op1=ALU.add, accum_out=cntt[0:n, 0:1]),
                   rd=[sc_state, bis_s], wr=[junk_state, bis_s])
                thr = 255.5
                if split:
                    op(DVE, lambda: nc.vector.scalar_tensor_tensor(out=cntt[0:n, :], in0=cnta[0:n, :], scalar=-0.5, in1=cntt[0:n, :], op0=ALU.mult,
                                                                   op1=ALU.add), rd=[cnta_s, bis_s], wr=[bis_s])
                    thr = 255.5 - S2 / 2.0
                op(DVE, lambda: nc.vector.tensor_scalar(out=geq[0:n, :], in0=cntt[0:n, :], scalar1=thr, scalar2=wd, op0=ALU.is_ge,
                                                        op1=ALU.mult), rd=[bis_s], wr=[bis_s])
                op(DVE, lambda: nc.vector.tensor_tensor(out=lo[0:n, :], in0=lo[0:n, :], in1=geq[0:n, :], op=ALU.add), rd=[bis_s, mid_s], wr=[bis_s])
                wd = wd / 2.0

        def attend(n, nch, sc_ap, sc_state, sel_ap, out_ap, out_state, ch0=0, kt_state=None, v_state=None, hook=None):
            kt_state = kt_state or KT_s
            v_state = v_state or V_s
            steps = [(ch, c) for ch in range(nch) for c in range(4)]

            def prep_group(gi):
                w = min(512, nch * 128 - gi * 512)
                ng = w // 128
                mb = MBg[gi % 2]
                bT = 0
                op(DVE, lambda: nc.vector.tensor_scalar(out=mb[0:n, 0:w], in0=sc_ap[0:n, gi * 512:gi * 512 + w], scalar1=lo[0:n, 0:1],
                                                        scalar2=None, op0=ALU.is_ge), rd=[sc_state, bis_s], wr=[MB_s[gi % 2]])
                for cj in range(ng):
                    op(PE, lambda cj=cj: nc.tensor.transpose(PBb[bT][:, cj, 0:n], mb[0:n, cj * 128:(cj + 1) * 128], identb[0:n, 0:n]),
                       rd=[MB_s[gi % 2], cst_s], wr=[PB_s[bT]])
                op(DVE, lambda: nc.vector.tensor_copy(out=mT[gi % 2][:, 0:ng, 0:n], in_=PBb[bT][:, 0:ng, 0:n]), rd=[PB_s[bT]], wr=[mT_s[gi % 2]])

            def emit_qk(k):
                ch, c = steps[k]
                gi, cj = ch // 4, ch % 4
                if cj == 0 and c == 0:
                    prep_group(gi)
                bank = 1 + (k % 3)
                pb = (c % 2) * 64
                outv = PB[bank][:, 0:4 * n].rearrange("p (g t) -> p g t", g=4)
                op(PE, lambda: nc.tensor.matmul(outv, lhsT=KT[:, c // 2, (ch0 + ch) * 128:(ch0 + ch + 1) * 128], rhs=qT[:, c, :, 0:n],
                                                start=True, stop=True), rd=[kt_state, qT_s], wr=[PB_s[bank]])

            emit_qk(0)
            if len(steps) > 1:
                emit_qk(1)
            for k in range(len(steps)):
                ch, c = steps[k]
                gi, cj = ch // 4, ch % 4
                if hook is not None:
                    hook()
                if k + 2 < len(steps):
                    emit_qk(k + 2)
                bank = 1 + (k % 3)
                pt_ = PT[k % 3]
                pts = PT_s[k % 3]
                op(ACT, lambda: nc.scalar.activation(out=pt_[:, 0:4 * n], in_=PB[bank][:, 0:4 * n], func=AF.Exp), rd=[PB_s[bank]], wr=[pts])
                pt3 = pt_[:, 0:4 * n].rearrange("p (g t) -> p g t", g=4)
                op(DVE, lambda: nc.vector.tensor_tensor(out=pt3, in0=pt3, in1=mT[gi % 2][:, cj, 0:n].unsqueeze(1).to_broadcast([128, 4, n]),
                                                        op=ALU.mult), rd=[mT_s[gi % 2]], wr=[pts])
                for g in range(4):
                    op(PE, lambda g=g: nc.tensor.matmul(PB[4 + c][0:n, g * 65:(g + 1) * 65], lhsT=pt_[:, g * n:(g + 1) * n], rhs=Vt[:, ch0 + ch, c, :],
                                                        start=(ch == 0 and g == 0), stop=(ch == nch - 1), skip_group_check=True),
                       rd=[pts, v_state], wr=[PB_s[4 + c]])
            for c in range(4):
                acc = PB[4 + c][0:n, 0:260].rearrange("p (g e) -> p g e", e=65)
                op(DVE, lambda: nc.vector.reciprocal(out=sm[0:n, 4 * c:4 * c + 4], in_=acc[:, :, 64]), rd=[PB_s[4 + c]], wr=[sm_s])
                op(DVE, lambda: nc.vector.tensor_tensor(out=out_ap[0:n, c * 256:(c + 1) * 256].rearrange("p (g d) -> p g d", d=64),
                                                        in0=acc[:, :, 0:64], in1=sm[0:n, 4 * c:4 * c + 4].unsqueeze(2).to_broadcast([n, 4, 64]),
                                                        op=ALU.mult), rd=[PB_s[4 + c], sm_s], wr=[out_state])

        STh_s = [[St(), St()], [St(), St()]]

        def attend_sample(nch, sc_ap, sc_state, out_ap, out_state, ch0, kt_state, v_state, hook=None):
            n = 16

            def prep_group(gi):
                w = min(512, nch * 128 - gi * 512)
                ng = w // 128
                mb = MBg[gi % 2]
                op(DVE, lambda: nc.vector.tensor_scalar(out=mb[0:n, 0:w], in0=sc_ap[0:n, gi * 512:gi * 512 + w], scalar1=lo[0:n, 0:1],
                                                        scalar2=None, op0=ALU.is_ge), rd=[sc_state, bis_s], wr=[MB_s[gi % 2]])
                for cj in range(ng):
                    op(PE, lambda cj=cj: nc.tensor.transpose(PBb[0][:, cj, 0:n], mb[0:n, cj * 128:(cj + 1) * 128], identb[0:n, 0:n]),
                       rd=[MB_s[gi % 2], cst_s], wr=[PB_s[0]])
                op(DVE, lambda: nc.vector.tensor_copy(out=mT[gi % 2][:, 0:ng, 0:n], in_=PBb[0][:, 0:ng, 0:n]), rd=[PB_s[0]], wr=[mT_s[gi % 2]])

            def emit_qk(ch):
                gi, cj = ch // 4, ch % 4
                if cj == 0:
                    prep_group(gi)
                hf = 0
                for c in range(4):
                    pb = (c % 2) * 64
                    col = hf * 128 + (c // 2) * 64
                    outv = PB[1 + (c % 2)][:, col:col + 64].rearrange("p (g t) -> p g t", g=4)
                    op(PE, lambda: nc.tensor.matmul(outv, lhsT=KT[:, c // 2, (ch0 + ch) * 128:(ch0 + ch + 1) * 128],
                                                    rhs=qT[:, c, :, 0:n], start=True, stop=True, skip_group_check=True),
                       rd=[kt_state, qT_s], wr=[PB_s[1 + (c % 2)]])

            emit_qk(0)
            for ch in range(nch):
                gi, cj = ch // 4, ch % 4
                hf = 0
                if hook is not None:
                    hook()
                    hook()
                pt_ = PT[ch % 3]
                pts = PT_s[ch % 3]
                for e in range(2):
                    op(ACT, lambda e=e: nc.scalar.activation(out=pt_[:, e * 128:(e + 1) * 128], in_=PB[1 + e][:, hf * 128:(hf + 1) * 128], func=AF.Exp),
                       rd=[PB_s[1 + e]], wr=[pts])
                pt3 = pt_[:, 0:256].rearrange("p (a t) -> p a t", t=n)
                op(DVE, lambda: nc.vector.tensor_tensor(out=pt3, in0=pt3, in1=mT[gi % 2][:, cj, 0:n].unsqueeze(1).to_broadcast([128, 16, n]),
                                                        op=ALU.mult), rd=[mT_s[gi % 2]], wr=[pts])
                if ch + 1 < nch:
                    emit_qk(ch + 1)
                for c in range(4):
                    for g in range(4):
                        a0 = (c % 2) * 128 + (c // 2) * 64 + g * n
                        op(PE, lambda: nc.tensor.matmul(PB[4 + c][0:n, g * 65:(g + 1) * 65], lhsT=pt_[:, a0:a0 + n], rhs=Vt[:, ch0 + ch, c, :],
                                                        start=(ch == 0 and g == 0), stop=(ch == nch - 1), skip_group_check=True),
                           rd=[pts, v_state], wr=[PB_s[4 + c]])
            for c in range(4):
                acc = PB[4 + c][0:n, 0:260].rearrange("p (g e) -> p g e", e=65)
                op(DVE, lambda: nc.vector.reciprocal(out=sm[0:n, 4 * c:4 * c + 4], in_=acc[:, :, 64]), rd=[PB_s[4 + c]], wr=[sm_s])
                op(DVE, lambda: nc.vector.tensor_tensor(out=out_ap[0:n, c * 256:(c + 1) * 256].rearrange("p (g d) -> p g d", d=64),
                                                        in0=acc[:, :, 0:64], in1=sm[0:n, 4 * c:4 * c + 4].unsqueeze(2).to_broadcast([n, 4, 64]),
                                                        op=ALU.mult), rd=[PB_s[4 + c], sm_s], wr=[out_state])

        def project(n, x_ap, x_s, outs):
            fence([sc_s], arena_states)
            rmsnorm_to_T(x_ap, x_s, gmixT, n)
            ug, ug_s = FB[1], FB_s[1]
            vn, vn_s = FB[2], FB_s[2]
            sga, sga_s = FB[3], FB_s[3]
            sgb, sgb_s = FB[4], FB_s[4]

            def after(gi, bank):
                pb_s = PB_s[bank]
                pbk = PB[bank]
                if gi in (0, 1):
                    op(ACT, lambda: nc.scalar.copy(out=k32[0:n, :], in_=pbk[0:n, :]), rd=[pb_s], wr=[k32_s])
                    head_norm(k32[0:n, :], n, 8, gqb, lambda ee: qn_bf[0:n, gi, :, ee * 64:(ee + 1) * 64], [qn_s], e=2)
                elif gi == 2:
                    op(ACT, lambda: nc.scalar.copy(out=k32[0:n, :], in_=pbk[0:n, :]), rd=[pb_s], wr=[k32_s])
                    head_norm(k32[0:n, 0:256], n, 4, gkb, lambda: o32[0:n, 256:512].rearrange("p (h d) -> p h d", d=64), [o32_s])
                    dma(POOL, outs["k"], o32[0:n, 256:512], rd=[o32_s])
                    dma(POOL, outs["v"], k32[0:n, 256:512], rd=[k32_s])
                    if "kv" in outs:
                        op(DVE, lambda: nc.vector.tensor_copy(out=kn_bf[0:n, :], in_=o32[0:n, 256:512]), rd=[o32_s], wr=[knbf_s])
                        op(DVE, lambda: nc.vector.tensor_copy(out=vs_bf[0:n, :], in_=k32[0:n, 256:512]), rd=[k32_s], wr=[smp_s])
                elif gi == 3:
                    op(ACT, lambda: nc.scalar.copy(out=qi_bf[0:n, :], in_=pbk[0:n, :]), rd=[pb_s], wr=[qibf_s])
                elif gi == 4:
                    op(ACT, lambda: nc.scalar.copy(out=ki32[0:n, :], in_=pbk[0:n, 0:72]), rd=[pb_s], wr=[ki32_s])
                    dma(POOL, outs["ki"], ki32[0:n, 0:64], rd=[ki32_s])
                    op(ACT, lambda: nc.scalar.activation(out=absw[0:n, :], in_=ki32[0:n, 64:72], func=AF.Abs, scale=IDXS), rd=[ki32_s], wr=[w_s_])
                    op(ACT, lambda: nc.scalar.activation(out=sgnw[0:n, :], in_=ki32[0:n, 64:72], func=AF.Sign), rd=[ki32_s], wr=[w_s_])
                    if "kv" in outs:
                        for a in range(2):
                            op(DVE, lambda a=a: nc.vector.tensor_copy(out=kid_bf[0:n, a, :], in_=ki32[0:n, 0:64]), rd=[ki32_s], wr=[kid_s])
                elif gi in (5, 6):
                    o = (gi - 5) * 512
                    op(ACT, lambda: nc.scalar.activation(out=ug[0:n, o:o + 512], in_=pbk[0:n, :], func=AF.Gelu_apprx_tanh), rd=[pb_s], wr=[ug_s])
                elif gi in (7, 8):
                    o = (gi - 7) * 512
                    op(ACT, lambda: nc.scalar.activation(out=vn[0:n, o:o + 512], in_=pbk[0:n, :], func=AF.Gelu_apprx_tanh), rd=[pb_s], wr=[vn_s])
                elif gi in (9, 10):
                    o = (gi - 9) * 512
                    op(ACT, lambda: nc.scalar.activation(out=sga[0:n, o:o + 512], in_=pbk[0:n, :], func=AF.Sigmoid), rd=[pb_s], wr=[sga_s])
                else:
                    o = (gi - 11) * 512
                    op(ACT, lambda: nc.scalar.activation(out=sgb[0:n, o:o + 512], in_=pbk[0:n, :], func=AF.Sigmoid), rd=[pb_s], wr=[sgb_s])

            dense(lambda kc: xnT[:, kc, 0:n], 8, n, [g[1] for g in GRPS], lambda gi: gi % 2, [xnT_s], after)
            for j in range(8):
                op(PE, lambda j=j: nc.tensor.transpose(PBb[2][:, j, 0:n], qn_bf[0:n, j // 4, j % 4, :], identb[0:n, 0:n]),
                   rd=[qn_s, cst_s], wr=[PB_s[2]])
            q5 = qT[:, :, :, :].rearrange("p (a e) g t -> p a e g t", e=2)
            for e in range(2):
                op(ACT, lambda e=e: nc.scalar.copy(out=q5[e * 64:(e + 1) * 64, :, e, :, 0:n],
                                                    in_=PBb[2][e * 64:(e + 1) * 64, 0:8, 0:n].rearrange("p (a g) t -> p a g t", g=4)),
                   rd=[PB_s[2]], wr=[qT_s])
            transposes(lambda j: qi_bf[0:n, j * 128:(j + 1) * 128], 4, n, 3,
                       lambda b0, nb: qiT[:, b0:b0 + nb, 0:n], [qiT_s], [qibf_s])
            op(DVE, lambda: nc.vector.scalar_tensor_tensor(out=vn_bf[0:n, :], in0=vn[0:n, :], scalar=1.0, in1=vn[0:n, :], op0=ALU.mult,
                                                           op1=ALU.mult, accum_out=ss16[0:n, 0:1]), rd=[vn_s], wr=[vnbf_s, ss16_s])
            rstd_from_ss(ss16[0:n, 0:1], rs16[0:n, 0:1], n, 1, 1.0 / D, ss16_s, rs16_s)
            op(DVE, lambda: nc.vector.scalar_tensor_tensor(out=vn[0:n, :], in0=vn[0:n, :], scalar=rs16[0:n, 0:1], in1=gsgub[0:n, :],
                                                           op0=ALU.mult, op1=ALU.mult), rd=[rs16_s, cst_s], wr=[vn_s])
            dma(POOL, outs["sg"], vn[0:n, :], rd=[vn_s])
            if n == 128:
                op(DVE, lambda: nc.vector.tensor_copy(out=vn_bf[:, :], in_=vn[:, :]), rd=[vn_s], wr=[vnbf_s])
                for g in range(8):
                    bank = 4 + g // 4
                    op(PE, lambda g=g: nc.tensor.matmul(PB[bank][:, (g % 4) * 128:(g % 4 + 1) * 128], lhsT=wsT[:, g, :],
                                                        rhs=vn_bf[:, g * 128:(g + 1) * 128], start=True, stop=True, skip_group_check=True),
                       rd=[vnbf_s, cst_s], wr=[PB_s[bank]])
                for hb in range(2):
                    op(DVE, lambda hb=hb: nc.vector.tensor_tensor(
                        out=vn[:, hb * 512:(hb + 1) * 512].rearrange("p (g c) -> p g c", c=128),
                        in0=PB[4 + hb][:, :].rearrange("p (g c) -> p g c", c=128),
                        in1=bsT[:, hb * 4:hb * 4 + 4].unsqueeze(2).to_broadcast([128, 4, 128]), op=ALU.add),
                       rd=[PB_s[4 + hb], cst_s], wr=[vn_s])
            else:
                v3s = vn[0:n, :].rearrange("p (g c) -> p g c", c=128)
                op(DVE, lambda: nc.vector.tensor_tensor(out=v3s, in0=v3s, in1=coefs[0:n, 0:8].unsqueeze(2).to_broadcast([n, 8, 128]), op=ALU.mult),
                   rd=[cst_s], wr=[vn_s])
                op(DVE, lambda: nc.vector.tensor_tensor(out=v3s, in0=v3s, in1=coefs[0:n, 8:16].unsqueeze(2).to_broadcast([n, 8, 128]), op=ALU.add),
                   rd=[cst_s], wr=[vn_s])
            op(DVE, lambda: nc.vector.tensor_tensor(out=vn[0:n, :], in0=vn[0:n, :], in1=ug[0:n, :], op=ALU.mult), rd=[ug_s], wr=[vn_s])
            op(DVE, lambda: nc.vector.tensor_tensor(out=sgb[0:n, :], in0=sgb[0:n, :], in1=vn[0:n, :], op=ALU.mult), rd=[vn_s], wr=[sgb_s])

        def finish(n, x_ap, x_s, oatt, oatt_s, p_src, y_dst):
            sga, sga_s = FB[3], FB_s[3]
            msgu, msgu_s = FB[4], FB_s[4]
            fence([sc_s], arena_states)
            dma(SP, p32[0:n, :], p_src, wr=[p32_s])
            op(DVE, lambda: nc.vector.tensor_tensor(out=oatt[0:n, :], in0=oatt[0:n, :], in1=sga[0:n, :], op=ALU.mult), rd=[sga_s], wr=[oatt_s])
            op(DVE, lambda: nc.vector.tensor_tensor(out=xn_bf[0:n, :], in0=oatt[0:n, :], in1=msgu[0:n, :], op=ALU.add),
               rd=[oatt_s, msgu_s], wr=[xnbf_s])
            transposes(lambda j: xn_bf[0:n, j * 128:(j + 1) * 128], 8, n, 2, lambda b0, nb: xnT[:, b0:b0 + nb, 0:n], [xnT_s], [xnbf_s])

            def after_o(gi, bank):
                op(DVE, lambda: nc.vector.tensor_tensor(out=x_ap[:, gi * 512:(gi + 1) * 512], in0=x_ap[:, gi * 512:(gi + 1) * 512],
                                                        in1=PB[bank][0:n, :], op=ALU.add), rd=[PB_s[bank]], wr=[x_s])
            dense(lambda kc: xnT[:, kc, 0:n], 8, n, [512, 512], lambda gi: gi % 2, [xnT_s], after_o)
            rmsnorm_to_T(x_ap, x_s, gffnT, n)

            def after_up(gi, bank):
                op(ACT, lambda: nc.scalar.activation(out=rt[gi % 2][0:n, :], in_=PB[bank][0:n, :], func=AF.Relu), rd=[PB_s[bank]], wr=[rt_s[gi % 2]])
                op(DVE, lambda: nc.vector.tensor_tensor(out=h_bf[0:n, gi * 512:(gi + 1) * 512], in0=rt[gi % 2][0:n, :], in1=rt[gi % 2][0:n, :],
                                                        op=ALU.mult), rd=[rt_s[gi % 2]], wr=[hbf_s])
            dense(lambda kc: xnT[:, kc, 0:n], 8, n, [512] * 8, lambda gi: gi % 2, [xnT_s], after_up)
            transposes(lambda j: h_bf[0:n, j * 128:(j + 1) * 128], 32, n, 2, lambda b0, nb: hT[:, b0:b0 + nb, 0:n], [hT_s], [hbf_s])
            for nn in range(2):
                bank = nn % 2
                for kg in range(4):
                    sl, k = w_get()
                    for kc in range(8):
                        op(PE, lambda kc=kc: nc.tensor.matmul(PB[bank][0:n, :], lhsT=hT[:, kg * 8 + kc, 0:n], rhs=WS[sl][:, kc, :],
                                                              start=(kg == 0 and kc == 0), stop=(kg == 3 and kc == 7)),
                           rd=[hT_s, WS_s[sl]], wr=[PB_s[bank]])
                    w_done(k)
                op(DVE, lambda: nc.vector.tensor_tensor(out=x_ap[:, nn * 512:(nn + 1) * 512], in0=x_ap[:, nn * 512:(nn + 1) * 512],
                                                        in1=PB[bank][0:n, :], op=ALU.add), rd=[PB_s[bank]], wr=[x_s])
            rmsnorm_to_T(x_ap, x_s, gpleT, n)
            gate, gate_s = FB[1], FB_s[1]

            def after_pg(gi, bank):
                op(ACT, lambda: nc.scalar.activation(out=gate[0:n, gi * 512:(gi + 1) * 512], in_=PB[bank][0:n, :], func=AF.Sigmoid),
                   rd=[PB_s[bank]], wr=[gate_s])
            dense(lambda kc: xnT[:, kc, 0:n], 8, n, [512, 512], lambda gi: gi % 2, [xnT_s], after_pg)
            op(DVE, lambda: nc.vector.tensor_copy(out=p_bf[0:n, :], in_=p32[0:n, :]), rd=[p32_s], wr=[pbf_s])
            transposes(lambda j: p_bf[0:n, j * 128:(j + 1) * 128], 2, n, 3, lambda b0, nb: pT[:, b0:b0 + nb, 0:n], [pT_s], [pbf_s])

            def after_p(gi, bank):
                op(DVE, lambda: nc.vector.tensor_tensor(out=gate[0:n, gi * 512:(gi + 1) * 512], in0=gate[0:n, gi * 512:(gi + 1) * 512],
                                                        in1=PB[bank][0:n, :], op=ALU.mult), rd=[PB_s[bank]], wr=[gate_s])
                op(DVE, lambda: nc.vector.tensor_tensor(out=x_ap[:, gi * 512:(gi + 1) * 512], in0=x_ap[:, gi * 512:(gi + 1) * 512],
                                                        in1=gate[0:n, gi * 512:(gi + 1) * 512], op=ALU.add), rd=[gate_s], wr=[x_s])
            dense(lambda kc: pT[:, kc, 0:n], 2, n, [512, 512], lambda gi: gi % 2, [pT_s], after_p)
            dma(POOL, y_dst, x_ap, rd=[x_s])

        n = 128
        fence(arena_states + [sc_s, sc16_s, sct_s], [wkvk_s])
        dma(SP, wkvk[:, :, 0:512], win_s[2][:, :, :], wr=[wkvk_s])
        dma(SP, wkvk[:, :, 512:576], win_s[4][:, :, 0:64], wr=[wkvk_s])
        xnT2 = al("xnT2", [128, 8, 128], BF16, 0); xnT2_s = St()
        fence(arena_states + [sc_s, sc16_s, sct_s], [xnT2_s])
        xnTs = [(xnT, xnT_s), (xnT2, xnT2_s)]
        ssA = sb("ssA", [128, 2], F32); ssA_s = St(); rsA_s = St()

        def stageA(kb):
            xt, xt_s = FB[kb % 2], FB_s[kb % 2]
            xT, xT_s = xnTs[kb % 2]
            dma(SP, xt[:, :], xb_d[kb * 128:(kb + 1) * 128, :], wr=[xt_s])
            op(DVE, lambda: nc.vector.scalar_tensor_tensor(out=xn_bf[:, :], in0=xt[:, :], scalar=1.0, in1=xt[:, :], op0=ALU.mult, op1=ALU.mult,
                                                           accum_out=ssA[:, 0:1]), rd=[xt_s], wr=[xnbf_s, ssA_s])
            rstd_from_ss(ssA[:, 0:1], ssA[:, 1:2], 128, 1, 1.0 / D, ssA_s, rsA_s)
            op(DVE, lambda: nc.vector.tensor_scalar(out=xn_bf[:, :], in0=xt[:, :], scalar1=ssA[:, 1:2], scalar2=None, op0=ALU.mult),
               rd=[xt_s, rsA_s], wr=[xnbf_s])
            transposes(lambda j: xn_bf[:, j * 128:(j + 1) * 128], 8, 128, 0,
                       lambda b0, nb: xT[:, b0:b0 + nb, :], [xT_s], [xnbf_s], gainT=gmixT)

        def stageB(kb):
            xT, xT_s = xnTs[kb % 2]
            for kc in range(8):
                op(PE, lambda kc=kc: nc.tensor.matmul(PB[0][:, :], lhsT=xT[:, kc, :], rhs=wkvk[:, kc, 0:512], start=(kc == 0), stop=(kc == 7)),
                   rd=[xT_s, wkvk_s], wr=[PB_s[0]])
            for kc in range(8):
                op(PE, lambda kc=kc: nc.tensor.matmul(PB[1][:, 0:64], lhsT=xT[:, kc, :], rhs=wkvk[:, kc, 512:576], start=(kc == 0), stop=(kc == 7)),
                   rd=[xT_s, wkvk_s], wr=[PB_s[1]])
            op(ACT, lambda: nc.scalar.copy(out=k32[:, :], in_=PB[0][:, :]), rd=[PB_s[0]], wr=[k32_s])
            head_norm(k32[:, 0:256], n, 4, gkb, lambda: kn_bf[:, :].rearrange("p (h d) -> p h d", d=64), [knbf_s])
            op(ACT, lambda: nc.scalar.copy(out=v_bf[:, :], in_=k32[:, 256:512]), rd=[k32_s], wr=[vbf_s])
            for a in range(2):
                op(ACT, lambda a=a: nc.scalar.copy(out=kid_bf[:, a, :], in_=PB[1][:, 0:64]), rd=[PB_s[1]], wr=[kid_s])
            kv_append(kb, n, kn_bf, knbf_s, v_bf, vbf_s, kid_bf, kid_s)

        RS["act"] = True
        stageA(0)

        def prepass_iter(kb):
            xT, xT_s = xnTs[kb % 2]
            hasA = kb + 1 < NBLK
            k3 = lambda ap: ap.rearrange("p (h d) -> p h d", d=64)
            for kc in range(8):
                op(PE, lambda kc=kc: nc.tensor.matmul(PB[0][:, :], lhsT=xT[:, kc, :], rhs=wkvk[:, kc, 0:512], start=(kc == 0), stop=(kc == 7)),
                   rd=[xT_s, wkvk_s], wr=[PB_s[0]])
            for kc in range(8):
                op(PE, lambda kc=kc: nc.tensor.matmul(PB[1][:, 0:64], lhsT=xT[:, kc, :], rhs=wkvk[:, kc, 512:576], start=(kc == 0), stop=(kc == 7)),
                   rd=[xT_s, wkvk_s], wr=[PB_s[1]])
            op(ACT, lambda: nc.scalar.copy(out=k32[:, :], in_=PB[0][:, :]), rd=[PB_s[0]], wr=[k32_s])
            if hasA:
                xt, xt_s = FB[(kb + 1) % 2], FB_s[(kb + 1) % 2]
                xT2, xT2_s = xnTs[(kb + 1) % 2]
                dma(SP, xt[:, :], xb_d[(kb + 1) * 128:(kb + 2) * 128, :], wr=[xt_s])
                op(DVE, lambda: nc.vector.scalar_tensor_tensor(out=xn_bf[:, :], in0=xt[:, :], scalar=1.0, in1=xt[:, :], op0=ALU.mult, op1=ALU.mult,
                                                               accum_out=ssA[:, 0:1]), rd=[xt_s], wr=[xnbf_s, ssA_s])
                op(DVE, lambda: nc.vector.tensor_scalar(out=ssA[:, 0:1], in0=ssA[:, 0:1], scalar1=1.0 / D, scalar2=EPS, op0=ALU.mult, op1=ALU.add),
                   rd=[], wr=[ssA_s])
                op(ACT, lambda: nc.scalar.activation(out=ssA[:, 1:2], in_=ssA[:, 0:1], func=AF.Sqrt), rd=[ssA_s], wr=[rsA_s])
            op(DVE, lambda: nc.vector.tensor_tensor(out=o32[:, 0:256], in0=k32[:, 0:256], in1=k32[:, 0:256], op=ALU.mult), rd=[k32_s], wr=[o32_s])
            op(DVE, lambda: nc.vector.tensor_reduce(out=ss16[:, 0:4], in_=k3(o32[:, 0:256]), axis=AX.X, op=ALU.add), rd=[o32_s], wr=[ss16_s])
            op(DVE, lambda: nc.vector.tensor_scalar(out=ss16[:, 0:4], in0=ss16[:, 0:4], scalar1=1.0 / 64, scalar2=EPS, op0=ALU.mult, op1=ALU.add),
               rd=[], wr=[ss16_s])
            op(ACT, lambda: nc.scalar.activation(out=rs16[:, 0:4], in_=ss16[:, 0:4], func=AF.Sqrt), rd=[ss16_s], wr=[rs16_s])
            if hasA:
                op(DVE, lambda: nc.vector.reciprocal(out=ssA[:, 1:2], in_=ssA[:, 1:2]), rd=[], wr=[rsA_s])
                op(DVE, lambda: nc.vector.tensor_scalar(out=xn_bf[:, :], in0=xt[:, :], scalar1=ssA[:, 1:2], scalar2=None, op0=ALU.mult),
                   rd=[xt_s, rsA_s], wr=[xnbf_s])
                for j in range(8):
                    op(PE, lambda j=j: nc.tensor.transpose(PBb[2][:, j, :], xn_bf[:, j * 128:(j + 1) * 128], identb[:, :]),
                       rd=[xnbf_s, cst_s], wr=[PB_s[2]])
            op(DVE, lambda: nc.vector.reciprocal(out=rs16[:, 0:4], in_=rs16[:, 0:4]), rd=[], wr=[rs16_s])
            op(DVE, lambda: nc.vector.tensor_tensor(out=k3(o32[:, 0:256]), in0=k3(k32[:, 0:256]),
                                                    in1=rs16[:, 0:4].unsqueeze(2).to_broadcast([128, 4, 64]), op=ALU.mult),
               rd=[k32_s, rs16_s], wr=[o32_s])
            op(DVE, lambda: nc.vector.tensor_tensor(out=k3(kn_bf[:, :]), in0=k3(o32[:, 0:256]),
                                                    in1=gkb[:, :].unsqueeze(1).to_broadcast([128, 4, 64]), op=ALU.mult),
               rd=[o32_s, cst_s], wr=[knbf_s])
            op(ACT, lambda: nc.scalar.copy(out=v_bf[:, :], in_=k32[:, 256:512]), rd=[k32_s], wr=[vbf_s])
            for a_ in range(2):
                op(ACT, lambda a_=a_: nc.scalar.copy(out=kid_bf[:, a_, :], in_=PB[1][:, 0:64]), rd=[PB_s[1]], wr=[kid_s])
            kv_append(kb, 128, kn_bf, knbf_s, v_bf, vbf_s, kid_bf, kid_s)
            if hasA:
                op(DVE, lambda: nc.vector.tensor_tensor(out=xT2[:, :, :], in0=PBb[2][:, 0:8, :],
                                                        in1=gmixT[:, 0:8].unsqueeze(2).to_broadcast([128, 8, 128]), op=ALU.mult),
                   rd=[PB_s[2], cst_s], wr=[xT2_s])

        for kb in range(NBLK):
            prepass_iter(kb)
            for _ in range(4):
                if conv_jobs:
                    conv_jobs.pop(0)()
        fence([xnT2_s], arena_states)
        RS["act"] = False
        while conv_jobs:
            conv_jobs.pop(0)()
        for i in range(NDS):
            if dcnt[i] > 0:
                SP.wait((dsems[i], dcnt[i]))

        for i in range(NSLOT):
            m, second = i // 2, i % 2
            nch = 8 * m + (8 if second else 4)
            xt, xt_s = FB[0], FB_s[0]
            dma(SP, xt[:, :], xo_d[i, :, :], wr=[xt_s])
            project(n, xt[:, :], xt_s, {"k": ko_d[i, :, :], "v": vo_d[i, :, :], "ki": kio_d[i, :, :], "sg": sgo_d[i, :, :]})
            fence(arena_states, [sc_s])
            indexer(n, nch, scores, sc_s)
            S = nch * 128
            op(DVE, lambda: nc.vector.tensor_tensor(out=scores[:, S - 512:S], in0=scores[:, S - 512:S], in1=pen[:, second, :], op=ALU.add),
               rd=[cst_s], wr=[sc_s])
            fence([FB_s[1], FB_s[2]], [junk_s, junka_s])
            bisect(n, S, scores[:, 0:S], sc_s, junk8[:, 0:S], junk_s, junk_act=junki8)
            fence([junk_s, junka_s], [FB_s[1], FB_s[2]])
            oat, oat_s = FB[1], FB_s[1]
            attend(n, nch, scores, sc_s, I4[:, :, :], oat, oat_s)
            finish(n, xt[:, :], xt_s, oat, oat_s, po_d[i, :, :], yo_d[i, :, :])

        n = NS
        xs_t, xs_s = FB[0], FB_s[0]
        dma(SP, xs_t[0:n, :], xs_d[:, :], wr=[xs_s])
        project(n, xs_t[0:n, :], xs_s, {"k": kss_d[:, :], "v": vss_d[:, :], "ki": kis_d[:, :], "sg": sgs_d[:, :], "kv": True})
        for j in range(2):
            op(PE, lambda j=j: nc.tensor.transpose(PBb[3][:, j, 0:n], kn_bf[0:n, j * 128:(j + 1) * 128], identb[0:n, 0:n]),
               rd=[knbf_s, cst_s], wr=[PB_s[3]])
        op(PE, lambda: nc.tensor.transpose(PBb[3][:, 2, 0:n], kid_bf[0:n, :, :].rearrange("p a d -> p (a d)"), identb[0:n, 0:n]),
           rd=[kid_s, cst_s], wr=[PB_s[3]])
        op(ACT, lambda: nc.scalar.copy(out=ksT[:, :, :], in_=PBb[3][:, 0:2, 0:n]), rd=[PB_s[3]], wr=[smp_s])
        op(ACT, lambda: nc.scalar.copy(out=kisT[:, :], in_=PBb[3][:, 2, 0:n]), rd=[PB_s[3]], wr=[smp_s])

        def gather(dst_ap, src2d, col):
            return lambda: nc.gpsimd.indirect_dma_start(out=dst_ap, out_offset=None, in_=src2d,
                                                        in_offset=bass.IndirectOffsetOnAxis(ap=idxa[:, col:col + 1], axis=0))
        fence(arena_states, [sc16_s] + sctR_s)
        kiR_s = [St(), St()]; KR_s = [St(), St()]; VR_s = [St(), St()]
        fence([kiT_s], kiR_s); fence([KT_s], KR_s); fence([V_s], VR_s)
        op(POOL, lambda: nc.gpsimd.memset(sc16[:, :], -1e30), wr=[sc16_s])
        for r in range(2):
            op(POOL, lambda r=r: nc.gpsimd.memset(kiT[:, r * S_S + 2048:(r + 1) * S_S], 0.0), wr=[kiR_s[r]])
            op(POOL, lambda r=r: nc.gpsimd.memset(KT[:, :, r * S_S + 2048:(r + 1) * S_S], 0.0), wr=[KR_s[r]])
            op(POOL, lambda r=r: nc.gpsimd.memset(Vt[:, r * 17 + 16, :, 0:64], 0.0), wr=[VR_s[r]])
        def s1_loads(b):
            r = b % 2
            k0 = r * S_S
            jobsA, jobsB = [], []
            for pg in range(NPG):
                def jobA(pg=pg):
                    sl = (b * NPG + pg) % NPB
                    col = b * NPG + pg
                    dma(POOL, None, None, rd=[cst_s], wr=[pgI_s[sl]], fn=gather(pgI[sl][:, 0, :], cki_d, col))

                def jobB(pg=pg):
                    sl = (b * NPG + pg) % NPB
                    bk = 3
                    for a in range(2):
                        op(PE, lambda a=a: nc.tensor.transpose(PBb[bk][a * 64:(a + 1) * 64, 0, :], pgI[sl][:, 0, :], identb[:]),
                           rd=[pgI_s[sl], cst_s], wr=[PB_s[bk]])
                    op(ACT, lambda: nc.scalar.copy(out=kiT[:, k0 + pg * 128:k0 + (pg + 1) * 128], in_=PBb[bk][:, 0, :]),
                       rd=[PB_s[bk]], wr=[kiR_s[r]])
                jobsA.append(jobA)
                jobsB.append(jobB)
            jobs = skew(jobsA, jobsB)
            jobs.append(lambda: op(ACT, lambda: nc.scalar.copy(out=kiT[:, k0 + 2048:k0 + 2049], in_=kisT[:, b:b + 1]), rd=[smp_s], wr=[kiR_s[r]]))
            return jobs

        def skew(jobsA, jobsB, ahead=3):
            out = list(jobsA[:ahead])
            for p in range(len(jobsB)):
                out.append(jobsB[p])
                if p + ahead < len(jobsA):
                    out.append(jobsA[p + ahead])
            return out

        def make_hook(jobs, every):
            st = {"i": 0}

            def hook():
                st["i"] += 1
                if st["i"] % every == 0 and jobs:
                    jobs.pop(0)()
            return hook

        pending = s1_loads(0)
        for b in range(NS):
            r = b % 2
            k0 = r * S_S
            while pending:
                pending.pop(0)()
            pending = s1_loads(b + 1) if b + 1 < NS else []
            indexer(n, 17, sctR[r], sctR_s[r], k0=k0, ki_state=kiR_s[r], hook=make_hook(pending, 1))
            op(DVE, lambda b=b: nc.vector.scalar_tensor_tensor(out=sc16[:, 0:2049], in0=sctR[r][:, 0:2049], scalar=identf[0:16, b:b + 1],
                                                               in1=sc16[:, 0:2049], op0=ALU.mult, op1=ALU.add) if b > 0 else
               nc.vector.tensor_scalar(out=sc16[:, 0:2049], in0=sctR[r][:, 0:2049], scalar1=identf[0:16, 0:1], scalar2=None, op0=ALU.mult),
               rd=[sctR_s[r], cst_s], wr=[sc16_s])
        fence([sctR_s[1]], [sct_s])
        bisect(n, 2049, sc16[:, 0:2049], sc16_s, sct[:, 0:2049], sct_s)
        oat, oat_s = FB[1], FB_s[1]
        oatt_s16, oas_s = FB[2], FB_s[2]
        def s2_loads(b):
            r = b % 2
            k0 = r * S_S
            c0 = r * 17
            jobsA, jobsB = [], []
            for pg in range(NPG):
                def jobA(pg=pg):
                    sl = (b * NPG + pg) % NPB
                    col = b * NPG + pg
                    dma(POOL, None, None, rd=[cst_s], wr=[pgK_s[sl]], fn=gather(pgK[sl][:, :], ck_d, col))
                    dma(POOL, None, None, rd=[cst_s], wr=[pgV_s[sl]], fn=gather(pgV[sl][:, :], cv_d, col))
                jobsA.append(jobA)

                def job(pg=pg):
                    sl = (b * NPG + pg) % NPB
                    bk = 0
                    for j in range(2):
                        op(PE, lambda j=j: nc.tensor.transpose(PBb[bk][:, j, :], pgK[sl][:, j * 128:(j + 1) * 128], identb[:]),
                           rd=[pgK_s[sl], cst_s], wr=[PB_s[bk]])
                    op(ACT, lambda: nc.scalar.copy(out=KT[:, :, k0 + pg * 128:k0 + (pg + 1) * 128], in_=PBb[bk][:, 0:2, :]),
                       rd=[PB_s[bk]], wr=[KR_s[r]])
                    op(DVE, lambda: nc.vector.tensor_copy(out=Vt[:, c0 + pg, :, 0:64], in_=pgV[sl][:, :].rearrange("p (c d) -> p c d", d=64)),
                       rd=[pgV_s[sl]], wr=[VR_s[r]])
                jobsB.append(job)
            jobs = skew(jobsA, jobsB)

            def last():
                op(ACT, lambda: nc.scalar.copy(out=KT[:, :, k0 + 2048:k0 + 2049], in_=ksT[:, :, b:b + 1]), rd=[smp_s], wr=[KR_s[r]])
                op(PE, lambda: nc.tensor.matmul(PB[0][0:1, 0:256], lhsT=identb[0:16, b:b + 1], rhs=vs_bf[0:16, :], start=True, stop=True),
                   rd=[smp_s, cst_s], wr=[PB_s[0]])
                op(ACT, lambda: nc.scalar.copy(out=Vt[0:1, c0 + 16, :, 0:64], in_=PB[0][0:1, 0:256].rearrange("p (c d) -> p c d", d=64)),
                   rd=[PB_s[0]], wr=[VR_s[r]])
            jobs.append(last)
            return jobs

        pending = s2_loads(0)
        for b in range(NS):
            r = b % 2
            c0 = r * 17
            while pending:
                pending.pop(0)()
            pending = s2_loads(b + 1) if b + 1 < NS else []
            attend_sample(17, sc16, sc16_s, oat, oat_s, c0, KR_s[r], VR_s[r], hook=make_hook(pending, 1))
            if b == 0:
                op(DVE, lambda: nc.vector.tensor_scalar(out=oatt_s16[0:16, :], in0=oat[0:16, :], scalar1=identf[0:16, 0:1], scalar2=None,
                                                        op0=ALU.mult), rd=[oat_s, cst_s], wr=[oas_s])
            else:
                op(DVE, lambda b=b: nc.vector.scalar_tensor_tensor(out=oatt_s16[0:16, :], in0=oat[0:16, :], scalar=identf[0:16, b:b + 1],
                                                                   in1=oatt_s16[0:16, :], op0=ALU.mult, op1=ALU.add),
                   rd=[oat_s, cst_s], wr=[oas_s])
        fence(sctR_s, [sct_s])
        fence([sc16_s, sct_s], arena_states)
        finish(n, xs_t[0:n, :], xs_s, oatt_s16, oas_s, ps_d[:, :], ys_d[:, :])

        for i in range(NDS):
            if dcnt[i] > 0:
                POOL.wait((dsems[i], dcnt[i]))
    return nc


_NC_CACHE = {}


def _slot_block(r, i):
    m, second = i // 2, i % 2
    return 8 * m + (7 - r if second else r)


def kernel(x_prompt, x_sample, cache_k, cache_v, cache_kidx, page_table, p_prompt, p_sample,
           g_mix, w_in, g_q, g_k, g_sgu, w_s, b_s, w_o, g_ffn, w_up, w_down, g_ple, w_pg, w_p):
    f32 = np.float32
    A = lambda a: np.ascontiguousarray(np.asarray(a))
    x_prompt = A(x_prompt); x_sample = A(x_sample); p_prompt = A(p_prompt); p_sample = A(p_sample)
    ck = A(cache_k).reshape(N_PHYS * 128, 256)
    cv = A(cache_v).reshape(N_PHYS * 128, 256)
    cki = A(cache_kidx).reshape(N_PHYS * 128, 64)
    pt_all = A(page_table).astype(np.int32)
    shared = {
        "ck": ck, "cv": cv, "cki": cki,
        "w_in": A(w_in)[0], "w_o": A(w_o)[0], "w_up": A(w_up)[0], "w_down": A(w_down)[0], "w_pg": A(w_pg)[0], "w_p": A(w_p)[0],
        "g_mix": A(g_mix)[0], "g_q": A(g_q)[0], "g_k": A(g_k)[0], "g_sgu": A(g_sgu)[0], "w_s": A(w_s)[0], "b_s": A(b_s)[0],
        "g_ffn": A(g_ffn)[0], "g_ple": A(g_ple)[0],
    }
    tt = np.arange(128)[:, None]
    ss = np.arange(512)[None, :]
    in_maps = []
    for c in range(8):
        bi, r = c // 4, c % 4
        blocks = [_slot_block(r, i) for i in range(NSLOT)]
        pen = np.zeros((128, 2, 512), f32)
        pen[:, 0, :] = np.where(ss <= r * 128 + tt, 0.0, -1e30)
        pen[:, 1, :] = np.where(ss <= (3 - r) * 128 + tt, 0.0, -1e30)
        m = dict(shared)
        m["xb"] = x_prompt[bi]
        m["xo"] = np.stack([x_prompt[bi, j * 128:(j + 1) * 128] for j in blocks])
        m["po"] = np.stack([p_prompt[0, bi, j * 128:(j + 1) * 128] for j in blocks])
        m["xs"] = x_sample[c * NS:(c + 1) * NS, 0]
        m["ps"] = p_sample[0, c * NS:(c + 1) * NS, 0]
        m["pt"] = np.ascontiguousarray(pt_all[c * NS:(c + 1) * NS].reshape(-1))
        m["pen"] = pen
        in_maps.append(m)
    if "nc" not in _NC_CACHE:
        _NC_CACHE["nc"] = build()
    res = run_bass_kernel_spmd(_NC_CACHE["nc"], in_maps, core_ids=list(range(8)))
    R = res.results
    y_p = np.zeros((2, SEQ, D), f32); y_s = np.zeros((128, 1, D), f32)
    nk = np.zeros((1, 2, SEQ, 4, 64), f32); nv = np.zeros((1, 2, SEQ, 4, 64), f32)
    nki = np.zeros((1, 2, SEQ, 64), f32); nsg = np.zeros((1, 2, SEQ, D), f32)
    sk = np.zeros((1, 128, 1, 4, 64), f32); sv = np.zeros((1, 128, 1, 4, 64), f32)
    ski = np.zeros((1, 128, 1, 64), f32); ssg = np.zeros((1, 128, 1, D), f32)
    for c in range(8):
        bi, r = c // 4, c % 4
        o = R[c]
        for i in range(NSLOT):
            j = _slot_block(r, i)
            sl = slice(j * 128, (j + 1) * 128)
            y_p[bi, sl] = o["yo"][i]
            nk[0, bi, sl] = o["ko"][i].reshape(128, 4, 64)
            nv[0, bi, sl] = o["vo"][i].reshape(128, 4, 64)
            nki[0, bi, sl] = o["kio"][i]
            nsg[0, bi, sl] = o["sgo"][i]
        s2 = slice(c * NS, (c + 1) * NS)
        y_s[s2, 0] = o["ys"]
        sk[0, s2, 0] = o["kss"].reshape(NS, 4, 64)
        sv[0, s2, 0] = o["vss"].reshape(NS, 4, 64)
        ski[0, s2, 0] = o["kis"]
        ssg[0, s2, 0] = o["sgs"]
    return (y_p, y_s, nk, nv, nki, nsg, sk, sv, ski, ssg)
```

```python
import contextlib
import numpy as np
import concourse.bass as bass
import concourse.mybir as mybir
from concourse.bass_utils import run_bass_kernel_spmd

F32 = mybir.dt.float32
BF16 = mybir.dt.bfloat16
I32 = mybir.dt.int32
ALU = mybir.AluOpType
AF = mybir.ActivationFunctionType
AX = mybir.AxisListType

D = 1024
SEQ = 8192
NBLK = 64
NSLOT = 16
NS = 16
NPG = 16
S_S = 2176
N_PHYS = 2560
INW = 6216
DFF = 4096
PLE = 256
EPS = 1e-6
ATTN_SCALE = 64 ** -0.5
IDXS = (64 ** -0.5) * (8 ** -0.5)
NIT = 20
BRK = 16.0
NEGM = -30000.0
GRPS = [(0, 512), (512, 512), (1024, 512), (1536, 512), (2048, 72), (2120, 512), (2632, 512),
        (3144, 512), (3656, 512), (4168, 512), (4680, 512), (5192, 512), (5704, 512)]


class St:
    __slots__ = ("w", "r")

    def __init__(self):
        self.w = None
        self.r = {}


class Eng:
    def __init__(self, nc, es, name, h):
        self.h = h
        self.name = name
        self.sem = es.enter_context(nc.semaphore("e_" + name))
        self.cnt = 0
        self.seen = {}

    def wait(self, tok):
        sem, val = tok
        if self.seen.get(sem.num, 0) < val:
            self.h.wait_ge(sem, val)
            self.seen[sem.num] = val


def build():
    nc = bass.Bass("TRN2", target_bir_lowering=False)
    dt_in = lambda n, s, d=F32: nc.dram_tensor(n, s, d, kind="ExternalInput").ap()
    dt_out = lambda n, s, d=F32: nc.dram_tensor(n, s, d, kind="ExternalOutput").ap()
    xb_d = dt_in("xb", [SEQ, D])
    xo_d = dt_in("xo", [NSLOT, 128, D])
    po_d = dt_in("po", [NSLOT, 128, PLE])
    xs_d = dt_in("xs", [NS, D])
    ps_d = dt_in("ps", [NS, PLE])
    ckv_d = dt_in("ckv", [N_PHYS * 128, 512])
    cki_d = dt_in("cki", [N_PHYS * 128, 64])
    pt_d = dt_in("pt", [NS * NPG], I32)
    pen_d = dt_in("pen", [128, 2, 512])
    win_d = dt_in("w_in", [D, INW])
    wo_d = dt_in("w_o", [D, D])
    wup_d = dt_in("w_up", [D, DFF])
    wdn_d = dt_in("w_down", [DFF, D])
    wpg_d = dt_in("w_pg", [D, D])
    wp_d = dt_in("w_p", [PLE, D])
    gmix_d = dt_in("g_mix", [D])
    gq_d = dt_in("g_q", [64])
    gk_d = dt_in("g_k", [64])
    gsgu_d = dt_in("g_sgu", [D])
    ws_d = dt_in("w_s", [8, 128, 128])
    bs_d = dt_in("b_s", [8, 128])
    gffn_d = dt_in("g_ffn", [D])
    gple_d = dt_in("g_ple", [D])

    yo_d = dt_out("yo", [NSLOT, 128, D])
    ko_d = dt_out("ko", [NSLOT, 128, 256])
    vo_d = dt_out("vo", [NSLOT, 128, 256])
    kio_d = dt_out("kio", [NSLOT, 128, 64])
    sgo_d = dt_out("sgo", [NSLOT, 128, D])
    ys_d = dt_out("ys", [NS, D])
    kss_d = dt_out("kss", [NS, 256])
    vss_d = dt_out("vss", [NS, 256])
    kis_d = dt_out("kis", [NS, 64])
    sgs_d = dt_out("sgs", [NS, D])

    def scr(n, shape):
        return nc.dram_tensor(n, shape, BF16, kind="Internal").ap()
    win_s = [scr("win_s%d" % i, [128, 8, w]) for i, (o, w) in enumerate(GRPS)]
    wo_s = [scr("wo_s%d" % i, [128, 8, 512]) for i in range(2)]
    wup_s = [scr("wup_s%d" % i, [128, 8, 512]) for i in range(8)]
    wdn_s = [scr("wdn_s%d" % i, [128, 8, 512]) for i in range(8)]
    wpg_s = [scr("wpg_s%d" % i, [128, 8, 512]) for i in range(2)]
    wp_s = [scr("wp_s%d" % i, [128, 2, 512]) for i in range(2)]

    es = contextlib.ExitStack()
    with es:
        PE = Eng(nc, es, "pe", nc.tensor)
        ACT = Eng(nc, es, "act", nc.scalar)
        DVE = Eng(nc, es, "dve", nc.vector)
        POOL = Eng(nc, es, "pool", nc.gpsimd)
        SP = Eng(nc, es, "sp", nc.sync)
        NDS = 24
        dsems = [es.enter_context(nc.semaphore("d%d" % i)) for i in range(NDS)]
        dcnt = [0] * NDS
        dstate = {"i": 0}

        def deps_of(rd, wr):
            deps = []
            for s in rd:
                if s.w is not None:
                    deps.append(s.w)
            for s in wr:
                if s.w is not None:
                    deps.append(s.w)
                deps.extend(s.r.values())
            return deps

        def op(E, fn, rd=(), wr=()):
            for tok in deps_of(rd, wr):
                if E is PE and tok[0] is PE.sem:
                    continue
                E.wait(tok)
            ins = fn()
            E.cnt += 1
            ins.then_inc(E.sem, 1)
            tok = (E.sem, E.cnt)
            for s in rd:
                s.r[E.name] = tok
            for s in wr:
                s.w = tok
                s.r = {}
            return tok

        def dma(Q, out, in_, rd=(), wr=(), fn=None):
            for tok in deps_of(rd, wr):
                Q.wait(tok)
            i = dstate["i"]
            dstate["i"] = (i + 1) % NDS
            if dcnt[i] > 0:
                Q.wait((dsems[i], dcnt[i]))
            if fn is None:
                ins = Q.h.dma_start(out=out, in_=in_)
            else:
                ins = fn()
            dcnt[i] += 16
            ins.then_inc(dsems[i], 16)
            tok = (dsems[i], dcnt[i])
            for s in rd:
                s.r["dma%d" % i] = tok
            for s in wr:
                s.w = tok
                s.r = {}
            return tok

        def fence(frm, to):
            for t in to:
                for f in frm:
                    if f is t:
                        continue
                    if f.w is not None:
                        t.r["f%d" % id(f)] = f.w
                    for k, v in list(f.r.items()):
                        t.r["f%d%s" % (id(f), k)] = v

        def sb(name, shape, dt):
            return nc.alloc_sbuf_tensor("sb_" + name, shape, dt)

        KT = sb("KT", [128, 2, SEQ], BF16); KT_s = St()
        Vt = sb("Vt", [128, NBLK, 4, 65], BF16); V_s = St()
        kiT = sb("kiT", [128, SEQ], BF16); kiT_s = St()
        arena_base = nc.sbuf_base
        scores = sb("scores", [128, SEQ], F32); sc_s = St()
        al = lambda n, shape, dt, off: nc.alloc_sbuf_tensor_at("al_" + n, shape, dt, offset=arena_base + off)
        h_bf = al("h_bf", [128, DFF], BF16, 0); hbf_s = St()
        hT = al("hT", [128, 32, 128], BF16, 8192); hT_s = St()
        xn_bf = al("xn_bf", [128, D], BF16, 16384); xnbf_s = St()
        xnT = al("xnT", [128, 8, 128], BF16, 18432); xnT_s = St()
        vn_bf = al("vn_bf", [128, D], BF16, 20480); vnbf_s = St()
        wkvk = al("wkvk", [128, 8, 576], BF16, 22528); wkvk_s = St()
        arena_states = [hbf_s, hT_s, xnbf_s, xnT_s, vnbf_s, wkvk_s]
        sc16 = scores[0:16, 0:S_S]; sc16_s = St()
        sct = scores[0:16, S_S:2 * S_S]; sct_s = St()
        sctR = [sct, scores[0:16, 2 * S_S:3 * S_S]]; sctR_s = [sct_s, St()]
        WS = [sb("ws%d" % i, [128, 8, 512], BF16) for i in range(3)]
        WS_s = [St() for _ in range(3)]
        FB0 = sb("fb0", [128, D], F32)
        FB12 = sb("fb12", [128, 2 * D], F32)
        FB3 = sb("fb3", [128, D], F32)
        FB4 = sb("fb4", [128, D], F32)
        FB = [FB0[:, :], FB12[:, 0:D], FB12[:, D:2 * D], FB3[:, :], FB4[:, :]]
        FB_s = [St() for _ in range(5)]
        junk8 = FB12[:, :].bitcast(mybir.dt.uint8)
        junki8 = FB12[:, :].bitcast(mybir.dt.int8)
        gmixT = sb("gmixT", [128, 8], F32); gffnT = sb("gffnT", [128, 8], F32); gpleT = sb("gpleT", [128, 8], F32)
        gsgub = sb("gsgub", [128, D], F32)
        coefs = sb("coefs", [16, 16], F32)
        gqb = sb("gqb", [128, 64], F32); gkb = sb("gkb", [128, 64], F32)
        cst_s = St()
        qn_bf = sb("qn_bf", [128, 2, 4, 128], BF16); qn_s = St()
        qT = sb("qT", [128, 4, 4, 128], BF16); qT_s = St()
        qi_bf = sb("qi_bf", [128, 512], BF16); qibf_s = St()
        qiT = sb("qiT", [128, 4, 128], BF16); qiT_s = St()
        absw = sb("absw", [128, 8], F32); sgnw = sb("sgnw", [128, 8], F32); w_s_ = St()
        kn_bf = sb("kn_bf", [128, 256], BF16); knbf_s = St()
        v_bf = sb("v_bf", [128, 256], BF16); vbf_s = St()
        kid_bf = sb("kid_bf", [128, 2, 64], BF16); kid_s = St()
        k32 = sb("k32", [128, 512], F32); k32_s = St()
        o32 = sb("o32", [128, 512], F32); o32_s = St()
        ki32 = sb("ki32", [128, 72], F32); ki32_s = St()
        rt = [sb("rt%d" % i, [128, 512], F32) for i in range(3)]; rt_s = [St(), St(), St()]
        PT = [sb("PT%d" % i, [128, 512], BF16) for i in range(3)]; PT_s = [St(), St(), St()]
        mT = [sb("mT%d" % i, [128, 4, 128], BF16) for i in range(2)]; mT_s = [St(), St()]
        MBg = [sb("MBg%d" % i, [128, 512], BF16) for i in range(2)]; MB_s = [St(), St()]
        identb = sb("identb", [128, 128], BF16); identf = sb("identf", [16, 16], F32)
        I4 = sb("I4", [128, 4, 128], BF16); I416 = sb("I416", [16, 4, 16], BF16)
        wsT = sb("wsT", [128, 8, 128], BF16); bsT = sb("bsT", [128, 8], F32)
        pen = sb("pen", [128, 2, 512], F32)
        p32 = sb("p32", [128, PLE], F32); p32_s = St()
        p_bf = sb("p_bf", [128, PLE], BF16); pbf_s = St()
        pT = sb("pT", [128, 2, 128], BF16); pT_s = St()
        sm = sb("sm", [128, 64], F32); sm_s = St()
        lo = sb("lo", [128, 1], F32); mid = sb("mid", [128, 1], F32); cntt = sb("cntt", [128, 1], F32)
        geq = sb("geq", [128, 1], F32); bis_s = St(); junk_s = St()
        cnta = sb("cnta", [128, 1], F32); cnta_s = St(); mid_s = St(); junka_s = St()
        mhalf = sb("mhalf", [128, 16], F32)
        ss16 = sb("ss16", [128, 16], F32); ss16_s = St()
        rs16 = sb("rs16", [128, 16], F32); rs16_s = St()
        ptb = sb("ptb", [128, NS * NPG], I32); idxa = sb("idxa", [128, NS * NPG], I32); iop = sb("iop", [128, 1], I32)
        NPB = 4
        pgKV = [sb("pgKV%d" % i, [128, 512], BF16) for i in range(NPB)]; pgK_s = [St() for _ in range(NPB)]
        pgK = [t[:, 0:256] for t in pgKV]
        pgV = [t[:, 256:512] for t in pgKV]
        pgV_s = pgK_s
        pgI = [sb("pgI%d" % i, [128, 2, 64], BF16) for i in range(NPB)]; pgI_s = [St() for _ in range(NPB)]
        ksT = sb("ksT", [128, 2, 16], BF16); kisT = sb("kisT", [128, 16], BF16); vs_bf = sb("vs_bf", [16, 256], BF16)
        smp_s = St()
        ws32 = sb("ws32", [128, 128], F32); ws32_s = St()
        wsb = sb("wsb", [128, 128], BF16); wsb_s = St()

        PB = [nc.alloc_psum_tensor("pb%d" % i, [128, 512], F32) for i in range(8)]
        PB_s = [St() for _ in range(8)]
        PBb = [PB[i][:].bitcast(BF16).rearrange("p (a b) -> p a b", a=8) for i in range(8)]

        def transposes(src_ap_fn, nblk, n, bank0, dst_fn, dst_states, src_states, evac=None, gainT=None):
            for b0 in range(0, nblk, 8):
                nb = min(8, nblk - b0)
                bank = 2 + (bank0 + b0 // 8) % 2
                if gainT is not None:
                    for j in range(nb):
                        op(PE, lambda j=j: nc.tensor.transpose(PBb[bank][:, j, 0:n], src_ap_fn(b0 + j), identb[0:n, 0:n]),
                           rd=list(src_states) + [cst_s], wr=[PB_s[bank]])
                    op(DVE, lambda: nc.vector.tensor_tensor(out=dst_fn(b0, nb), in0=PBb[bank][:, 0:nb, 0:n],
                                                            in1=gainT[:, b0:b0 + nb].unsqueeze(2).to_broadcast([128, nb, n]), op=ALU.mult),
                       rd=[PB_s[bank], cst_s], wr=dst_states)
                    continue
                for j in range(nb):
                    tok = op(PE, lambda j=j: nc.tensor.transpose(PBb[bank][:, j, 0:n], src_ap_fn(b0 + j), identb[0:n, 0:n]),
                             rd=list(src_states) + [cst_s], wr=[PB_s[bank]])
                E = evac or ACT
                if E is ACT:
                    op(ACT, lambda: nc.scalar.copy(out=dst_fn(b0, nb), in_=PBb[bank][:, 0:nb, 0:n]),
                       rd=[PB_s[bank]], wr=dst_states)
                else:
                    op(DVE, lambda: nc.vector.tensor_copy(out=dst_fn(b0, nb), in_=PBb[bank][:, 0:nb, 0:n]),
                       rd=[PB_s[bank]], wr=dst_states)

        RS = {"act": False}

        def rstd_from_ss(ss_ap, out_ap, n, ncol, inv_d, rd_s, wr_s):
            op(DVE, lambda: nc.vector.tensor_scalar(out=ss_ap, in0=ss_ap, scalar1=inv_d, scalar2=EPS, op0=ALU.mult, op1=ALU.add),
               rd=[], wr=[rd_s])
            if RS["act"]:
                op(ACT, lambda: nc.scalar.activation(out=out_ap, in_=ss_ap, func=AF.Sqrt), rd=[rd_s], wr=[wr_s])
                op(DVE, lambda: nc.vector.reciprocal(out=out_ap, in_=out_ap), rd=[], wr=[wr_s])
            else:
                op(POOL, lambda: nc.gpsimd.tensor_tensor(out=out_ap, in0=ss_ap, in1=mhalf[0:n, 0:ncol], op=ALU.pow),
                   rd=[rd_s, cst_s], wr=[wr_s])

        def rmsnorm_to_T(x_ap, x_s, gain, n):
            op(DVE, lambda: nc.vector.scalar_tensor_tensor(out=xn_bf[0:n, :], in0=x_ap, scalar=1.0, in1=x_ap, op0=ALU.mult, op1=ALU.mult,
                                                           accum_out=ss16[0:n, 0:1]),
               rd=[x_s], wr=[xnbf_s, ss16_s])
            rstd_from_ss(ss16[0:n, 0:1], rs16[0:n, 0:1], n, 1, 1.0 / D, ss16_s, rs16_s)
            op(DVE, lambda: nc.vector.tensor_scalar(out=xn_bf[0:n, :], in0=x_ap, scalar1=rs16[0:n, 0:1], scalar2=None, op0=ALU.mult),
               rd=[x_s, rs16_s], wr=[xnbf_s])
            transposes(lambda j: xn_bf[0:n, j * 128:(j + 1) * 128], 8, n, 0,
                       lambda b0, nb: xnT[:, b0:b0 + nb, 0:n], [xnT_s], [xnbf_s], gainT=gain)

        wseq = []
        one_slot = ([(win_s[i], 8, GRPS[i][1]) for i in range(13)] + [(wo_s[i], 8, 512) for i in range(2)]
                    + [(wup_s[i], 8, 512) for i in range(8)] + [(wdn_s[i], 8, 512) for i in range(8)]
                    + [(wpg_s[i], 8, 512) for i in range(2)] + [(wp_s[i], 2, 512) for i in range(2)])
        for _ in range(NSLOT + 1):
            wseq.extend(one_slot)
        wst = {"issued": 0, "next": 0}
        conv_s = St()

        def w_issue(upto):
            while wst["issued"] < min(upto, len(wseq)):
                k = wst["issued"]
                src, kc, ncol = wseq[k]
                sl = k % 3
                dma(SP, WS[sl][:, 0:kc, 0:ncol], src[:, :, :], rd=[conv_s], wr=[WS_s[sl]])
                wst["issued"] += 1

        def w_get():
            k = wst["next"]
            wst["next"] += 1
            w_issue(k + 1)
            return k % 3, k

        def w_done(k):
            w_issue(k + 3)

        def dense(lhsT_fn, kcs, n, ncols_list, bank_fn, lhs_states, after_fn):
            for gi, ncol in enumerate(ncols_list):
                sl, k = w_get()
                bank = bank_fn(gi)
                for kc in range(kcs):
                    op(PE, lambda kc=kc: nc.tensor.matmul(PB[bank][0:n, 0:ncol], lhsT=lhsT_fn(kc), rhs=WS[sl][:, kc, 0:ncol],
                                                          start=(kc == 0), stop=(kc == kcs - 1)),
                       rd=list(lhs_states) + [WS_s[sl]], wr=[PB_s[bank]])
                w_done(k)
                after_fn(gi, bank)

        with nc.allow_non_contiguous_dma(reason="tiny constant loads"):
            for (dst, src) in ((gsgub, gsgu_d), (gqb, gq_d), (gkb, gk_d)):
                dma(SP, dst[:], src.partition_broadcast(128), wr=[cst_s])
            for (dst, src) in ((gmixT, gmix_d), (gffnT, gffn_d), (gpleT, gple_d)):
                dma(SP, None, None, wr=[cst_s], fn=lambda dst=dst, src=src: nc.sync.dma_start(out=dst[:], in_=src.rearrange("(k p) -> p k", p=128)))
            dma(SP, None, None, wr=[cst_s], fn=lambda: nc.sync.dma_start(
                out=coefs[:, 0:8], in_=ws_d[:, 0, 0:1].rearrange("g a -> (g a)").partition_broadcast(16)))
            dma(SP, None, None, wr=[cst_s], fn=lambda: nc.sync.dma_start(
                out=coefs[:, 8:16], in_=bs_d[:, 0:1].rearrange("g a -> (g a)").partition_broadcast(16)))
            dma(SP, bsT[:], bs_d.rearrange("g t -> t g"), wr=[cst_s], fn=lambda: nc.sync.dma_start(
                out=bsT[:], in_=bs_d.rearrange("g t -> t g")))
            dma(SP, pen[:], pen_d[:, :, :], wr=[cst_s])
            dma(SP, ptb[:], pt_d.partition_broadcast(128), wr=[cst_s])
        op(POOL, lambda: nc.gpsimd.memset(identb[:], 1.0), wr=[cst_s])
        op(POOL, lambda: nc.gpsimd.affine_select(out=identb[:], in_=identb[:], pattern=[[-1, 128]], compare_op=ALU.is_equal,
                                                 fill=0.0, base=0, channel_multiplier=1), wr=[cst_s])
        op(POOL, lambda: nc.gpsimd.memset(identf[:], 1.0), wr=[cst_s])
        op(POOL, lambda: nc.gpsimd.affine_select(out=identf[:], in_=identf[:], pattern=[[-1, 16]], compare_op=ALU.is_equal,
                                                 fill=0.0, base=0, channel_multiplier=1), wr=[cst_s])
        op(POOL, lambda: nc.gpsimd.memset(mhalf[:], -0.5), wr=[cst_s])
        op(POOL, lambda: nc.gpsimd.iota(iop[:], pattern=[[0, 1]], base=0, channel_multiplier=1), wr=[cst_s])
        for g in range(4):
            op(DVE, lambda g=g: nc.vector.tensor_copy(out=I4[:, g, :], in_=identb[:]), rd=[], wr=[cst_s])
            op(DVE, lambda g=g: nc.vector.tensor_copy(out=I416[:, g, :], in_=identb[0:16, 0:16]), rd=[], wr=[cst_s])
        op(DVE, lambda: nc.vector.tensor_scalar(out=gqb[:], in0=gqb[:], scalar1=ATTN_SCALE, scalar2=None, op0=ALU.mult), wr=[cst_s])
        op(DVE, lambda: nc.vector.tensor_scalar(out=idxa[:], in0=ptb[:], scalar1=128, scalar2=iop[:, 0:1], op0=ALU.mult, op1=ALU.add),
           wr=[cst_s])
        op(POOL, lambda: nc.gpsimd.memset(qT[:], 0.0), wr=[qT_s])
        op(POOL, lambda: nc.gpsimd.memset(Vt[:], 0.0), wr=[V_s])
        op(POOL, lambda: nc.gpsimd.memset(Vt[:, :, :, 64:65], 1.0), wr=[V_s])
        for g in range(8):
            dma(SP, ws32[:], ws_d[g, :, :], wr=[ws32_s])
            op(POOL, lambda: nc.gpsimd.affine_select(out=ws32[:], in_=ws32[:], pattern=[[-1, 128]], compare_op=ALU.is_ge,
                                                     fill=0.0, base=0, channel_multiplier=1), wr=[ws32_s])
            op(DVE, lambda: nc.vector.tensor_copy(out=wsb[:], in_=ws32[:]), rd=[ws32_s], wr=[wsb_s])
            op(PE, lambda: nc.tensor.transpose(PBb[3][:, 0, :], wsb[:], identb[:]), rd=[wsb_s, cst_s], wr=[PB_s[3]])
            op(ACT, lambda g=g: nc.scalar.copy(out=wsT[:, g, :], in_=PBb[3][:, 0, :]), rd=[PB_s[3]], wr=[cst_s])

        conv_jobs = []
        conv_toks = []

        def conv(dst, src2d, kc, col0, ncol, first=False):
            for k in range(kc):
                job = (lambda k=k: conv_toks.append(dma(
                    POOL, None, None, wr=[],
                    fn=lambda: nc.gpsimd.dma_start(out=dst[:, k, :], in_=src2d[k * 128:(k + 1) * 128, col0:col0 + ncol]))))
                if first:
                    job()
                else:
                    conv_jobs.append(job)
        for i, (o, w) in enumerate(GRPS):
            conv(win_s[i], win_d, 8, o, w, first=(i in (2, 4)))
        for i in range(2):
            conv(wo_s[i], wo_d, 8, i * 512, 512)
        for i in range(8):
            conv(wup_s[i], wup_d, 8, i * 512, 512)
        for i in range(8):
            nn, kg = i // 4, i % 4
            conv(wdn_s[i], wdn_d[kg * 1024:(kg + 1) * 1024, :], 8, nn * 512, 512)
        for i in range(2):
            conv(wpg_s[i], wpg_d, 8, i * 512, 512)
        for i in range(2):
            conv(wp_s[i], wp_d, 2, i * 512, 512)
        for tok in conv_toks:
            SP.wait(tok)

        def head_norm(src32, n, nh, gain, out_fn, out_states, e=None):
            v3 = src32.rearrange("p (h d) -> p h d", d=64)
            if e is not None:
                tmp4 = o32[0:n, 0:nh * 64].rearrange("p (e g d) -> p e g d", e=e, d=64)
                op(DVE, lambda: nc.vector.tensor_tensor(out=o32[0:n, 0:nh * 64], in0=src32, in1=src32, op=ALU.mult), rd=[k32_s], wr=[o32_s])
                op(DVE, lambda: nc.vector.tensor_reduce(out=ss16[0:n, 0:nh], in_=o32[0:n, 0:nh * 64].rearrange("p (h d) -> p h d", d=64),
                                                        axis=AX.X, op=ALU.add), rd=[o32_s], wr=[ss16_s])
                rstd_from_ss(ss16[0:n, 0:nh], rs16[0:n, 0:nh], n, nh, 1.0 / 64, ss16_s, rs16_s)
                op(DVE, lambda: nc.vector.tensor_tensor(out=o32[0:n, 0:nh * 64].rearrange("p (h d) -> p h d", d=64), in0=v3,
                                                        in1=rs16[0:n, 0:nh].unsqueeze(2).to_broadcast([n, nh, 64]), op=ALU.mult),
                   rd=[k32_s, rs16_s], wr=[o32_s])
                for ee in range(e):
                    op(DVE, lambda ee=ee: nc.vector.tensor_tensor(out=out_fn(ee), in0=tmp4[:, ee, :, :],
                                                                  in1=gain[0:n, :].unsqueeze(1).to_broadcast([n, nh // e, 64]), op=ALU.mult),
                       rd=[o32_s, cst_s], wr=out_states)
                return
            op(DVE, lambda: nc.vector.tensor_tensor(out=o32[0:n, 0:nh * 64], in0=src32, in1=src32, op=ALU.mult), rd=[k32_s], wr=[o32_s])
            op(DVE, lambda: nc.vector.tensor_reduce(out=ss16[0:n, 0:nh], in_=o32[0:n, 0:nh * 64].rearrange("p (h d) -> p h d", d=64),
                                                    axis=AX.X, op=ALU.add), rd=[o32_s], wr=[ss16_s])
            rstd_from_ss(ss16[0:n, 0:nh], rs16[0:n, 0:nh], n, nh, 1.0 / 64, ss16_s, rs16_s)
            op(DVE, lambda: nc.vector.tensor_tensor(out=o32[0:n, 0:nh * 64].rearrange("p (h d) -> p h d", d=64), in0=v3,
                                                    in1=rs16[0:n, 0:nh].unsqueeze(2).to_broadcast([n, nh, 64]), op=ALU.mult),
               rd=[k32_s, rs16_s], wr=[o32_s])
            op(DVE, lambda: nc.vector.tensor_tensor(out=out_fn(), in0=o32[0:n, 0:nh * 64].rearrange("p (h d) -> p h d", d=64),
                                                    in1=gain[0:n, :].unsqueeze(1).to_broadcast([n, nh, 64]), op=ALU.mult),
               rd=[o32_s, cst_s], wr=out_states)

        def kv_append(blk, n, k_ap, k_s, v_ap, v_s, kid_ap, kid_s_):
            c0 = blk * 128
            for j in range(2):
                op(PE, lambda j=j: nc.tensor.transpose(PBb[3][:, j, 0:n], k_ap[0:n, j * 128:(j + 1) * 128], identb[0:n, 0:n]),
                   rd=[k_s, cst_s], wr=[PB_s[3]])
            op(PE, lambda: nc.tensor.transpose(PBb[3][:, 2, 0:n], kid_ap[0:n, :, :].rearrange("p a d -> p (a d)"), identb[0:n, 0:n]),
               rd=[kid_s_, cst_s], wr=[PB_s[3]])
            op(ACT, lambda: nc.scalar.copy(out=KT[:, :, c0:c0 + n], in_=PBb[3][:, 0:2, 0:n]), rd=[PB_s[3]], wr=[KT_s])
            op(ACT, lambda: nc.scalar.copy(out=kiT[:, c0:c0 + n], in_=PBb[3][:, 2, 0:n]), rd=[PB_s[3]], wr=[kiT_s])
            op(ACT, lambda: nc.scalar.copy(out=Vt[0:n, blk, :, 0:64], in_=v_ap[0:n, :].rearrange("p (c d) -> p c d", d=64)),
               rd=[v_s], wr=[V_s])

        def indexer(n, nch, sc_ap, sc_state, k0=0, ki_state=None, hook=None):
            ki_state = ki_state or kiT_s
            S = nch * 128
            j = 0
            for g0 in range(0, S, 512):
                w = min(512, S - g0)
                for h in range(8):
                    if hook is not None:
                        hook()
                    bank = j % 3
                    r = rt[j % 3]
                    rs = rt_s[j % 3]
                    j += 1
                    pb = (h % 2) * 64
                    op(PE, lambda: nc.tensor.matmul(PB[bank][0:n, 0:w], lhsT=qiT[pb:pb + 64, h // 2, 0:n], rhs=kiT[pb:pb + 64, k0 + g0:k0 + g0 + w],
                                                    start=True, stop=True), rd=[qiT_s, ki_state], wr=[PB_s[bank]])
                    op(ACT, lambda: nc.scalar.activation(out=r[0:n, 0:w], in_=PB[bank][0:n, 0:w], func=AF.Relu, scale=absw[0:n, h:h + 1]),
                       rd=[PB_s[bank], w_s_], wr=[rs])
                    if h == 0:
                        op(DVE, lambda: nc.vector.tensor_scalar(out=sc_ap[0:n, g0:g0 + w], in0=r[0:n, 0:w], scalar1=sgnw[0:n, 0:1], scalar2=None,
                                                                op0=ALU.mult), rd=[rs, w_s_], wr=[sc_state])
                    else:
                        op(DVE, lambda: nc.vector.scalar_tensor_tensor(out=sc_ap[0:n, g0:g0 + w], in0=r[0:n, 0:w], scalar=sgnw[0:n, h:h + 1],
                                                                       in1=sc_ap[0:n, g0:g0 + w], op0=ALU.mult, op1=ALU.add),
                           rd=[rs, w_s_], wr=[sc_state])

        def bisect(n, S, sc_ap, sc_state, junk_ap, junk_state, junk_act=None):
            op(DVE, lambda: nc.vector.memset(lo[0:n, :], -BRK), wr=[bis_s])
            split = junk_act is not None and S >= 2048
            S1 = (int(S * 0.46) // 128) * 128 if split else S
            S2 = S - S1
            wd = BRK
            for it in range(NIT):
                op(DVE, lambda: nc.vector.tensor_scalar(out=mid[0:n, :], in0=lo[0:n, :], scalar1=wd, scalar2=None, op0=ALU.add),
                   rd=[bis_s], wr=[bis_s, mid_s])
                if split:
                    op(ACT, lambda: nc.scalar.activation(out=junk_act[0:n, S1:S], in_=sc_ap[0:n, S1:S], func=AF.Sign, bias=mid[0:n, 0:1], scale=-1.0,
                                                         accum_out=cnta[0:n, 0:1]), rd=[sc_state, mid_s], wr=[junka_s, cnta_s])
                op(DVE, lambda: nc.vector.tensor_scalar(out=junk_ap[0:n, 0:S1], in0=sc_ap[0:n, 0:S1], scalar1=mid[0:n, 0:1], scalar2=None, op0=ALU.is_ge,
                                                        op1=ALU.add, accum_out=cntt[0:n, 0:1]),
                   rd=[sc_state, bis_s], wr=[junk_state, bis_s])
                thr = 255.5
                if split:
                    op(DVE, lambda: nc.vector.scalar_tensor_tensor(out=cntt[0:n, :], in0=cnta[0:n, :], scalar=-0.5, in1=cntt[0:n, :], op0=ALU.mult,
                                                                   op1=ALU.add), rd=[cnta_s, bis_s], wr=[bis_s])
                    thr = 255.5 - S2 / 2.0
                op(DVE, lambda: nc.vector.tensor_scalar(out=geq[0:n, :], in0=cntt[0:n, :], scalar1=thr, scalar2=wd, op0=ALU.is_ge,
                                                        op1=ALU.mult), rd=[bis_s], wr=[bis_s])
                op(DVE, lambda: nc.vector.tensor_tensor(out=lo[0:n, :], in0=lo[0:n, :], in1=geq[0:n, :], op=ALU.add), rd=[bis_s, mid_s], wr=[bis_s])
                wd = wd / 2.0

        def attend(n, nch, sc_ap, sc_state, sel_ap, out_ap, out_state, ch0=0, kt_state=None, v_state=None, hook=None):
            kt_state = kt_state or KT_s
            v_state = v_state or V_s
            steps = [(ch, c) for ch in range(nch) for c in range(4)]

            def prep_group(gi):
                w = min(512, nch * 128 - gi * 512)
                ng = w // 128
                mb = MBg[gi % 2]
                bT = 0
                op(DVE, lambda: nc.vector.tensor_scalar(out=mb[0:n, 0:w], in0=sc_ap[0:n, gi * 512:gi * 512 + w], scalar1=lo[0:n, 0:1],
                                                        scalar2=None, op0=ALU.is_ge), rd=[sc_state, bis_s], wr=[MB_s[gi % 2]])
                for cj in range(ng):
                    op(PE, lambda cj=cj: nc.tensor.transpose(PBb[bT][:, cj, 0:n], mb[0:n, cj * 128:(cj + 1) * 128], identb[0:n, 0:n]),
                       rd=[MB_s[gi % 2], cst_s], wr=[PB_s[bT]])
                op(DVE, lambda: nc.vector.tensor_copy(out=mT[gi % 2][:, 0:ng, 0:n], in_=PBb[bT][:, 0:ng, 0:n]), rd=[PB_s[bT]], wr=[mT_s[gi % 2]])

            def emit_qk(k):
                ch, c = steps[k]
                gi, cj = ch // 4, ch % 4
                if cj == 0 and c == 0:
                    prep_group(gi)
                bank = 1 + (k % 3)
                pb = (c % 2) * 64
                outv = PB[bank][:, 0:4 * n].rearrange("p (g t) -> p g t", g=4)
                op(PE, lambda: nc.tensor.matmul(outv, lhsT=KT[:, c // 2, (ch0 + ch) * 128:(ch0 + ch + 1) * 128], rhs=qT[:, c, :, 0:n],
                                                start=True, stop=True), rd=[kt_state, qT_s], wr=[PB_s[bank]])

            emit_qk(0)
            if len(steps) > 1:
                emit_qk(1)
            for k in range(len(steps)):
                ch, c = steps[k]
                gi, cj = ch // 4, ch % 4
                if hook is not None:
                    hook()
                if k + 2 < len(steps):
                    emit_qk(k + 2)
                bank = 1 + (k % 3)
                pt_ = PT[k % 3]
                pts = PT_s[k % 3]
                op(ACT, lambda: nc.scalar.activation(out=pt_[:, 0:4 * n], in_=PB[bank][:, 0:4 * n], func=AF.Exp), rd=[PB_s[bank]], wr=[pts])
                pt3 = pt_[:, 0:4 * n].rearrange("p (g t) -> p g t", g=4)
                op(DVE, lambda: nc.vector.tensor_tensor(out=pt3, in0=pt3, in1=mT[gi % 2][:, cj, 0:n].unsqueeze(1).to_broadcast([128, 4, n]),
                                                        op=ALU.mult), rd=[mT_s[gi % 2]], wr=[pts])
                for g in range(4):
                    op(PE, lambda g=g: nc.tensor.matmul(PB[4 + c][0:n, g * 65:(g + 1) * 65], lhsT=pt_[:, g * n:(g + 1) * n], rhs=Vt[:, ch0 + ch, c, :],
                                                        start=(ch == 0 and g == 0), stop=(ch == nch - 1), skip_group_check=True),
                       rd=[pts, v_state], wr=[PB_s[4 + c]])
            for c in range(4):
                acc = PB[4 + c][0:n, 0:260].rearrange("p (g e) -> p g e", e=65)
                op(DVE, lambda: nc.vector.reciprocal(out=sm[0:n, 4 * c:4 * c + 4], in_=acc[:, :, 64]), rd=[PB_s[4 + c]], wr=[sm_s])
                op(DVE, lambda: nc.vector.tensor_tensor(out=out_ap[0:n, c * 256:(c + 1) * 256].rearrange("p (g d) -> p g d", d=64),
                                                        in0=acc[:, :, 0:64], in1=sm[0:n, 4 * c:4 * c + 4].unsqueeze(2).to_broadcast([n, 4, 64]),
                                                        op=ALU.mult), rd=[PB_s[4 + c], sm_s], wr=[out_state])

        STh_s = [[St(), St()], [St(), St()]]

        def attend_sample(nch, sc_ap, sc_state, out_ap, out_state, ch0, kt_state, v_state, hook=None):
            n = 16

            def prep_group(gi):
                w = min(512, nch * 128 - gi * 512)
                ng = w // 128
                mb = MBg[gi % 2]
                op(DVE, lambda: nc.vector.tensor_scalar(out=mb[0:n, 0:w], in0=sc_ap[0:n, gi * 512:gi * 512 + w], scalar1=lo[0:n, 0:1],
                                                        scalar2=None, op0=ALU.is_ge), rd=[sc_state, bis_s], wr=[MB_s[gi % 2]])
                for cj in range(ng):
                    op(PE, lambda cj=cj: nc.tensor.transpose(PBb[0][:, cj, 0:n], mb[0:n, cj * 128:(cj + 1) * 128], identb[0:n, 0:n]),
                       rd=[MB_s[gi % 2], cst_s], wr=[PB_s[0]])
                op(DVE, lambda: nc.vector.tensor_copy(out=mT[gi % 2][:, 0:ng, 0:n], in_=PBb[0][:, 0:ng, 0:n]), rd=[PB_s[0]], wr=[mT_s[gi % 2]])

            def emit_qk(ch):
                gi, cj = ch // 4, ch % 4
                if cj == 0:
                    prep_group(gi)
                hf = 0
                for c in range(4):
                    pb = (c % 2) * 64
                    col = hf * 128 + (c // 2) * 64
                    outv = PB[1 + (c % 2)][:, col:col + 64].rearrange("p (g t) -> p g t", g=4)
                    op(PE, lambda: nc.tensor.matmul(outv, lhsT=KT[:, c // 2, (ch0 + ch) * 128:(ch0 + ch + 1) * 128],
                                                    rhs=qT[:, c, :, 0:n], start=True, stop=True, skip_group_check=True),
                       rd=[kt_state, qT_s], wr=[PB_s[1 + (c % 2)]])

            emit_qk(0)
            for ch in range(nch):
                gi, cj = ch // 4, ch % 4
                hf = 0
                if hook is not None:
                    hook()
                    hook()
                pt_ = PT[ch % 3]
                pts = PT_s[ch % 3]
                for e in range(2):
                    op(ACT, lambda e=e: nc.scalar.activation(out=pt_[:, e * 128:(e + 1) * 128], in_=PB[1 + e][:, hf * 128:(hf + 1) * 128], func=AF.Exp),
                       rd=[PB_s[1 + e]], wr=[pts])
                pt3 = pt_[:, 0:256].rearrange("p (a t) -> p a t", t=n)
                op(DVE, lambda: nc.vector.tensor_tensor(out=pt3, in0=pt3, in1=mT[gi % 2][:, cj, 0:n].unsqueeze(1).to_broadcast([128, 16, n]),
                                                        op=ALU.mult), rd=[mT_s[gi % 2]], wr=[pts])
                if ch + 1 < nch:
                    emit_qk(ch + 1)
                for c in range(4):
                    for g in range(4):
                        a0 = (c % 2) * 128 + (c // 2) * 64 + g * n
                        op(PE, lambda: nc.tensor.matmul(PB[4 + c][0:n, g * 65:(g + 1) * 65], lhsT=pt_[:, a0:a0 + n], rhs=Vt[:, ch0 + ch, c, :],
                                                        start=(ch == 0 and g == 0), stop=(ch == nch - 1), skip_group_check=True),
                           rd=[pts, v_state], wr=[PB_s[4 + c]])
            for c in range(4):
                acc = PB[4 + c][0:n, 0:260].rearrange("p (g e) -> p g e", e=65)
                op(DVE, lambda: nc.vector.reciprocal(out=sm[0:n, 4 * c:4 * c + 4], in_=acc[:, :, 64]), rd=[PB_s[4 + c]], wr=[sm_s])
                op(DVE, lambda: nc.vector.tensor_tensor(out=out_ap[0:n, c * 256:(c + 1) * 256].rearrange("p (g d) -> p g d", d=64),
                                                        in0=acc[:, :, 0:64], in1=sm[0:n, 4 * c:4 * c + 4].unsqueeze(2).to_broadcast([n, 4, 64]),
                                                        op=ALU.mult), rd=[PB_s[4 + c], sm_s], wr=[out_state])

        def project(n, x_ap, x_s, outs):
            fence([sc_s], arena_states)
            rmsnorm_to_T(x_ap, x_s, gmixT, n)
            ug, ug_s = FB[1], FB_s[1]
            vn, vn_s = FB[2], FB_s[2]
            sga, sga_s = FB[3], FB_s[3]
            sgb, sgb_s = FB[4], FB_s[4]

            def after(gi, bank):
                pb_s = PB_s[bank]
                pbk = PB[bank]
                if gi in (0, 1):
                    op(ACT, lambda: nc.scalar.copy(out=k32[0:n, :], in_=pbk[0:n, :]), rd=[pb_s], wr=[k32_s])
                    head_norm(k32[0:n, :], n, 8, gqb, lambda ee: qn_bf[0:n, gi, :, ee * 64:(ee + 1) * 64], [qn_s], e=2)
                elif gi == 2:
                    op(ACT, lambda: nc.scalar.copy(out=k32[0:n, :], in_=pbk[0:n, :]), rd=[pb_s], wr=[k32_s])
                    head_norm(k32[0:n, 0:256], n, 4, gkb, lambda: o32[0:n, 256:512].rearrange("p (h d) -> p h d", d=64), [o32_s])
                    dma(POOL, outs["k"], o32[0:n, 256:512], rd=[o32_s])
                    dma(POOL, outs["v"], k32[0:n, 256:512], rd=[k32_s])
                    if "kv" in outs:
                        op(DVE, lambda: nc.vector.tensor_copy(out=kn_bf[0:n, :], in_=o32[0:n, 256:512]), rd=[o32_s], wr=[knbf_s])
                        op(DVE, lambda: nc.vector.tensor_copy(out=vs_bf[0:n, :], in_=k32[0:n, 256:512]), rd=[k32_s], wr=[smp_s])
                elif gi == 3:
                    op(ACT, lambda: nc.scalar.copy(out=qi_bf[0:n, :], in_=pbk[0:n, :]), rd=[pb_s], wr=[qibf_s])
                elif gi == 4:
                    op(ACT, lambda: nc.scalar.copy(out=ki32[0:n, :], in_=pbk[0:n, 0:72]), rd=[pb_s], wr=[ki32_s])
                    dma(POOL, outs["ki"], ki32[0:n, 0:64], rd=[ki32_s])
                    op(ACT, lambda: nc.scalar.activation(out=absw[0:n, :], in_=ki32[0:n, 64:72], func=AF.Abs, scale=IDXS), rd=[ki32_s], wr=[w_s_])
                    op(ACT, lambda: nc.scalar.activation(out=sgnw[0:n, :], in_=ki32[0:n, 64:72], func=AF.Sign), rd=[ki32_s], wr=[w_s_])
                    if "kv" in outs:
                        for a in range(2):
                            op(DVE, lambda a=a: nc.vector.tensor_copy(out=kid_bf[0:n, a, :], in_=ki32[0:n, 0:64]), rd=[ki32_s], wr=[kid_s])
                elif gi in (5, 6):
                    o = (gi - 5) * 512
                    op(ACT, lambda: nc.scalar.activation(out=ug[0:n, o:o + 512], in_=pbk[0:n, :], func=AF.Gelu_apprx_tanh), rd=[pb_s], wr=[ug_s])
                elif gi in (7, 8):
                    o = (gi - 7) * 512
                    op(ACT, lambda: nc.scalar.activation(out=vn[0:n, o:o + 512], in_=pbk[0:n, :], func=AF.Gelu_apprx_tanh), rd=[pb_s], wr=[vn_s])
                elif gi in (9, 10):
                    o = (gi - 9) * 512
                    op(ACT, lambda: nc.scalar.activation(out=sga[0:n, o:o + 512], in_=pbk[0:n, :], func=AF.Sigmoid), rd=[pb_s], wr=[sga_s])
                else:
                    o = (gi - 11) * 512
                    op(ACT, lambda: nc.scalar.activation(out=sgb[0:n, o:o + 512], in_=pbk[0:n, :], func=AF.Sigmoid), rd=[pb_s], wr=[sgb_s])

            dense(lambda kc: xnT[:, kc, 0:n], 8, n, [g[1] for g in GRPS], lambda gi: gi % 2, [xnT_s], after)
            for j in range(8):
                op(PE, lambda j=j: nc.tensor.transpose(PBb[2][:, j, 0:n], qn_bf[0:n, j // 4, j % 4, :], identb[0:n, 0:n]),
                   rd=[qn_s, cst_s], wr=[PB_s[2]])
            q5 = qT[:, :, :, :].rearrange("p (a e) g t -> p a e g t", e=2)
            for e in range(2):
                op(ACT, lambda e=e: nc.scalar.copy(out=q5[e * 64:(e + 1) * 64, :, e, :, 0:n],
                                                    in_=PBb[2][e * 64:(e + 1) * 64, 0:8, 0:n].rearrange("p (a g) t -> p a g t", g=4)),
                   rd=[PB_s[2]], wr=[qT_s])
            transposes(lambda j: qi_bf[0:n, j * 128:(j + 1) * 128], 4, n, 3,
                       lambda b0, nb: qiT[:, b0:b0 + nb, 0:n], [qiT_s], [qibf_s])
            op(DVE, lambda: nc.vector.scalar_tensor_tensor(out=vn_bf[0:n, :], in0=vn[0:n, :], scalar=1.0, in1=vn[0:n, :], op0=ALU.mult,
                                                           op1=ALU.mult, accum_out=ss16[0:n, 0:1]), rd=[vn_s], wr=[vnbf_s, ss16_s])
            rstd_from_ss(ss16[0:n, 0:1], rs16[0:n, 0:1], n, 1, 1.0 / D, ss16_s, rs16_s)
            op(DVE, lambda: nc.vector.scalar_tensor_tensor(out=vn[0:n, :], in0=vn[0:n, :], scalar=rs16[0:n, 0:1], in1=gsgub[0:n, :],
                                                           op0=ALU.mult, op1=ALU.mult), rd=[rs16_s, cst_s], wr=[vn_s])
            dma(POOL, outs["sg"], vn[0:n, :], rd=[vn_s])
            if n == 128:
                op(DVE, lambda: nc.vector.tensor_copy(out=vn_bf[:, :], in_=vn[:, :]), rd=[vn_s], wr=[vnbf_s])
                for g in range(8):
                    bank = 4 + g // 4
                    op(PE, lambda g=g: nc.tensor.matmul(PB[bank][:, (g % 4) * 128:(g % 4 + 1) * 128], lhsT=wsT[:, g, :],
                                                        rhs=vn_bf[:, g * 128:(g + 1) * 128], start=True, stop=True, skip_group_check=True),
                       rd=[vnbf_s, cst_s], wr=[PB_s[bank]])
                for hb in range(2):
                    op(DVE, lambda hb=hb: nc.vector.tensor_tensor(
                        out=vn[:, hb * 512:(hb + 1) * 512].rearrange("p (g c) -> p g c", c=128),
                        in0=PB[4 + hb][:, :].rearrange("p (g c) -> p g c", c=128),
                        in1=bsT[:, hb * 4:hb * 4 + 4].unsqueeze(2).to_broadcast([128, 4, 128]), op=ALU.add),
                       rd=[PB_s[4 + hb], cst_s], wr=[vn_s])
            else:
                v3s = vn[0:n, :].rearrange("p (g c) -> p g c", c=128)
                op(DVE, lambda: nc.vector.tensor_tensor(out=v3s, in0=v3s, in1=coefs[0:n, 0:8].unsqueeze(2).to_broadcast([n, 8, 128]), op=ALU.mult),
                   rd=[cst_s], wr=[vn_s])
                op(DVE, lambda: nc.vector.tensor_tensor(out=v3s, in0=v3s, in1=coefs[0:n, 8:16].unsqueeze(2).to_broadcast([n, 8, 128]), op=ALU.add),
                   rd=[cst_s], wr=[vn_s])
            op(DVE, lambda: nc.vector.tensor_tensor(out=vn[0:n, :], in0=vn[0:n, :], in1=ug[0:n, :], op=ALU.mult), rd=[ug_s], wr=[vn_s])
            op(DVE, lambda: nc.vector.tensor_tensor(out=sgb[0:n, :], in0=sgb[0:n, :], in1=vn[0:n, :], op=ALU.mult), rd=[vn_s], wr=[sgb_s])

        def finish(n, x_ap, x_s, oatt, oatt_s, p_src, y_dst):
            sga, sga_s = FB[3], FB_s[3]
            msgu, msgu_s = FB[4], FB_s[4]
            fence([sc_s], arena_states)
            dma(SP, p32[0:n, :], p_src, wr=[p32_s])
            op(DVE, lambda: nc.vector.tensor_tensor(out=oatt[0:n, :], in0=oatt[0:n, :], in1=sga[0:n, :], op=ALU.mult), rd=[sga_s], wr=[oatt_s])
            op(DVE, lambda: nc.vector.tensor_tensor(out=xn_bf[0:n, :], in0=oatt[0:n, :], in1=msgu[0:n, :], op=ALU.add),
               rd=[oatt_s, msgu_s], wr=[xnbf_s])
            transposes(lambda j: xn_bf[0:n, j * 128:(j + 1) * 128], 8, n, 2, lambda b0, nb: xnT[:, b0:b0 + nb, 0:n], [xnT_s], [xnbf_s])

            def after_o(gi, bank):
                op(DVE, lambda: nc.vector.tensor_tensor(out=x_ap[:, gi * 512:(gi + 1) * 512], in0=x_ap[:, gi * 512:(gi + 1) * 512],
                                                        in1=PB[bank][0:n, :], op=ALU.add), rd=[PB_s[bank]], wr=[x_s])
            dense(lambda kc: xnT[:, kc, 0:n], 8, n, [512, 512], lambda gi: gi % 2, [xnT_s], after_o)
            rmsnorm_to_T(x_ap, x_s, gffnT, n)

            def after_up(gi, bank):
                op(ACT, lambda: nc.scalar.activation(out=rt[gi % 2][0:n, :], in_=PB[bank][0:n, :], func=AF.Relu), rd=[PB_s[bank]], wr=[rt_s[gi % 2]])
                op(DVE, lambda: nc.vector.tensor_tensor(out=h_bf[0:n, gi * 512:(gi + 1) * 512], in0=rt[gi % 2][0:n, :], in1=rt[gi % 2][0:n, :],
                                                        op=ALU.mult), rd=[rt_s[gi % 2]], wr=[hbf_s])
            dense(lambda kc: xnT[:, kc, 0:n], 8, n, [512] * 8, lambda gi: gi % 2, [xnT_s], after_up)
            transposes(lambda j: h_bf[0:n, j * 128:(j + 1) * 128], 32, n, 2, lambda b0, nb: hT[:, b0:b0 + nb, 0:n], [hT_s], [hbf_s])
            for nn in range(2):
                bank = nn % 2
                for kg in range(4):
                    sl, k = w_get()
                    for kc in range(8):
                        op(PE, lambda kc=kc: nc.tensor.matmul(PB[bank][0:n, :], lhsT=hT[:, kg * 8 + kc, 0:n], rhs=WS[sl][:, kc, :],
                                                              start=(kg == 0 and kc == 0), stop=(kg == 3 and kc == 7)),
                           rd=[hT_s, WS_s[sl]], wr=[PB_s[bank]])
                    w_done(k)
                op(DVE, lambda: nc.vector.tensor_tensor(out=x_ap[:, nn * 512:(nn + 1) * 512], in0=x_ap[:, nn * 512:(nn + 1) * 512],
                                                        in1=PB[bank][0:n, :], op=ALU.add), rd=[PB_s[bank]], wr=[x_s])
            rmsnorm_to_T(x_ap, x_s, gpleT, n)
            gate, gate_s = FB[1], FB_s[1]

            def after_pg(gi, bank):
                op(ACT, lambda: nc.scalar.activation(out=gate[0:n, gi * 512:(gi + 1) * 512], in_=PB[bank][0:n, :], func=AF.Sigmoid),
                   rd=[PB_s[bank]], wr=[gate_s])
            dense(lambda kc: xnT[:, kc, 0:n], 8, n, [512, 512], lambda gi: gi % 2, [xnT_s], after_pg)
            op(DVE, lambda: nc.vector.tensor_copy(out=p_bf[0:n, :], in_=p32[0:n, :]), rd=[p32_s], wr=[pbf_s])
            transposes(lambda j: p_bf[0:n, j * 128:(j + 1) * 128], 2, n, 3, lambda b0, nb: pT[:, b0:b0 + nb, 0:n], [pT_s], [pbf_s])

            def after_p(gi, bank):
                op(DVE, lambda: nc.vector.tensor_tensor(out=gate[0:n, gi * 512:(gi + 1) * 512], in0=gate[0:n, gi * 512:(gi + 1) * 512],
                                                        in1=PB[bank][0:n, :], op=ALU.mult), rd=[PB_s[bank]], wr=[gate_s])
                op(DVE, lambda: nc.vector.tensor_tensor(out=x_ap[:, gi * 512:(gi + 1) * 512], in0=x_ap[:, gi * 512:(gi + 1) * 512],
                                                        in1=gate[0:n, gi * 512:(gi + 1) * 512], op=ALU.add), rd=[gate_s], wr=[x_s])
            dense(lambda kc: pT[:, kc, 0:n], 2, n, [512, 512], lambda gi: gi % 2, [pT_s], after_p)
            dma(POOL, y_dst, x_ap, rd=[x_s])

        n = 128
        fence(arena_states + [sc_s, sc16_s, sct_s], [wkvk_s])
        dma(SP, wkvk[:, :, 0:512], win_s[2][:, :, :], wr=[wkvk_s])
        dma(SP, wkvk[:, :, 512:576], win_s[4][:, :, 0:64], wr=[wkvk_s])
        xnT2 = al("xnT2", [128, 8, 128], BF16, 0); xnT2_s = St()
        fence(arena_states + [sc_s, sc16_s, sct_s], [xnT2_s])
        xnTs = [(xnT, xnT_s), (xnT2, xnT2_s)]
        ssA = sb("ssA", [128, 2], F32); ssA_s = St(); rsA_s = St()

        def stageA(kb):
            xt, xt_s = FB[kb % 2], FB_s[kb % 2]
            xT, xT_s = xnTs[kb % 2]
            dma(SP, xt[:, :], xb_d[kb * 128:(kb + 1) * 128, :], wr=[xt_s])
            op(DVE, lambda: nc.vector.scalar_tensor_tensor(out=xn_bf[:, :], in0=xt[:, :], scalar=1.0, in1=xt[:, :], op0=ALU.mult, op1=ALU.mult,
                                                           accum_out=ssA[:, 0:1]), rd=[xt_s], wr=[xnbf_s, ssA_s])
            rstd_from_ss(ssA[:, 0:1], ssA[:, 1:2], 128, 1, 1.0 / D, ssA_s, rsA_s)
            op(DVE, lambda: nc.vector.tensor_scalar(out=xn_bf[:, :], in0=xt[:, :], scalar1=ssA[:, 1:2], scalar2=None, op0=ALU.mult),
               rd=[xt_s, rsA_s], wr=[xnbf_s])
            transposes(lambda j: xn_bf[:, j * 128:(j + 1) * 128], 8, 128, 0,
                       lambda b0, nb: xT[:, b0:b0 + nb, :], [xT_s], [xnbf_s], gainT=gmixT)

        def stageB(kb):
            xT, xT_s = xnTs[kb % 2]
            for kc in range(8):
                op(PE, lambda kc=kc: nc.tensor.matmul(PB[0][:, :], lhsT=xT[:, kc, :], rhs=wkvk[:, kc, 0:512], start=(kc == 0), stop=(kc == 7)),
                   rd=[xT_s, wkvk_s], wr=[PB_s[0]])
            for kc in range(8):
                op(PE, lambda kc=kc: nc.tensor.matmul(PB[1][:, 0:64], lhsT=xT[:, kc, :], rhs=wkvk[:, kc, 512:576], start=(kc == 0), stop=(kc == 7)),
                   rd=[xT_s, wkvk_s], wr=[PB_s[1]])
            op(ACT, lambda: nc.scalar.copy(out=k32[:, :], in_=PB[0][:, :]), rd=[PB_s[0]], wr=[k32_s])
            head_norm(k32[:, 0:256], n, 4, gkb, lambda: kn_bf[:, :].rearrange("p (h d) -> p h d", d=64), [knbf_s])
            op(ACT, lambda: nc.scalar.copy(out=v_bf[:, :], in_=k32[:, 256:512]), rd=[k32_s], wr=[vbf_s])
            for a in range(2):
                op(ACT, lambda a=a: nc.scalar.copy(out=kid_bf[:, a, :], in_=PB[1][:, 0:64]), rd=[PB_s[1]], wr=[kid_s])
            kv_append(kb, n, kn_bf, knbf_s, v_bf, vbf_s, kid_bf, kid_s)

        RS["act"] = True
        stageA(0)

        def prepass_iter(kb):
            xT, xT_s = xnTs[kb % 2]
            hasA = kb + 1 < NBLK
            k3 = lambda ap: ap.rearrange("p (h d) -> p h d", d=64)
            for kc in range(8):
                op(PE, lambda kc=kc: nc.tensor.matmul(PB[0][:, :], lhsT=xT[:, kc, :], rhs=wkvk[:, kc, 0:512], start=(kc == 0), stop=(kc == 7)),
                   rd=[xT_s, wkvk_s], wr=[PB_s[0]])
            for kc in range(8):
                op(PE, lambda kc=kc: nc.tensor.matmul(PB[1][:, 0:64], lhsT=xT[:, kc, :], rhs=wkvk[:, kc, 512:576], start=(kc == 0), stop=(kc == 7)),
                   rd=[xT_s, wkvk_s], wr=[PB_s[1]])
            op(ACT, lambda: nc.scalar.copy(out=k32[:, :], in_=PB[0][:, :]), rd=[PB_s[0]], wr=[k32_s])
            if hasA:
                xt, xt_s = FB[(kb + 1) % 2], FB_s[(kb + 1) % 2]
                xT2, xT2_s = xnTs[(kb + 1) % 2]
                dma(SP, xt[:, :], xb_d[(kb + 1) * 128:(kb + 2) * 128, :], wr=[xt_s])
                op(DVE, lambda: nc.vector.scalar_tensor_tensor(out=xn_bf[:, :], in0=xt[:, :], scalar=1.0, in1=xt[:, :], op0=ALU.mult, op1=ALU.mult,
                                                               accum_out=ssA[:, 0:1]), rd=[xt_s], wr=[xnbf_s, ssA_s])
                op(DVE, lambda: nc.vector.tensor_scalar(out=ssA[:, 0:1], in0=ssA[:, 0:1], scalar1=1.0 / D, scalar2=EPS, op0=ALU.mult, op1=ALU.add),
                   rd=[], wr=[ssA_s])
                op(ACT, lambda: nc.scalar.activation(out=ssA[:, 1:2], in_=ssA[:, 0:1], func=AF.Sqrt), rd=[ssA_s], wr=[rsA_s])
            op(DVE, lambda: nc.vector.tensor_tensor(out=o32[:, 0:256], in0=k32[:, 0:256], in1=k32[:, 0:256], op=ALU.mult), rd=[k32_s], wr=[o32_s])
            op(DVE, lambda: nc.vector.tensor_reduce(out=ss16[:, 0:4], in_=k3(o32[:, 0:256]), axis=AX.X, op=ALU.add), rd=[o32_s], wr=[ss16_s])
            op(DVE, lambda: nc.vector.tensor_scalar(out=ss16[:, 0:4], in0=ss16[:, 0:4], scalar1=1.0 / 64, scalar2=EPS, op0=ALU.mult, op1=ALU.add),
               rd=[], wr=[ss16_s])
            op(ACT, lambda: nc.scalar.activation(out=rs16[:, 0:4], in_=ss16[:, 0:4], func=AF.Sqrt), rd=[ss16_s], wr=[rs16_s])
            if hasA:
                op(DVE, lambda: nc.vector.reciprocal(out=ssA[:, 1:2], in_=ssA[:, 1:2]), rd=[], wr=[rsA_s])
                op(DVE, lambda: nc.vector.tensor_scalar(out=xn_bf[:, :], in0=xt[:, :], scalar1=ssA[:, 1:2], scalar2=None, op0=ALU.mult),
                   rd=[xt_s, rsA_s], wr=[xnbf_s])
                for j in range(8):
                    op(PE, lambda j=j: nc.tensor.transpose(PBb[2][:, j, :], xn_bf[:, j * 128:(j + 1) * 128], identb[:, :]),
                       rd=[xnbf_s, cst_s], wr=[PB_s[2]])
            op(DVE, lambda: nc.vector.reciprocal(out=rs16[:, 0:4], in_=rs16[:, 0:4]), rd=[], wr=[rs16_s])
            op(DVE, lambda: nc.vector.tensor_tensor(out=k3(o32[:, 0:256]), in0=k3(k32[:, 0:256]),
                                                    in1=rs16[:, 0:4].unsqueeze(2).to_broadcast([128, 4, 64]), op=ALU.mult),
               rd=[k32_s, rs16_s], wr=[o32_s])
            op(DVE, lambda: nc.vector.tensor_tensor(out=k3(kn_bf[:, :]), in0=k3(o32[:, 0:256]),
                                                    in1=gkb[:, :].unsqueeze(1).to_broadcast([128, 4, 64]), op=ALU.mult),
               rd=[o32_s, cst_s], wr=[knbf_s])
            op(ACT, lambda: nc.scalar.copy(out=v_bf[:, :], in_=k32[:, 256:512]), rd=[k32_s], wr=[vbf_s])
            for a_ in range(2):
                op(ACT, lambda a_=a_: nc.scalar.copy(out=kid_bf[:, a_, :], in_=PB[1][:, 0:64]), rd=[PB_s[1]], wr=[kid_s])
            kv_append(kb, 128, kn_bf, knbf_s, v_bf, vbf_s, kid_bf, kid_s)
            if hasA:
                op(DVE, lambda: nc.vector.tensor_tensor(out=xT2[:, :, :], in0=PBb[2][:, 0:8, :],
                                                        in1=gmixT[:, 0:8].unsqueeze(2).to_broadcast([128, 8, 128]), op=ALU.mult),
                   rd=[PB_s[2], cst_s], wr=[xT2_s])

        for kb in range(NBLK):
            prepass_iter(kb)
            for _ in range(4):
                if conv_jobs:
                    conv_jobs.pop(0)()
        fence([xnT2_s], arena_states)
        RS["act"] = False
        while conv_jobs:
            conv_jobs.pop(0)()
        for i in range(NDS):
            if dcnt[i] > 0:
                SP.wait((dsems[i], dcnt[i]))

        for i in range(NSLOT):
            m, second = i // 2, i % 2
            nch = 8 * m + (8 if second else 4)
            xt, xt_s = FB[0], FB_s[0]
            dma(SP, xt[:, :], xo_d[i, :, :], wr=[xt_s])
            project(n, xt[:, :], xt_s, {"k": ko_d[i, :, :], "v": vo_d[i, :, :], "ki": kio_d[i, :, :], "sg": sgo_d[i, :, :]})
            fence(arena_states, [sc_s])
            indexer(n, nch, scores, sc_s)
            S = nch * 128
            op(DVE, lambda: nc.vector.tensor_tensor(out=scores[:, S - 512:S], in0=scores[:, S - 512:S], in1=pen[:, second, :], op=ALU.add),
               rd=[cst_s], wr=[sc_s])
            fence([FB_s[1], FB_s[2]], [junk_s, junka_s])
            bisect(n, S, scores[:, 0:S], sc_s, junk8[:, 0:S], junk_s, junk_act=junki8)
            fence([junk_s, junka_s], [FB_s[1], FB_s[2]])
            oat, oat_s = FB[1], FB_s[1]
            attend(n, nch, scores, sc_s, I4[:, :, :], oat, oat_s)
            finish(n, xt[:, :], xt_s, oat, oat_s, po_d[i, :, :], yo_d[i, :, :])

        n = NS
        xs_t, xs_s = FB[0], FB_s[0]
        dma(SP, xs_t[0:n, :], xs_d[:, :], wr=[xs_s])
        project(n, xs_t[0:n, :], xs_s, {"k": kss_d[:, :], "v": vss_d[:, :], "ki": kis_d[:, :], "sg": sgs_d[:, :], "kv": True})
        for j in range(2):
            op(PE, lambda j=j: nc.tensor.transpose(PBb[3][:, j, 0:n], kn_bf[0:n, j * 128:(j + 1) * 128], identb[0:n, 0:n]),
               rd=[knbf_s, cst_s], wr=[PB_s[3]])
        op(PE, lambda: nc.tensor.transpose(PBb[3][:, 2, 0:n], kid_bf[0:n, :, :].rearrange("p a d -> p (a d)"), identb[0:n, 0:n]),
           rd=[kid_s, cst_s], wr=[PB_s[3]])
        op(ACT, lambda: nc.scalar.copy(out=ksT[:, :, :], in_=PBb[3][:, 0:2, 0:n]), rd=[PB_s[3]], wr=[smp_s])
        op(ACT, lambda: nc.scalar.copy(out=kisT[:, :], in_=PBb[3][:, 2, 0:n]), rd=[PB_s[3]], wr=[smp_s])

        def gather(dst_ap, src2d, col):
            return lambda: nc.gpsimd.indirect_dma_start(out=dst_ap, out_offset=None, in_=src2d,
                                                        in_offset=bass.IndirectOffsetOnAxis(ap=idxa[:, col:col + 1], axis=0))
        fence(arena_states, [sc16_s] + sctR_s)
        kiR_s = [St(), St()]; KR_s = [St(), St()]; VR_s = [St(), St()]
        fence([kiT_s], kiR_s); fence([KT_s], KR_s); fence([V_s], VR_s)
        op(POOL, lambda: nc.gpsimd.memset(sc16[:, :], -1e30), wr=[sc16_s])
        for r in range(2):
            op(POOL, lambda r=r: nc.gpsimd.memset(kiT[:, r * S_S + 2048:(r + 1) * S_S], 0.0), wr=[kiR_s[r]])
            op(POOL, lambda r=r: nc.gpsimd.memset(KT[:, :, r * S_S + 2048:(r + 1) * S_S], 0.0), wr=[KR_s[r]])
            op(POOL, lambda r=r: nc.gpsimd.memset(Vt[:, r * 17 + 16, :, 0:64], 0.0), wr=[VR_s[r]])
        def s1_loads(b):
            r = b % 2
            k0 = r * S_S
            jobsA, jobsB = [], []
            for pg in range(NPG):
                def jobA(pg=pg):
                    sl = (b * NPG + pg) % NPB
                    col = b * NPG + pg
                    dma(POOL, None, None, rd=[cst_s], wr=[pgI_s[sl]], fn=gather(pgI[sl][:, 0, :], cki_d, col))

                def jobB(pg=pg):
                    sl = (b * NPG + pg) % NPB
                    bk = 3
                    for a in range(2):
                        op(PE, lambda a=a: nc.tensor.transpose(PBb[bk][a * 64:(a + 1) * 64, 0, :], pgI[sl][:, 0, :], identb[:]),
                           rd=[pgI_s[sl], cst_s], wr=[PB_s[bk]])
                    op(ACT, lambda: nc.scalar.copy(out=kiT[:, k0 + pg * 128:k0 + (pg + 1) * 128], in_=PBb[bk][:, 0, :]),
                       rd=[PB_s[bk]], wr=[kiR_s[r]])
                jobsA.append(jobA)
                jobsB.append(jobB)
            jobs = skew(jobsA, jobsB)
            jobs.append(lambda: op(ACT, lambda: nc.scalar.copy(out=kiT[:, k0 + 2048:k0 + 2049], in_=kisT[:, b:b + 1]), rd=[smp_s], wr=[kiR_s[r]]))
            return jobs

        def skew(jobsA, jobsB, ahead=3):
            out = list(jobsA[:ahead])
            for p in range(len(jobsB)):
                out.append(jobsB[p])
                if p + ahead < len(jobsA):
                    out.append(jobsA[p + ahead])
            return out

        def make_hook(jobs, every):
            st = {"i": 0}

            def hook():
                st["i"] += 1
                if st["i"] % every == 0 and jobs:
                    jobs.pop(0)()
            return hook

        pending = s1_loads(0)
        for b in range(NS):
            r = b % 2
            k0 = r * S_S
            while pending:
                pending.pop(0)()
            pending = s1_loads(b + 1) if b + 1 < NS else []
            indexer(n, 17, sctR[r], sctR_s[r], k0=k0, ki_state=kiR_s[r], hook=make_hook(pending, 1))
            op(DVE, lambda b=b: nc.vector.scalar_tensor_tensor(out=sc16[:, 0:2049], in0=sctR[r][:, 0:2049], scalar=identf[0:16, b:b + 1],
                                                               in1=sc16[:, 0:2049], op0=ALU.mult, op1=ALU.add) if b > 0 else
               nc.vector.tensor_scalar(out=sc16[:, 0:2049], in0=sctR[r][:, 0:2049], scalar1=identf[0:16, 0:1], scalar2=None, op0=ALU.mult),
               rd=[sctR_s[r], cst_s], wr=[sc16_s])
        fence([sctR_s[1]], [sct_s])
        bisect(n, 2049, sc16[:, 0:2049], sc16_s, sct[:, 0:2049], sct_s)
        oat, oat_s = FB[1], FB_s[1]
        oatt_s16, oas_s = FB[2], FB_s[2]
        def s2_loads(b):
            r = b % 2
            k0 = r * S_S
            c0 = r * 17
            jobsA, jobsB = [], []
            for pg in range(NPG):
                def jobA(pg=pg):
                    sl = (b * NPG + pg) % NPB
                    col = b * NPG + pg
                    dma(POOL, None, None, rd=[cst_s], wr=[pgK_s[sl]], fn=gather(pgKV[sl][:, :], ckv_d, col))
                jobsA.append(jobA)

                def job(pg=pg):
                    sl = (b * NPG + pg) % NPB
                    bk = 0
                    for j in range(2):
                        op(PE, lambda j=j: nc.tensor.transpose(PBb[bk][:, j, :], pgK[sl][:, j * 128:(j + 1) * 128], identb[:]),
                           rd=[pgK_s[sl], cst_s], wr=[PB_s[bk]])
                    op(ACT, lambda: nc.scalar.copy(out=KT[:, :, k0 + pg * 128:k0 + (pg + 1) * 128], in_=PBb[bk][:, 0:2, :]),
                       rd=[PB_s[bk]], wr=[KR_s[r]])
                    op(DVE, lambda: nc.vector.tensor_copy(out=Vt[:, c0 + pg, :, 0:64], in_=pgV[sl][:, :].rearrange("p (c d) -> p c d", d=64)),
                       rd=[pgV_s[sl]], wr=[VR_s[r]])
                jobsB.append(job)
            jobs = skew(jobsA, jobsB)

            def last():
                op(ACT, lambda: nc.scalar.copy(out=KT[:, :, k0 + 2048:k0 + 2049], in_=ksT[:, :, b:b + 1]), rd=[smp_s], wr=[KR_s[r]])
                op(PE, lambda: nc.tensor.matmul(PB[0][0:1, 0:256], lhsT=identb[0:16, b:b + 1], rhs=vs_bf[0:16, :], start=True, stop=True),
                   rd=[smp_s, cst_s], wr=[PB_s[0]])
                op(ACT, lambda: nc.scalar.copy(out=Vt[0:1, c0 + 16, :, 0:64], in_=PB[0][0:1, 0:256].rearrange("p (c d) -> p c d", d=64)),
                   rd=[PB_s[0]], wr=[VR_s[r]])
            jobs.append(last)
            return jobs

        pending = s2_loads(0)
        for b in range(NS):
            r = b % 2
            c0 = r * 17
            while pending:
                pending.pop(0)()
            pending = s2_loads(b + 1) if b + 1 < NS else []
            attend_sample(17, sc16, sc16_s, oat, oat_s, c0, KR_s[r], VR_s[r], hook=make_hook(pending, 1))
            if b == 0:
                op(DVE, lambda: nc.vector.tensor_scalar(out=oatt_s16[0:16, :], in0=oat[0:16, :], scalar1=identf[0:16, 0:1], scalar2=None,
                                                        op0=ALU.mult), rd=[oat_s, cst_s], wr=[oas_s])
            else:
                op(DVE, lambda b=b: nc.vector.scalar_tensor_tensor(out=oatt_s16[0:16, :], in0=oat[0:16, :], scalar=identf[0:16, b:b + 1],
                                                                   in1=oatt_s16[0:16, :], op0=ALU.mult, op1=ALU.add),
                   rd=[oat_s, cst_s], wr=[oas_s])
        fence(sctR_s, [sct_s])
        fence([sc16_s, sct_s], arena_states)
        finish(n, xs_t[0:n, :], xs_s, oatt_s16, oas_s, ps_d[:, :], ys_d[:, :])

        for i in range(NDS):
            if dcnt[i] > 0:
                POOL.wait((dsems[i], dcnt[i]))
    return nc


_NC_CACHE = {}


def _slot_block(r, i):
    m, second = i // 2, i % 2
    return 8 * m + (7 - r if second else r)


def kernel(x_prompt, x_sample, cache_k, cache_v, cache_kidx, page_table, p_prompt, p_sample,
           g_mix, w_in, g_q, g_k, g_sgu, w_s, b_s, w_o, g_ffn, w_up, w_down, g_ple, w_pg, w_p):
    f32 = np.float32
    A = lambda a: np.ascontiguousarray(np.asarray(a))
    x_prompt = A(x_prompt); x_sample = A(x_sample); p_prompt = A(p_prompt); p_sample = A(p_sample)
    ck = A(cache_k).reshape(N_PHYS * 128, 256)
    cv = A(cache_v).reshape(N_PHYS * 128, 256)
    cki = A(cache_kidx).reshape(N_PHYS * 128, 64)
    pt_all = A(page_table).astype(np.int32)
    shared = {
        "ckv": np.concatenate([ck, cv], axis=1), "cki": cki,
        "w_in": A(w_in)[0], "w_o": A(w_o)[0], "w_up": A(w_up)[0], "w_down": A(w_down)[0], "w_pg": A(w_pg)[0], "w_p": A(w_p)[0],
        "g_mix": A(g_mix)[0], "g_q": A(g_q)[0], "g_k": A(g_k)[0], "g_sgu": A(g_sgu)[0], "w_s": A(w_s)[0], "b_s": A(b_s)[0],
        "g_ffn": A(g_ffn)[0], "g_ple": A(g_ple)[0],
    }
    tt = np.arange(128)[:, None]
    ss = np.arange(512)[None, :]
    in_maps = []
    for c in range(8):
        bi, r = c // 4, c % 4
        blocks = [_slot_block(r, i) for i in range(NSLOT)]
        pen = np.zeros((128, 2, 512), f32)
        pen[:, 0, :] = np.where(ss <= r * 128 + tt, 0.0, -1e30)
        pen[:, 1, :] = np.where(ss <= (3 - r) * 128 + tt, 0.0, -1e30)
        m = dict(shared)
        m["xb"] = x_prompt[bi]
        m["xo"] = np.stack([x_prompt[bi, j * 128:(j + 1) * 128] for j in blocks])
        m["po"] = np.stack([p_prompt[0, bi, j * 128:(j + 1) * 128] for j in blocks])
        m["xs"] = x_sample[c * NS:(c + 1) * NS, 0]
        m["ps"] = p_sample[0, c * NS:(c + 1) * NS, 0]
        m["pt"] = np.ascontiguousarray(pt_all[c * NS:(c + 1) * NS].reshape(-1))
        m["pen"] = pen
        in_maps.append(m)
    if "nc" not in _NC_CACHE:
        _NC_CACHE["nc"] = build()
    res = run_bass_kernel_spmd(_NC_CACHE["nc"], in_maps, core_ids=list(range(8)))
    R = res.results
    y_p = np.zeros((2, SEQ, D), f32); y_s = np.zeros((128, 1, D), f32)
    nk = np.zeros((1, 2, SEQ, 4, 64), f32); nv = np.zeros((1, 2, SEQ, 4, 64), f32)
    nki = np.zeros((1, 2, SEQ, 64), f32); nsg = np.zeros((1, 2, SEQ, D), f32)
    sk = np.zeros((1, 128, 1, 4, 64), f32); sv = np.zeros((1, 128, 1, 4, 64), f32)
    ski = np.zeros((1, 128, 1, 64), f32); ssg = np.zeros((1, 128, 1, D), f32)
    for c in range(8):
        bi, r = c // 4, c % 4
        o = R[c]
        for i in range(NSLOT):
            j = _slot_block(r, i)
            sl = slice(j * 128, (j + 1) * 128)
            y_p[bi, sl] = o["yo"][i]
            nk[0, bi, sl] = o["ko"][i].reshape(128, 4, 64)
            nv[0, bi, sl] = o["vo"][i].reshape(128, 4, 64)
            nki[0, bi, sl] = o["kio"][i]
            nsg[0, bi, sl] = o["sgo"][i]
        s2 = slice(c * NS, (c + 1) * NS)
        y_s[s2, 0] = o["ys"]
        sk[0, s2, 0] = o["kss"].reshape(NS, 4, 64)
        sv[0, s2, 0] = o["vss"].reshape(NS, 4, 64)
        ski[0, s2, 0] = o["kis"]
        ssg[0, s2, 0] = o["sgs"]
    return (y_p, y_s, nk, nv, nki, nsg, sk, sv, ski, ssg)
```

```python
import contextlib
import numpy as np
import concourse.bass as bass
import concourse.mybir as mybir
from concourse.bass_utils import run_bass_kernel_spmd

F32 = mybir.dt.float32
BF16 = mybir.dt.bfloat16
I32 = mybir.dt.int32
ALU = mybir.AluOpType
AF = mybir.ActivationFunctionType
AX = mybir.AxisListType

D = 1024
SEQ = 8192
NBLK = 64
NSLOT = 16
NS = 16
NPG = 16
S_S = 2176
N_PHYS = 2560
INW = 6216
DFF = 4096
PLE = 256
EPS = 1e-6
ATTN_SCALE = 64 ** -0.5
IDXS = (64 ** -0.5) * (8 ** -0.5)
NIT = 20
BRK = 16.0
NEGM = -30000.0
GRPS = [(0, 512), (512, 512), (1024, 512), (1536, 512), (2048, 72), (2120, 512), (2632, 512),
        (3144, 512), (3656, 512), (4168, 512), (4680, 512), (5192, 512), (5704, 512)]


class St:
    __slots__ = ("w", "r")

    def __init__(self):
        self.w = None
        self.r = {}


class Eng:
    def __init__(self, nc, es, name, h):
        self.h = h
        self.name = name
        self.sem = es.enter_context(nc.semaphore("e_" + name))
        self.cnt = 0
        self.seen = {}

    def wait(self, tok):
        sem, val = tok
        if self.seen.get(sem.num, 0) < val:
            self.h.wait_ge(sem, val)
            self.seen[sem.num] = val


def build():
    nc = bass.Bass("TRN2", target_bir_lowering=False)
    dt_in = lambda n, s, d=F32: nc.dram_tensor(n, s, d, kind="ExternalInput").ap()
    dt_out = lambda n, s, d=F32: nc.dram_tensor(n, s, d, kind="ExternalOutput").ap()
    xb_d = dt_in("xb", [SEQ, D])
    xo_d = dt_in("xo", [NSLOT, 128, D])
    po_d = dt_in("po", [NSLOT, 128, PLE])
    xs_d = dt_in("xs", [NS, D])
    ps_d = dt_in("ps", [NS, PLE])
    ckv_d = dt_in("ckv", [N_PHYS * 128, 512])
    cki_d = dt_in("cki", [N_PHYS * 128, 64])
    pt_d = dt_in("pt", [NS * NPG], I32)
    pen_d = dt_in("pen", [128, 2, 512])
    win_d = dt_in("w_in", [D, INW])
    wo_d = dt_in("w_o", [D, D])
    wup_d = dt_in("w_up", [D, DFF])
    wdn_d = dt_in("w_down", [DFF, D])
    wpg_d = dt_in("w_pg", [D, D])
    wp_d = dt_in("w_p", [PLE, D])
    gmix_d = dt_in("g_mix", [D])
    gq_d = dt_in("g_q", [64])
    gk_d = dt_in("g_k", [64])
    gsgu_d = dt_in("g_sgu", [D])
    ws_d = dt_in("w_s", [8, 128, 128])
    bs_d = dt_in("b_s", [8, 128])
    gffn_d = dt_in("g_ffn", [D])
    gple_d = dt_in("g_ple", [D])

    yo_d = dt_out("yo", [NSLOT, 128, D])
    ko_d = dt_out("ko", [NSLOT, 128, 256])
    vo_d = dt_out("vo", [NSLOT, 128, 256])
    kio_d = dt_out("kio", [NSLOT, 128, 64])
    sgo_d = dt_out("sgo", [NSLOT, 128, D])
    ys_d = dt_out("ys", [NS, D])
    kss_d = dt_out("kss", [NS, 256])
    vss_d = dt_out("vss", [NS, 256])
    kis_d = dt_out("kis", [NS, 64])
    sgs_d = dt_out("sgs", [NS, D])

    def scr(n, shape):
        return nc.dram_tensor(n, shape, BF16, kind="Internal").ap()
    win_s = [scr("win_s%d" % i, [128, 8, w]) for i, (o, w) in enumerate(GRPS)]
    wo_s = [scr("wo_s%d" % i, [128, 8, 512]) for i in range(2)]
    wup_s = [scr("wup_s%d" % i, [128, 8, 512]) for i in range(8)]
    wdn_s = [scr("wdn_s%d" % i, [128, 8, 512]) for i in range(8)]
    wpg_s = [scr("wpg_s%d" % i, [128, 8, 512]) for i in range(2)]
    wp_s = [scr("wp_s%d" % i, [128, 2, 512]) for i in range(2)]

    es = contextlib.ExitStack()
    with es:
        PE = Eng(nc, es, "pe", nc.tensor)
        ACT = Eng(nc, es, "act", nc.scalar)
        DVE = Eng(nc, es, "dve", nc.vector)
        POOL = Eng(nc, es, "pool", nc.gpsimd)
        SP = Eng(nc, es, "sp", nc.sync)
        NDS = 24
        dsems = [es.enter_context(nc.semaphore("d%d" % i)) for i in range(NDS)]
        dcnt = [0] * NDS
        dstate = {"i": 0}

        def deps_of(rd, wr):
            deps = []
            for s in rd:
                if s.w is not None:
                    deps.append(s.w)
            for s in wr:
                if s.w is not None:
                    deps.append(s.w)
                deps.extend(s.r.values())
            return deps

        def op(E, fn, rd=(), wr=()):
            for tok in deps_of(rd, wr):
                if E is PE and tok[0] is PE.sem:
                    continue
                E.wait(tok)
            ins = fn()
            E.cnt += 1
            ins.then_inc(E.sem, 1)
            tok = (E.sem, E.cnt)
            for s in rd:
                s.r[E.name] = tok
            for s in wr:
                s.w = tok
                s.r = {}
            return tok

        def dma(Q, out, in_, rd=(), wr=(), fn=None):
            for tok in deps_of(rd, wr):
                Q.wait(tok)
            i = dstate["i"]
            dstate["i"] = (i + 1) % NDS
            if dcnt[i] > 0:
                Q.wait((dsems[i], dcnt[i]))
            if fn is None:
                ins = Q.h.dma_start(out=out, in_=in_)
            else:
                ins = fn()
            dcnt[i] += 16
            ins.then_inc(dsems[i], 16)
            tok = (dsems[i], dcnt[i])
            for s in rd:
                s.r["dma%d" % i] = tok
            for s in wr:
                s.w = tok
                s.r = {}
            return tok

        def fence(frm, to):
            for t in to:
                for f in frm:
                    if f is t:
                        continue
                    if f.w is not None:
                        t.r["f%d" % id(f)] = f.w
                    for k, v in list(f.r.items()):
                        t.r["f%d%s" % (id(f), k)] = v

        def sb(name, shape, dt):
            return nc.alloc_sbuf_tensor("sb_" + name, shape, dt)

        KT = sb("KT", [128, 2, SEQ], BF16); KT_s = St()
        Vt = sb("Vt", [128, NBLK, 4, 65], BF16); V_s = St()
        kiT = sb("kiT", [128, SEQ], BF16); kiT_s = St()
        arena_base = nc.sbuf_base
        scores = sb("scores", [128, SEQ], F32); sc_s = St()
        al = lambda n, shape, dt, off: nc.alloc_sbuf_tensor_at("al_" + n, shape, dt, offset=arena_base + off)
        h_bf = al("h_bf", [128, DFF], BF16, 0); hbf_s = St()
        hT = al("hT", [128, 32, 128], BF16, 8192); hT_s = St()
        xn_bf = al("xn_bf", [128, D], BF16, 16384); xnbf_s = St()
        xnT = al("xnT", [128, 8, 128], BF16, 18432); xnT_s = St()
        vn_bf = al("vn_bf", [128, D], BF16, 20480); vnbf_s = St()
        wkvk = al("wkvk", [128, 8, 576], BF16, 22528); wkvk_s = St()
        arena_states = [hbf_s, hT_s, xnbf_s, xnT_s, vnbf_s, wkvk_s]
        sc16 = scores[0:16, 0:S_S]; sc16_s = St()
        sct = scores[0:16, S_S:2 * S_S]; sct_s = St()
        sctR = [sct, scores[0:16, 2 * S_S:3 * S_S]]; sctR_s = [sct_s, St()]
        WS = [sb("ws%d" % i, [128, 8, 512], BF16) for i in range(3)]
        WS_s = [[St(), St()] for _ in range(3)]
        FB0 = sb("fb0", [128, D], F32)
        FB12 = sb("fb12", [128, 2 * D], F32)
        FB3 = sb("fb3", [128, D], F32)
        FB4 = sb("fb4", [128, D], F32)
        FB = [FB0[:, :], FB12[:, 0:D], FB12[:, D:2 * D], FB3[:, :], FB4[:, :]]
        FB_s = [St() for _ in range(5)]
        junk8 = FB12[:, :].bitcast(mybir.dt.uint8)
        junki8 = FB12[:, :].bitcast(mybir.dt.int8)
        gmixT = sb("gmixT", [128, 8], F32); gffnT = sb("gffnT", [128, 8], F32); gpleT = sb("gpleT", [128, 8], F32)
        gsgub = sb("gsgub", [128, D], F32)
        coefs = sb("coefs", [16, 16], F32)
        gqb = sb("gqb", [128, 64], F32); gkb = sb("gkb", [128, 64], F32)
        cst_s = St()
        qn_bf = sb("qn_bf", [128, 2, 4, 128], BF16); qn_s = St()
        qT = sb("qT", [128, 4, 4, 128], BF16); qT_s = St()
        qi_bf = sb("qi_bf", [128, 512], BF16); qibf_s = St()
        qiT = sb("qiT", [128, 4, 128], BF16); qiT_s = St()
        absw = sb("absw", [128, 8], F32); sgnw = sb("sgnw", [128, 8], F32); w_s_ = St()
        kn_bf = sb("kn_bf", [128, 256], BF16); knbf_s = St()
        v_bf = sb("v_bf", [128, 256], BF16); vbf_s = St()
        kid_bf = sb("kid_bf", [128, 2, 64], BF16); kid_s = St()
        k32 = sb("k32", [128, 512], F32); k32_s = St()
        o32 = sb("o32", [128, 512], F32); o32_s = St()
        ki32 = sb("ki32", [128, 72], F32); ki32_s = St()
        rt = [sb("rt%d" % i, [128, 512], F32) for i in range(3)]; rt_s = [St(), St(), St()]
        PT = [sb("PT%d" % i, [128, 512], BF16) for i in range(3)]; PT_s = [St(), St(), St()]
        mT = [sb("mT%d" % i, [128, 4, 128], BF16) for i in range(2)]; mT_s = [St(), St()]
        MBg = [sb("MBg%d" % i, [128, 512], BF16) for i in range(2)]; MB_s = [St(), St()]
        identb = sb("identb", [128, 128], BF16); identf = sb("identf", [16, 16], F32)
        I4 = sb("I4", [128, 4, 128], BF16); I416 = sb("I416", [16, 4, 16], BF16)
        wsT = sb("wsT", [128, 8, 128], BF16); bsT = sb("bsT", [128, 8], F32)
        pen = sb("pen", [128, 2, 512], F32)
        p32 = sb("p32", [128, PLE], F32); p32_s = St()
        p_bf = sb("p_bf", [128, PLE], BF16); pbf_s = St()
        pT = sb("pT", [128, 2, 128], BF16); pT_s = St()
        sm = sb("sm", [128, 64], F32); sm_s = St()
        lo = sb("lo", [128, 1], F32); mid = sb("mid", [128, 1], F32); cntt = sb("cntt", [128, 1], F32)
        geq = sb("geq", [128, 1], F32); bis_s = St(); junk_s = St()
        cnta = sb("cnta", [128, 1], F32); cnta_s = St(); mid_s = St(); junka_s = St()
        mhalf = sb("mhalf", [128, 16], F32)
        ss16 = sb("ss16", [128, 16], F32); ss16_s = St()
        rs16 = sb("rs16", [128, 16], F32); rs16_s = St()
        ptb = sb("ptb", [128, NS * NPG], I32); idxa = sb("idxa", [128, NS * NPG], I32); iop = sb("iop", [128, 1], I32)
        NPB = 4
        pgKV = [sb("pgKV%d" % i, [128, 512], BF16) for i in range(NPB)]; pgK_s = [St() for _ in range(NPB)]
        pgK = [t[:, 0:256] for t in pgKV]
        pgV = [t[:, 256:512] for t in pgKV]
        pgV_s = pgK_s
        pgI = [sb("pgI%d" % i, [128, 2, 64], BF16) for i in range(NPB)]; pgI_s = [St() for _ in range(NPB)]
        ksT = sb("ksT", [128, 2, 16], BF16); kisT = sb("kisT", [128, 16], BF16); vs_bf = sb("vs_bf", [16, 256], BF16)
        smp_s = St()
        ws32 = sb("ws32", [128, 128], F32); ws32_s = St()
        wsb = sb("wsb", [128, 128], BF16); wsb_s = St()

        PB = [nc.alloc_psum_tensor("pb%d" % i, [128, 512], F32) for i in range(8)]
        PB_s = [St() for _ in range(8)]
        PBb = [PB[i][:].bitcast(BF16).rearrange("p (a b) -> p a b", a=8) for i in range(8)]

        def transposes(src_ap_fn, nblk, n, bank0, dst_fn, dst_states, src_states, evac=None, gainT=None):
            for b0 in range(0, nblk, 8):
                nb = min(8, nblk - b0)
                bank = 2 + (bank0 + b0 // 8) % 2
                if gainT is not None:
                    for j in range(nb):
                        op(PE, lambda j=j: nc.tensor.transpose(PBb[bank][:, j, 0:n], src_ap_fn(b0 + j), identb[0:n, 0:n]),
                           rd=list(src_states) + [cst_s], wr=[PB_s[bank]])
                    op(DVE, lambda: nc.vector.tensor_tensor(out=dst_fn(b0, nb), in0=PBb[bank][:, 0:nb, 0:n],
                                                            in1=gainT[:, b0:b0 + nb].unsqueeze(2).to_broadcast([128, nb, n]), op=ALU.mult),
                       rd=[PB_s[bank], cst_s], wr=dst_states)
                    continue
                for j in range(nb):
                    tok = op(PE, lambda j=j: nc.tensor.transpose(PBb[bank][:, j, 0:n], src_ap_fn(b0 + j), identb[0:n, 0:n]),
                             rd=list(src_states) + [cst_s], wr=[PB_s[bank]])
                E = evac or ACT
                if E is ACT:
                    op(ACT, lambda: nc.scalar.copy(out=dst_fn(b0, nb), in_=PBb[bank][:, 0:nb, 0:n]),
                       rd=[PB_s[bank]], wr=dst_states)
                else:
                    op(DVE, lambda: nc.vector.tensor_copy(out=dst_fn(b0, nb), in_=PBb[bank][:, 0:nb, 0:n]),
                       rd=[PB_s[bank]], wr=dst_states)

        RS = {"act": False}

        def rstd_from_ss(ss_ap, out_ap, n, ncol, inv_d, rd_s, wr_s):
            op(DVE, lambda: nc.vector.tensor_scalar(out=ss_ap, in0=ss_ap, scalar1=inv_d, scalar2=EPS, op0=ALU.mult, op1=ALU.add),
               rd=[], wr=[rd_s])
            if RS["act"]:
                op(ACT, lambda: nc.scalar.activation(out=out_ap, in_=ss_ap, func=AF.Sqrt), rd=[rd_s], wr=[wr_s])
                op(DVE, lambda: nc.vector.reciprocal(out=out_ap, in_=out_ap), rd=[], wr=[wr_s])
            else:
                op(POOL, lambda: nc.gpsimd.tensor_tensor(out=out_ap, in0=ss_ap, in1=mhalf[0:n, 0:ncol], op=ALU.pow),
                   rd=[rd_s, cst_s], wr=[wr_s])

        def rmsnorm_to_T(x_ap, x_s, gain, n):
            op(DVE, lambda: nc.vector.scalar_tensor_tensor(out=xn_bf[0:n, :], in0=x_ap, scalar=1.0, in1=x_ap, op0=ALU.mult, op1=ALU.mult,
                                                           accum_out=ss16[0:n, 0:1]),
               rd=[x_s], wr=[xnbf_s, ss16_s])
            rstd_from_ss(ss16[0:n, 0:1], rs16[0:n, 0:1], n, 1, 1.0 / D, ss16_s, rs16_s)
            op(DVE, lambda: nc.vector.tensor_scalar(out=xn_bf[0:n, :], in0=x_ap, scalar1=rs16[0:n, 0:1], scalar2=None, op0=ALU.mult),
               rd=[x_s, rs16_s], wr=[xnbf_s])
            transposes(lambda j: xn_bf[0:n, j * 128:(j + 1) * 128], 8, n, 0,
                       lambda b0, nb: xnT[:, b0:b0 + nb, 0:n], [xnT_s], [xnbf_s], gainT=gain)

        wseq = []
        one_slot = ([(win_s[i], 8, GRPS[i][1]) for i in range(13)] + [(wo_s[i], 8, 512) for i in range(2)]
                    + [(wup_s[i], 8, 512) for i in range(8)] + [(wdn_s[i], 8, 512) for i in range(8)]
                    + [(wpg_s[i], 8, 512) for i in range(2)] + [(wp_s[i], 2, 512) for i in range(2)])
        for _ in range(NSLOT + 1):
            wseq.extend(one_slot)
        wst = {"issued": 0, "next": 0}
        conv_s = St()

        def w_issue(upto):
            while wst["issued"] < min(upto, len(wseq)):
                k = wst["issued"]
                src, kc, ncol = wseq[k]
                sl = k % 3
                hk = kc // 2
                for hf in range(2):
                    dma(SP, WS[sl][:, hf * hk:(hf + 1) * hk, 0:ncol], src[:, hf * hk:(hf + 1) * hk, :], rd=[conv_s], wr=[WS_s[sl][hf]])
                wst["issued"] += 1

        def w_get():
            k = wst["next"]
            wst["next"] += 1
            w_issue(k + 1)
            return k % 3, k

        def w_done(k):
            w_issue(k + 3)

        def dense(lhsT_fn, kcs, n, ncols_list, bank_fn, lhs_states, after_fn):
            for gi, ncol in enumerate(ncols_list):
                sl, k = w_get()
                bank = bank_fn(gi)
                for kc in range(kcs):
                    op(PE, lambda kc=kc: nc.tensor.matmul(PB[bank][0:n, 0:ncol], lhsT=lhsT_fn(kc), rhs=WS[sl][:, kc, 0:ncol],
                                                          start=(kc == 0), stop=(kc == kcs - 1)),
                       rd=list(lhs_states) + [WS_s[sl][kc // (kcs // 2)]], wr=[PB_s[bank]])
                w_done(k)
                after_fn(gi, bank)

        with nc.allow_non_contiguous_dma(reason="tiny constant loads"):
            for (dst, src) in ((gsgub, gsgu_d), (gqb, gq_d), (gkb, gk_d)):
                dma(SP, dst[:], src.partition_broadcast(128), wr=[cst_s])
            for (dst, src) in ((gmixT, gmix_d), (gffnT, gffn_d), (gpleT, gple_d)):
                dma(SP, None, None, wr=[cst_s], fn=lambda dst=dst, src=src: nc.sync.dma_start(out=dst[:], in_=src.rearrange("(k p) -> p k", p=128)))
            dma(SP, None, None, wr=[cst_s], fn=lambda: nc.sync.dma_start(
                out=coefs[:, 0:8], in_=ws_d[:, 0, 0:1].rearrange("g a -> (g a)").partition_broadcast(16)))
            dma(SP, None, None, wr=[cst_s], fn=lambda: nc.sync.dma_start(
                out=coefs[:, 8:16], in_=bs_d[:, 0:1].rearrange("g a -> (g a)").partition_broadcast(16)))
            dma(SP, bsT[:], bs_d.rearrange("g t -> t g"), wr=[cst_s], fn=lambda: nc.sync.dma_start(
                out=bsT[:], in_=bs_d.rearrange("g t -> t g")))
            dma(SP, pen[:], pen_d[:, :, :], wr=[cst_s])
            dma(SP, ptb[:], pt_d.partition_broadcast(128), wr=[cst_s])
        op(POOL, lambda: nc.gpsimd.memset(identb[:], 1.0), wr=[cst_s])
        op(POOL, lambda: nc.gpsimd.affine_select(out=identb[:], in_=identb[:], pattern=[[-1, 128]], compare_op=ALU.is_equal,
                                                 fill=0.0, base=0, channel_multiplier=1), wr=[cst_s])
        op(POOL, lambda: nc.gpsimd.memset(identf[:], 1.0), wr=[cst_s])
        op(POOL, lambda: nc.gpsimd.affine_select(out=identf[:], in_=identf[:], pattern=[[-1, 16]], compare_op=ALU.is_equal,
                                                 fill=0.0, base=0, channel_multiplier=1), wr=[cst_s])
        op(POOL, lambda: nc.gpsimd.memset(mhalf[:], -0.5), wr=[cst_s])
        op(POOL, lambda: nc.gpsimd.iota(iop[:], pattern=[[0, 1]], base=0, channel_multiplier=1), wr=[cst_s])
        for g in range(4):
            op(DVE, lambda g=g: nc.vector.tensor_copy(out=I4[:, g, :], in_=identb[:]), rd=[], wr=[cst_s])
            op(DVE, lambda g=g: nc.vector.tensor_copy(out=I416[:, g, :], in_=identb[0:16, 0:16]), rd=[], wr=[cst_s])
        op(DVE, lambda: nc.vector.tensor_scalar(out=gqb[:], in0=gqb[:], scalar1=ATTN_SCALE, scalar2=None, op0=ALU.mult), wr=[cst_s])
        op(DVE, lambda: nc.vector.tensor_scalar(out=idxa[:], in0=ptb[:], scalar1=128, scalar2=iop[:, 0:1], op0=ALU.mult, op1=ALU.add),
           wr=[cst_s])
        op(POOL, lambda: nc.gpsimd.memset(qT[:], 0.0), wr=[qT_s])
        op(POOL, lambda: nc.gpsimd.memset(Vt[:], 0.0), wr=[V_s])
        op(POOL, lambda: nc.gpsimd.memset(Vt[:, :, :, 64:65], 1.0), wr=[V_s])
        for g in range(8):
            dma(SP, ws32[:], ws_d[g, :, :], wr=[ws32_s])
            op(POOL, lambda: nc.gpsimd.affine_select(out=ws32[:], in_=ws32[:], pattern=[[-1, 128]], compare_op=ALU.is_ge,
                                                     fill=0.0, base=0, channel_multiplier=1), wr=[ws32_s])
            op(DVE, lambda: nc.vector.tensor_copy(out=wsb[:], in_=ws32[:]), rd=[ws32_s], wr=[wsb_s])
            op(PE, lambda: nc.tensor.transpose(PBb[3][:, 0, :], wsb[:], identb[:]), rd=[wsb_s, cst_s], wr=[PB_s[3]])
            op(ACT, lambda g=g: nc.scalar.copy(out=wsT[:, g, :], in_=PBb[3][:, 0, :]), rd=[PB_s[3]], wr=[cst_s])

        conv_jobs = []
        conv_toks = []

        def conv(dst, src2d, kc, col0, ncol, first=False):
            for k in range(kc):
                job = (lambda k=k: conv_toks.append(dma(
                    POOL, None, None, wr=[],
                    fn=lambda: nc.gpsimd.dma_start(out=dst[:, k, :], in_=src2d[k * 128:(k + 1) * 128, col0:col0 + ncol]))))
                if first:
                    job()
                else:
                    conv_jobs.append(job)
        for i, (o, w) in enumerate(GRPS):
            conv(win_s[i], win_d, 8, o, w, first=(i in (2, 4)))
        for i in range(2):
            conv(wo_s[i], wo_d, 8, i * 512, 512)
        for i in range(8):
            conv(wup_s[i], wup_d, 8, i * 512, 512)
        for i in range(8):
            nn, kg = i // 4, i % 4
            conv(wdn_s[i], wdn_d[kg * 1024:(kg + 1) * 1024, :], 8, nn * 512, 512)
        for i in range(2):
            conv(wpg_s[i], wpg_d, 8, i * 512, 512)
        for i in range(2):
            conv(wp_s[i], wp_d, 2, i * 512, 512)
        for tok in conv_toks:
            SP.wait(tok)

        def head_norm(src32, n, nh, gain, out_fn, out_states, e=None):
            v3 = src32.rearrange("p (h d) -> p h d", d=64)
            if e is not None:
                tmp4 = o32[0:n, 0:nh * 64].rearrange("p (e g d) -> p e g d", e=e, d=64)
                op(DVE, lambda: nc.vector.tensor_tensor(out=o32[0:n, 0:nh * 64], in0=src32, in1=src32, op=ALU.mult), rd=[k32_s], wr=[o32_s])
                op(DVE, lambda: nc.vector.tensor_reduce(out=ss16[0:n, 0:nh], in_=o32[0:n, 0:nh * 64].rearrange("p (h d) -> p h d", d=64),
                                                        axis=AX.X, op=ALU.add), rd=[o32_s], wr=[ss16_s])
                rstd_from_ss(ss16[0:n, 0:nh], rs16[0:n, 0:nh], n, nh, 1.0 / 64, ss16_s, rs16_s)
                op(DVE, lambda: nc.vector.tensor_tensor(out=o32[0:n, 0:nh * 64].rearrange("p (h d) -> p h d", d=64), in0=v3,
                                                        in1=rs16[0:n, 0:nh].unsqueeze(2).to_broadcast([n, nh, 64]), op=ALU.mult),
                   rd=[k32_s, rs16_s], wr=[o32_s])
                for ee in range(e):
                    op(DVE, lambda ee=ee: nc.vector.tensor_tensor(out=out_fn(ee), in0=tmp4[:, ee, :, :],
                                                                  in1=gain[0:n, :].unsqueeze(1).to_broadcast([n, nh // e, 64]), op=ALU.mult),
                       rd=[o32_s, cst_s], wr=out_states)
                return
            op(DVE, lambda: nc.vector.tensor_tensor(out=o32[0:n, 0:nh * 64], in0=src32, in1=src32, op=ALU.mult), rd=[k32_s], wr=[o32_s])
            op(DVE, lambda: nc.vector.tensor_reduce(out=ss16[0:n, 0:nh], in_=o32[0:n, 0:nh * 64].rearrange("p (h d) -> p h d", d=64),
                                                    axis=AX.X, op=ALU.add), rd=[o32_s], wr=[ss16_s])
            rstd_from_ss(ss16[0:n, 0:nh], rs16[0:n, 0:nh], n, nh, 1.0 / 64, ss16_s, rs16_s)
            op(DVE, lambda: nc.vector.tensor_tensor(out=o32[0:n, 0:nh * 64].rearrange("p (h d) -> p h d", d=64), in0=v3,
                                                    in1=rs16[0:n, 0:nh].unsqueeze(2).to_broadcast([n, nh, 64]), op=ALU.mult),
               rd=[k32_s, rs16_s], wr=[o32_s])
            op(DVE, lambda: nc.vector.tensor_tensor(out=out_fn(), in0=o32[0:n, 0:nh * 64].rearrange("p (h d) -> p h d", d=64),
                                                    in1=gain[0:n, :].unsqueeze(1).to_broadcast([n, nh, 64]), op=ALU.mult),
               rd=[o32_s, cst_s], wr=out_states)

        def kv_append(blk, n, k_ap, k_s, v_ap, v_s, kid_ap, kid_s_):
            c0 = blk * 128
            for j in range(2):
                op(PE, lambda j=j: nc.tensor.transpose(PBb[3][:, j, 0:n], k_ap[0:n, j * 128:(j + 1) * 128], identb[0:n, 0:n]),
                   rd=[k_s, cst_s], wr=[PB_s[3]])
            op(PE, lambda: nc.tensor.transpose(PBb[3][:, 2, 0:n], kid_ap[0:n, :, :].rearrange("p a d -> p (a d)"), identb[0:n, 0:n]),
               rd=[kid_s_, cst_s], wr=[PB_s[3]])
            op(ACT, lambda: nc.scalar.copy(out=KT[:, :, c0:c0 + n], in_=PBb[3][:, 0:2, 0:n]), rd=[PB_s[3]], wr=[KT_s])
            op(ACT, lambda: nc.scalar.copy(out=kiT[:, c0:c0 + n], in_=PBb[3][:, 2, 0:n]), rd=[PB_s[3]], wr=[kiT_s])
            op(ACT, lambda: nc.scalar.copy(out=Vt[0:n, blk, :, 0:64], in_=v_ap[0:n, :].rearrange("p (c d) -> p c d", d=64)),
               rd=[v_s], wr=[V_s])

        def indexer(n, nch, sc_ap, sc_state, k0=0, ki_state=None, hook=None):
            ki_state = ki_state or kiT_s
            S = nch * 128
            j = 0
            for g0 in range(0, S, 512):
                w = min(512, S - g0)
                for h in range(8):
                    if hook is not None:
                        hook()
                    bank = j % 3
                    r = rt[j % 3]
                    rs = rt_s[j % 3]
                    j += 1
                    pb = (h % 2) * 64
                    op(PE, lambda: nc.tensor.matmul(PB[bank][0:n, 0:w], lhsT=qiT[pb:pb + 64, h // 2, 0:n], rhs=kiT[pb:pb + 64, k0 + g0:k0 + g0 + w],
                                                    start=True, stop=True), rd=[qiT_s, ki_state], wr=[PB_s[bank]])
                    op(ACT, lambda: nc.scalar.activation(out=r[0:n, 0:w], in_=PB[bank][0:n, 0:w], func=AF.Relu, scale=absw[0:n, h:h + 1]),
                       rd=[PB_s[bank], w_s_], wr=[rs])
                    if h == 0:
                        op(DVE, lambda: nc.vector.tensor_scalar(out=sc_ap[0:n, g0:g0 + w], in0=r[0:n, 0:w], scalar1=sgnw[0:n, 0:1], scalar2=None,
                                                                op0=ALU.mult), rd=[rs, w_s_], wr=[sc_state])
                    else:
                        op(DVE, lambda: nc.vector.scalar_tensor_tensor(out=sc_ap[0:n, g0:g0 + w], in0=r[0:n, 0:w], scalar=sgnw[0:n, h:h + 1],
                                                                       in1=sc_ap[0:n, g0:g0 + w], op0=ALU.mult, op1=ALU.add),
                           rd=[rs, w_s_], wr=[sc_state])

        def bisect(n, S, sc_ap, sc_state, junk_ap, junk_state, junk_act=None):
            op(DVE, lambda: nc.vector.memset(lo[0:n, :], -BRK), wr=[bis_s])
            split = junk_act is not None and S >= 2048
            S1 = (int(S * 0.46) // 128) * 128 if split else S
            S2 = S - S1
            wd = BRK
            for it in range(NIT):
                op(DVE, lambda: nc.vector.tensor_scalar(out=mid[0:n, :], in0=lo[0:n, :], scalar1=wd, scalar2=None, op0=ALU.add),
                   rd=[bis_s], wr=[bis_s, mid_s])
                if split:
                    op(ACT, lambda: nc.scalar.activation(out=junk_act[0:n, S1:S], in_=sc_ap[0:n, S1:S], func=AF.Sign, bias=mid[0:n, 0:1], scale=-1.0,
                                                         accum_out=cnta[0:n, 0:1]), rd=[sc_state, mid_s], wr=[junka_s, cnta_s])
                op(DVE, lambda: nc.vector.tensor_scalar(out=junk_ap[0:n, 0:S1], in0=sc_ap[0:n, 0:S1], scalar1=mid[0:n, 0:1], scalar2=None, op0=ALU.is_ge,
                                                        op1=ALU.add, accum_out=cntt[0:n, 0:1]),
                   rd=[sc_state, bis_s], wr=[junk_state, bis_s])
                thr = 255.5
                if split:
                    op(DVE, lambda: nc.vector.scalar_tensor_tensor(out=cntt[0:n, :], in0=cnta[0:n, :], scalar=-0.5, in1=cntt[0:n, :], op0=ALU.mult,
                                                                   op1=ALU.add), rd=[cnta_s, bis_s], wr=[bis_s])
                    thr = 255.5 - S2 / 2.0
                op(DVE, lambda: nc.vector.tensor_scalar(out=geq[0:n, :], in0=cntt[0:n, :], scalar1=thr, scalar2=wd, op0=ALU.is_ge,
                                                        op1=ALU.mult), rd=[bis_s], wr=[bis_s])
                op(DVE, lambda: nc.vector.tensor_tensor(out=lo[0:n, :], in0=lo[0:n, :], in1=geq[0:n, :], op=ALU.add), rd=[bis_s, mid_s], wr=[bis_s])
                wd = wd / 2.0

        def attend(n, nch, sc_ap, sc_state, sel_ap, out_ap, out_state, ch0=0, kt_state=None, v_state=None, hook=None):
            kt_state = kt_state or KT_s
            v_state = v_state or V_s
            steps = [(ch, c) for ch in range(nch) for c in range(4)]

            def prep_group(gi):
                w = min(512, nch * 128 - gi * 512)
                ng = w // 128
                mb = MBg[gi % 2]
                bT = 0
                op(DVE, lambda: nc.vector.tensor_scalar(out=mb[0:n, 0:w], in0=sc_ap[0:n, gi * 512:gi * 512 + w], scalar1=lo[0:n, 0:1],
                                                        scalar2=None, op0=ALU.is_ge), rd=[sc_state, bis_s], wr=[MB_s[gi % 2]])
                for cj in range(ng):
                    op(PE, lambda cj=cj: nc.tensor.transpose(PBb[bT][:, cj, 0:n], mb[0:n, cj * 128:(cj + 1) * 128], identb[0:n, 0:n]),
                       rd=[MB_s[gi % 2], cst_s], wr=[PB_s[bT]])
                op(DVE, lambda: nc.vector.tensor_copy(out=mT[gi % 2][:, 0:ng, 0:n], in_=PBb[bT][:, 0:ng, 0:n]), rd=[PB_s[bT]], wr=[mT_s[gi % 2]])

            def emit_qk(k):
                ch, c = steps[k]
                gi, cj = ch // 4, ch % 4
                if cj == 0 and c == 0:
                    prep_group(gi)
                bank = 1 + (k % 3)
                pb = (c % 2) * 64
                outv = PB[bank][:, 0:4 * n].rearrange("p (g t) -> p g t", g=4)
                op(PE, lambda: nc.tensor.matmul(outv, lhsT=KT[:, c // 2, (ch0 + ch) * 128:(ch0 + ch + 1) * 128], rhs=qT[:, c, :, 0:n],
                                                start=True, stop=True), rd=[kt_state, qT_s], wr=[PB_s[bank]])

            emit_qk(0)
            if len(steps) > 1:
                emit_qk(1)
            for k in range(len(steps)):
                ch, c = steps[k]
                gi, cj = ch // 4, ch % 4
                if hook is not None:
                    hook()
                if k + 2 < len(steps):
                    emit_qk(k + 2)
                bank = 1 + (k % 3)
                pt_ = PT[k % 3]
                pts = PT_s[k % 3]
                op(ACT, lambda: nc.scalar.activation(out=pt_[:, 0:4 * n], in_=PB[bank][:, 0:4 * n], func=AF.Exp), rd=[PB_s[bank]], wr=[pts])
                pt3 = pt_[:, 0:4 * n].rearrange("p (g t) -> p g t", g=4)
                op(DVE, lambda: nc.vector.tensor_tensor(out=pt3, in0=pt3, in1=mT[gi % 2][:, cj, 0:n].unsqueeze(1).to_broadcast([128, 4, n]),
                                                        op=ALU.mult), rd=[mT_s[gi % 2]], wr=[pts])
                for g in range(4):
                    op(PE, lambda g=g: nc.tensor.matmul(PB[4 + c][0:n, g * 65:(g + 1) * 65], lhsT=pt_[:, g * n:(g + 1) * n], rhs=Vt[:, ch0 + ch, c, :],
                                                        start=(ch == 0 and g == 0), stop=(ch == nch - 1), skip_group_check=True),
                       rd=[pts, v_state], wr=[PB_s[4 + c]])
            for c in range(4):
                acc = PB[4 + c][0:n, 0:260].rearrange("p (g e) -> p g e", e=65)
                op(DVE, lambda: nc.vector.reciprocal(out=sm[0:n, 4 * c:4 * c + 4], in_=acc[:, :, 64]), rd=[PB_s[4 + c]], wr=[sm_s])
                op(DVE, lambda: nc.vector.tensor_tensor(out=out_ap[0:n, c * 256:(c + 1) * 256].rearrange("p (g d) -> p g d", d=64),
                                                        in0=acc[:, :, 0:64], in1=sm[0:n, 4 * c:4 * c + 4].unsqueeze(2).to_broadcast([n, 4, 64]),
                                                        op=ALU.mult), rd=[PB_s[4 + c], sm_s], wr=[out_state])

        STh_s = [[St(), St()], [St(), St()]]

        def attend_sample(nch, sc_ap, sc_state, out_ap, out_state, ch0, kt_state, v_state, hook=None):
            n = 16

            def prep_group(gi):
                w = min(512, nch * 128 - gi * 512)
                ng = w // 128
                mb = MBg[gi % 2]
                op(DVE, lambda: nc.vector.tensor_scalar(out=mb[0:n, 0:w], in0=sc_ap[0:n, gi * 512:gi * 512 + w], scalar1=lo[0:n, 0:1],
                                                        scalar2=None, op0=ALU.is_ge), rd=[sc_state, bis_s], wr=[MB_s[gi % 2]])
                for cj in range(ng):
                    op(PE, lambda cj=cj: nc.tensor.transpose(PBb[0][:, cj, 0:n], mb[0:n, cj * 128:(cj + 1) * 128], identb[0:n, 0:n]),
                       rd=[MB_s[gi % 2], cst_s], wr=[PB_s[0]])
                op(DVE, lambda: nc.vector.tensor_copy(out=mT[gi % 2][:, 0:ng, 0:n], in_=PBb[0][:, 0:ng, 0:n]), rd=[PB_s[0]], wr=[mT_s[gi % 2]])

            def emit_qk(ch):
                gi, cj = ch // 4, ch % 4
                if cj == 0:
                    prep_group(gi)
                hf = 0
                for c in range(4):
                    pb = (c % 2) * 64
                    col = hf * 128 + (c // 2) * 64
                    outv = PB[1 + (c % 2)][:, col:col + 64].rearrange("p (g t) -> p g t", g=4)
                    op(PE, lambda: nc.tensor.matmul(outv, lhsT=KT[:, c // 2, (ch0 + ch) * 128:(ch0 + ch + 1) * 128],
                                                    rhs=qT[:, c, :, 0:n], start=True, stop=True, skip_group_check=True),
                       rd=[kt_state, qT_s], wr=[PB_s[1 + (c % 2)]])

            emit_qk(0)
            for ch in range(nch):
                gi, cj = ch // 4, ch % 4
                hf = 0
                if hook is not None:
                    hook()
                    hook()
                pt_ = PT[ch % 3]
                pts = PT_s[ch % 3]
                for e in range(2):
                    op(ACT, lambda e=e: nc.scalar.activation(out=pt_[:, e * 128:(e + 1) * 128], in_=PB[1 + e][:, hf * 128:(hf + 1) * 128], func=AF.Exp),
                       rd=[PB_s[1 + e]], wr=[pts])
                pt3 = pt_[:, 0:256].rearrange("p (a t) -> p a t", t=n)
                op(DVE, lambda: nc.vector.tensor_tensor(out=pt3, in0=pt3, in1=mT[gi % 2][:, cj, 0:n].unsqueeze(1).to_broadcast([128, 16, n]),
                                                        op=ALU.mult), rd=[mT_s[gi % 2]], wr=[pts])
                if ch + 1 < nch:
                    emit_qk(ch + 1)
                for c in range(4):
                    for g in range(4):
                        a0 = (c % 2) * 128 + (c // 2) * 64 + g * n
                        op(PE, lambda: nc.tensor.matmul(PB[4 + c][0:n, g * 65:(g + 1) * 65], lhsT=pt_[:, a0:a0 + n], rhs=Vt[:, ch0 + ch, c, :],
                                                        start=(ch == 0 and g == 0), stop=(ch == nch - 1), skip_group_check=True),
                           rd=[pts, v_state], wr=[PB_s[4 + c]])
            for c in range(4):
                acc = PB[4 + c][0:n, 0:260].rearrange("p (g e) -> p g e", e=65)
                op(DVE, lambda: nc.vector.reciprocal(out=sm[0:n, 4 * c:4 * c + 4], in_=acc[:, :, 64]), rd=[PB_s[4 + c]], wr=[sm_s])
                op(DVE, lambda: nc.vector.tensor_tensor(out=out_ap[0:n, c * 256:(c + 1) * 256].rearrange("p (g d) -> p g d", d=64),
                                                        in0=acc[:, :, 0:64], in1=sm[0:n, 4 * c:4 * c + 4].unsqueeze(2).to_broadcast([n, 4, 64]),
                                                        op=ALU.mult), rd=[PB_s[4 + c], sm_s], wr=[out_state])

        def project(n, x_ap, x_s, outs):
            fence([sc_s], arena_states)
            rmsnorm_to_T(x_ap, x_s, gmixT, n)
            ug, ug_s = FB[1], FB_s[1]
            vn, vn_s = FB[2], FB_s[2]
            sga, sga_s = FB[3], FB_s[3]
            sgb, sgb_s = FB[4], FB_s[4]

            def after(gi, bank):
                pb_s = PB_s[bank]
                pbk = PB[bank]
                if gi in (0, 1):
                    op(ACT, lambda: nc.scalar.copy(out=k32[0:n, :], in_=pbk[0:n, :]), rd=[pb_s], wr=[k32_s])
                    head_norm(k32[0:n, :], n, 8, gqb, lambda ee: qn_bf[0:n, gi, :, ee * 64:(ee + 1) * 64], [qn_s], e=2)
                elif gi == 2:
                    op(ACT, lambda: nc.scalar.copy(out=k32[0:n, :], in_=pbk[0:n, :]), rd=[pb_s], wr=[k32_s])
                    head_norm(k32[0:n, 0:256], n, 4, gkb, lambda: o32[0:n, 256:512].rearrange("p (h d) -> p h d", d=64), [o32_s])
                    dma(POOL, outs["k"], o32[0:n, 256:512], rd=[o32_s])
                    dma(POOL, outs["v"], k32[0:n, 256:512], rd=[k32_s])
                    if "kv" in outs:
                        op(DVE, lambda: nc.vector.tensor_copy(out=kn_bf[0:n, :], in_=o32[0:n, 256:512]), rd=[o32_s], wr=[knbf_s])
                        op(DVE, lambda: nc.vector.tensor_copy(out=vs_bf[0:n, :], in_=k32[0:n, 256:512]), rd=[k32_s], wr=[smp_s])
                elif gi == 3:
                    op(ACT, lambda: nc.scalar.copy(out=qi_bf[0:n, :], in_=pbk[0:n, :]), rd=[pb_s], wr=[qibf_s])
                elif gi == 4:
                    op(ACT, lambda: nc.scalar.copy(out=ki32[0:n, :], in_=pbk[0:n, 0:72]), rd=[pb_s], wr=[ki32_s])
                    dma(POOL, outs["ki"], ki32[0:n, 0:64], rd=[ki32_s])
                    op(ACT, lambda: nc.scalar.activation(out=absw[0:n, :], in_=ki32[0:n, 64:72], func=AF.Abs, scale=IDXS), rd=[ki32_s], wr=[w_s_])
                    op(ACT, lambda: nc.scalar.activation(out=sgnw[0:n, :], in_=ki32[0:n, 64:72], func=AF.Sign), rd=[ki32_s], wr=[w_s_])
                    if "kv" in outs:
                        for a in range(2):
                            op(DVE, lambda a=a: nc.vector.tensor_copy(out=kid_bf[0:n, a, :], in_=ki32[0:n, 0:64]), rd=[ki32_s], wr=[kid_s])
                elif gi in (5, 6):
                    o = (gi - 5) * 512
                    op(ACT, lambda: nc.scalar.activation(out=ug[0:n, o:o + 512], in_=pbk[0:n, :], func=AF.Gelu_apprx_tanh), rd=[pb_s], wr=[ug_s])
                elif gi in (7, 8):
                    o = (gi - 7) * 512
                    op(ACT, lambda: nc.scalar.activation(out=vn[0:n, o:o + 512], in_=pbk[0:n, :], func=AF.Gelu_apprx_tanh), rd=[pb_s], wr=[vn_s])
                elif gi in (9, 10):
                    o = (gi - 9) * 512
                    op(ACT, lambda: nc.scalar.activation(out=sga[0:n, o:o + 512], in_=pbk[0:n, :], func=AF.Sigmoid), rd=[pb_s], wr=[sga_s])
                else:
                    o = (gi - 11) * 512
                    op(ACT, lambda: nc.scalar.activation(out=sgb[0:n, o:o + 512], in_=pbk[0:n, :], func=AF.Sigmoid), rd=[pb_s], wr=[sgb_s])

            dense(lambda kc: xnT[:, kc, 0:n], 8, n, [g[1] for g in GRPS], lambda gi: gi % 2, [xnT_s], after)
            for j in range(8):
                op(PE, lambda j=j: nc.tensor.transpose(PBb[2][:, j, 0:n], qn_bf[0:n, j // 4, j % 4, :], identb[0:n, 0:n]),
                   rd=[qn_s, cst_s], wr=[PB_s[2]])
            q5 = qT[:, :, :, :].rearrange("p (a e) g t -> p a e g t", e=2)
            for e in range(2):
                op(ACT, lambda e=e: nc.scalar.copy(out=q5[e * 64:(e + 1) * 64, :, e, :, 0:n],
                                                    in_=PBb[2][e * 64:(e + 1) * 64, 0:8, 0:n].rearrange("p (a g) t -> p a g t", g=4)),
                   rd=[PB_s[2]], wr=[qT_s])
            transposes(lambda j: qi_bf[0:n, j * 128:(j + 1) * 128], 4, n, 3,
                       lambda b0, nb: qiT[:, b0:b0 + nb, 0:n], [qiT_s], [qibf_s])
            op(DVE, lambda: nc.vector.scalar_tensor_tensor(out=vn_bf[0:n, :], in0=vn[0:n, :], scalar=1.0, in1=vn[0:n, :], op0=ALU.mult,
                                                           op1=ALU.mult, accum_out=ss16[0:n, 0:1]), rd=[vn_s], wr=[vnbf_s, ss16_s])
            rstd_from_ss(ss16[0:n, 0:1], rs16[0:n, 0:1], n, 1, 1.0 / D, ss16_s, rs16_s)
            op(DVE, lambda: nc.vector.scalar_tensor_tensor(out=vn[0:n, :], in0=vn[0:n, :], scalar=rs16[0:n, 0:1], in1=gsgub[0:n, :],
                                                           op0=ALU.mult, op1=ALU.mult), rd=[rs16_s, cst_s], wr=[vn_s])
            dma(POOL, outs["sg"], vn[0:n, :], rd=[vn_s])
            if n == 128:
                op(DVE, lambda: nc.vector.tensor_copy(out=vn_bf[:, :], in_=vn[:, :]), rd=[vn_s], wr=[vnbf_s])
                for g in range(8):
                    bank = 4 + g // 4
                    op(PE, lambda g=g: nc.tensor.matmul(PB[bank][:, (g % 4) * 128:(g % 4 + 1) * 128], lhsT=wsT[:, g, :],
                                                        rhs=vn_bf[:, g * 128:(g + 1) * 128], start=True, stop=True, skip_group_check=True),
                       rd=[vnbf_s, cst_s], wr=[PB_s[bank]])
                for hb in range(2):
                    op(DVE, lambda hb=hb: nc.vector.tensor_tensor(
                        out=vn[:, hb * 512:(hb + 1) * 512].rearrange("p (g c) -> p g c", c=128),
                        in0=PB[4 + hb][:, :].rearrange("p (g c) -> p g c", c=128),
                        in1=bsT[:, hb * 4:hb * 4 + 4].unsqueeze(2).to_broadcast([128, 4, 128]), op=ALU.add),
                       rd=[PB_s[4 + hb], cst_s], wr=[vn_s])
            else:
                v3s = vn[0:n, :].rearrange("p (g c) -> p g c", c=128)
                op(DVE, lambda: nc.vector.tensor_tensor(out=v3s, in0=v3s, in1=coefs[0:n, 0:8].unsqueeze(2).to_broadcast([n, 8, 128]), op=ALU.mult),
                   rd=[cst_s], wr=[vn_s])
                op(DVE, lambda: nc.vector.tensor_tensor(out=v3s, in0=v3s, in1=coefs[0:n, 8:16].unsqueeze(2).to_broadcast([n, 8, 128]), op=ALU.add),
                   rd=[cst_s], wr=[vn_s])
            op(DVE, lambda: nc.vector.tensor_tensor(out=vn[0:n, :], in0=vn[0:n, :], in1=ug[0:n, :], op=ALU.mult), rd=[ug_s], wr=[vn_s])
            op(DVE, lambda: nc.vector.tensor_tensor(out=sgb[0:n, :], in0=sgb[0:n, :], in1=vn[0:n, :], op=ALU.mult), rd=[vn_s], wr=[sgb_s])

        def finish(n, x_ap, x_s, oatt, oatt_s, p_src, y_dst):
            sga, sga_s = FB[3], FB_s[3]
            msgu, msgu_s = FB[4], FB_s[4]
            fence([sc_s], arena_states)
            dma(SP, p32[0:n, :], p_src, wr=[p32_s])
            op(DVE, lambda: nc.vector.tensor_tensor(out=oatt[0:n, :], in0=oatt[0:n, :], in1=sga[0:n, :], op=ALU.mult), rd=[sga_s], wr=[oatt_s])
            op(DVE, lambda: nc.vector.tensor_tensor(out=xn_bf[0:n, :], in0=oatt[0:n, :], in1=msgu[0:n, :], op=ALU.add),
               rd=[oatt_s, msgu_s], wr=[xnbf_s])
            transposes(lambda j: xn_bf[0:n, j * 128:(j + 1) * 128], 8, n, 2, lambda b0, nb: xnT[:, b0:b0 + nb, 0:n], [xnT_s], [xnbf_s])

            def after_o(gi, bank):
                op(DVE, lambda: nc.vector.tensor_tensor(out=x_ap[:, gi * 512:(gi + 1) * 512], in0=x_ap[:, gi * 512:(gi + 1) * 512],
                                                        in1=PB[bank][0:n, :], op=ALU.add), rd=[PB_s[bank]], wr=[x_s])
            dense(lambda kc: xnT[:, kc, 0:n], 8, n, [512, 512], lambda gi: gi % 2, [xnT_s], after_o)
            rmsnorm_to_T(x_ap, x_s, gffnT, n)

            def after_up(gi, bank):
                op(ACT, lambda: nc.scalar.activation(out=rt[gi % 2][0:n, :], in_=PB[bank][0:n, :], func=AF.Relu), rd=[PB_s[bank]], wr=[rt_s[gi % 2]])
                op(DVE, lambda: nc.vector.tensor_tensor(out=h_bf[0:n, gi * 512:(gi + 1) * 512], in0=rt[gi % 2][0:n, :], in1=rt[gi % 2][0:n, :],
                                                        op=ALU.mult), rd=[rt_s[gi % 2]], wr=[hbf_s])
            dense(lambda kc: xnT[:, kc, 0:n], 8, n, [512] * 8, lambda gi: gi % 2, [xnT_s], after_up)
            transposes(lambda j: h_bf[0:n, j * 128:(j + 1) * 128], 32, n, 2, lambda b0, nb: hT[:, b0:b0 + nb, 0:n], [hT_s], [hbf_s])
            for nn in range(2):
                bank = nn % 2
                for kg in range(4):
                    sl, k = w_get()
                    for kc in range(8):
                        op(PE, lambda kc=kc: nc.tensor.matmul(PB[bank][0:n, :], lhsT=hT[:, kg * 8 + kc, 0:n], rhs=WS[sl][:, kc, :],
                                                              start=(kg == 0 and kc == 0), stop=(kg == 3 and kc == 7)),
                           rd=[hT_s, WS_s[sl][kc // 4]], wr=[PB_s[bank]])
                    w_done(k)
                op(DVE, lambda: nc.vector.tensor_tensor(out=x_ap[:, nn * 512:(nn + 1) * 512], in0=x_ap[:, nn * 512:(nn + 1) * 512],
                                                        in1=PB[bank][0:n, :], op=ALU.add), rd=[PB_s[bank]], wr=[x_s])
            rmsnorm_to_T(x_ap, x_s, gpleT, n)
            gate, gate_s = FB[1], FB_s[1]

            def after_pg(gi, bank):
                op(ACT, lambda: nc.scalar.activation(out=gate[0:n, gi * 512:(gi + 1) * 512], in_=PB[bank][0:n, :], func=AF.Sigmoid),
                   rd=[PB_s[bank]], wr=[gate_s])
            dense(lambda kc: xnT[:, kc, 0:n], 8, n, [512, 512], lambda gi: gi % 2, [xnT_s], after_pg)
            op(DVE, lambda: nc.vector.tensor_copy(out=p_bf[0:n, :], in_=p32[0:n, :]), rd=[p32_s], wr=[pbf_s])
            transposes(lambda j: p_bf[0:n, j * 128:(j + 1) * 128], 2, n, 3, lambda b0, nb: pT[:, b0:b0 + nb, 0:n], [pT_s], [pbf_s])

            def after_p(gi, bank):
                op(DVE, lambda: nc.vector.tensor_tensor(out=gate[0:n, gi * 512:(gi + 1) * 512], in0=gate[0:n, gi * 512:(gi + 1) * 512],
                                                        in1=PB[bank][0:n, :], op=ALU.mult), rd=[PB_s[bank]], wr=[gate_s])
                op(DVE, lambda: nc.vector.tensor_tensor(out=x_ap[:, gi * 512:(gi + 1) * 512], in0=x_ap[:, gi * 512:(gi + 1) * 512],
                                                        in1=gate[0:n, gi * 512:(gi + 1) * 512], op=ALU.add), rd=[gate_s], wr=[x_s])
            dense(lambda kc: pT[:, kc, 0:n], 2, n, [512, 512], lambda gi: gi % 2, [pT_s], after_p)
            dma(POOL, y_dst, x_ap, rd=[x_s])

        n = 128
        fence(arena_states + [sc_s, sc16_s, sct_s], [wkvk_s])
        dma(SP, wkvk[:, :, 0:512], win_s[2][:, :, :], wr=[wkvk_s])
        dma(SP, wkvk[:, :, 512:576], win_s[4][:, :, 0:64], wr=[wkvk_s])
        xnT2 = al("xnT2", [128, 8, 128], BF16, 0); xnT2_s = St()
        fence(arena_states + [sc_s, sc16_s, sct_s], [xnT2_s])
        xnTs = [(xnT, xnT_s), (xnT2, xnT2_s)]
        ssA = sb("ssA", [128, 2], F32); ssA_s = St(); rsA_s = St()

        def stageA(kb):
            xt, xt_s = FB[kb % 2], FB_s[kb % 2]
            xT, xT_s = xnTs[kb % 2]
            dma(SP, xt[:, :], xb_d[kb * 128:(kb + 1) * 128, :], wr=[xt_s])
            op(DVE, lambda: nc.vector.scalar_tensor_tensor(out=xn_bf[:, :], in0=xt[:, :], scalar=1.0, in1=xt[:, :], op0=ALU.mult, op1=ALU.mult,
                                                           accum_out=ssA[:, 0:1]), rd=[xt_s], wr=[xnbf_s, ssA_s])
            rstd_from_ss(ssA[:, 0:1], ssA[:, 1:2], 128, 1, 1.0 / D, ssA_s, rsA_s)
            op(DVE, lambda: nc.vector.tensor_scalar(out=xn_bf[:, :], in0=xt[:, :], scalar1=ssA[:, 1:2], scalar2=None, op0=ALU.mult),
               rd=[xt_s, rsA_s], wr=[xnbf_s])
            transposes(lambda j: xn_bf[:, j * 128:(j + 1) * 128], 8, 128, 0,
                       lambda b0, nb: xT[:, b0:b0 + nb, :], [xT_s], [xnbf_s], gainT=gmixT)

        def stageB(kb):
            xT, xT_s = xnTs[kb % 2]
            for kc in range(8):
                op(PE, lambda kc=kc: nc.tensor.matmul(PB[0][:, :], lhsT=xT[:, kc, :], rhs=wkvk[:, kc, 0:512], start=(kc == 0), stop=(kc == 7)),
                   rd=[xT_s, wkvk_s], wr=[PB_s[0]])
            for kc in range(8):
                op(PE, lambda kc=kc: nc.tensor.matmul(PB[1][:, 0:64], lhsT=xT[:, kc, :], rhs=wkvk[:, kc, 512:576], start=(kc == 0), stop=(kc == 7)),
                   rd=[xT_s, wkvk_s], wr=[PB_s[1]])
            op(ACT, lambda: nc.scalar.copy(out=k32[:, :], in_=PB[0][:, :]), rd=[PB_s[0]], wr=[k32_s])
            head_norm(k32[:, 0:256], n, 4, gkb, lambda: kn_bf[:, :].rearrange("p (h d) -> p h d", d=64), [knbf_s])
            op(ACT, lambda: nc.scalar.copy(out=v_bf[:, :], in_=k32[:, 256:512]), rd=[k32_s], wr=[vbf_s])
            for a in range(2):
                op(ACT, lambda a=a: nc.scalar.copy(out=kid_bf[:, a, :], in_=PB[1][:, 0:64]), rd=[PB_s[1]], wr=[kid_s])
            kv_append(kb, n, kn_bf, knbf_s, v_bf, vbf_s, kid_bf, kid_s)

        RS["act"] = True
        stageA(0)

        def prepass_iter(kb):
            xT, xT_s = xnTs[kb % 2]
            hasA = kb + 1 < NBLK
            k3 = lambda ap: ap.rearrange("p (h d) -> p h d", d=64)
            for kc in range(8):
                op(PE, lambda kc=kc: nc.tensor.matmul(PB[0][:, :], lhsT=xT[:, kc, :], rhs=wkvk[:, kc, 0:512], start=(kc == 0), stop=(kc == 7)),
                   rd=[xT_s, wkvk_s], wr=[PB_s[0]])
            for kc in range(8):
                op(PE, lambda kc=kc: nc.tensor.matmul(PB[1][:, 0:64], lhsT=xT[:, kc, :], rhs=wkvk[:, kc, 512:576], start=(kc == 0), stop=(kc == 7)),
                   rd=[xT_s, wkvk_s], wr=[PB_s[1]])
            op(ACT, lambda: nc.scalar.copy(out=k32[:, :], in_=PB[0][:, :]), rd=[PB_s[0]], wr=[k32_s])
            if hasA:
                xt, xt_s = FB[(kb + 1) % 2], FB_s[(kb + 1) % 2]
                xT2, xT2_s = xnTs[(kb + 1) % 2]
                dma(SP, xt[:, :], xb_d[(kb + 1) * 128:(kb + 2) * 128, :], wr=[xt_s])
                op(DVE, lambda: nc.vector.scalar_tensor_tensor(out=xn_bf[:, :], in0=xt[:, :], scalar=1.0, in1=xt[:, :], op0=ALU.mult, op1=ALU.mult,
                                                               accum_out=ssA[:, 0:1]), rd=[xt_s], wr=[xnbf_s, ssA_s])
                op(DVE, lambda: nc.vector.tensor_scalar(out=ssA[:, 0:1], in0=ssA[:, 0:1], scalar1=1.0 / D, scalar2=EPS, op0=ALU.mult, op1=ALU.add),
                   rd=[], wr=[ssA_s])
                op(ACT, lambda: nc.scalar.activation(out=ssA[:, 1:2], in_=ssA[:, 0:1], func=AF.Sqrt), rd=[ssA_s], wr=[rsA_s])
            op(DVE, lambda: nc.vector.tensor_tensor(out=o32[:, 0:256], in0=k32[:, 0:256], in1=k32[:, 0:256], op=ALU.mult), rd=[k32_s], wr=[o32_s])
            op(DVE, lambda: nc.vector.tensor_reduce(out=ss16[:, 0:4], in_=k3(o32[:, 0:256]), axis=AX.X, op=ALU.add), rd=[o32_s], wr=[ss16_s])
            op(DVE, lambda: nc.vector.tensor_scalar(out=ss16[:, 0:4], in0=ss16[:, 0:4], scalar1=1.0 / 64, scalar2=EPS, op0=ALU.mult, op1=ALU.add),
               rd=[], wr=[ss16_s])
            op(ACT, lambda: nc.scalar.activation(out=rs16[:, 0:4], in_=ss16[:, 0:4], func=AF.Sqrt), rd=[ss16_s], wr=[rs16_s])
            if hasA:
                op(DVE, lambda: nc.vector.reciprocal(out=ssA[:, 1:2], in_=ssA[:, 1:2]), rd=[], wr=[rsA_s])
                op(DVE, lambda: nc.vector.tensor_scalar(out=xn_bf[:, :], in0=xt[:, :], scalar1=ssA[:, 1:2], scalar2=None, op0=ALU.mult),
                   rd=[xt_s, rsA_s], wr=[xnbf_s])
                for j in range(8):
                    op(PE, lambda j=j: nc.tensor.transpose(PBb[2][:, j, :], xn_bf[:, j * 128:(j + 1) * 128], identb[:, :]),
                       rd=[xnbf_s, cst_s], wr=[PB_s[2]])
            op(DVE, lambda: nc.vector.reciprocal(out=rs16[:, 0:4], in_=rs16[:, 0:4]), rd=[], wr=[rs16_s])
            op(DVE, lambda: nc.vector.tensor_tensor(out=k3(o32[:, 0:256]), in0=k3(k32[:, 0:256]),
                                                    in1=rs16[:, 0:4].unsqueeze(2).to_broadcast([128, 4, 64]), op=ALU.mult),
               rd=[k32_s, rs16_s], wr=[o32_s])
            op(DVE, lambda: nc.vector.tensor_tensor(out=k3(kn_bf[:, :]), in0=k3(o32[:, 0:256]),
                                                    in1=gkb[:, :].unsqueeze(1).to_broadcast([128, 4, 64]), op=ALU.mult),
               rd=[o32_s, cst_s], wr=[knbf_s])
            op(ACT, lambda: nc.scalar.copy(out=v_bf[:, :], in_=k32[:, 256:512]), rd=[k32_s], wr=[vbf_s])
            for a_ in range(2):
                op(ACT, lambda a_=a_: nc.scalar.copy(out=kid_bf[:, a_, :], in_=PB[1][:, 0:64]), rd=[PB_s[1]], wr=[kid_s])
            kv_append(kb, 128, kn_bf, knbf_s, v_bf, vbf_s, kid_bf, kid_s)
            if hasA:
                op(DVE, lambda: nc.vector.tensor_tensor(out=xT2[:, :, :], in0=PBb[2][:, 0:8, :],
                                                        in1=gmixT[:, 0:8].unsqueeze(2).to_broadcast([128, 8, 128]), op=ALU.mult),
                   rd=[PB_s[2], cst_s], wr=[xT2_s])

        for kb in range(NBLK):
            prepass_iter(kb)
            for _ in range(4):
                if conv_jobs:
                    conv_jobs.pop(0)()
        fence([xnT2_s], arena_states)
        RS["act"] = False
        while conv_jobs:
            conv_jobs.pop(0)()
        for i in range(NDS):
            if dcnt[i] > 0:
                SP.wait((dsems[i], dcnt[i]))

        for i in range(NSLOT):
            m, second = i // 2, i % 2
            nch = 8 * m + (8 if second else 4)
            xt, xt_s = FB[0], FB_s[0]
            dma(SP, xt[:, :], xo_d[i, :, :], wr=[xt_s])
            project(n, xt[:, :], xt_s, {"k": ko_d[i, :, :], "v": vo_d[i, :, :], "ki": kio_d[i, :, :], "sg": sgo_d[i, :, :]})
            fence(arena_states, [sc_s])
            indexer(n, nch, scores, sc_s)
            S = nch * 128
            op(DVE, lambda: nc.vector.tensor_tensor(out=scores[:, S - 512:S], in0=scores[:, S - 512:S], in1=pen[:, second, :], op=ALU.add),
               rd=[cst_s], wr=[sc_s])
            fence([FB_s[1], FB_s[2]], [junk_s, junka_s])
            bisect(n, S, scores[:, 0:S], sc_s, junk8[:, 0:S], junk_s, junk_act=junki8)
            fence([junk_s, junka_s], [FB_s[1], FB_s[2]])
            oat, oat_s = FB[1], FB_s[1]
            attend(n, nch, scores, sc_s, I4[:, :, :], oat, oat_s)
            finish(n, xt[:, :], xt_s, oat, oat_s, po_d[i, :, :], yo_d[i, :, :])

        n = NS
        xs_t, xs_s = FB[0], FB_s[0]
        dma(SP, xs_t[0:n, :], xs_d[:, :], wr=[xs_s])
        project(n, xs_t[0:n, :], xs_s, {"k": kss_d[:, :], "v": vss_d[:, :], "ki": kis_d[:, :], "sg": sgs_d[:, :], "kv": True})
        for j in range(2):
            op(PE, lambda j=j: nc.tensor.transpose(PBb[3][:, j, 0:n], kn_bf[0:n, j * 128:(j + 1) * 128], identb[0:n, 0:n]),
               rd=[knbf_s, cst_s], wr=[PB_s[3]])
        op(PE, lambda: nc.tensor.transpose(PBb[3][:, 2, 0:n], kid_bf[0:n, :, :].rearrange("p a d -> p (a d)"), identb[0:n, 0:n]),
           rd=[kid_s, cst_s], wr=[PB_s[3]])
        op(ACT, lambda: nc.scalar.copy(out=ksT[:, :, :], in_=PBb[3][:, 0:2, 0:n]), rd=[PB_s[3]], wr=[smp_s])
        op(ACT, lambda: nc.scalar.copy(out=kisT[:, :], in_=PBb[3][:, 2, 0:n]), rd=[PB_s[3]], wr=[smp_s])

        def gather(dst_ap, src2d, col):
            return lambda: nc.gpsimd.indirect_dma_start(out=dst_ap, out_offset=None, in_=src2d,
                                                        in_offset=bass.IndirectOffsetOnAxis(ap=idxa[:, col:col + 1], axis=0))
        fence(arena_states, [sc16_s] + sctR_s)
        kiR_s = [St(), St()]; KR_s = [St(), St()]; VR_s = [St(), St()]
        fence([kiT_s], kiR_s); fence([KT_s], KR_s); fence([V_s], VR_s)
        op(POOL, lambda: nc.gpsimd.memset(sc16[:, :], -1e30), wr=[sc16_s])
        for r in range(2):
            op(POOL, lambda r=r: nc.gpsimd.memset(kiT[:, r * S_S + 2048:(r + 1) * S_S], 0.0), wr=[kiR_s[r]])
            op(POOL, lambda r=r: nc.gpsimd.memset(KT[:, :, r * S_S + 2048:(r + 1) * S_S], 0.0), wr=[KR_s[r]])
            op(POOL, lambda r=r: nc.gpsimd.memset(Vt[:, r * 17 + 16, :, 0:64], 0.0), wr=[VR_s[r]])
        def s1_loads(b):
            r = b % 2
            k0 = r * S_S
            jobsA, jobsB = [], []
            for pg in range(NPG):
                def jobA(pg=pg):
                    sl = (b * NPG + pg) % NPB
                    col = b * NPG + pg
                    dma(POOL, None, None, rd=[cst_s], wr=[pgI_s[sl]], fn=gather(pgI[sl][:, 0, :], cki_d, col))

                def jobB(pg=pg):
                    sl = (b * NPG + pg) % NPB
                    bk = 3
                    for a in range(2):
                        op(PE, lambda a=a: nc.tensor.transpose(PBb[bk][a * 64:(a + 1) * 64, 0, :], pgI[sl][:, 0, :], identb[:]),
                           rd=[pgI_s[sl], cst_s], wr=[PB_s[bk]])
                    op(ACT, lambda: nc.scalar.copy(out=kiT[:, k0 + pg * 128:k0 + (pg + 1) * 128], in_=PBb[bk][:, 0, :]),
                       rd=[PB_s[bk]], wr=[kiR_s[r]])
                jobsA.append(jobA)
                jobsB.append(jobB)
            jobs = skew(jobsA, jobsB)
            jobs.append(lambda: op(ACT, lambda: nc.scalar.copy(out=kiT[:, k0 + 2048:k0 + 2049], in_=kisT[:, b:b + 1]), rd=[smp_s], wr=[kiR_s[r]]))
            return jobs

        def skew(jobsA, jobsB, ahead=3):
            out = list(jobsA[:ahead])
            for p in range(len(jobsB)):
                out.append(jobsB[p])
                if p + ahead < len(jobsA):
                    out.append(jobsA[p + ahead])
            return out

        def make_hook(jobs, every):
            st = {"i": 0}

            def hook():
                st["i"] += 1
                if st["i"] % every == 0 and jobs:
                    jobs.pop(0)()
            return hook

        pending = s1_loads(0)
        for b in range(NS):
            r = b % 2
            k0 = r * S_S
            while pending:
                pending.pop(0)()
            pending = s1_loads(b + 1) if b + 1 < NS else []
            indexer(n, 17, sctR[r], sctR_s[r], k0=k0, ki_state=kiR_s[r], hook=make_hook(pending, 1))
            op(DVE, lambda b=b: nc.vector.scalar_tensor_tensor(out=sc16[:, 0:2049], in0=sctR[r][:, 0:2049], scalar=identf[0:16, b:b + 1],
                                                               in1=sc16[:, 0:2049], op0=ALU.mult, op1=ALU.add) if b > 0 else
               nc.vector.tensor_scalar(out=sc16[:, 0:2049], in0=sctR[r][:, 0:2049], scalar1=identf[0:16, 0:1], scalar2=None, op0=ALU.mult),
               rd=[sctR_s[r], cst_s], wr=[sc16_s])
        fence([sctR_s[1]], [sct_s])
        bisect(n, 2049, sc16[:, 0:2049], sc16_s, sct[:, 0:2049], sct_s)
        oat, oat_s = FB[1], FB_s[1]
        oatt_s16, oas_s = FB[2], FB_s[2]
        def s2_loads(b):
            r = b % 2
            k0 = r * S_S
            c0 = r * 17
            jobsA, jobsB = [], []
            for pg in range(NPG):
                def jobA(pg=pg):
                    sl = (b * NPG + pg) % NPB
                    col = b * NPG + pg
                    dma(POOL, None, None, rd=[cst_s], wr=[pgK_s[sl]], fn=gather(pgKV[sl][:, :], ckv_d, col))
                jobsA.append(jobA)

                def job(pg=pg):
                    sl = (b * NPG + pg) % NPB
                    bk = 0
                    for j in range(2):
                        op(PE, lambda j=j: nc.tensor.transpose(PBb[bk][:, j, :], pgK[sl][:, j * 128:(j + 1) * 128], identb[:]),
                           rd=[pgK_s[sl], cst_s], wr=[PB_s[bk]])
                    op(ACT, lambda: nc.scalar.copy(out=KT[:, :, k0 + pg * 128:k0 + (pg + 1) * 128], in_=PBb[bk][:, 0:2, :]),
                       rd=[PB_s[bk]], wr=[KR_s[r]])
                    op(DVE, lambda: nc.vector.tensor_copy(out=Vt[:, c0 + pg, :, 0:64], in_=pgV[sl][:, :].rearrange("p (c d) -> p c d", d=64)),
                       rd=[pgV_s[sl]], wr=[VR_s[r]])
                jobsB.append(job)
            jobs = skew(jobsA, jobsB)

            def last():
                op(ACT, lambda: nc.scalar.copy(out=KT[:, :, k0 + 2048:k0 + 2049], in_=ksT[:, :, b:b + 1]), rd=[smp_s], wr=[KR_s[r]])
                op(PE, lambda: nc.tensor.matmul(PB[0][0:1, 0:256], lhsT=identb[0:16, b:b + 1], rhs=vs_bf[0:16, :], start=True, stop=True),
                   rd=[smp_s, cst_s], wr=[PB_s[0]])
                op(ACT, lambda: nc.scalar.copy(out=Vt[0:1, c0 + 16, :, 0:64], in_=PB[0][0:1, 0:256].rearrange("p (c d) -> p c d", d=64)),
                   rd=[PB_s[0]], wr=[VR_s[r]])
            jobs.append(last)
            return jobs

        pending = s2_loads(0)
        for b in range(NS):
            r = b % 2
            c0 = r * 17
            while pending:
                pending.pop(0)()
            pending = s2_loads(b + 1) if b + 1 < NS else []
            attend_sample(17, sc16, sc16_s, oat, oat_s, c0, KR_s[r], VR_s[r], hook=make_hook(pending, 1))
            if b == 0:
                op(DVE, lambda: nc.vector.tensor_scalar(out=oatt_s16[0:16, :], in0=oat[0:16, :], scalar1=identf[0:16, 0:1], scalar2=None,
                                                        op0=ALU.mult), rd=[oat_s, cst_s], wr=[oas_s])
            else:
                op(DVE, lambda b=b: nc.vector.scalar_tensor_tensor(out=oatt_s16[0:16, :], in0=oat[0:16, :], scalar=identf[0:16, b:b + 1],
                                                                   in1=oatt_s16[0:16, :], op0=ALU.mult, op1=ALU.add),
                   rd=[oat_s, cst_s], wr=[oas_s])
        fence(sctR_s, [sct_s])
        fence([sc16_s, sct_s], arena_states)
        finish(n, xs_t[0:n, :], xs_s, oatt_s16, oas_s, ps_d[:, :], ys_d[:, :])

        for i in range(NDS):
            if dcnt[i] > 0:
                POOL.wait((dsems[i], dcnt[i]))
    return nc


_NC_CACHE = {}


def _slot_block(r, i):
    m, second = i // 2, i % 2
    return 8 * m + (7 - r if second else r)


def kernel(x_prompt, x_sample, cache_k, cache_v, cache_kidx, page_table, p_prompt, p_sample,
           g_mix, w_in, g_q, g_k, g_sgu, w_s, b_s, w_o, g_ffn, w_up, w_down, g_ple, w_pg, w_p):
    f32 = np.float32
    A = lambda a: np.ascontiguousarray(np.asarray(a))
    x_prompt = A(x_prompt); x_sample = A(x_sample); p_prompt = A(p_prompt); p_sample = A(p_sample)
    ck = A(cache_k).reshape(N_PHYS * 128, 256)
    cv = A(cache_v).reshape(N_PHYS * 128, 256)
    cki = A(cache_kidx).reshape(N_PHYS * 128, 64)
    pt_all = A(page_table).astype(np.int32)
    shared = {
        "ckv": np.concatenate([ck, cv], axis=1), "cki": cki,
        "w_in": A(w_in)[0], "w_o": A(w_o)[0], "w_up": A(w_up)[0], "w_down": A(w_down)[0], "w_pg": A(w_pg)[0], "w_p": A(w_p)[0],
        "g_mix": A(g_mix)[0], "g_q": A(g_q)[0], "g_k": A(g_k)[0], "g_sgu": A(g_sgu)[0], "w_s": A(w_s)[0], "b_s": A(b_s)[0],
        "g_ffn": A(g_ffn)[0], "g_ple": A(g_ple)[0],
    }
    tt = np.arange(128)[:, None]
    ss = np.arange(512)[None, :]
    in_maps = []
    for c in range(8):
        bi, r = c // 4, c % 4
        blocks = [_slot_block(r, i) for i in range(NSLOT)]
        pen = np.zeros((128, 2, 512), f32)
        pen[:, 0, :] = np.where(ss <= r * 128 + tt, 0.0, -1e30)
        pen[:, 1, :] = np.where(ss <= (3 - r) * 128 + tt, 0.0, -1e30)
        m = dict(shared)
        m["xb"] = x_prompt[bi]
        m["xo"] = np.stack([x_prompt[bi, j * 128:(j + 1) * 128] for j in blocks])
        m["po"] = np.stack([p_prompt[0, bi, j * 128:(j + 1) * 128] for j in blocks])
        m["xs"] = x_sample[c * NS:(c + 1) * NS, 0]
        m["ps"] = p_sample[0, c * NS:(c + 1) * NS, 0]
        m["pt"] = np.ascontiguousarray(pt_all[c * NS:(c + 1) * NS].reshape(-1))
        m["pen"] = pen
        in_maps.append(m)
    if "nc" not in _NC_CACHE:
        _NC_CACHE["nc"] = build()
    res = run_bass_kernel_spmd(_NC_CACHE["nc"], in_maps, core_ids=list(range(8)))
    R = res.results
    y_p = np.zeros((2, SEQ, D), f32); y_s = np.zeros((128, 1, D), f32)
    nk = np.zeros((1, 2, SEQ, 4, 64), f32); nv = np.zeros((1, 2, SEQ, 4, 64), f32)
    nki = np.zeros((1, 2, SEQ, 64), f32); nsg = np.zeros((1, 2, SEQ, D), f32)
    sk = np.zeros((1, 128, 1, 4, 64), f32); sv = np.zeros((1, 128, 1, 4, 64), f32)
    ski = np.zeros((1, 128, 1, 64), f32); ssg = np.zeros((1, 128, 1, D), f32)
    for c in range(8):
        bi, r = c // 4, c % 4
        o = R[c]
        for i in range(NSLOT):
            j = _slot_block(r, i)
            sl = slice(j * 128, (j + 1) * 128)
            y_p[bi, sl] = o["yo"][i]
            nk[0, bi, sl] = o["ko"][i].reshape(128, 4, 64)
            nv[0, bi, sl] = o["vo"][i].reshape(128, 4, 64)
            nki[0, bi, sl] = o["kio"][i]
            nsg[0, bi, sl] = o["sgo"][i]
        s2 = slice(c * NS, (c + 1) * NS)
        y_s[s2, 0] = o["ys"]
        sk[0, s2, 0] = o["kss"].reshape(NS, 4, 64)
        sv[0, s2, 0] = o["vss"].reshape(NS, 4, 64)
        ski[0, s2, 0] = o["kis"]
        ssg[0, s2, 0] = o["sgs"]
    return (y_p, y_s, nk, nv, nki, nsg, sk, sv, ski, ssg)
```
